# Optimizing a Trainium2 kernel written in Bass

```python
import math
import numpy as np
import jax
import jax.numpy as jnp
from jax import lax

D_MODEL = 2048
BATCH = 8
SEQ = 2048
DEPTH = 1

RWKV_HEAD_DIM = 64
RWKV_WIDTH = D_MODEL // 2
RWKV_HEADS = RWKV_WIDTH // RWKV_HEAD_DIM
DECAY_LORA = 64
ICLR_LORA = 64
GATE_LORA = 160
RWKV_GN_EPS = 64e-5

NSA_HEAD_DIM = 64
NSA_WIDTH = D_MODEL // 2
NSA_HEADS = NSA_WIDTH // NSA_HEAD_DIM
NSA_KV_HEADS = 4
NSA_GROUP = NSA_HEADS // NSA_KV_HEADS
NSA_KV_WIDTH = NSA_KV_HEADS * NSA_HEAD_DIM
CMP_BLOCK = 32
CMP_STRIDE = 16
SEL_BLOCK = 64
N_SEL = 8
WINDOW = 512
QUERY_BLOCK = 128

REL_BUCKETS = 32
REL_MAX_DIST = 128

D_FF = 4 * D_MODEL
NORM_EPS = 1e-6
NEG_INF = -1e30
FORCE_SCORE = 1e4

RWKV_COLS = 3 * RWKV_WIDTH + DECAY_LORA + ICLR_LORA + GATE_LORA
NSA_COLS = NSA_WIDTH + 6 * NSA_KV_WIDTH + 3 * NSA_HEADS
MERGE_COLS = 2 * D_MODEL
IN_COLS = RWKV_COLS + NSA_COLS + MERGE_COLS

kernel_name = 'rwkv7_nsa_hybrid_block'


def rms_norm(x, g, eps=NORM_EPS):
    xf = x.astype(jnp.float32)
    y = xf * lax.rsqrt(jnp.mean(xf * xf, axis=-1, keepdims=True) + eps)
    return (y * g.astype(jnp.float32)).astype(x.dtype)


def token_shift(p):
    return jnp.pad(p, ((0, 0), (1, 0), (0, 0)))[:, :-1]


def rel_bucket(rel):
    n = jnp.maximum(rel, 0)
    max_exact = REL_BUCKETS // 2
    nf = jnp.maximum(n, max_exact).astype(jnp.float32)
    large = max_exact + (jnp.log(nf / max_exact) / math.log(REL_MAX_DIST / max_exact)
                         * (REL_BUCKETS - max_exact)).astype(jnp.int32)
    large = jnp.minimum(large, REL_BUCKETS - 1)
    return jnp.where(n < max_exact, n, large)


def masked_softmax(s, mask):
    s = jnp.where(mask, s.astype(jnp.float32), NEG_INF)
    p = jax.nn.softmax(s, axis=-1)
    return jnp.where(mask, p, 0.0)


def cmp_to_sel_matrix(T):
    nc = T // CMP_STRIDE - CMP_BLOCK // CMP_STRIDE + 1
    ns = T // SEL_BLOCK
    cs = np.arange(nc) * CMP_STRIDE
    ss = np.arange(ns) * SEL_BLOCK
    lo = np.maximum(cs[:, None], ss[None, :])
    hi = np.minimum(cs[:, None] + CMP_BLOCK, ss[None, :] + SEL_BLOCK)
    return (np.maximum(hi - lo, 0) / CMP_BLOCK).astype(np.float32)


def compress(x, pe, w1, w2):
    B, T, G, hd = x.shape
    n_sub = CMP_BLOCK // CMP_STRIDE
    nc = T // CMP_STRIDE - n_sub + 1
    sub = x.reshape(B, T // CMP_STRIDE, CMP_STRIDE, G, hd)
    blocks = jnp.concatenate([sub[:, j:j + nc] for j in range(n_sub)], axis=2)
    blocks = blocks + pe[:, None, :]
    flat = jnp.moveaxis(blocks, 3, 2).reshape(B, nc, G, CMP_BLOCK * hd)
    return jax.nn.gelu(flat @ w1) @ w2


def rwkv7_mix(p, mu, w0, w2, a0, a2, g2, k_k, k_a, r_k, ln_w, ln_b):
    B, T, _ = p.shape
    H, N, C = RWKV_HEADS, RWKV_HEAD_DIM, RWKV_WIDTH
    p = (p + (token_shift(p) - p) * mu).astype(jnp.float32)
    cuts = np.cumsum([C, C, C, DECAY_LORA, ICLR_LORA]).tolist()
    r, k, v, xw, xa, xg = jnp.split(p, cuts, axis=-1)
    w = -jax.nn.softplus(-(w0 + jnp.tanh(xw) @ w2)) - 0.5
    a = jax.nn.sigmoid(a0 + xa @ a2)
    g = jax.nn.sigmoid(xg) @ g2
    heads = lambda z: z.reshape(B, T, H, N)
    kk = heads(k * k_k)
    kk = kk / jnp.maximum(jnp.sqrt(jnp.sum(kk * kk, axis=-1, keepdims=True)), 1e-12)
    k = heads(k * (1.0 + (a - 1.0) * k_a))
    r, v, a = heads(r), heads(v), heads(a)
    decay = jnp.exp(-jnp.exp(heads(w)))
    seq = tuple(jnp.moveaxis(z, 1, 0) for z in (r, decay, k, v, -kk, kk * a))

    def step(S, inp):
        r_t, w_t, k_t, v_t, a_t, b_t = inp
        sa = jnp.einsum('bhij,bhj->bhi', S, a_t)
        S = S * w_t[:, :, None, :] + sa[..., None] * b_t[:, :, None, :] + v_t[..., None] * k_t[:, :, None, :]
        return S, jnp.einsum('bhij,bhj->bhi', S, r_t)

    S0 = jnp.zeros((B, H, N, N), jnp.float32)
    _, y = lax.scan(step, S0, seq)
    y = jnp.moveaxis(y, 0, 1)
    mean = jnp.mean(y, axis=-1, keepdims=True)
    var = jnp.mean(jnp.square(y - mean), axis=-1, keepdims=True)
    y = ((y - mean) * lax.rsqrt(var + RWKV_GN_EPS)).reshape(B, T, C) * ln_w + ln_b
    bonus = (jnp.sum(r * k * r_k, axis=-1, keepdims=True) * v).reshape(B, T, C)
    return (y + bonus) * g


def nsa_mix(p, pe_k, w1_k, w2_k, pe_v, w1_v, w2_v, q_g, k_g, rel_bias):
    B, T, _ = p.shape
    G, Hg, hd = NSA_KV_HEADS, NSA_GROUP, NSA_HEAD_DIM
    QB = QUERY_BLOCK
    q = p[..., :NSA_WIDTH].reshape(B, T, G, Hg, hd)
    kv = p[..., NSA_WIDTH:NSA_WIDTH + 6 * NSA_KV_WIDTH].reshape(B, T, 6, G, hd)
    gates = jax.nn.sigmoid(p[..., NSA_WIDTH + 6 * NSA_KV_WIDTH:]).reshape(B, T, G, Hg, 3)
    q = rms_norm(q, q_g) * (hd ** -0.5)
    ns = T // SEL_BLOCK
    kc = rms_norm(compress(kv[:, :, 0], pe_k, w1_k, w2_k), k_g[0])
    vc = compress(kv[:, :, 1], pe_v, w1_v, w2_v)
    nc = kc.shape[1]
    ks = rms_norm(kv[:, :, 2], k_g[1]).reshape(B, ns, SEL_BLOCK, G, hd).transpose(0, 3, 1, 2, 4)
    vs = kv[:, :, 3].reshape(B, ns, SEL_BLOCK, G, hd).transpose(0, 3, 1, 2, 4)
    pad = ((0, 0), (WINDOW, 0), (0, 0), (0, 0))
    kw = jnp.pad(rms_norm(kv[:, :, 4], k_g[2]), pad)
    vw = jnp.pad(kv[:, :, 5], pad)
    table = rel_bias.reshape(REL_BUCKETS, G, Hg)
    table_g = jnp.transpose(table, (1, 0, 2))
    cmp_end = jnp.arange(nc) * CMP_STRIDE + CMP_BLOCK - 1
    sel_m = jnp.asarray(cmp_to_sel_matrix(T))
    k_sel = min(N_SEL, ns)
    b_ix = jnp.arange(B)[:, None, None, None]
    g_ix = jnp.arange(G)[None, :, None, None]
    blk = jnp.arange(ns)

    def query_block(i):
        t0 = i * QB
        t = t0 + jnp.arange(QB)
        qb = lax.dynamic_slice_in_dim(q, t0, QB, axis=1)
        gb = lax.dynamic_slice_in_dim(gates, t0, QB, axis=1)
        rel_c = t[:, None] - cmp_end[None, :]
        bias_c = jnp.transpose(table[rel_bucket(rel_c)], (2, 3, 0, 1))
        s_c = jnp.einsum('btghd,bngd->bghtn', qb, kc) + bias_c
        p_c = masked_softmax(s_c, rel_c >= 0)
        o_c = jnp.einsum('bghtn,bngd->btghd', p_c.astype(vc.dtype), vc)
        imp = jnp.einsum('bghtn,nj->bgtj', p_c, sel_m)
        cur = t[:, None] // SEL_BLOCK
        allowed = blk[None, :] <= cur
        forced = (blk[None, :] == 0) | (blk[None, :] == cur) | (blk[None, :] == cur - 1)
        score = jnp.where(forced, FORCE_SCORE, jnp.where(allowed, imp, -1.0))
        _, idx = lax.top_k(score, k_sel)
        kg = ks[b_ix, g_ix, idx]
        vg = vs[b_ix, g_ix, idx]
        pos = idx[..., None] * SEL_BLOCK + jnp.arange(SEL_BLOCK)
        rel_s = t[None, None, :, None, None] - pos
        bias_s = jnp.moveaxis(table_g[g_ix[..., None], rel_bucket(rel_s)], -1, 2)
        s_s = jnp.einsum('btghd,bgtksd->bghtks', qb, kg) + bias_s
        mask_s = (rel_s >= 0)[:, :, None].reshape(B, G, 1, QB, k_sel * SEL_BLOCK)
        p_s = masked_softmax(s_s.reshape(B, G, Hg, QB, k_sel * SEL_BLOCK), mask_s)
        p_s = p_s.reshape(B, G, Hg, QB, k_sel, SEL_BLOCK).astype(vg.dtype)
        o_s = jnp.einsum('bghtks,bgtksd->btghd', p_s, vg)
        kwb = lax.dynamic_slice_in_dim(kw, t0, QB + WINDOW, axis=1)
        vwb = lax.dynamic_slice_in_dim(vw, t0, QB + WINDOW, axis=1)
        kpos = t0 - WINDOW + jnp.arange(QB + WINDOW)
        rel_w = t[:, None] - kpos[None, :]
        mask_w = (rel_w >= 0) & (rel_w < WINDOW) & (kpos[None, :] >= 0)
        bias_w = jnp.transpose(table[rel_bucket(rel_w)], (2, 3, 0, 1))
        s_w = jnp.einsum('btghd,bsgd->bghts', qb, kwb) + bias_w
        p_w = masked_softmax(s_w, mask_w).astype(vwb.dtype)
        o_w = jnp.einsum('bghts,bsgd->btghd', p_w, vwb)
        return gb[..., 0:1] * o_c + gb[..., 1:2] * o_s + gb[..., 2:3] * o_w

    outs = lax.map(query_block, jnp.arange(T // QB))
    return jnp.moveaxis(outs, 0, 1).reshape(B, T, NSA_WIDTH)


def setup_inputs(seed: int = 0) -> dict:
    key = jax.random.key(seed)
    keys = iter(jax.random.split(key, 40))
    nrm = lambda shape, scale: scale * jax.random.normal(next(keys), shape, jnp.float32)
    L, D, hd = DEPTH, D_MODEL, NSA_HEAD_DIM
    return {
        'x': nrm((BATCH, SEQ, D), 1.0),
        'c': nrm((BATCH, D), 1.0),
        'w_ada': nrm((L, D, 6 * D), 0.5 * D ** -0.5),
        'b_ada': nrm((L, 6 * D), 0.01),
        'norm1_g': 1.0 + nrm((L, D), 0.02),
        'norm2_g': 1.0 + nrm((L, D), 0.02),
        'w_in': nrm((L, D, IN_COLS), D ** -0.5),
        'rwkv_mu': jax.random.uniform(next(keys), (L, RWKV_COLS), jnp.float32),
        'rwkv_w0': jax.random.uniform(next(keys), (L, RWKV_WIDTH), jnp.float32, -6.0, -1.0),
        'rwkv_w2': nrm((L, DECAY_LORA, RWKV_WIDTH), 0.1 * DECAY_LORA ** -0.5),
        'rwkv_a0': nrm((L, RWKV_WIDTH), 0.1),
        'rwkv_a2': nrm((L, ICLR_LORA, RWKV_WIDTH), 0.1 * ICLR_LORA ** -0.5),
        'rwkv_g2': nrm((L, GATE_LORA, RWKV_WIDTH), GATE_LORA ** -0.5),
        'rwkv_k_k': 0.85 + nrm((L, RWKV_WIDTH), 0.05),
        'rwkv_k_a': 1.0 + nrm((L, RWKV_WIDTH), 0.05),
        'rwkv_r_k': nrm((L, RWKV_HEADS, RWKV_HEAD_DIM), 0.1),
        'rwkv_ln_w': 1.0 + nrm((L, RWKV_WIDTH), 0.02),
        'rwkv_ln_b': nrm((L, RWKV_WIDTH), 0.01),
        'cmp_pe_k': nrm((L, CMP_BLOCK, hd), 0.02),
        'cmp_w1_k': nrm((L, CMP_BLOCK * hd, hd), (CMP_BLOCK * hd) ** -0.5),
        'cmp_w2_k': nrm((L, hd, hd), hd ** -0.5),
        'cmp_pe_v': nrm((L, CMP_BLOCK, hd), 0.02),
        'cmp_w1_v': nrm((L, CMP_BLOCK * hd, hd), (CMP_BLOCK * hd) ** -0.5),
        'cmp_w2_v': nrm((L, hd, hd), hd ** -0.5),
        'q_norm_g': 1.0 + nrm((L, hd), 0.02),
        'k_norm_g': 1.0 + nrm((L, 3, hd), 0.02),
        'rel_bias': nrm((REL_BUCKETS, NSA_HEADS), 0.5),
        'w_o_rwkv': nrm((L, RWKV_WIDTH, D), RWKV_WIDTH ** -0.5),
        'w_o_nsa': nrm((L, NSA_WIDTH, D), NSA_WIDTH ** -0.5),
        'w_out': nrm((L, D, D), D ** -0.5),
        'w_up': nrm((L, D, D_FF), D ** -0.5),
        'w_down': nrm((L, D_FF, D), D_FF ** -0.5),
    }


def reference(x, c, w_ada, b_ada, norm1_g, norm2_g, w_in, rwkv_mu, rwkv_w0, rwkv_w2, rwkv_a0, rwkv_a2,
              rwkv_g2, rwkv_k_k, rwkv_k_a, rwkv_r_k, rwkv_ln_w, rwkv_ln_b, cmp_pe_k, cmp_w1_k, cmp_w2_k,
              cmp_pe_v, cmp_w1_v, cmp_w2_v, q_norm_g, k_norm_g, rel_bias, w_o_rwkv, w_o_nsa, w_out,
              w_up, w_down):
    D = D_MODEL
    for l in range(DEPTH):
        mod = jnp.einsum('bd,de->be', jax.nn.silu(c), w_ada[l]) + b_ada[l]
        sh1, sc1, gt1, sh2, sc2, gt2 = jnp.split(mod[:, None, :], 6, axis=-1)
        h = rms_norm(x, norm1_g[l]) * (1.0 + sc1) + sh1
        proj = jnp.einsum('btd,dc->btc', h, w_in[l])
        p_rwkv = proj[..., :RWKV_COLS]
        p_nsa = proj[..., RWKV_COLS:RWKV_COLS + NSA_COLS]
        merge_g = jax.nn.sigmoid(proj[..., RWKV_COLS + NSA_COLS:])
        o_a = rwkv7_mix(p_rwkv, rwkv_mu[l], rwkv_w0[l], rwkv_w2[l], rwkv_a0[l], rwkv_a2[l], rwkv_g2[l],
                        rwkv_k_k[l], rwkv_k_a[l], rwkv_r_k[l], rwkv_ln_w[l], rwkv_ln_b[l]).astype(x.dtype)
        o_b = nsa_mix(p_nsa, cmp_pe_k[l], cmp_w1_k[l], cmp_w2_k[l], cmp_pe_v[l], cmp_w1_v[l], cmp_w2_v[l],
                      q_norm_g[l], k_norm_g[l], rel_bias)
        y_a = o_a @ w_o_rwkv[l]
        y_b = o_b @ w_o_nsa[l]
        mixed = merge_g[..., :D] * y_a + merge_g[..., D:] * y_b
        x = x + gt1 * (mixed @ w_out[l])
        h = rms_norm(x, norm2_g[l]) * (1.0 + sc2) + sh2
        x = x + gt2 * (jnp.square(jax.nn.relu(h @ w_up[l])) @ w_down[l])
    return x
```

```python
import numpy as np
import concourse.bass as bass
import concourse.mybir as mybir

F32 = mybir.dt.float32
BF16 = mybir.dt.bfloat16
AF = mybir.ActivationFunctionType
ALU = mybir.AluOpType
AX = mybir.AxisListType

ENGS = ("pe", "act", "dve", "pool", "sp")
NSLOT = {"sp": 40, "pool": 24}


class Buf:
    __slots__ = ("w", "r_eng", "r_dma", "name")

    def __init__(self, name=""):
        self.w = None
        self.r_eng = {}
        self.r_dma = []
        self.name = name


class T(Buf):
    __slots__ = ("ap", "subs")

    def __init__(self, ap, name=""):
        Buf.__init__(self, name)
        self.ap = ap
        self.subs = {}

    def __getitem__(self, idx):
        return self.ap[idx]

    def sub(self, key):
        b = self.subs.get(key)
        if b is None:
            b = self.subs[key] = Buf(f"{self.name}.{key}")
        return b


class Op:
    __slots__ = ("eng", "fn", "deps", "is_dma", "slot", "sig", "val", "dsem", "dval", "idx")

    def __init__(self, eng, fn, is_dma):
        self.eng = eng
        self.fn = fn
        self.is_dma = is_dma
        self.deps = []
        self.sig = False
        self.val = 0
        self.slot = -1
        self.dsem = None
        self.dval = 0


class Sched:
    def __init__(self, nc):
        self.nc = nc
        self.ops = {e: [] for e in ENGS}
        self.bar = {e: [] for e in ENGS}
        self.dma_since_bar = []
        self.slot_last = {q: [None] * n for q, n in NSLOT.items()}
        self.slot_n = {q: 0 for q in NSLOT}

    def rec(self, eng, fn, reads=(), writes=(), is_dma=False):
        op = Op(eng, fn, is_dma)
        deps = []
        for b in reads:
            if b.w is not None:
                deps.append(b.w)
        for b in writes:
            if b.w is not None:
                deps.append(b.w)
            deps.extend(b.r_eng.values())
            deps.extend(b.r_dma)
        if self.bar[eng]:
            deps.extend(self.bar[eng])
            self.bar[eng] = []
        if is_dma:
            n = self.slot_n[eng]
            self.slot_n[eng] = n + 1
            s = n % NSLOT[eng]
            op.slot = s
            prev = self.slot_last[eng][s]
            if prev is not None:
                deps.append(prev)
            self.slot_last[eng][s] = op
            self.dma_since_bar.append(op)
        seen = set()
        for d in deps:
            if d is op or id(d) in seen:
                continue
            seen.add(id(d))
            op.deps.append(d)
        for b in reads:
            if is_dma:
                b.r_dma.append(op)
            else:
                b.r_eng[eng] = op
        for b in writes:
            b.w = op
            b.r_eng = {}
            b.r_dma = []
        self.ops[eng].append(op)
        return op

    def barrier(self):
        deps = [self.ops[e][-1] for e in ENGS if self.ops[e]] + self.dma_since_bar
        self.dma_since_bar = []
        for e in ENGS:
            self.bar[e] = list(deps)

    def mm(self, out, lhsT, rhs, start, stop, reads, writes, **kw):
        return self.rec("pe", lambda e: e.matmul(out, lhsT, rhs, start=start, stop=stop, **kw), reads, writes)

    def tr(self, out, in_, ident, reads, writes):
        return self.rec("pe", lambda e: e.transpose(out, in_, ident), reads, writes)

    def act(self, out, in_, func, reads, writes, bias=None, scale=None, accum_out=None):
        kw = {}
        if bias is not None:
            kw["bias"] = bias
        if scale is not None:
            kw["scale"] = scale
        if accum_out is not None:
            kw["accum_out"] = accum_out
        return self.rec("act", lambda e: e.activation(out=out, in_=in_, func=func, **kw), reads, writes)

    def v(self, eng, meth, reads, writes, *a, **kw):
        return self.rec(eng, lambda e: getattr(e, meth)(*a, **kw), reads, writes)

    def dma(self, q, out, in_, reads, writes, **kw):
        return self.rec(q, lambda e: e.dma_start(out=out, in_=in_, **kw), reads, writes, is_dma=True)

    def finalize(self):
        for e in ENGS:
            for op in self.ops[e]:
                for d in op.deps:
                    if d.is_dma:
                        continue
                    if d.eng == "pe" and op.eng == "pe" and not op.is_dma:
                        continue
                    d.sig = True
        for e in ENGS:
            n = 0
            for op in self.ops[e]:
                if op.sig and not op.is_dma:
                    n += 1
                    op.val = n

    def emit_all(self, stack):
        nc = self.nc
        self.finalize()
        self.esem = {e: stack.enter_context(nc.semaphore("es_" + e)) for e in ENGS}
        self.dsem = {q: [stack.enter_context(nc.semaphore(f"ds_{q}{i}")) for i in range(n)] for q, n in NSLOT.items()}
        uses = {q: [0] * n for q, n in NSLOT.items()}
        for q in NSLOT:
            for op in self.ops[q]:
                if op.is_dma:
                    uses[q][op.slot] += 1
                    op.dsem = self.dsem[q][op.slot]
                    op.dval = 16 * uses[q][op.slot]
        block = stack.enter_context(nc.Block())
        sched = self

        def emit(name, eng):
            known = {}
            for op in sched.ops[name]:
                for d in op.deps:
                    if d.is_dma:
                        sem, val = d.dsem, d.dval
                    else:
                        if d.eng == "pe" and name == "pe" and not op.is_dma:
                            continue
                        sem, val = sched.esem[d.eng], d.val
                    k = id(sem)
                    if known.get(k, 0) >= val:
                        continue
                    eng.wait_ge(sem, val)
                    known[k] = val
                ins = op.fn(eng)
                if op.is_dma:
                    ins.then_inc(op.dsem, 16)
                elif op.sig:
                    ins.then_inc(sched.esem[name], 1)
            if name == "sp":
                for q in NSLOT:
                    for i, u in enumerate(uses[q]):
                        if u:
                            eng.wait_ge(sched.dsem[q][i], 16 * u)

        @block.tensor
        def _(e):
            emit("pe", e)

        @block.scalar
        def _(e):
            emit("act", e)

        @block.vector
        def _(e):
            emit("dve", e)

        @block.gpsimd
        def _(e):
            emit("pool", e)

        @block.sync
        def _(e):
            emit("sp", e)


class Arena:
    def __init__(self, ap, nwords):
        self.ap = ap
        self.n = nwords
        self.off = 0
        self.marks = []

    def push(self):
        self.marks.append(self.off)

    def pop(self):
        self.off = self.marks.pop()

    def alloc(self, shape, dtype=F32, name="", parts=128):
        n = int(np.prod(shape))
        words = n if dtype == F32 else (n + 1) // 2
        words = (words + 7) // 8 * 8
        assert self.off + words <= self.n, f"arena overflow {name} {self.off}+{words}>{self.n}"
        ap = self.ap[0:parts, self.off:self.off + words]
        self.off += words
        if dtype != F32:
            ap = ap.bitcast(dtype)
        ap = ap[:, 0:n]
        if len(shape) == 2:
            ap = ap.rearrange("p (a b) -> p a b", a=shape[0])
        elif len(shape) == 3:
            ap = ap.rearrange("p (a b c) -> p a b c", a=shape[0], b=shape[1])
        elif len(shape) == 4:
            ap = ap.rearrange("p (a b c d) -> p a b c d", a=shape[0], b=shape[1], c=shape[2])
        return T(ap, name)

from contextlib import ExitStack
from concourse.bass_utils import run_bass_kernel_spmd

D = 2048
SEQ = 2048
NT = 16
RWC = 3360
NB = 3360
MB = 5968
INC = 10064
DFF = 8192
EPS = 1e-6
GN_EPS = 64e-5
NEG = -30000.0
ARENA_WORDS = 52000


class KB:
    def __init__(self, dbg=False, stages=(0, 1, 2, 3, 4, 5, 6)):
        self.nc = bass.Bass("TRN2", target_bir_lowering=False)
        self.S = Sched(self.nc)
        self.dbg = dbg
        self.stages = stages
        self.bank_i = 0

    def din(self, name, shape, dt=F32):
        return T(self.nc.dram_tensor(name, list(shape), dt, kind="ExternalInput").ap(), name)

    def dscr(self, name, shape, dt=F32, out=False):
        kind = "ExternalOutput" if (self.dbg or out) else "Internal"
        return T(self.nc.dram_tensor(name, list(shape), dt, kind=kind).ap(), name)

    def bank(self):
        b = self.ps[self.bank_i % 8]
        self.bank_i += 1
        return b

    def load(self, dst, src, q="sp"):
        self.S.dma(q, dst.ap, src[1], [src[0]], [dst])

    def build(self):
        nc, S = self.nc, self.S
        with ExitStack() as st:
            arena_t = st.enter_context(nc.sbuf_tensor("arena", [128, ARENA_WORDS], F32))
            self.ar = ar = Arena(arena_t, ARENA_WORDS)
            self.ps = [T(st.enter_context(nc.psum_tensor(f"ps{i}", [128, 512], F32))[:], f"ps{i}") for i in range(8)]
            self.declare()
            self.persistent()
            if 0 in self.stages:
                self.stage0()
            if 1 in self.stages:
                ar.push()
                self.hT = ar.alloc([16, SEQ], BF16, "hT")
                self.stage1(self.x_d, self.coef1, self.sh1, self.hT)
                if 2 in self.stages:
                    self.stage2()
                ar.pop()
            if 4 in self.stages:
                self.stage4()
            if 3 in self.stages:
                self.stage3()
            if 5 in self.stages:
                self.stage5()
            if 6 in self.stages:
                ar.push()
                self.hT = ar.alloc([16, SEQ], BF16, "h2T")
                self.stage1(self.x1_d, self.coef2, self.sh2, self.hT)
                self.stage6()
                ar.pop()
            S.emit_all(st)
        return nc

    def declare(self):
        d = self.din
        self.x_d = d("x", [SEQ, D])
        self.c_fm = d("c_fm", [128, 16])
        self.w_ada = d("w_ada", [D, 6 * D])
        self.b_ada = d("b_ada", [1, 6 * D])
        self.n1g = d("n1g_fm", [128, 16])
        self.n2g = d("n2g_fm", [128, 16])
        self.w_in = d("w_in", [D, INC])
        self.ident_d = d("ident", [128, 128])
        self.bones_d = d("bones", [128, 128])
        self.mu_d = d("mu_fm", [128, 27])
        self.qkg_d = d("qkg_fm", [128, 5])
        s = self.dscr
        self.rwT_d = s("rwT", [27 * 128, SEQ])
        self.qT_d = s("qT", [1024, SEQ], BF16)
        self.kvcT_d = s("kvcT", [512, SEQ], BF16)
        self.ksT_d = s("ksT", [4, 2, 128, SEQ], BF16)
        self.kwT_d = s("kwT", [4, 2, 128, SEQ], BF16)
        self.vv_d = s("vv", [SEQ, 512], BF16)
        self.gates_d = s("gates", [SEQ, 48])
        self.mgT_d = s("mgT", [4096, SEQ], BF16)
        self.x1_d = s("x1", [SEQ, D])
        self.out_d = self.dscr("out", [SEQ, D], out=True)

    def persistent(self):
        ar, S = self.ar, self.S
        self.ident = ar.alloc([128], F32, "ident")
        self.bones = ar.alloc([128], F32, "bones")
        S.dma("sp", self.ident.ap, self.ident_d.ap, [self.ident_d], [self.ident])
        S.dma("sp", self.bones.ap, self.bones_d.ap, [self.bones_d], [self.bones])
        self.coef1 = ar.alloc([16], F32, "coef1")
        self.sh1 = ar.alloc([16], F32, "sh1")
        self.coef2 = ar.alloc([16], F32, "coef2")
        self.sh2 = ar.alloc([16], F32, "sh2")
        self.gt1 = ar.alloc([D], F32, "gt1")
        self.gt2 = ar.alloc([D], F32, "gt2")

    def silu_rep(self):
        ar, S = self.ar, self.S
        cs = ar.alloc([16], F32, "cs")
        S.dma("sp", cs.ap, self.c_fm.ap, [self.c_fm], [cs])
        csb = ar.alloc([16], F32, "csb")
        S.act(csb.ap, cs.ap, AF.Silu, [cs], [csb])
        crep = ar.alloc([16, 128], BF16, "crep")
        S.v("dve", "tensor_copy", [csb], [crep], crep.ap, csb.ap.unsqueeze(2).to_broadcast([128, 16, 128]))
        return crep

    def stage0(self):
        ar, S = self.ar, self.S
        ar.push()
        bias_steps = self.nsa_bias_build() if 4 in self.stages else []
        mod = ar.alloc([6 * D], F32, "mod")
        crep = self.silu_rep()
        wbs = [ar.alloc([16, 512], BF16, f"wada{i}") for i in range(2)]
        bbs = [ar.alloc([512], F32, f"bada{i}") for i in range(2)]
        wsrc = self.w_ada.ap.rearrange("(k p) c -> p k c", p=128)
        for blk in range(24):
            wb, bb = wbs[blk % 2], bbs[blk % 2]
            c0 = blk * 512
            S.dma("pool", wb.ap, wsrc[:, :, c0:c0 + 512], [self.w_ada], [wb])
            S.dma("sp", bb.ap, self.b_ada.ap[0:1, c0:c0 + 512].partition_broadcast(128), [self.b_ada], [bb])
            P = self.bank()
            for k in range(16):
                S.mm(P.ap, crep.ap[:, k, :], wb.ap[:, k, :], k == 0, k == 15, [crep, wb], [P])
            S.v("dve", "tensor_tensor", [P, bb], [mod], mod.ap[:, c0:c0 + 512], P.ap, bb.ap, ALU.add)
            for _ in range(2):
                if bias_steps:
                    bias_steps.pop(0)()
        while bias_steps:
            bias_steps.pop(0)()
        tmp = ar.alloc([16, 128], F32, "dtmp")
        sc1 = ar.alloc([16], F32, "sc1")
        sc2 = ar.alloc([16], F32, "sc2")
        for dst, idx in ((self.sh1, 0), (sc1, 1), (self.sh2, 3), (sc2, 4)):
            src = mod.ap[:, idx * D:(idx + 1) * D].rearrange("p (k m) -> p k m", k=16)
            S.v("dve", "tensor_tensor", [mod, self.ident], [tmp], tmp.ap, src,
                self.ident.ap.unsqueeze(1).to_broadcast([128, 16, 128]), ALU.mult)
            S.v("dve", "tensor_reduce", [tmp], [dst], dst.ap, tmp.ap, AX.X, ALU.add)
        g = ar.alloc([16], F32, "gload")
        S.dma("sp", g.ap, self.n1g.ap, [self.n1g], [g])
        S.v("dve", "scalar_tensor_tensor", [sc1, g], [self.coef1], self.coef1.ap, sc1.ap, 1.0, g.ap, ALU.add, ALU.mult)
        g2 = ar.alloc([16], F32, "gload2")
        S.dma("sp", g2.ap, self.n2g.ap, [self.n2g], [g2])
        S.v("dve", "scalar_tensor_tensor", [sc2, g2], [self.coef2], self.coef2.ap, sc2.ap, 1.0, g2.ap, ALU.add, ALU.mult)
        S.v("dve", "tensor_copy", [mod], [self.gt1], self.gt1.ap, mod.ap[:, 2 * D:3 * D])
        S.v("dve", "tensor_copy", [mod], [self.gt2], self.gt2.ap, mod.ap[:, 5 * D:6 * D])
        S.barrier()
        ar.pop()

    def stage0b_setup(self):
        ar, S = self.ar, self.S
        crep = self.silu_rep()
        wb2 = [ar.alloc([16, 256], BF16, f"wada_b{i}") for i in range(2)]
        bb2 = [ar.alloc([256], F32, f"bada_b{i}") for i in range(2)]
        tmp = ar.alloc([2, 128], F32, "dtmp_b")
        sc2 = ar.alloc([16], F32, "sc2")
        wsrc = self.w_ada.ap.rearrange("(k p) c -> p k c", p=128)
        steps = []

        def mk(sb):
            def step():
                wb, bb = wb2[sb % 2], bb2[sb % 2]
                c0 = 3 * D + sb * 256
                S.dma("pool", wb.ap, wsrc[:, :, c0:c0 + 256], [self.w_ada], [wb])
                S.dma("sp", bb.ap, self.b_ada.ap[0:1, c0:c0 + 256].partition_broadcast(128), [self.b_ada], [bb])
                P = self.bank()
                for k in range(16):
                    S.mm(P.ap[:, 0:256], crep.ap[:, k, :], wb.ap[:, k, :], k == 0, k == 15, [crep, wb], [P])
                if sb < 16:
                    dst = self.sh2 if sb < 8 else sc2
                    j = sb % 8
                    t2 = tmp.ap.rearrange("p a b -> p (a b)")
                    S.v("dve", "tensor_tensor", [P, bb], [tmp], t2, P.ap[:, 0:256], bb.ap, ALU.add)
                    S.v("dve", "tensor_tensor", [tmp, self.ident], [tmp], tmp.ap, tmp.ap,
                        self.ident.ap.unsqueeze(1).to_broadcast([128, 2, 128]), ALU.mult)
                    S.v("dve", "tensor_reduce", [tmp], [dst], dst.ap[:, 2 * j:2 * j + 2], tmp.ap, AX.X, ALU.add)
                else:
                    o = (sb - 16) * 256
                    S.v("dve", "tensor_tensor", [P, bb], [self.gt2], self.gt2.ap[:, o:o + 256], P.ap[:, 0:256], bb.ap, ALU.add)
                if sb == 23:
                    g = ar.alloc([16], F32, "gload2")
                    S.dma("sp", g.ap, self.n2g.ap, [self.n2g], [g])
                    S.v("dve", "scalar_tensor_tensor", [sc2, g], [self.coef2], self.coef2.ap, sc2.ap, 1.0, g.ap, ALU.add, ALU.mult)
            return step
        return [mk(sb) for sb in range(24)]

    def stage1(self, src_d, coef, sh, hT):
        ar, S = self.ar, self.S
        ar.push()
        xbs = [ar.alloc([D], F32, f"xb{i}") for i in range(2)]
        junk = ar.alloc([D], F32, "junk")
        xs4s = [ar.alloc([4, D], F32, f"xs4_{i}") for i in range(1)]
        ss = ar.alloc([NT], F32, "ss")
        sr = ar.alloc([NT], F32, "sr")
        rstd = ar.alloc([NT], F32, "rstd")
        for grp in range(4):
            xs4 = xs4s[0]
            for tt in range(4):
                ti = grp * 4 + tt
                xb = xbs[ti % 2]
                S.dma("sp", xb.ap, src_d.ap[ti * 128:(ti + 1) * 128, :], [src_d], [xb])
                sst = ss.sub(ti)
                S.act(junk.ap, xb.ap, AF.Square, [xb], [sst], accum_out=ss.ap[:, ti:ti + 1])
                S.act(sr.ap[:, ti:ti + 1], ss.ap[:, ti:ti + 1], AF.Sqrt, [sst], [sr.sub(ti)], bias=EPS, scale=1.0 / D)
                S.v("dve", "reciprocal", [sr.sub(ti)], [rstd.sub(ti)], rstd.ap[:, ti:ti + 1], sr.ap[:, ti:ti + 1])
                S.v("dve", "tensor_scalar", [xb, rstd.sub(ti)], [xs4.sub(tt)], xs4.ap[:, tt, :], xb.ap,
                    rstd.ap[:, ti:ti + 1], None, ALU.mult)
            for k in range(16):
                P = self.bank()
                for tt in range(4):
                    S.tr(P.ap[:, tt * 128:(tt + 1) * 128], xs4.ap[:, tt, k * 128:(k + 1) * 128], self.ident.ap,
                         [xs4.sub(tt), self.ident], [P])
                S.act(hT.ap[:, k, grp * 512:(grp + 1) * 512], P.ap, AF.Identity, [P, coef, sh], [hT.sub(grp)],
                      bias=sh.ap[:, k:k + 1], scale=coef.ap[:, k:k + 1])
        S.barrier()
        ar.pop()

    def stage2(self):
        ar, S = self.ar, self.S
        hT = self.hT
        ar.push()
        wbs = [ar.alloc([16, 512], BF16, f"win{i}") for i in range(2)]
        wtm = ar.alloc([16, 560], BF16, "wtm")
        raws = [ar.alloc([2056], F32, f"raw{i}") for i in range(2)]
        tmps = [ar.alloc([SEQ], F32, f"mixt{i}") for i in range(2)]
        stg = [ar.alloc([SEQ], BF16, f"stg{i}") for i in range(4)]
        sqs = [ar.alloc([512], F32, f"sq{i}") for i in range(2)]
        srs = [ar.alloc([512], F32, f"sr{i}") for i in range(2)]
        ris = [ar.alloc([512], F32, f"ri{i}") for i in range(2)]
        mu = ar.alloc([27], F32, "mu")
        omu = ar.alloc([27], F32, "omu")
        qkg = ar.alloc([5], F32, "qkg")
        S.dma("sp", mu.ap, self.mu_d.ap, [self.mu_d], [mu])
        S.dma("sp", qkg.ap, self.qkg_d.ap, [self.qkg_d], [qkg])
        S.v("dve", "tensor_scalar", [mu], [omu], omu.ap, mu.ap, -1.0, 1.0, ALU.mult, ALU.add)
        S.v("dve", "tensor_scalar", [qkg], [qkg], qkg.ap[:, 1:5], qkg.ap[:, 1:5], 8.0, None, ALU.mult)
        for r in raws:
            S.v("dve", "memset", [], [r], r.ap[:, 0:1], 0.0)
        wsrc = self.w_in.ap.rearrange("(k p) c -> p k c", p=128)
        cnt = {"w": 0, "raw": 0, "stg": 0, "sq": 0}

        def proj_chunk(wb, m0, M, n):
            P = self.bank()
            for k in range(16):
                S.mm(P.ap[0:M, :], wb.ap[:, k, m0:m0 + M], hT.ap[:, k, n * 512:(n + 1) * 512], k == 0, k == 15,
                     [wb, hT.sub(n)], [P])
            return P

        def next_stg():
            t = stg[cnt["stg"] % 4]
            cnt["stg"] += 1
            return t

        def ep_rw(wb, m0, M, ti):
            raw = raws[cnt["raw"] % 2]
            tmp = tmps[cnt["raw"] % 2]
            cnt["raw"] += 1
            for n in range(4):
                P = proj_chunk(wb, m0, M, n)
                S.act(raw.ap[0:M, 1 + n * 512:1 + (n + 1) * 512], P.ap[0:M, :], AF.Copy, [P], [raw])
            S.v("dve", "tensor_scalar", [raw, mu], [tmp], tmp.ap[0:M, :], raw.ap[0:M, 0:SEQ], mu.ap[0:M, ti:ti + 1], None, ALU.mult)
            S.v("dve", "scalar_tensor_tensor", [raw, omu, tmp], [tmp], tmp.ap[0:M, :], raw.ap[0:M, 1:SEQ + 1],
                omu.ap[0:M, ti:ti + 1], tmp.ap[0:M, :], ALU.mult, ALU.add)
            S.dma("sp", self.rwT_d.ap[ti * 128:ti * 128 + M, :], tmp.ap[0:M, :], [tmp], [self.rwT_d])

        def ep_qk(wb, m0, gcols, dsts):
            outs = [next_stg() for _ in gcols]
            for n in range(4):
                P = proj_chunk(wb, m0, 128, n)
                i = cnt["sq"] % 2
                cnt["sq"] += 1
                sq, sr, ri = sqs[i], srs[i], ris[i]
                S.act(sq.ap, P.ap, AF.Square, [P], [sq])
                P2 = self.bank()
                S.mm(P2.ap, self.bones.ap, sq.ap, True, True, [self.bones, sq], [P2])
                S.act(sr.ap, P2.ap, AF.Sqrt, [P2], [sr], bias=64 * EPS, scale=1.0)
                S.v("dve", "reciprocal", [sr], [ri], ri.ap, sr.ap)
                for gc, o in zip(gcols, outs):
                    S.v("dve", "scalar_tensor_tensor", [P, qkg, ri], [o], o.ap[:, n * 512:(n + 1) * 512], P.ap,
                        qkg.ap[:, gc:gc + 1], ri.ap, ALU.mult, ALU.mult)
            for o, (dt_, dap) in zip(outs, dsts):
                S.dma("sp", dap, o.ap, [o], [dt_])

        def ep_act(wb, m0, func, dt_, dap):
            o = next_stg()
            for n in range(4):
                P = proj_chunk(wb, m0, 128, n)
                S.act(o.ap[:, n * 512:(n + 1) * 512], P.ap, func, [P], [o])
            S.dma("sp", dap, o.ap, [o], [dt_])

        def load_block(segs):
            wb = wbs[cnt["w"] % 2]
            cnt["w"] += 1
            for (c0, n, off) in segs:
                S.dma("pool", wb.ap[:, :, off:off + n], wsrc[:, :, c0:c0 + n], [self.w_in], [wb])
            return wb

        for b in range(7):
            if b < 6:
                wb = load_block([(512 * b, 512, 0)])
                for j in range(4):
                    ep_rw(wb, j * 128, 128, 4 * b + j)
            else:
                wb = load_block([(3072, 288, 0)])
                ep_rw(wb, 0, 128, 24)
                ep_rw(wb, 128, 128, 25)
                ep_rw(wb, 256, 32, 26)
        for b in range(2):
            wb = load_block([(NB + 512 * b, 512, 0)])
            for j in range(4):
                ti = 4 * b + j
                ep_qk(wb, j * 128, [0], [(self.qT_d, self.qT_d.ap[ti * 128:(ti + 1) * 128, :])])
        wb = load_block([(NB + 1024, 512, 0)])
        for j in range(4):
            ep_act(wb, j * 128, AF.Copy, self.kvcT_d, self.kvcT_d.ap[j * 128:(j + 1) * 128, :])
        for (c_base, dst, gc) in ((NB + 1024 + 512, self.ksT_d, 1), (NB + 1024 + 1024, self.kwT_d, 3)):
            segs = []
            for g in range(4):
                segs.append((c_base + 64 * g, 64, g * 128))
                segs.append((c_base + 64 * g, 64, g * 128 + 64))
            wb = load_block(segs)
            for g in range(4):
                ep_qk(wb, g * 128, [gc, gc + 1], [(dst, dst.ap[g, 0]), (dst, dst.ap[g, 1])])
        for b in range(8):
            wb = load_block([(MB + 512 * b, 512, 0)])
            for j in range(4):
                ti = 4 * b + j
                ep_act(wb, j * 128, AF.Sigmoid, self.mgT_d, self.mgT_d.ap[ti * 128:(ti + 1) * 128, :])
        for (c0, n, off) in ((NB + 1024 + 768, 256, 0), (NB + 1024 + 1280, 256, 256), (NB + 2560, 48, 512)):
            S.dma("pool", wtm.ap[:, :, off:off + n], wsrc[:, :, c0:c0 + n], [self.w_in], [wtm])
        vst = [ar.alloc([512], BF16, f"vst{i}") for i in range(2)]
        gst = [ar.alloc([48], F32, f"gst{i}") for i in range(2)]
        for tt in range(NT):
            P = self.bank()
            for k in range(16):
                S.mm(P.ap, hT.ap[:, k, tt * 128:(tt + 1) * 128], wtm.ap[:, k, 0:512], k == 0, k == 15,
                     [hT.sub(tt // 4), wtm], [P])
            v = vst[tt % 2]
            S.act(v.ap, P.ap, AF.Copy, [P], [v])
            S.dma("sp", self.vv_d.ap[tt * 128:(tt + 1) * 128, :], v.ap, [v], [self.vv_d])
            P = self.bank()
            for k in range(16):
                S.mm(P.ap[:, 0:48], hT.ap[:, k, tt * 128:(tt + 1) * 128], wtm.ap[:, k, 512:560], k == 0, k == 15,
                     [hT.sub(tt // 4), wtm], [P])
            gt = gst[tt % 2]
            S.act(gt.ap, P.ap[:, 0:48], AF.Sigmoid, [P], [gt])
            S.dma("sp", self.gates_d.ap[tt * 128:(tt + 1) * 128, :], gt.ap, [gt], [self.gates_d])
        S.barrier()
        ar.pop()


def _fm(v, ntile=None):
    v = np.asarray(v, np.float32).reshape(-1)
    n = (len(v) + 127) // 128 if ntile is None else ntile
    buf = np.zeros(n * 128, np.float32)
    buf[:len(v)] = v
    return np.ascontiguousarray(buf.reshape(n, 128).T)


def host_consts():
    c = {}
    c["ident"] = np.eye(128, dtype=np.float32)
    p = np.arange(128)
    c["bones"] = (p[:, None] // 64 == p[None, :] // 64).astype(np.float32)
    return c


def prep_core(inp, b, consts):
    m = dict(consts)
    m["x"] = np.ascontiguousarray(inp["x"][b])
    m["c_fm"] = _fm(inp["c"][b])
    m["w_ada"] = inp["w_ada"][0]
    m["b_ada"] = inp["b_ada"][0].reshape(1, -1)
    m["n1g_fm"] = _fm(inp["norm1_g"][0])
    m["n2g_fm"] = _fm(inp["norm2_g"][0])
    m["w_in"] = inp["w_in"][0]
    m["mu_fm"] = _fm(inp["rwkv_mu"][0], 27)
    qg = np.tile(inp["q_norm_g"][0], 2)
    kg = inp["k_norm_g"][0]
    z = np.zeros(64, np.float32)
    cols = [qg, np.concatenate([kg[1], z]), np.concatenate([z, kg[1]]),
            np.concatenate([kg[2], z]), np.concatenate([z, kg[2]])]
    m["qkg_fm"] = np.ascontiguousarray(np.stack(cols, axis=1).astype(np.float32))
    return m


def _rel_bucket_np(rel):
    n = np.maximum(rel, 0)
    nf = np.maximum(n, 16).astype(np.float32)
    large = 16 + (np.log(nf / np.float32(16)) / np.float32(np.log(8.0)) * np.float32(16)).astype(np.int32)
    large = np.minimum(large, 31)
    return np.where(n < 16, n, large)


def nsa_consts():
    c = {}
    NOH = 3 * 16384 + 17 * 128
    oh = np.zeros((33, NOH), np.float32)
    pos = np.arange(128)[:, None]
    t = np.arange(128)[None, :]
    for d, base in ((0, 0), (1, 128), (2, 512)):
        rel = base + t - pos
        if d == 0:
            mask = rel < 0
        elif d == 1:
            mask = np.zeros_like(rel, bool)
        else:
            mask = rel >= 512
        b = _rel_bucket_np(rel)
        sec = np.zeros((33, 128, 128), np.float32)
        for bb in range(32):
            sec[bb][(b == bb) & ~mask] = 1.0
        sec[32][mask] = 1.0
        oh[:, d * 16384:(d + 1) * 16384] = sec.reshape(33, -1)
    sec = np.zeros((33, 17, 128), np.float32)
    ti = np.arange(128)
    for r in range(16):
        m = r - 9
        rel = ti - 16 * m - 31
        b = _rel_bucket_np(rel)
        for bb in range(32):
            sec[bb, r, (b == bb) & (rel >= 0)] = 1.0
        sec[32, r, rel < 0] = 1.0
    sec[32, 16, :] = 1.0
    oh[:, 3 * 16384:] = sec.reshape(33, -1)
    c["nsa_oh"] = oh
    S = np.zeros((17, 16, 128), np.float32)
    for i in range(16):
        for n in range(127):
            m = n - 8 * i
            if -9 <= m <= 6:
                S[m + 9, i, n] = 1.0
            elif m > 6:
                S[16, i, n] = 1.0
    c["nsa_S"] = S
    E = np.zeros((32, 2048), np.float32)
    for p in range(2048):
        E[p // 64, p] = 1.0
    c["nsa_E"] = E
    allowed = np.zeros((128, 16, 32), np.float32)
    addc = np.zeros((128, 16, 32), np.float32)
    blk = np.arange(32)
    for i in range(16):
        for tt in range(128):
            cur = (i * 128 + tt) // 64
            al = blk <= cur
            forced = (blk == 0) | (blk == cur) | (blk == cur - 1)
            allowed[tt, i] = (al & ~forced).astype(np.float32)
            addc[tt, i] = np.where(forced, 1e4, np.where(al, 0.0, -1.0))
    c["nsa_allowed"] = allowed
    c["nsa_addc"] = addc
    ncmp = 127
    cs = np.arange(ncmp) * 16
    ss = np.arange(32) * 64
    lo = np.maximum(cs[:, None], ss[None, :])
    hi = np.minimum(cs[:, None] + 32, ss[None, :] + 64)
    c["nsa_selm"] = (np.maximum(hi - lo, 0) / 32).astype(np.float32)
    return c


def nsa_prep(inp, m):
    m["rel_bias"] = np.ascontiguousarray(inp["rel_bias"])
    for kv in ("k", "v"):
        m[f"pe_{kv}T"] = np.ascontiguousarray(inp[f"cmp_pe_{kv}"][0].T)
        m[f"w1_{kv}"] = inp[f"cmp_w1_{kv}"][0]
        m[f"w2_{kv}"] = inp[f"cmp_w2_{kv}"][0]
    kg0 = inp["k_norm_g"][0][0]
    z = np.zeros(64, np.float32)
    m["kcg_fm"] = np.ascontiguousarray(np.stack([np.concatenate([kg0, z]), np.concatenate([z, kg0])], 1).astype(np.float32))


def stage4(self):
    ar, S, nc = self.ar, self.S, self.nc
    d = self.din
    S_d = d("nsa_S", [17, 16, 128])
    E_d = d("nsa_E", [32, 2048])
    al_d = d("nsa_allowed", [128, 16, 32])
    ad_d = d("nsa_addc", [128, 16, 32])
    selm_d = d("nsa_selm", [127, 32])
    kcg_d = d("kcg_fm", [128, 2])
    cmp_d = {}
    for kv in ("k", "v"):
        cmp_d[kv] = (d(f"pe_{kv}T", [64, 32]), d(f"w1_{kv}", [2048, 64]), d(f"w2_{kv}", [64, 64]))
    NOH = 3 * 16384 + 17 * 128
    self.obT_d = self.dscr("obT", [1024, SEQ], BF16)
    ident, bones = self.ident, self.bones

    ar.push()
    ks = ar.alloc([4, 2, SEQ], BF16, "ks")
    kw = ar.alloc([4, 2, SEQ], BF16, "kw")
    vs = ar.alloc([16, 4, 65], BF16, "vs")
    vw = ar.alloc([16, 4, 65], BF16, "vw")
    gates = ar.alloc([16, 48], F32, "gates")
    biasT = ar.alloc([3, 16, 128], F32, "biasT")
    Mst = ar.alloc([16, 128], F32, "Mst", parts=17)
    Sc = ar.alloc([16, 128], F32, "Sc", parts=17)
    allowed = ar.alloc([16, 32], F32, "allowed")
    addc = ar.alloc([16, 32], F32, "addc")
    kc = ar.alloc([4, 2, 128], BF16, "kc")
    rhsc = ar.alloc([4, 97], BF16, "rhsc", parts=127)
    for g in range(4):
        for h in range(2):
            S.dma("sp", ks.ap[:, g, h, :], self.ksT_d.ap[g, h], [self.ksT_d], [ks])
            S.dma("sp", kw.ap[:, g, h, :], self.kwT_d.ap[g, h], [self.kwT_d], [kw])
    for (dst, c0) in ((vs, 0), (vw, 256)):
        S.v("dve", "memset", [], [dst], dst.ap[:, :, :, 64:65], 1.0)
        for j in range(16):
            S.dma("sp", dst.ap[:, j, :, 0:64],
                  self.vv_d.ap[j * 128:(j + 1) * 128, c0:c0 + 256].rearrange("p (g d) -> p g d", g=4), [self.vv_d], [dst])
    S.dma("sp", gates.ap, self.gates_d.ap.rearrange("(j p) c -> p j c", p=128), [self.gates_d], [gates])
    S.dma("sp", Sc.ap, S_d.ap, [S_d], [Sc])
    S.dma("sp", allowed.ap, al_d.ap, [al_d], [allowed])
    S.dma("sp", addc.ap, ad_d.ap, [ad_d], [addc])

    ar.push()
    bias_d = self.bias_d
    for dd in range(3):
        S.dma("sp", biasT.ap[:, dd, :, :],
              bias_d.ap[:, dd * 16384:(dd + 1) * 16384].rearrange("h (p t) -> p h t", p=128), [bias_d], [biasT])
    S.dma("sp", Mst.ap, bias_d.ap[:, 3 * 16384:].rearrange("h (r t) -> r h t", r=17), [bias_d], [Mst])
    if self.dbg:
        dbb = self.dscr("dbg_biasT", [128, 3 * 16 * 128])
        S.dma("sp", dbb.ap, biasT.ap.rearrange("p a b c -> p (a b c)"), [biasT], [dbb])
    ar.pop()

    ar.push()
    kvc = ar.alloc([4, SEQ], BF16, "kvc")
    S.dma("sp", kvc.ap, self.kvcT_d.ap.rearrange("(a p) t -> p a t", p=128), [self.kvcT_d], [kvc])
    kcg = ar.alloc([2], F32, "kcg")
    S.dma("sp", kcg.ap, kcg_d.ap, [kcg_d], [kcg])
    S.v("dve", "tensor_scalar", [kcg], [kcg], kcg.ap, kcg.ap, 8.0, None, ALU.mult)
    S.v("dve", "memset", [], [rhsc], rhsc.ap[:, :, 64:65], 1.0)
    for g in range(4):
        S.dma("pool", rhsc.ap[:, g, 65:97], selm_d.ap, [selm_d], [rhsc])
    for kvi, kv in enumerate(("k", "v")):
        pe_d, w1_d, w2_d = cmp_d[kv]
        w1p = ar.alloc([2, 32, 64], BF16, f"w1p{kv}")
        S.v("dve", "memset", [], [w1p], w1p.ap, 0.0)
        w1v = w1_d.ap.rearrange("(i d) e -> d i e", d=64)
        S.dma("pool", w1p.ap[0:64, 0, :, :], w1v, [w1_d, w1p], [w1p])
        S.dma("pool", w1p.ap[64:128, 1, :, :], w1v, [w1_d, w1p], [w1p])
        peT = ar.alloc([32], BF16, f"peT{kv}", parts=64)
        S.dma("pool", peT.ap, pe_d.ap, [pe_d], [peT])
        w2 = ar.alloc([128], BF16, f"w2{kv}", parts=64)
        S.dma("pool", w2.ap[:, 0:64], w2_d.ap, [w2_d], [w2])
        S.dma("pool", w2.ap[:, 64:128], w2_d.ap, [w2_d, w2], [w2])
        Pb = self.bank()
        for i in range(32):
            S.mm(Pb.ap[0:64, 0:1], w1p.ap[0:64, 0, i, :], peT.ap[:, i:i + 1], i == 0, i == 31, [w1p, peT], [Pb])
        cb = ar.alloc([1], F32, f"cb{kv}", parts=64)
        S.act(cb.ap, Pb.ap[0:64, 0:1], AF.Copy, [Pb], [cb])
        for g in range(4):
            tile_, half = kvi * 2 + g // 2, g % 2
            Ph = self.bank()
            for i in range(32):
                S.mm(Ph.ap[0:64, 0:127], w1p.ap[:, half, i, :], kvc.ap[:, tile_, i:i + 16 * 126 + 1:16], i == 0, i == 31,
                     [w1p, kvc], [Ph])
            u = ar.alloc([127], F32, "cu", parts=64)
            t1 = ar.alloc([127], F32, "ct1", parts=64)
            sg = ar.alloc([127], F32, "csg", parts=64)
            hid = ar.alloc([127], BF16, "chid", parts=64)
            S.act(u.ap, Ph.ap[0:64, 0:127], AF.Identity, [Ph, cb], [u], bias=cb.ap[:, 0:1], scale=1.0)
            S.v("dve", "tensor_tensor", [u], [t1], t1.ap, u.ap, u.ap, ALU.mult)
            S.v("dve", "tensor_scalar", [t1], [t1], t1.ap, t1.ap, 0.044715, 1.0, ALU.mult, ALU.add)
            S.v("dve", "tensor_tensor", [t1, u], [t1], t1.ap, t1.ap, u.ap, ALU.mult)
            S.act(sg.ap, t1.ap, AF.Sigmoid, [t1], [sg], scale=1.5957691216057308)
            S.v("dve", "tensor_tensor", [u, sg], [hid], hid.ap, u.ap, sg.ap, ALU.mult)
            if kv == "k":
                Pk = self.bank()
                S.mm(Pk.ap[:, 0:127], w2.ap, hid.ap, True, True, [w2, hid], [Pk])
                sq = ar.alloc([127], F32, "csq")
                S.act(sq.ap, Pk.ap[:, 0:127], AF.Square, [Pk], [sq])
                P2 = self.bank()
                S.mm(P2.ap[:, 0:127], bones.ap, sq.ap, True, True, [bones, sq], [P2])
                sr = ar.alloc([127], F32, "csr")
                S.act(sr.ap, P2.ap[:, 0:127], AF.Sqrt, [P2], [sr], bias=64 * EPS, scale=1.0)
                S.v("dve", "reciprocal", [sr], [sr], sr.ap, sr.ap)
                for h in range(2):
                    S.v("dve", "scalar_tensor_tensor", [Pk, kcg, sr], [kc], kc.ap[:, g, h, 0:127], Pk.ap[:, 0:127],
                        kcg.ap[:, h:h + 1], sr.ap, ALU.mult, ALU.mult)
            else:
                Pv = self.bank()
                S.mm(Pv.ap[0:127, 0:64], hid.ap, w2.ap[:, 0:64], True, True, [w2, hid], [Pv])
                S.act(rhsc.ap[:, g, 0:64], Pv.ap[0:127, 0:64], AF.Copy, [Pv], [rhsc])
    if self.dbg:
        dkc = self.dscr("dbg_kc", [128, 4 * 2 * 128], BF16)
        S.dma("sp", dkc.ap, kc.ap.rearrange("p a b c -> p (a b c)"), [kc], [dkc])
        drc = self.dscr("dbg_rhsc", [127, 4 * 97], BF16)
        S.dma("sp", drc.ap, rhsc.ap.rearrange("p a b -> p (a b)"), [rhsc], [drc])
    S.barrier()
    ar.pop()
    if 6 in self.stages:
        self.precast()

    qis = [ar.alloc([4, 2, 2, 128], BF16, f"qa{i}") for i in range(2)]
    for qa_ in qis:
        S.v("dve", "memset", [], [qa_.sub("q")] + [qa_.sub(("m", g_)) for g_ in range(4)], qa_.ap, 0.0)
    for g_ in range(4):
        S.dma("pool", ks.ap[64:96, g_, 0, :], E_d.ap, [E_d, ks], [ks])
        S.dma("pool", ks.ap[0:32, g_, 1, :], E_d.ap, [E_d, ks], [ks])
    pxs = [ar.alloc([512], BF16, f"px{i}") for i in range(4)]
    ssbs = [ar.alloc([512], F32, f"ssb{i}") for i in range(2)]
    oaccs = [ar.alloc([1024], F32, f"oacc{i}") for i in range(2)]
    obst = [ar.alloc([8, 128], BF16, f"obst{i}") for i in range(1)] * 2
    sm = [dict(rl=ar.alloc([12], F32, f"rl{i}"), imp=ar.alloc([32], F32, f"imp{i}"), m8=ar.alloc([8], F32, f"m8{i}"),
               ns=ar.alloc([128], F32, f"ns{i}"), tmp=ar.alloc([256], F32, f"otmp{i}")) for i in range(2)]
    for w_ in sm:
        S.v("dve", "memset", [], [w_["ns"]], w_["ns"].ap, 0.0)
    score_banks = self.ps[0:4]
    NSB = 4
    Poc_b = self.ps[4]
    Pow_b = [self.ps[5], self.ps[5]]
    Pos_b = self.ps[6:8]
    cnt = {"sb": 0, "px": 0, "ssb": 0}
    jobs = []
    qT_v = self.qT_d.ap.rearrange("(kt p) t -> p kt t", p=128)
    obT_v = self.obT_d.ap.rearrange("(kt p) t -> p kt t", p=128)

    def score_job(qi, g, lhs_lo, lhs_hi, M, extra_mm, bias_ap, pv_fn, deps_k, use_mask=False):
        st = {}

        def qk():
            P = score_banks[cnt["sb"] % NSB]
            cnt["sb"] += 1
            pv4 = P.ap[0:M, :].rearrange("p (a b t) -> p a b t", a=2, b=2)
            qdeps = [qi.sub("q")] + ([qi.sub(("m", g))] if use_mask else [])
            S.mm(pv4[:, :, 0, :], lhs_lo, qi.ap[:, g, 0, :, :], True, False, deps_k + qdeps, [P])
            S.mm(pv4[:, :, 1, :], lhs_hi, qi.ap[:, g, 1, :, :], False, extra_mm is None, deps_k + qdeps, [P])
            if extra_mm is not None:
                lt, rt, dps = extra_mm
                S.mm(P.ap[0:M, :], lt, rt, False, True, dps, [P])
            px = pxs[cnt["px"] % 4]
            cnt["px"] += 1
            if bias_ap is not None:
                sb_ = ssbs[cnt["ssb"] % 2]
                cnt["ssb"] += 1
                S.v("dve", "tensor_tensor", [P, biasT], [sb_], sb_.ap[0:M, :], P.ap[0:M, :], bias_ap, ALU.add)
                S.act(px.ap[0:M, :], sb_.ap[0:M, :], AF.Exp, [sb_], [px])
            else:
                S.act(px.ap[0:M, :], P.ap[0:M, :], AF.Exp, [P], [px])
            st["px"] = px

        def pv():
            pv_fn(st["px"])

        return (qk, pv)

    for i in range(NT):
        qi = qis[i % 2]
        oacc = oaccs[i % 2]

        def load_q(i=i):
            if i < NT:
                qa_ = qis[i % 2]
                for (r0, lh) in ((0, 0), (64, 1)):
                    for g_ in range(4):
                        S.dma("sp", qa_.ap[r0:r0 + 64, g_, lh, :, :],
                              qT_v[r0:r0 + 64, 2 * g_:2 * g_ + 2, i * 128:(i + 1) * 128],
                              [self.qT_d], [qa_.sub("q")])
        if i == 0:
            jobs.append((load_q, None))
        load_next = (lambda i=i: load_q(i + 1))
        for g in range(4):
            it = i * 4 + g
            w = sm[it % 2]
            Pow_, Pos_ = Pow_b[it % 2], Pos_b[it % 2]
            gv = gates.ap[:, i, g * 12:(g + 1) * 12].rearrange("p (h c) -> p h c", c=3)
            osl = oacc.ap[:, g * 256:(g + 1) * 256].rearrange("p (h d) -> p h d", h=4)

            def pv_c(px, g=g, i=i, w=w, gv=gv, osl=osl, qi=qi):
                Poc = Poc_b
                for hs in range(4):
                    S.mm(Poc.ap[:, hs * 97:(hs + 1) * 97], px.ap[0:127, hs * 128:(hs + 1) * 128], rhsc.ap[:, g, :],
                         hs == 0, hs == 3, [px, rhsc], [Poc])
                pc3 = Poc.ap[:, 0:388].rearrange("p (h c) -> p h c", h=4)
                rl = w["rl"]
                S.v("dve", "tensor_scalar", [Poc], [rl], rl.ap[:, 0:4], pc3[:, :, 64], 1e-30, None, ALU.max)
                S.v("dve", "reciprocal", [rl], [rl], rl.ap[:, 0:4], rl.ap[:, 0:4])
                imp = w["imp"]
                S.v("dve", "tensor_scalar", [Poc, rl], [imp], imp.ap, pc3[:, 0, 65:97], rl.ap[:, 0:1], None, ALU.mult)
                for hs in range(1, 4):
                    S.v("dve", "scalar_tensor_tensor", [Poc, rl, imp], [imp], imp.ap, pc3[:, hs, 65:97], rl.ap[:, hs:hs + 1],
                        imp.ap, ALU.mult, ALU.add)
                S.v("dve", "tensor_tensor", [imp, allowed], [imp], imp.ap, imp.ap, allowed.ap[:, i, :], ALU.mult)
                S.v("dve", "tensor_tensor", [imp, addc], [imp], imp.ap, imp.ap, addc.ap[:, i, :], ALU.add)
                m8 = w["m8"]
                S.v("dve", "max", [imp], [m8], out=m8.ap, in_=imp.ap)
                ns = w["ns"]
                S.v("dve", "tensor_scalar", [imp, m8], [ns], ns.ap[:, 0:32], imp.ap, m8.ap[:, 7:8], None, ALU.is_ge)
                S.v("dve", "tensor_scalar", [ns], [ns], ns.ap[:, 64:96], ns.ap[:, 0:32], -1.0, -NEG, ALU.add, ALU.mult)
                S.v("dve", "tensor_scalar", [ns], [ns], ns.ap[:, 0:32], ns.ap[:, 0:32], -1.0, -NEG, ALU.add, ALU.mult)
                Pt = score_banks[cnt["sb"] % NSB]
                cnt["sb"] += 1
                S.tr(Pt.ap[:, 0:128], ns.ap, ident.ap, [ns, ident], [Pt])
                S.act(qi.ap[64:96, g, 0, :, :], Pt.ap[64:96, 0:128].unsqueeze(1).to_broadcast([32, 2, 128]), AF.Copy, [Pt], [qi.sub(("m", g))])
                S.act(qi.ap[0:32, g, 1, :, :], Pt.ap[0:32, 0:128].unsqueeze(1).to_broadcast([32, 2, 128]), AF.Copy, [Pt], [qi.sub(("m", g))])
                S.v("dve", "tensor_tensor", [rl, gates], [rl], rl.ap[:, 0:4], rl.ap[:, 0:4], gv[:, :, 0], ALU.mult)
                S.v("dve", "tensor_tensor", [Poc, rl], [oacc.sub(g)], osl, pc3[:, :, 0:64],
                    rl.ap[:, 0:4].unsqueeze(2).to_broadcast([128, 4, 64]), ALU.mult)
            extra = (Sc.ap[:, i, 0:127], Mst.ap[:, 4 * g:4 * g + 4, :], [Sc, Mst])
            jobs.append(score_job(qi, g, kc.ap[:, g, 0, 0:127], kc.ap[:, g, 1, 0:127], 127, extra, None, pv_c, [kc]))
            if g == 1:
                jobs.append((load_next, None))

            def mk_pv(Pacc, vv, j, first, last, br, g=g, w=w, gv=gv, osl=osl, oacc=oacc):
                def pv(px):
                    for hs in range(4):
                        S.mm(Pacc.ap[:, hs * 65:(hs + 1) * 65], px.ap[:, hs * 128:(hs + 1) * 128], vv.ap[:, j, g, :],
                             first and hs == 0, last and hs == 3, [px, vv], [Pacc])
                    if last:
                        p3 = Pacc.ap[:, 0:260].rearrange("p (h c) -> p h c", h=4)
                        rl = w["rl"]
                        o = 4 * br
                        S.v("dve", "tensor_scalar", [Pacc], [rl], rl.ap[:, o:o + 4], p3[:, :, 64], 1e-30, None, ALU.max)
                        S.v("dve", "reciprocal", [rl], [rl], rl.ap[:, o:o + 4], rl.ap[:, o:o + 4])
                        S.v("dve", "tensor_tensor", [rl, gates], [rl], rl.ap[:, o:o + 4], rl.ap[:, o:o + 4], gv[:, :, br], ALU.mult)
                        tmp = w["tmp"]
                        t3 = tmp.ap.rearrange("p (h d) -> p h d", h=4)
                        S.v("dve", "tensor_tensor", [Pacc, rl], [tmp], t3, p3[:, :, 0:64],
                            rl.ap[:, o:o + 4].unsqueeze(2).to_broadcast([128, 4, 64]), ALU.mult)
                        S.v("dve", "tensor_tensor", [tmp, oacc.sub(g)], [oacc.sub(g)], osl, osl, t3, ALU.add)
                return pv

            js = list(range(max(0, i - 4), i + 1))
            for j in js:
                dd = {0: 0, 1: 1, 4: 2}.get(i - j)
                bias_ap = None if dd is None else biasT.ap[:, dd, 4 * g:4 * g + 4, :].rearrange("p h t -> p (h t)")
                jobs.append(score_job(qi, g, kw.ap[:, g, 0, j * 128:(j + 1) * 128], kw.ap[:, g, 1, j * 128:(j + 1) * 128],
                                      128, None, bias_ap, mk_pv(Pow_, vw, j, j == js[0], j == js[-1], 2), [kw]))
            for j in range(i + 1):
                dd = {0: 0, 1: 1}.get(i - j)
                bias_ap = None if dd is None else biasT.ap[:, dd, 4 * g:4 * g + 4, :].rearrange("p h t -> p (h t)")
                jobs.append(score_job(qi, g, ks.ap[:, g, 0, j * 128:(j + 1) * 128], ks.ap[:, g, 1, j * 128:(j + 1) * 128],
                                      128, None, bias_ap, mk_pv(Pos_, vs, j, j == 0, j == i, 1), [ks], use_mask=True))

        def finish(i=i, oacc=oacc):
            ob = obst[i % 2]
            for half in range(2):
                P = score_banks[cnt["sb"] % NSB]
                cnt["sb"] += 1
                for q in range(4):
                    kt = half * 4 + q
                    S.tr(P.ap[:, q * 128:(q + 1) * 128], oacc.ap[:, kt * 128:(kt + 1) * 128], ident.ap,
                         [oacc.sub(kt // 2), ident], [P])
                S.act(ob.ap[:, half * 4:(half + 1) * 4, :], P.ap.rearrange("p (q t) -> p q t", q=4), AF.Copy, [P], [ob])
            S.dma("sp", obT_v[:, :, i * 128:(i + 1) * 128], ob.ap, [ob], [self.obT_d])
        jobs.append((None, finish))

    pend = []
    for (qk, pv) in jobs:
        if len(pend) >= 2:
            f = pend.pop(0)
            if f is not None:
                f()
        if qk is not None:
            qk()
        pend.append(pv)
    for f in pend:
        if f is not None:
            f()
    S.barrier()
    ar.pop()


KB.stage4 = stage4


def nsa_bias_build(self):
    ar, S = self.ar, self.S
    d = self.din
    NOH = 3 * 16384 + 17 * 128
    oh_d = d("nsa_oh", [33, NOH])
    relb_d = d("rel_bias", [32, 16])
    self.bias_d = bias_d = self.dscr("bias_scr", [16, NOH])
    trel = ar.alloc([16], F32, "trel", parts=33)
    tbl = ar.alloc([16], F32, "tbl", parts=32)
    t31 = ar.alloc([16], F32, "t31", parts=32)
    S.dma("sp", tbl.ap, relb_d.ap, [relb_d], [tbl])
    S.dma("sp", t31.ap, relb_d.ap[31:32, :].partition_broadcast(32), [relb_d], [t31])
    S.v("dve", "memset", [], [trel], trel.ap, NEG)
    S.v("dve", "tensor_tensor", [tbl, t31, trel], [trel], trel.ap[0:32, :], tbl.ap, t31.ap, ALU.subtract)
    ohb = [ar.alloc([2048], F32, f"ohb{i}", parts=33) for i in range(2)]
    bsb = [ar.alloc([2048], F32, f"bsb{i}", parts=16) for i in range(2)]
    nblk = (NOH + 2047) // 2048

    def mk(bi):
        def step():
            c0 = bi * 2048
            n = min(2048, NOH - c0)
            ob, bs = ohb[bi % 2], bsb[bi % 2]
            S.dma("sp", ob.ap[:, 0:n], oh_d.ap[:, c0:c0 + n], [oh_d], [ob])
            for q in range((n + 511) // 512):
                w = min(512, n - q * 512)
                P = self.bank()
                S.mm(P.ap[0:16, 0:w], trel.ap, ob.ap[:, q * 512:q * 512 + w], True, True, [trel, ob], [P])
                S.act(bs.ap[:, q * 512:q * 512 + w], P.ap[0:16, 0:w], AF.Copy, [P], [bs])
            S.dma("sp", bias_d.ap[:, c0:c0 + n], bs.ap[:, 0:n], [bs], [bias_d])
        return step
    return [mk(bi) for bi in range(nblk)]


KB.nsa_bias_build = nsa_bias_build


LAM = 0.6065306597126334


def rwkv_consts():
    c = {}
    p = np.arange(128)
    ut_strict = (p[:, None] < p[None, :]).astype(np.float32)
    ut_incl = (p[:, None] <= p[None, :]).astype(np.float32)
    c["rw_mAB"] = np.ascontiguousarray(np.concatenate([ut_strict, ut_incl], 1))
    c["rw_mLT"] = (p[:, None] > p[None, :]).astype(np.float32)
    rs = np.ones((128, 8, 128), np.float32)
    rs[:, :, 0] = 0.0
    c["rw_reset"] = rs.reshape(128, 1024)
    hm = np.zeros((128, 2), np.float32)
    hm[:64, 0] = 1.0
    hm[64:, 1] = 1.0
    c["rw_hsel"] = hm
    return c


def rwkv_prep(inp, m):
    g = lambda k: inp[k][0]
    m["rw_w0"] = _fm(g("rwkv_w0"))
    m["rw_a0"] = _fm(g("rwkv_a0"))
    m["rw_kk"] = _fm(g("rwkv_k_k"))
    m["rw_ka"] = _fm(g("rwkv_k_a"))
    m["rw_rk"] = _fm(g("rwkv_r_k").reshape(-1))
    m["rw_lnw"] = np.ascontiguousarray(np.broadcast_to(g("rwkv_ln_w")[None, :], (128, 1024)).astype(np.float32))
    m["rw_lnb"] = np.ascontiguousarray(np.broadcast_to(g("rwkv_ln_b")[None, :], (128, 1024)).astype(np.float32))
    z = np.zeros((64, 1024), np.float32)
    m["rw_w2pad"] = np.ascontiguousarray(np.concatenate([g("rwkv_w2"), z], 0))
    m["rw_a2pad"] = np.ascontiguousarray(np.concatenate([z, g("rwkv_a2")], 0))
    m["rw_g2"] = g("rwkv_g2")


def stage3(self):
    ar, S = self.ar, self.S
    d = self.din
    ident, bones = self.ident, self.bones
    self.oaT_d = self.dscr("oaT", [1024, SEQ], BF16)
    ar.push()

    def cload(name, shape, parts=128, src=None):
        dt_ = d(name, [parts] + list(shape)) if src is None else src
        t = ar.alloc(shape, F32, name, parts=parts)
        S.dma("sp", t.ap, dt_.ap, [dt_], [t])
        return t
    w0 = cload("rw_w0", [8])
    a0 = cload("rw_a0", [8])
    kkf = cload("rw_kk", [8])
    kaf = cload("rw_ka", [8])
    rkf = cload("rw_rk", [8])
    lnw = cload("rw_lnw", [1024])
    lnb = cload("rw_lnb", [1024])
    w2p = cload("rw_w2pad", [1024])
    a2p = cload("rw_a2pad", [1024])
    g2_d = d("rw_g2", [160, 1024])
    g2a = ar.alloc([1024], F32, "g2a")
    g2b = ar.alloc([1024], F32, "g2b", parts=32)
    S.dma("sp", g2a.ap, g2_d.ap[0:128, :], [g2_d], [g2a])
    S.dma("sp", g2b.ap, g2_d.ap[128:160, :], [g2_d], [g2b])
    mAB = cload("rw_mAB", [256])
    mLT = cload("rw_mLT", [128])
    reset = cload("rw_reset", [1024])
    hsel = cload("rw_hsel", [2])
    omka = ar.alloc([8], F32, "omka")
    S.v("dve", "tensor_scalar", [kaf], [omka], omka.ap, kaf.ap, -1.0, 1.0, ALU.mult, ALU.add)
    Hp = ar.alloc([16, 64], F32, "Hp")
    S.v("dve", "memset", [], [Hp], Hp.ap, 0.0)

    A = lambda n, shape=(8, 128): ar.alloc(list(shape), F32, n)
    raw = A("raw", (27, 128))
    sgw, cs, E, Eex = A("sgw"), A("cs"), A("E"), A("Eex")
    aT, kkn, kp, bb, btl = A("aT"), A("kkn"), A("kp"), A("bb"), A("btl")
    AR = A("AR", (8, 2, 128))
    blo, bhi, klo, khi, alo, ahi = A("blo"), A("bhi"), A("klo"), A("khi"), A("alo"), A("ahi")
    tmpA, tmpB = A("tmpA"), A("tmpB")
    bh, kh = sgw, cs
    v_tok, bh_tok, kh_tok, g_tok = A("v_tok", (1024,)), A("bh_tok", (1024,)), A("kh_tok", (1024,)), A("g_tok", (1024,))
    th = A("th", (128,))
    sx = A("sx", (128,))
    sx2 = ar.alloc([128], F32, "sx2", parts=32)
    PLs = [A("PL0", (8,)), A("PL1", (8,))]
    nb = A("nb", (8,))
    rk16 = A("rk16", (16,))
    st16 = [A(f"st16_{i}", (16,)) for i in range(4)]
    import os
    SQDT = BF16 if os.environ.get("RW_SQ") == "bf16" else F32
    MASK_POOL = os.environ.get("RW_MASK") == "pool"
    NO_IL = os.environ.get("RW_IL") == "0"
    slots = []
    for s_ in range(2):
        slots.append(dict(
            ABm=A(f"ABm{s_}", (4, 256)), AKm=A(f"AKm{s_}", (4, 256)),
            Yf=[A(f"Yf{s_}{i}", (4, 128)) for i in range(2)] if SQDT != F32 else None,
            Yb=[ar.alloc([4, 128], SQDT, f"Yb{s_}{i}") for i in range(2)],
            XW=[A(f"XW{s_}{i}", (4, 192)) for i in range(2)]))
        if SQDT == F32:
            slots[-1]["Yf"] = slots[-1]["Yb"]
    if SQDT == F32:
        ar.off -= 0
    oast = [ar.alloc([8, 128], BF16, f"oast{i}") for i in range(1)] * 2
    f2 = lambda t: t.ap.rearrange("p a b -> p (a b)")
    rw_v = self.rwT_d.ap.rearrange("(kt p) t -> p kt t", p=128)
    oaT_v = self.oaT_d.ap.rearrange("(kt p) t -> p kt t", p=128)
    dv = lambda meth, reads, writes, *a, **k: S.v("dve", meth, reads, writes, *a, **k)
    pl = lambda meth, reads, writes, *a, **k: S.v("pool", meth, reads, writes, *a, **k)
    bc8 = lambda t: t.ap.unsqueeze(2).to_broadcast([128, 8, 128])

    def P1(c):
            S.dma("sp", raw.ap, rw_v[:, :, c * 128:(c + 1) * 128], [self.rwT_d], [raw])
            yield
            rT, kT, vT = raw.ap[:, 0:8, :], raw.ap[:, 8:16, :], raw.ap[:, 16:24, :]
            yield
            t24 = raw.ap[:, 24, :]
            yield
            S.act(th.ap, t24, AF.Tanh, [raw], [th])
            yield
            Pz = [self.bank(), self.bank()]
            yield
            for kt in range(8):
                P = Pz[kt // 4]
                S.mm(P.ap[:, (kt % 4) * 128:(kt % 4 + 1) * 128], w2p.ap[:, kt * 128:(kt + 1) * 128], th.ap, True, True, [w2p, th], [P])
            yield
            for kt in range(8):
                S.act(sgw.ap[:, kt, :], Pz[kt // 4].ap[:, (kt % 4) * 128:(kt % 4 + 1) * 128], AF.Sigmoid, [Pz[kt // 4], w0], [sgw],
                      bias=w0.ap[:, kt:kt + 1], scale=1.0)
            yield
            Pa = [self.bank(), self.bank()]
            yield
            for kt in range(8):
                P = Pa[kt // 4]
                S.mm(P.ap[:, (kt % 4) * 128:(kt % 4 + 1) * 128], a2p.ap[:, kt * 128:(kt + 1) * 128], t24, True, True, [a2p, raw], [P])
            yield
            for kt in range(8):
                S.act(aT.ap[:, kt, :], Pa[kt // 4].ap[:, (kt % 4) * 128:(kt % 4 + 1) * 128], AF.Sigmoid, [Pa[kt // 4], a0], [aT],
                      bias=a0.ap[:, kt:kt + 1], scale=1.0)
            yield
            dv("tensor_tensor_scan", [reset, sgw], [cs], f2(cs), reset.ap, f2(sgw), 0.0, ALU.mult, ALU.add)
            yield
            S.act(f2(E), f2(cs), AF.Exp, [cs], [E], scale=-LAM)
            yield
            dv("tensor_tensor", [cs, sgw], [Eex], f2(Eex), f2(cs), f2(sgw), ALU.subtract)
            yield
            S.act(f2(Eex), f2(Eex), AF.Exp, [Eex], [Eex], scale=-LAM)
            yield
            dv("tensor_scalar", [cs], [nb], nb.ap, cs.ap[:, :, 127], -LAM, None, ALU.mult)
            yield
            S.act(PLs[c % 2].ap, nb.ap, AF.Exp, [nb], [PLs[c % 2]])
            yield
            dv("tensor_tensor", [raw, kkf], [kkn], kkn.ap, kT, bc8(kkf), ALU.mult)
            yield
            dv("tensor_tensor", [kkn], [tmpB], tmpB.ap, kkn.ap, kkn.ap, ALU.mult)
            yield
            Pn = [self.bank(), self.bank()]
            yield
            for hh in range(2):
                S.mm(Pn[hh].ap, bones.ap, tmpB.ap[:, hh * 4:(hh + 1) * 4, :], True, True, [bones, tmpB], [Pn[hh]])
            yield
            for hh in range(2):
                S.act(tmpB.ap[:, hh * 4:(hh + 1) * 4, :], Pn[hh].ap.rearrange("p (a b) -> p a b", a=4), AF.Sqrt, [Pn[hh]], [tmpB])
            yield
            dv("tensor_scalar", [tmpB], [tmpB], f2(tmpB), f2(tmpB), 1e-12, None, ALU.max)
            yield
            dv("reciprocal", [tmpB], [tmpB], f2(tmpB), f2(tmpB))
            yield
            dv("tensor_tensor", [kkn, tmpB], [kkn], f2(kkn), f2(kkn), f2(tmpB), ALU.mult)
            yield
            dv("tensor_tensor", [aT, kaf], [kp], kp.ap, aT.ap, bc8(kaf), ALU.mult)
            yield
            dv("tensor_tensor", [kp, omka], [kp], kp.ap, kp.ap, bc8(omka), ALU.add)
            yield
            dv("tensor_tensor", [kp, raw], [kp], kp.ap, kp.ap, kT, ALU.mult)
            yield
            dv("tensor_tensor", [kkn, aT], [bb], f2(bb), f2(kkn), f2(aT), ALU.mult)
            yield


    def P2(c):
            rT, kT, vT = raw.ap[:, 0:8, :], raw.ap[:, 8:16, :], raw.ap[:, 16:24, :]
            dv("tensor_tensor", [raw, E], [AR], AR.ap[:, :, 1, :], rT, E.ap, ALU.mult)
            dv("scalar_tensor_tensor", [kkn, Eex], [AR], AR.ap[:, :, 0, :], kkn.ap, -1.0, Eex.ap, ALU.mult, ALU.mult)
            Einv, Elast = E, Eex
            S.act(f2(Einv), f2(cs), AF.Exp, [cs], [Einv], scale=LAM)
            for kt in range(8):
                S.act(Elast.ap[:, kt, :], cs.ap[:, kt, :], AF.Exp, [cs, nb], [Elast], bias=nb.ap[:, kt:kt + 1], scale=LAM)
            dv("tensor_tensor", [bb, Einv], [btl], f2(btl), f2(bb), f2(Einv), ALU.mult)
            dv("tensor_tensor", [kp, Einv], [tmpA], f2(tmpA), f2(kp), f2(Einv), ALU.mult)
            dv("tensor_tensor", [bb, Elast], [bh], f2(bh), f2(bb), f2(Elast), ALU.mult)
            dv("tensor_tensor", [kp, Elast], [kh], f2(kh), f2(kp), f2(Elast), ALU.mult)
            for (dst, src, col) in ((blo, btl, 0), (bhi, btl, 1), (klo, tmpA, 0), (khi, tmpA, 1)):
                if MASK_POOL:
                    pl("tensor_scalar", [src, hsel], [dst], f2(dst), f2(src), hsel.ap[:, col:col + 1], None, ALU.mult)
                    continue
                S.act(f2(dst), f2(src), AF.Identity, [src, hsel], [dst], scale=hsel.ap[:, col:col + 1], bias=0.0)
            for (dst, col) in ((alo, 0), (ahi, 1)):
                if MASK_POOL:
                    pl("tensor_scalar", [AR, hsel], [dst], dst.ap, AR.ap[:, :, 0, :], hsel.ap[:, col:col + 1], None, ALU.mult)
                    continue
                S.act(dst.ap, AR.ap[:, :, 0, :], AF.Identity, [AR, hsel], [dst], scale=hsel.ap[:, col:col + 1], bias=0.0)
            dv("tensor_tensor", [raw, kp], [tmpB], tmpB.ap, rT, kp.ap, ALU.mult)
            dv("tensor_tensor", [tmpB, rkf], [tmpB], tmpB.ap, tmpB.ap, bc8(rkf), ALU.mult)
            Pr = self.bank()
            for kt in range(8):
                S.mm(Pr.ap[:, 2 * kt:2 * kt + 2], tmpB.ap[:, kt, :], hsel.ap, kt == 0, kt == 7, [tmpB, hsel], [Pr])
            S.act(rk16.ap, Pr.ap[:, 0:16], AF.Copy, [Pr], [rk16])
            S.act(sx.ap, raw.ap[:, 25, :], AF.Sigmoid, [raw], [sx])
            S.act(sx2.ap, raw.ap[0:32, 26, :], AF.Sigmoid, [raw], [sx2])
            for hh in range(2):
                P = self.bank()
                S.mm(P.ap, sx.ap, g2a.ap[:, hh * 512:(hh + 1) * 512], True, False, [sx, g2a], [P])
                S.mm(P.ap, sx2.ap, g2b.ap[:, hh * 512:(hh + 1) * 512], False, True, [sx2, g2b], [P])
                S.act(g_tok.ap[:, hh * 512:(hh + 1) * 512], P.ap, AF.Copy, [P], [g_tok])
            for (src_ap, src_t, dst) in ((vT, raw, v_tok), (bh.ap, bh, bh_tok), (kh.ap, kh, kh_tok)):
                for hh in range(2):
                    P = self.bank()
                    for q in range(4):
                        kt = hh * 4 + q
                        S.tr(P.ap[:, q * 128:(q + 1) * 128], src_ap[:, kt, :], ident.ap, [src_t, ident], [P])
                    S.act(dst.ap[:, hh * 512:(hh + 1) * 512], P.ap, AF.Copy, [P], [dst])


    def heads(c, step):
            y_tok = tmpA
            def phaseA(hg, sl):
                heads = [4 * hg + x for x in range(4)]
                ABm, AKm = sl["ABm"], sl["AKm"]
                PA = [self.bank(), self.bank()]
                PB = [self.bank(), self.bank()]
                PX = self.bank()
                for hl, h in enumerate(heads):
                    kt, half = h // 2, h % 2
                    bsel = (blo, bhi)[half]
                    ksel = (klo, khi)[half]
                    asel = (alo, ahi)[half]
                    ar_rhs = AR.ap[:, kt, :, :]
                    oa = PA[hl // 2].ap[:, (hl % 2) * 256:(hl % 2 + 1) * 256]
                    ob_ = PB[hl // 2].ap[:, (hl % 2) * 256:(hl % 2 + 1) * 256]
                    S.mm(oa, bsel.ap[:, kt, :], ar_rhs, hl % 2 == 0, hl % 2 == 1, [bsel, AR], [PA[hl // 2]])
                    S.mm(ob_, ksel.ap[:, kt, :], ar_rhs, hl % 2 == 0, hl % 2 == 1, [ksel, AR], [PB[hl // 2]])
                    S.mm(PX.ap[:, hl * 128:(hl + 1) * 128], asel.ap[:, kt, :], btl.ap[:, kt, :], hl == 0, hl == 3, [asel, btl], [PX])
                mAB2 = mAB.ap.unsqueeze(1).to_broadcast([128, 2, 256])
                for q in range(2):
                    dv("tensor_tensor", [PA[q], mAB], [ABm], ABm.ap[:, 2 * q:2 * q + 2, :],
                       PA[q].ap.rearrange("p (a b) -> p a b", a=2), mAB2, ALU.mult)
                    dv("tensor_tensor", [PB[q], mAB], [AKm], AKm.ap[:, 2 * q:2 * q + 2, :],
                       PB[q].ap.rearrange("p (a b) -> p a b", a=2), mAB2, ALU.mult)
                dv("tensor_tensor", [PX, mLT], [sl["XW"][0]], sl["XW"][0].ap[:, :, 0:128], PX.ap.rearrange("p (a b) -> p a b", a=4),
                   mLT.ap.unsqueeze(1).to_broadcast([128, 4, 128]), ALU.mult)
                S.act(sl["Yb"][0].ap, ABm.ap[:, :, 0:128], AF.Copy, [ABm], [sl["Yb"][0]])
                PW = self.bank()
                for hl, h in enumerate(heads):
                    kt = h // 2
                    o = PW.ap[:, hl * 64:(hl + 1) * 64]
                    S.mm(o, AR.ap[:, kt, 0, :], Hp.ap[:, h, :], hl == 0, False, [AR, Hp.sub(h)], [PW])
                    S.mm(o, AKm.ap[:, hl, 0:128], v_tok.ap[:, h * 64:(h + 1) * 64], False, hl == 3, [AKm, v_tok], [PW])
                S.act(sl["XW"][0].ap[:, :, 128:192], PW.ap[:, 0:256].rearrange("p (h d) -> p h d", h=4), AF.Copy, [PW], [sl["XW"][0]])

            def level(sl, lv):
                Y, XW = sl["Yb"][lv % 2], sl["XW"][lv % 2]
                Yn, XWn = sl["Yb"][(lv + 1) % 2], sl["XW"][(lv + 1) % 2]
                if lv < 6:
                    PUX = [self.bank(), self.bank()]
                    for hl in range(4):
                        o = PUX[hl // 2].ap[:, (hl % 2) * 192:(hl % 2 + 1) * 192]
                        S.mm(o, Y.ap[:, hl, :], XW.ap[:, hl, :], hl % 2 == 0, hl % 2 == 1, [Y, XW], [PUX[hl // 2]])
                    PY2 = self.bank()
                    for hl in range(4):
                        S.mm(PY2.ap[:, hl * 128:(hl + 1) * 128], XW.ap[:, hl, 0:128], Y.ap[:, hl, :], hl == 0, hl == 3, [XW, Y], [PY2])
                    for q in range(2):
                        view = PUX[q].ap[:, 0:384].rearrange("p (a c) -> p a c", a=2)
                        dv("tensor_tensor", [PUX[q], XW], [XWn], XWn.ap[:, 2 * q:2 * q + 2, 128:192], view[:, :, 128:192],
                           XW.ap[:, 2 * q:2 * q + 2, 128:192], ALU.add)
                        S.act(XWn.ap[:, 2 * q:2 * q + 2, 0:128], view[:, :, 0:128], AF.Copy, [PUX[q]], [XWn])
                    dv("tensor_copy", [PY2], [Yn], f2(Yn), PY2.ap)
                else:
                    PU = self.bank()
                    for hl in range(4):
                        S.mm(PU.ap[:, hl * 64:(hl + 1) * 64], Y.ap[:, hl, :], XW.ap[:, hl, 128:192], hl == 0, hl == 3, [Y, XW], [PU])
                    dv("tensor_tensor", [PU, XW], [XWn], XWn.ap[:, :, 128:192], PU.ap[:, 0:256].rearrange("p (h d) -> p h d", h=4),
                       XW.ap[:, :, 128:192], ALU.add)

            def phaseY(hg, sl):
                heads = [4 * hg + x for x in range(4)]
                ABm, AKm = sl["ABm"], sl["AKm"]
                U = sl["XW"][1]
                PY = self.bank()
                for hl, h in enumerate(heads):
                    kt = h // 2
                    o = PY.ap[:, hl * 64:(hl + 1) * 64]
                    S.mm(o, AR.ap[:, kt, 1, :], Hp.ap[:, h, :], hl == 0, False, [AR, Hp.sub(h)], [PY])
                    S.mm(o, ABm.ap[:, hl, 128:256], U.ap[:, hl, 128:192], False, False, [ABm, U], [PY])
                    S.mm(o, AKm.ap[:, hl, 128:256], v_tok.ap[:, h * 64:(h + 1) * 64], False, hl == 3, [AKm, v_tok], [PY])
                PH = self.bank()
                for hl, h in enumerate(heads):
                    kt = h // 2
                    o = PH.ap[:, hl * 64:(hl + 1) * 64]
                    S.mm(o, bh_tok.ap[:, kt * 128:(kt + 1) * 128], U.ap[:, hl, 128:192], hl == 0, False, [bh_tok, U], [PH])
                    S.mm(o, kh_tok.ap[:, kt * 128:(kt + 1) * 128], v_tok.ap[:, h * 64:(h + 1) * 64], False, hl == 3, [kh_tok, v_tok], [PH])
                S.act(y_tok.ap.rearrange("p a b -> p (a b)")[:, hg * 256:(hg + 1) * 256], PY.ap[:, 0:256], AF.Copy, [PY], [y_tok])
                for hl, h in enumerate(heads):
                    kt, half = h // 2, h % 2
                    r0 = 64 * half
                    dv("scalar_tensor_tensor", [Hp.sub(h), PLs[c % 2], PH], [Hp.sub(h)], Hp.ap[r0:r0 + 64, h, :], Hp.ap[r0:r0 + 64, h, :],
                       PLs[c % 2].ap[r0:r0 + 64, kt:kt + 1], PH.ap[r0:r0 + 64, hl * 64:(hl + 1) * 64], ALU.mult, ALU.add)

            for pair in range(2):
                gA, gB = 2 * pair, 2 * pair + 1
                if NO_IL:
                    for (g_, sl_) in ((gA, slots[0]), (gB, slots[1])):
                        phaseA(g_, sl_)
                        for lv in range(7):
                            level(sl_, lv)
                        phaseY(g_, sl_)
                    continue
                phaseA(gA, slots[0])
                phaseA(gB, slots[1])
                for lv in range(7):
                    level(slots[0], lv)
                    step(); step()
                    level(slots[1], lv)
                    step(); step()
                phaseY(gA, slots[0])
                phaseY(gB, slots[1])


    def post(c):
            y_tok = tmpA
            yf = y_tok.ap.rearrange("p a b -> p (a b)")
            y3 = yf.rearrange("p (h d) -> p h d", h=16)
            sum_, sq_, mean, rstd = st16
            t1, t2 = y_tok, btl
            t1f, t2f = f2(t1), f2(t2)
            dv("tensor_reduce", [y_tok], [sum_], sum_.ap, y3, AX.X, ALU.add)
            dv("tensor_tensor", [y_tok], [t2], t2f, yf, yf, ALU.mult)
            dv("tensor_reduce", [t2], [sq_], sq_.ap, t2f.rearrange("p (h d) -> p h d", h=16), AX.X, ALU.add)
            dv("tensor_scalar", [sum_], [mean], mean.ap, sum_.ap, 1.0 / 64, None, ALU.mult)
            dv("tensor_tensor", [mean], [rstd], rstd.ap, mean.ap, mean.ap, ALU.mult)
            dv("scalar_tensor_tensor", [sq_, rstd], [rstd], rstd.ap, sq_.ap, 1.0 / 64, rstd.ap, ALU.mult, ALU.subtract)
            S.act(rstd.ap, rstd.ap, AF.Sqrt, [rstd], [rstd], bias=GN_EPS, scale=1.0)
            dv("reciprocal", [rstd], [rstd], rstd.ap, rstd.ap)
            b16 = lambda t: t.ap.unsqueeze(2).to_broadcast([128, 16, 64])
            t13 = t1f.rearrange("p (h d) -> p h d", h=16)
            t23 = t2f.rearrange("p (h d) -> p h d", h=16)
            dv("tensor_tensor", [y_tok, mean], [t1], t13, y3, b16(mean), ALU.subtract)
            dv("tensor_tensor", [t1, rstd], [t1], t13, t13, b16(rstd), ALU.mult)
            dv("tensor_tensor", [t1, lnw], [t1], t1f, t1f, lnw.ap, ALU.mult)
            dv("tensor_tensor", [t1, lnb], [t1], t1f, t1f, lnb.ap, ALU.add)
            dv("tensor_tensor", [v_tok, rk16], [t2], t23, v_tok.ap.rearrange("p (h d) -> p h d", h=16), b16(rk16), ALU.mult)
            dv("tensor_tensor", [t1, t2], [t1], t1f, t1f, t2f, ALU.add)
            dv("tensor_tensor", [t1, g_tok], [t1], t1f, t1f, g_tok.ap, ALU.mult)
            if self.dbg and c == 0:
                dd = self.dscr("dbg_oa0", [128, 1024])
                S.dma("sp", dd.ap, t1f, [t1], [dd])
                dd2 = self.dscr("dbg_y0", [128, 1024])
                S.dma("sp", dd2.ap, yf, [y_tok], [dd2])
            ob = oast[c % 2]
            for hh in range(2):
                P = self.bank()
                for q in range(4):
                    kt = hh * 4 + q
                    S.tr(P.ap[:, q * 128:(q + 1) * 128], t1f[:, kt * 128:(kt + 1) * 128], ident.ap, [t1, ident], [P])
                S.act(ob.ap[:, hh * 4:(hh + 1) * 4, :], P.ap.rearrange("p (q t) -> p q t", q=4), AF.Copy, [P], [ob])
            S.dma("sp", oaT_v[:, :, c * 128:(c + 1) * 128], ob.ap, [ob], [self.oaT_d])


    gen = P1(0)
    for _ in gen:
        pass
    for c in range(NT):
        P2(c)
        gen = P1(c + 1) if c + 1 < NT else iter(())

        def step(gen=gen):
            next(gen, None)
        heads(c, step)
        for _ in gen:
            pass
        post(c)
    S.barrier()
    ar.pop()


KB.stage3 = stage3


def stage5(self):
    ar, S = self.ar, self.S
    d = self.din
    wor_d = d("w_o_rwkv", [1024, D])
    won_d = d("w_o_nsa", [1024, D])
    wout_d = d("w_out", [D, D])
    ar.push()
    mixT = ar.alloc([16, SEQ], BF16, "mixT")
    ar.push()
    oaT = ar.alloc([8, SEQ], BF16, "oaT")
    obT = ar.alloc([8, SEQ], BF16, "obT")
    S.dma("sp", oaT.ap, self.oaT_d.ap.rearrange("(k p) t -> p k t", p=128), [self.oaT_d], [oaT])
    S.dma("sp", obT.ap, self.obT_d.ap.rearrange("(k p) t -> p k t", p=128), [self.obT_d], [obT])
    woa = [ar.alloc([8, 128], BF16, f"woa{i}") for i in range(2)]
    wob = [ar.alloc([8, 128], BF16, f"wob{i}") for i in range(2)]
    sga = [ar.alloc([SEQ], BF16, f"sga{i}") for i in range(2)]
    sgb = [ar.alloc([SEQ], BF16, f"sgb{i}") for i in range(2)]
    t1s = [ar.alloc([512], F32, f"t1_{i}") for i in range(2)]
    t2s = [ar.alloc([512], F32, f"t2_{i}") for i in range(2)]
    wor_v = wor_d.ap.rearrange("(k p) c -> p k c", p=128)
    won_v = won_d.ap.rearrange("(k p) c -> p k c", p=128)
    cnt = 0
    mod_steps = []
    for jt in range(16):
        wa, wb, sa, sb = woa[jt % 2], wob[jt % 2], sga[jt % 2], sgb[jt % 2]
        S.dma("pool", wa.ap, wor_v[:, :, jt * 128:(jt + 1) * 128], [wor_d], [wa])
        S.dma("pool", wb.ap, won_v[:, :, jt * 128:(jt + 1) * 128], [won_d], [wb])
        S.dma("sp", sa.ap, self.mgT_d.ap[jt * 128:(jt + 1) * 128, :], [self.mgT_d], [sa])
        S.dma("sp", sb.ap, self.mgT_d.ap[2048 + jt * 128:2048 + (jt + 1) * 128, :], [self.mgT_d], [sb])
        if jt >= 1:
            for _ in range(2):
                if mod_steps:
                    mod_steps.pop(0)()
        for n in range(4):
            Pa = self.bank()
            for k in range(8):
                S.mm(Pa.ap, wa.ap[:, k, :], oaT.ap[:, k, n * 512:(n + 1) * 512], k == 0, k == 7, [wa, oaT], [Pa])
            Pb = self.bank()
            for k in range(8):
                S.mm(Pb.ap, wb.ap[:, k, :], obT.ap[:, k, n * 512:(n + 1) * 512], k == 0, k == 7, [wb, obT], [Pb])
            t1, t2 = t1s[cnt % 2], t2s[cnt % 2]
            cnt += 1
            S.v("dve", "tensor_tensor", [Pa, sa], [t1], t1.ap, Pa.ap, sa.ap[:, n * 512:(n + 1) * 512], ALU.mult)
            S.v("dve", "tensor_tensor", [Pb, sb], [t2], t2.ap, Pb.ap, sb.ap[:, n * 512:(n + 1) * 512], ALU.mult)
            S.v("dve", "tensor_tensor", [t1, t2], [mixT.sub(n)], mixT.ap[:, jt, n * 512:(n + 1) * 512], t1.ap, t2.ap, ALU.add)
    while mod_steps:
        mod_steps.pop(0)()
    S.barrier()
    ar.pop()
    wout = ar.alloc([16, D], BF16, "wout")
    wout_v = wout_d.ap.rearrange("(k p) c -> p k c", p=128)
    for nn in range(4):
        S.dma("pool", wout.ap[:, :, nn * 512:(nn + 1) * 512], wout_v[:, :, nn * 512:(nn + 1) * 512], [wout_d], [wout.sub(nn)])
    xbs = [ar.alloc([D], F32, f"x5_{i}") for i in range(2)]
    obs = [ar.alloc([D], F32, f"o5_{i}") for i in range(2)]
    tms = [ar.alloc([512], F32, f"tm5_{i}") for i in range(2)]
    cnt = 0
    for tt in range(NT):
        xb, ob = xbs[tt % 2], obs[tt % 2]
        S.dma("sp", xb.ap, self.x_d.ap[tt * 128:(tt + 1) * 128, :], [self.x_d], [xb])
        for nn in range(4):
            P = self.bank()
            for k in range(16):
                S.mm(P.ap, mixT.ap[:, k, tt * 128:(tt + 1) * 128], wout.ap[:, k, nn * 512:(nn + 1) * 512], k == 0, k == 15,
                     [mixT.sub(tt // 4), wout.sub(nn)], [P])
            tm = tms[cnt % 2]
            cnt += 1
            S.v("dve", "tensor_tensor", [P, self.gt1], [tm], tm.ap, P.ap, self.gt1.ap[:, nn * 512:(nn + 1) * 512], ALU.mult)
            S.v("dve", "tensor_tensor", [tm, xb], [ob], ob.ap[:, nn * 512:(nn + 1) * 512], tm.ap, xb.ap[:, nn * 512:(nn + 1) * 512], ALU.add)
        S.dma("pool", self.x1_d.ap[tt * 128:(tt + 1) * 128, :], ob.ap, [ob], [self.x1_d])
    S.barrier()
    ar.pop()


def stage6(self):
    ar, S = self.ar, self.S
    d = self.din
    hT = self.hT
    ar.push()
    uT = ar.alloc([64, 512], BF16, "uT")
    wups = [ar.alloc([16, 256], BF16, f"wup{i}") for i in range(2)]
    wdns = [ar.alloc([8, 512], BF16, f"wdn{i}") for i in range(3)]
    rts = [ar.alloc([512], F32, f"rt{i}") for i in range(2)]
    xps = [ar.alloc([512], F32, f"xp{i}") for i in range(2)]
    ops_ = [ar.alloc([512], F32, f"op{i}") for i in range(2)]
    tms = [ar.alloc([512], F32, f"tm6_{i}") for i in range(2)]
    c_up = c_dn = c_e = 0
    for c in range(4):
        for fb in range(32):
            wu = wups[c_up % 2]
            c_up += 1
            S.dma("sp", wu.ap, self.wupb.ap[fb].rearrange("p (k c) -> p k c", k=16), [self.wupb], [wu])
            for ft in range(2):
                f = fb * 2 + ft
                P = self.ps[(c_e) % 4]
                rt = rts[c_e % 2]
                c_e += 1
                for k in range(16):
                    S.mm(P.ap, wu.ap[:, k, ft * 128:(ft + 1) * 128], hT.ap[:, k, c * 512:(c + 1) * 512], k == 0, k == 15,
                         [wu, hT.sub(c)], [P])
                S.act(rt.ap, P.ap, AF.Relu, [P], [rt])
                S.v("dve", "tensor_tensor", [rt], [uT.sub(f)], uT.ap[:, f, :], rt.ap, rt.ap, ALU.mult)
        for nn in range(4):
            accs = self.ps[4:8] if (nn % 2 == 0) else self.ps[0:4]
            for f8 in range(8):
                wd = wdns[c_dn % 3]
                c_dn += 1
                S.dma("sp", wd.ap, self.wdnb.ap[nn, f8].rearrange("p (f c) -> p f c", f=8), [self.wdnb], [wd])
                for fi in range(8):
                    f = f8 * 8 + fi
                    for tt in range(4):
                        S.mm(accs[tt].ap, uT.ap[:, f, tt * 128:(tt + 1) * 128], wd.ap[:, fi, :], f == 0, f == 63,
                             [uT.sub(f), wd], [accs[tt]])
            for tt in range(4):
                row = (c * 4 + tt) * 128
                xp, op, tm = xps[c_e % 2], ops_[c_e % 2], tms[c_e % 2]
                c_e += 1
                S.dma("pool", xp.ap, self.x1_d.ap[row:row + 128, nn * 512:(nn + 1) * 512], [self.x1_d], [xp])
                S.v("dve", "tensor_tensor", [accs[tt], self.gt2], [tm], tm.ap, accs[tt].ap, self.gt2.ap[:, nn * 512:(nn + 1) * 512], ALU.mult)
                S.v("dve", "tensor_tensor", [tm, xp], [op], op.ap, tm.ap, xp.ap, ALU.add)
                S.dma("pool", self.out_d.ap[row:row + 128, nn * 512:(nn + 1) * 512], op.ap, [op], [self.out_d])
    S.barrier()
    ar.pop()


KB.stage5 = stage5
KB.stage6 = stage6


def precast(self):
    S = self.S
    self.wup_in = self.din("w_up", [D, DFF])
    self.wdn_in = self.din("w_down", [DFF, D])
    self.wupb = T(self.nc.dram_tensor("wupb_i", [32, 128, 16 * 256], BF16).ap(), "wupb")
    self.wdnb = T(self.nc.dram_tensor("wdnb_i", [4, 8, 128, 8 * 512], BF16).ap(), "wdnb")
    wup_v = self.wup_in.ap.rearrange("(k p) c -> p k c", p=128)
    wdn_v = self.wdn_in.ap.rearrange("(f p) c -> p f c", p=128)
    for fb in range(32):
        S.dma("pool", self.wupb.ap[fb].rearrange("p (k c) -> p k c", k=16), wup_v[:, :, fb * 256:(fb + 1) * 256],
              [self.wup_in], [self.wupb])
    for nn in range(4):
        for f8 in range(8):
            S.dma("pool", self.wdnb.ap[nn, f8].rearrange("p (f c) -> p f c", f=8),
                  wdn_v[:, f8 * 8:(f8 + 1) * 8, nn * 512:(nn + 1) * 512], [self.wdn_in], [self.wdnb])


KB.precast = precast


_CACHE = {}


def _all_consts():
    c = host_consts()
    c.update(nsa_consts())
    c.update(rwkv_consts())
    return c


def prep_all(inp, b, consts):
    m = prep_core(inp, b, consts)
    nsa_prep(inp, m)
    rwkv_prep(inp, m)
    m["w_o_rwkv"] = inp["w_o_rwkv"][0]
    m["w_o_nsa"] = inp["w_o_nsa"][0]
    m["w_out"] = inp["w_out"][0]
    m["w_up"] = inp["w_up"][0]
    m["w_down"] = inp["w_down"][0]
    return m


def kernel(**inputs):
    inp = {k: np.asarray(v) for k, v in inputs.items()}
    if "nc" not in _CACHE:
        _CACHE["nc"] = KB(dbg=False).build()
        _CACHE["consts"] = _all_consts()
    nc = _CACHE["nc"]
    consts = _CACHE["consts"]
    in_maps = [prep_all(inp, b, consts) for b in range(8)]
    res = run_bass_kernel_spmd(nc, in_maps, core_ids=list(range(8)))
    out = np.stack([np.asarray(r["out"]) for r in res.results], axis=0)
    return out.astype(np.float32)
```

```python
import numpy as np
import concourse.bass as bass
import concourse.mybir as mybir

F32 = mybir.dt.float32
BF16 = mybir.dt.bfloat16
AF = mybir.ActivationFunctionType
ALU = mybir.AluOpType
AX = mybir.AxisListType

ENGS = ("pe", "act", "dve", "pool", "sp")
NSLOT = {"sp": 40, "pool": 24}


class Buf:
    __slots__ = ("w", "r_eng", "r_dma", "name")

    def __init__(self, name=""):
        self.w = None
        self.r_eng = {}
        self.r_dma = []
        self.name = name


class T(Buf):
    __slots__ = ("ap", "subs")

    def __init__(self, ap, name=""):
        Buf.__init__(self, name)
        self.ap = ap
        self.subs = {}

    def __getitem__(self, idx):
        return self.ap[idx]

    def sub(self, key):
        b = self.subs.get(key)
        if b is None:
            b = self.subs[key] = Buf(f"{self.name}.{key}")
        return b


class Op:
    __slots__ = ("eng", "fn", "deps", "is_dma", "slot", "sig", "val", "dsem", "dval", "idx")

    def __init__(self, eng, fn, is_dma):
        self.eng = eng
        self.fn = fn
        self.is_dma = is_dma
        self.deps = []
        self.sig = False
        self.val = 0
        self.slot = -1
        self.dsem = None
        self.dval = 0


class Sched:
    def __init__(self, nc):
        self.nc = nc
        self.ops = {e: [] for e in ENGS}
        self.bar = {e: [] for e in ENGS}
        self.dma_since_bar = []
        self.slot_last = {q: [None] * n for q, n in NSLOT.items()}
        self.slot_n = {q: 0 for q in NSLOT}

    def rec(self, eng, fn, reads=(), writes=(), is_dma=False):
        op = Op(eng, fn, is_dma)
        deps = []
        for b in reads:
            if b.w is not None:
                deps.append(b.w)
        for b in writes:
            if b.w is not None:
                deps.append(b.w)
            deps.extend(b.r_eng.values())
            deps.extend(b.r_dma)
        if self.bar[eng]:
            deps.extend(self.bar[eng])
            self.bar[eng] = []
        if is_dma:
            n = self.slot_n[eng]
            self.slot_n[eng] = n + 1
            s = n % NSLOT[eng]
            op.slot = s
            prev = self.slot_last[eng][s]
            if prev is not None:
                deps.append(prev)
            self.slot_last[eng][s] = op
            self.dma_since_bar.append(op)
        seen = set()
        for d in deps:
            if d is op or id(d) in seen:
                continue
            seen.add(id(d))
            op.deps.append(d)
        for b in reads:
            if is_dma:
                b.r_dma.append(op)
            else:
                b.r_eng[eng] = op
        for b in writes:
            b.w = op
            b.r_eng = {}
            b.r_dma = []
        self.ops[eng].append(op)
        return op

    def barrier(self):
        deps = [self.ops[e][-1] for e in ENGS if self.ops[e]] + self.dma_since_bar
        self.dma_since_bar = []
        for e in ENGS:
            self.bar[e] = list(deps)

    def mm(self, out, lhsT, rhs, start, stop, reads, writes, **kw):
        return self.rec("pe", lambda e: e.matmul(out, lhsT, rhs, start=start, stop=stop, **kw), reads, writes)

    def tr(self, out, in_, ident, reads, writes):
        return self.rec("pe", lambda e: e.transpose(out, in_, ident), reads, writes)

    def act(self, out, in_, func, reads, writes, bias=None, scale=None, accum_out=None):
        kw = {}
        if bias is not None:
            kw["bias"] = bias
        if scale is not None:
            kw["scale"] = scale
        if accum_out is not None:
            kw["accum_out"] = accum_out
        return self.rec("act", lambda e: e.activation(out=out, in_=in_, func=func, **kw), reads, writes)

    def v(self, eng, meth, reads, writes, *a, **kw):
        return self.rec(eng, lambda e: getattr(e, meth)(*a, **kw), reads, writes)

    def dma(self, q, out, in_, reads, writes, **kw):
        return self.rec(q, lambda e: e.dma_start(out=out, in_=in_, **kw), reads, writes, is_dma=True)

    def finalize(self):
        for e in ENGS:
            for op in self.ops[e]:
                for d in op.deps:
                    if d.is_dma:
                        continue
                    if d.eng == "pe" and op.eng == "pe" and not op.is_dma:
                        continue
                    d.sig = True
        for e in ENGS:
            n = 0
            for op in self.ops[e]:
                if op.sig and not op.is_dma:
                    n += 1
                    op.val = n

    def emit_all(self, stack):
        nc = self.nc
        self.finalize()
        self.esem = {e: stack.enter_context(nc.semaphore("es_" + e)) for e in ENGS}
        self.dsem = {q: [stack.enter_context(nc.semaphore(f"ds_{q}{i}")) for i in range(n)] for q, n in NSLOT.items()}
        uses = {q: [0] * n for q, n in NSLOT.items()}
        for q in NSLOT:
            for op in self.ops[q]:
                if op.is_dma:
                    uses[q][op.slot] += 1
                    op.dsem = self.dsem[q][op.slot]
                    op.dval = 16 * uses[q][op.slot]
        block = stack.enter_context(nc.Block())
        sched = self

        def emit(name, eng):
            known = {}
            for op in sched.ops[name]:
                for d in op.deps:
                    if d.is_dma:
                        sem, val = d.dsem, d.dval
                    else:
                        if d.eng == "pe" and name == "pe" and not op.is_dma:
                            continue
                        sem, val = sched.esem[d.eng], d.val
                    k = id(sem)
                    if known.get(k, 0) >= val:
                        continue
                    eng.wait_ge(sem, val)
                    known[k] = val
                ins = op.fn(eng)
                if op.is_dma:
                    ins.then_inc(op.dsem, 16)
                elif op.sig:
                    ins.then_inc(sched.esem[name], 1)
            if name == "sp":
                for q in NSLOT:
                    for i, u in enumerate(uses[q]):
                        if u:
                            eng.wait_ge(sched.dsem[q][i], 16 * u)

        @block.tensor
        def _(e):
            emit("pe", e)

        @block.scalar
        def _(e):
            emit("act", e)

        @block.vector
        def _(e):
            emit("dve", e)

        @block.gpsimd
        def _(e):
            emit("pool", e)

        @block.sync
        def _(e):
            emit("sp", e)


class Arena:
    def __init__(self, ap, nwords):
        self.ap = ap
        self.n = nwords
        self.off = 0
        self.marks = []

    def push(self):
        self.marks.append(self.off)

    def pop(self):
        self.off = self.marks.pop()

    def alloc(self, shape, dtype=F32, name="", parts=128):
        n = int(np.prod(shape))
        words = n if dtype == F32 else (n + 1) // 2
        words = (words + 7) // 8 * 8
        assert self.off + words <= self.n, f"arena overflow {name} {self.off}+{words}>{self.n}"
        ap = self.ap[0:parts, self.off:self.off + words]
        self.off += words
        if dtype != F32:
            ap = ap.bitcast(dtype)
        ap = ap[:, 0:n]
        if len(shape) == 2:
            ap = ap.rearrange("p (a b) -> p a b", a=shape[0])
        elif len(shape) == 3:
            ap = ap.rearrange("p (a b c) -> p a b c", a=shape[0], b=shape[1])
        elif len(shape) == 4:
            ap = ap.rearrange("p (a b c d) -> p a b c d", a=shape[0], b=shape[1], c=shape[2])
        return T(ap, name)

from contextlib import ExitStack
from concourse.bass_utils import run_bass_kernel_spmd

D = 2048
SEQ = 2048
NT = 16
RWC = 3360
NB = 3360
MB = 5968
INC = 10064
DFF = 8192
EPS = 1e-6
GN_EPS = 64e-5
NEG = -30000.0
ARENA_WORDS = 52000


class KB:
    def __init__(self, dbg=False, stages=(0, 1, 2, 3, 4, 5, 6)):
        self.nc = bass.Bass("TRN2", target_bir_lowering=False)
        self.S = Sched(self.nc)
        self.dbg = dbg
        self.stages = stages
        self.bank_i = 0

    def din(self, name, shape, dt=F32):
        return T(self.nc.dram_tensor(name, list(shape), dt, kind="ExternalInput").ap(), name)

    def dscr(self, name, shape, dt=F32, out=False):
        kind = "ExternalOutput" if (self.dbg or out) else "Internal"
        return T(self.nc.dram_tensor(name, list(shape), dt, kind=kind).ap(), name)

    def bank(self):
        b = self.ps[self.bank_i % 8]
        self.bank_i += 1
        return b

    def load(self, dst, src, q="sp"):
        self.S.dma(q, dst.ap, src[1], [src[0]], [dst])

    def build(self):
        nc, S = self.nc, self.S
        with ExitStack() as st:
            arena_t = st.enter_context(nc.sbuf_tensor("arena", [128, ARENA_WORDS], F32))
            self.ar = ar = Arena(arena_t, ARENA_WORDS)
            self.ps = [T(st.enter_context(nc.psum_tensor(f"ps{i}", [128, 512], F32))[:], f"ps{i}") for i in range(8)]
            self.declare()
            self.persistent()
            if 0 in self.stages:
                self.stage0()
            if 1 in self.stages:
                ar.push()
                self.hT = ar.alloc([16, SEQ], BF16, "hT")
                self.stage1(self.x_d, self.coef1, self.sh1, self.hT)
                if 2 in self.stages:
                    self.stage2()
                ar.pop()
            if 4 in self.stages:
                self.stage4()
            if 3 in self.stages:
                self.stage3()
            if 5 in self.stages:
                self.stage5()
            if 6 in self.stages:
                ar.push()
                self.hT = ar.alloc([16, SEQ], BF16, "h2T")
                self.stage1(self.x1_d, self.coef2, self.sh2, self.hT)
                self.stage6()
                ar.pop()
            S.emit_all(st)
        return nc

    def declare(self):
        d = self.din
        self.x_d = d("x", [SEQ, D])
        self.c_fm = d("c_fm", [128, 16])
        self.w_ada = d("w_ada", [D, 6 * D])
        self.b_ada = d("b_ada", [1, 6 * D])
        self.n1g = d("n1g_fm", [128, 16])
        self.n2g = d("n2g_fm", [128, 16])
        self.w_in = d("w_in", [D, INC])
        self.ident_d = d("ident", [128, 128])
        self.bones_d = d("bones", [128, 128])
        self.mu_d = d("mu_fm", [128, 27])
        self.qkg_d = d("qkg_fm", [128, 5])
        s = self.dscr
        self.rwT_d = s("rwT", [27 * 128, SEQ])
        self.qT_d = s("qT", [1024, SEQ], BF16)
        self.kvcT_d = s("kvcT", [512, SEQ], BF16)
        self.ksT_d = s("ksT", [4, 2, 128, SEQ], BF16)
        self.kwT_d = s("kwT", [4, 2, 128, SEQ], BF16)
        self.vv_d = s("vv", [SEQ, 512], BF16)
        self.gates_d = s("gates", [SEQ, 48])
        self.mgT_d = s("mgT", [4096, SEQ], BF16)
        self.x1_d = s("x1", [SEQ, D])
        self.out_d = self.dscr("out", [SEQ, D], out=True)

    def persistent(self):
        ar, S = self.ar, self.S
        self.ident = ar.alloc([128], F32, "ident")
        self.bones = ar.alloc([128], F32, "bones")
        S.dma("sp", self.ident.ap, self.ident_d.ap, [self.ident_d], [self.ident])
        S.dma("sp", self.bones.ap, self.bones_d.ap, [self.bones_d], [self.bones])
        self.coef1 = ar.alloc([16], F32, "coef1")
        self.sh1 = ar.alloc([16], F32, "sh1")
        self.coef2 = ar.alloc([16], F32, "coef2")
        self.sh2 = ar.alloc([16], F32, "sh2")
        self.gt1 = ar.alloc([D], F32, "gt1")
        self.gt2 = ar.alloc([D], F32, "gt2")

    def silu_rep(self):
        ar, S = self.ar, self.S
        cs = ar.alloc([16], F32, "cs")
        S.dma("sp", cs.ap, self.c_fm.ap, [self.c_fm], [cs])
        csb = ar.alloc([16], F32, "csb")
        S.act(csb.ap, cs.ap, AF.Silu, [cs], [csb])
        crep = ar.alloc([16, 128], BF16, "crep")
        S.v("dve", "tensor_copy", [csb], [crep], crep.ap, csb.ap.unsqueeze(2).to_broadcast([128, 16, 128]))
        return crep

    def stage0(self):
        ar, S = self.ar, self.S
        ar.push()
        bias_steps = self.nsa_bias_build() if 4 in self.stages else []
        mod = ar.alloc([6 * D], F32, "mod")
        crep = self.silu_rep()
        wbs = [ar.alloc([16, 512], BF16, f"wada{i}") for i in range(2)]
        bbs = [ar.alloc([512], F32, f"bada{i}") for i in range(2)]
        wsrc = self.w_ada.ap.rearrange("(k p) c -> p k c", p=128)
        for blk in range(24):
            wb, bb = wbs[blk % 2], bbs[blk % 2]
            c0 = blk * 512
            S.dma("pool", wb.ap, wsrc[:, :, c0:c0 + 512], [self.w_ada], [wb])
            S.dma("sp", bb.ap, self.b_ada.ap[0:1, c0:c0 + 512].partition_broadcast(128), [self.b_ada], [bb])
            P = self.bank()
            for k in range(16):
                S.mm(P.ap, crep.ap[:, k, :], wb.ap[:, k, :], k == 0, k == 15, [crep, wb], [P])
            S.v("dve", "tensor_tensor", [P, bb], [mod], mod.ap[:, c0:c0 + 512], P.ap, bb.ap, ALU.add)
            for _ in range(2):
                if bias_steps:
                    bias_steps.pop(0)()
        while bias_steps:
            bias_steps.pop(0)()
        tmp = ar.alloc([16, 128], F32, "dtmp")
        sc1 = ar.alloc([16], F32, "sc1")
        sc2 = ar.alloc([16], F32, "sc2")
        for dst, idx in ((self.sh1, 0), (sc1, 1), (self.sh2, 3), (sc2, 4)):
            src = mod.ap[:, idx * D:(idx + 1) * D].rearrange("p (k m) -> p k m", k=16)
            S.v("dve", "tensor_tensor", [mod, self.ident], [tmp], tmp.ap, src,
                self.ident.ap.unsqueeze(1).to_broadcast([128, 16, 128]), ALU.mult)
            S.v("dve", "tensor_reduce", [tmp], [dst], dst.ap, tmp.ap, AX.X, ALU.add)
        g = ar.alloc([16], F32, "gload")
        S.dma("sp", g.ap, self.n1g.ap, [self.n1g], [g])
        S.v("dve", "scalar_tensor_tensor", [sc1, g], [self.coef1], self.coef1.ap, sc1.ap, 1.0, g.ap, ALU.add, ALU.mult)
        g2 = ar.alloc([16], F32, "gload2")
        S.dma("sp", g2.ap, self.n2g.ap, [self.n2g], [g2])
        S.v("dve", "scalar_tensor_tensor", [sc2, g2], [self.coef2], self.coef2.ap, sc2.ap, 1.0, g2.ap, ALU.add, ALU.mult)
        S.v("dve", "tensor_copy", [mod], [self.gt1], self.gt1.ap, mod.ap[:, 2 * D:3 * D])
        S.v("dve", "tensor_copy", [mod], [self.gt2], self.gt2.ap, mod.ap[:, 5 * D:6 * D])
        S.barrier()
        ar.pop()

    def stage0b_setup(self):
        ar, S = self.ar, self.S
        crep = self.silu_rep()
        wb2 = [ar.alloc([16, 256], BF16, f"wada_b{i}") for i in range(2)]
        bb2 = [ar.alloc([256], F32, f"bada_b{i}") for i in range(2)]
        tmp = ar.alloc([2, 128], F32, "dtmp_b")
        sc2 = ar.alloc([16], F32, "sc2")
        wsrc = self.w_ada.ap.rearrange("(k p) c -> p k c", p=128)
        steps = []

        def mk(sb):
            def step():
                wb, bb = wb2[sb % 2], bb2[sb % 2]
                c0 = 3 * D + sb * 256
                S.dma("pool", wb.ap, wsrc[:, :, c0:c0 + 256], [self.w_ada], [wb])
                S.dma("sp", bb.ap, self.b_ada.ap[0:1, c0:c0 + 256].partition_broadcast(128), [self.b_ada], [bb])
                P = self.bank()
                for k in range(16):
                    S.mm(P.ap[:, 0:256], crep.ap[:, k, :], wb.ap[:, k, :], k == 0, k == 15, [crep, wb], [P])
                if sb < 16:
                    dst = self.sh2 if sb < 8 else sc2
                    j = sb % 8
                    t2 = tmp.ap.rearrange("p a b -> p (a b)")
                    S.v("dve", "tensor_tensor", [P, bb], [tmp], t2, P.ap[:, 0:256], bb.ap, ALU.add)
                    S.v("dve", "tensor_tensor", [tmp, self.ident], [tmp], tmp.ap, tmp.ap,
                        self.ident.ap.unsqueeze(1).to_broadcast([128, 2, 128]), ALU.mult)
                    S.v("dve", "tensor_reduce", [tmp], [dst], dst.ap[:, 2 * j:2 * j + 2], tmp.ap, AX.X, ALU.add)
                else:
                    o = (sb - 16) * 256
                    S.v("dve", "tensor_tensor", [P, bb], [self.gt2], self.gt2.ap[:, o:o + 256], P.ap[:, 0:256], bb.ap, ALU.add)
                if sb == 23:
                    g = ar.alloc([16], F32, "gload2")
                    S.dma("sp", g.ap, self.n2g.ap, [self.n2g], [g])
                    S.v("dve", "scalar_tensor_tensor", [sc2, g], [self.coef2], self.coef2.ap, sc2.ap, 1.0, g.ap, ALU.add, ALU.mult)
            return step
        return [mk(sb) for sb in range(24)]

    def stage1(self, src_d, coef, sh, hT):
        ar, S = self.ar, self.S
        ar.push()
        xbs = [ar.alloc([D], F32, f"xb{i}") for i in range(2)]
        junk = ar.alloc([D], F32, "junk")
        xs4s = [ar.alloc([4, D], F32, f"xs4_{i}") for i in range(1)]
        ss = ar.alloc([NT], F32, "ss")
        sr = ar.alloc([NT], F32, "sr")
        rstd = ar.alloc([NT], F32, "rstd")
        for grp in range(4):
            xs4 = xs4s[0]
            for tt in range(4):
                ti = grp * 4 + tt
                xb = xbs[ti % 2]
                S.dma("sp", xb.ap, src_d.ap[ti * 128:(ti + 1) * 128, :], [src_d], [xb])
                sst = ss.sub(ti)
                S.act(junk.ap, xb.ap, AF.Square, [xb], [sst], accum_out=ss.ap[:, ti:ti + 1])
                S.act(sr.ap[:, ti:ti + 1], ss.ap[:, ti:ti + 1], AF.Sqrt, [sst], [sr.sub(ti)], bias=EPS, scale=1.0 / D)
                S.v("dve", "reciprocal", [sr.sub(ti)], [rstd.sub(ti)], rstd.ap[:, ti:ti + 1], sr.ap[:, ti:ti + 1])
                S.v("dve", "tensor_scalar", [xb, rstd.sub(ti)], [xs4.sub(tt)], xs4.ap[:, tt, :], xb.ap,
                    rstd.ap[:, ti:ti + 1], None, ALU.mult)
            for k in range(16):
                P = self.bank()
                for tt in range(4):
                    S.tr(P.ap[:, tt * 128:(tt + 1) * 128], xs4.ap[:, tt, k * 128:(k + 1) * 128], self.ident.ap,
                         [xs4.sub(tt), self.ident], [P])
                S.act(hT.ap[:, k, grp * 512:(grp + 1) * 512], P.ap, AF.Identity, [P, coef, sh], [hT.sub(grp)],
                      bias=sh.ap[:, k:k + 1], scale=coef.ap[:, k:k + 1])
        S.barrier()
        ar.pop()

    def stage2(self):
        ar, S = self.ar, self.S
        hT = self.hT
        ar.push()
        wbs = [ar.alloc([16, 512], BF16, f"win{i}") for i in range(2)]
        wtm = ar.alloc([16, 560], BF16, "wtm")
        raws = [ar.alloc([2056], F32, f"raw{i}") for i in range(2)]
        tmps = [ar.alloc([SEQ], F32, f"mixt{i}") for i in range(2)]
        stg = [ar.alloc([SEQ], BF16, f"stg{i}") for i in range(4)]
        sqs = [ar.alloc([512], F32, f"sq{i}") for i in range(2)]
        srs = [ar.alloc([512], F32, f"sr{i}") for i in range(2)]
        ris = [ar.alloc([512], F32, f"ri{i}") for i in range(2)]
        mu = ar.alloc([27], F32, "mu")
        omu = ar.alloc([27], F32, "omu")
        qkg = ar.alloc([5], F32, "qkg")
        S.dma("sp", mu.ap, self.mu_d.ap, [self.mu_d], [mu])
        S.dma("sp", qkg.ap, self.qkg_d.ap, [self.qkg_d], [qkg])
        S.v("dve", "tensor_scalar", [mu], [omu], omu.ap, mu.ap, -1.0, 1.0, ALU.mult, ALU.add)
        S.v("dve", "tensor_scalar", [qkg], [qkg], qkg.ap[:, 1:5], qkg.ap[:, 1:5], 8.0, None, ALU.mult)
        for r in raws:
            S.v("dve", "memset", [], [r], r.ap[:, 0:1], 0.0)
        wsrc = self.w_in.ap.rearrange("(k p) c -> p k c", p=128)
        cnt = {"w": 0, "raw": 0, "stg": 0, "sq": 0}

        def proj_chunk(wb, m0, M, n):
            P = self.bank()
            for k in range(16):
                S.mm(P.ap[0:M, :], wb.ap[:, k, m0:m0 + M], hT.ap[:, k, n * 512:(n + 1) * 512], k == 0, k == 15,
                     [wb, hT.sub(n)], [P])
            return P

        def next_stg():
            t = stg[cnt["stg"] % 4]
            cnt["stg"] += 1
            return t

        def ep_rw(wb, m0, M, ti):
            raw = raws[cnt["raw"] % 2]
            tmp = tmps[cnt["raw"] % 2]
            cnt["raw"] += 1
            for n in range(4):
                P = proj_chunk(wb, m0, M, n)
                S.act(raw.ap[0:M, 1 + n * 512:1 + (n + 1) * 512], P.ap[0:M, :], AF.Copy, [P], [raw])
            S.v("dve", "tensor_scalar", [raw, mu], [tmp], tmp.ap[0:M, :], raw.ap[0:M, 0:SEQ], mu.ap[0:M, ti:ti + 1], None, ALU.mult)
            S.v("dve", "scalar_tensor_tensor", [raw, omu, tmp], [tmp], tmp.ap[0:M, :], raw.ap[0:M, 1:SEQ + 1],
                omu.ap[0:M, ti:ti + 1], tmp.ap[0:M, :], ALU.mult, ALU.add)
            S.dma("sp", self.rwT_d.ap[ti * 128:ti * 128 + M, :], tmp.ap[0:M, :], [tmp], [self.rwT_d])

        def ep_qk(wb, m0, gcols, dsts):
            outs = [next_stg() for _ in gcols]
            for n in range(4):
                P = proj_chunk(wb, m0, 128, n)
                i = cnt["sq"] % 2
                cnt["sq"] += 1
                sq, sr, ri = sqs[i], srs[i], ris[i]
                S.act(sq.ap, P.ap, AF.Square, [P], [sq])
                P2 = self.bank()
                S.mm(P2.ap, self.bones.ap, sq.ap, True, True, [self.bones, sq], [P2])
                S.act(sr.ap, P2.ap, AF.Sqrt, [P2], [sr], bias=64 * EPS, scale=1.0)
                S.v("dve", "reciprocal", [sr], [ri], ri.ap, sr.ap)
                for gc, o in zip(gcols, outs):
                    S.v("dve", "scalar_tensor_tensor", [P, qkg, ri], [o], o.ap[:, n * 512:(n + 1) * 512], P.ap,
                        qkg.ap[:, gc:gc + 1], ri.ap, ALU.mult, ALU.mult)
            for o, (dt_, dap) in zip(outs, dsts):
                S.dma("sp", dap, o.ap, [o], [dt_])

        def ep_act(wb, m0, func, dt_, dap):
            o = next_stg()
            for n in range(4):
                P = proj_chunk(wb, m0, 128, n)
                S.act(o.ap[:, n * 512:(n + 1) * 512], P.ap, func, [P], [o])
            S.dma("sp", dap, o.ap, [o], [dt_])

        def load_block(segs):
            wb = wbs[cnt["w"] % 2]
            cnt["w"] += 1
            for (c0, n, off) in segs:
                S.dma("pool", wb.ap[:, :, off:off + n], wsrc[:, :, c0:c0 + n], [self.w_in], [wb])
            return wb

        for b in range(7):
            if b < 6:
                wb = load_block([(512 * b, 512, 0)])
                for j in range(4):
                    ep_rw(wb, j * 128, 128, 4 * b + j)
            else:
                wb = load_block([(3072, 288, 0)])
                ep_rw(wb, 0, 128, 24)
                ep_rw(wb, 128, 128, 25)
                ep_rw(wb, 256, 32, 26)
        for b in range(2):
            wb = load_block([(NB + 512 * b, 512, 0)])
            for j in range(4):
                ti = 4 * b + j
                ep_qk(wb, j * 128, [0], [(self.qT_d, self.qT_d.ap[ti * 128:(ti + 1) * 128, :])])
        wb = load_block([(NB + 1024, 512, 0)])
        for j in range(4):
            ep_act(wb, j * 128, AF.Copy, self.kvcT_d, self.kvcT_d.ap[j * 128:(j + 1) * 128, :])
        for (c_base, dst, gc) in ((NB + 1024 + 512, self.ksT_d, 1), (NB + 1024 + 1024, self.kwT_d, 3)):
            segs = []
            for g in range(4):
                segs.append((c_base + 64 * g, 64, g * 128))
                segs.append((c_base + 64 * g, 64, g * 128 + 64))
            wb = load_block(segs)
            for g in range(4):
                ep_qk(wb, g * 128, [gc, gc + 1], [(dst, dst.ap[g, 0]), (dst, dst.ap[g, 1])])
        for b in range(8):
            wb = load_block([(MB + 512 * b, 512, 0)])
            for j in range(4):
                ti = 4 * b + j
                ep_act(wb, j * 128, AF.Sigmoid, self.mgT_d, self.mgT_d.ap[ti * 128:(ti + 1) * 128, :])
        for (c0, n, off) in ((NB + 1024 + 768, 256, 0), (NB + 1024 + 1280, 256, 256), (NB + 2560, 48, 512)):
            S.dma("pool", wtm.ap[:, :, off:off + n], wsrc[:, :, c0:c0 + n], [self.w_in], [wtm])
        vst = [ar.alloc([512], BF16, f"vst{i}") for i in range(2)]
        gst = [ar.alloc([48], F32, f"gst{i}") for i in range(2)]
        for tt in range(NT):
            P = self.bank()
            for k in range(16):
                S.mm(P.ap, hT.ap[:, k, tt * 128:(tt + 1) * 128], wtm.ap[:, k, 0:512], k == 0, k == 15,
                     [hT.sub(tt // 4), wtm], [P])
            v = vst[tt % 2]
            S.act(v.ap, P.ap, AF.Copy, [P], [v])
            S.dma("sp", self.vv_d.ap[tt * 128:(tt + 1) * 128, :], v.ap, [v], [self.vv_d])
            P = self.bank()
            for k in range(16):
                S.mm(P.ap[:, 0:48], hT.ap[:, k, tt * 128:(tt + 1) * 128], wtm.ap[:, k, 512:560], k == 0, k == 15,
                     [hT.sub(tt // 4), wtm], [P])
            gt = gst[tt % 2]
            S.act(gt.ap, P.ap[:, 0:48], AF.Sigmoid, [P], [gt])
            S.dma("sp", self.gates_d.ap[tt * 128:(tt + 1) * 128, :], gt.ap, [gt], [self.gates_d])
        S.barrier()
        ar.pop()


def _fm(v, ntile=None):
    v = np.asarray(v, np.float32).reshape(-1)
    n = (len(v) + 127) // 128 if ntile is None else ntile
    buf = np.zeros(n * 128, np.float32)
    buf[:len(v)] = v
    return np.ascontiguousarray(buf.reshape(n, 128).T)


def host_consts():
    c = {}
    c["ident"] = np.eye(128, dtype=np.float32)
    p = np.arange(128)
    c["bones"] = (p[:, None] // 64 == p[None, :] // 64).astype(np.float32)
    return c


def prep_core(inp, b, consts):
    m = dict(consts)
    m["x"] = np.ascontiguousarray(inp["x"][b])
    m["c_fm"] = _fm(inp["c"][b])
    m["w_ada"] = inp["w_ada"][0]
    m["b_ada"] = inp["b_ada"][0].reshape(1, -1)
    m["n1g_fm"] = _fm(inp["norm1_g"][0])
    m["n2g_fm"] = _fm(inp["norm2_g"][0])
    m["w_in"] = inp["w_in"][0]
    m["mu_fm"] = _fm(inp["rwkv_mu"][0], 27)
    qg = np.tile(inp["q_norm_g"][0], 2)
    kg = inp["k_norm_g"][0]
    z = np.zeros(64, np.float32)
    cols = [qg, np.concatenate([kg[1], z]), np.concatenate([z, kg[1]]),
            np.concatenate([kg[2], z]), np.concatenate([z, kg[2]])]
    m["qkg_fm"] = np.ascontiguousarray(np.stack(cols, axis=1).astype(np.float32))
    return m


def _rel_bucket_np(rel):
    n = np.maximum(rel, 0)
    nf = np.maximum(n, 16).astype(np.float32)
    large = 16 + (np.log(nf / np.float32(16)) / np.float32(np.log(8.0)) * np.float32(16)).astype(np.int32)
    large = np.minimum(large, 31)
    return np.where(n < 16, n, large)


def nsa_consts():
    c = {}
    NOH = 3 * 16384 + 17 * 128
    oh = np.zeros((33, NOH), np.float32)
    pos = np.arange(128)[:, None]
    t = np.arange(128)[None, :]
    for d, base in ((0, 0), (1, 128), (2, 512)):
        rel = base + t - pos
        if d == 0:
            mask = rel < 0
        elif d == 1:
            mask = np.zeros_like(rel, bool)
        else:
            mask = rel >= 512
        b = _rel_bucket_np(rel)
        sec = np.zeros((33, 128, 128), np.float32)
        for bb in range(32):
            sec[bb][(b == bb) & ~mask] = 1.0
        sec[32][mask] = 1.0
        oh[:, d * 16384:(d + 1) * 16384] = sec.reshape(33, -1)
    sec = np.zeros((33, 17, 128), np.float32)
    ti = np.arange(128)
    for r in range(16):
        m = r - 9
        rel = ti - 16 * m - 31
        b = _rel_bucket_np(rel)
        for bb in range(32):
            sec[bb, r, (b == bb) & (rel >= 0)] = 1.0
        sec[32, r, rel < 0] = 1.0
    sec[32, 16, :] = 1.0
    oh[:, 3 * 16384:] = sec.reshape(33, -1)
    c["nsa_oh"] = oh
    S = np.zeros((17, 16, 128), np.float32)
    for i in range(16):
        for n in range(127):
            m = n - 8 * i
            if -9 <= m <= 6:
                S[m + 9, i, n] = 1.0
            elif m > 6:
                S[16, i, n] = 1.0
    c["nsa_S"] = S
    E = np.zeros((32, 2048), np.float32)
    for p in range(2048):
        E[p // 64, p] = 1.0
    c["nsa_E"] = E
    allowed = np.zeros((128, 16, 32), np.float32)
    addc = np.zeros((128, 16, 32), np.float32)
    blk = np.arange(32)
    for i in range(16):
        for tt in range(128):
            cur = (i * 128 + tt) // 64
            al = blk <= cur
            forced = (blk == 0) | (blk == cur) | (blk == cur - 1)
            allowed[tt, i] = (al & ~forced).astype(np.float32)
            addc[tt, i] = np.where(forced, 1e4, np.where(al, 0.0, -1.0))
    c["nsa_allowed"] = allowed
    c["nsa_addc"] = addc
    ncmp = 127
    cs = np.arange(ncmp) * 16
    ss = np.arange(32) * 64
    lo = np.maximum(cs[:, None], ss[None, :])
    hi = np.minimum(cs[:, None] + 32, ss[None, :] + 64)
    c["nsa_selm"] = (np.maximum(hi - lo, 0) / 32).astype(np.float32)
    return c


def nsa_prep(inp, m):
    m["rel_bias"] = np.ascontiguousarray(inp["rel_bias"])
    for kv in ("k", "v"):
        m[f"pe_{kv}T"] = np.ascontiguousarray(inp[f"cmp_pe_{kv}"][0].T)
        m[f"w1_{kv}"] = inp[f"cmp_w1_{kv}"][0]
        m[f"w2_{kv}"] = inp[f"cmp_w2_{kv}"][0]
    kg0 = inp["k_norm_g"][0][0]
    z = np.zeros(64, np.float32)
    m["kcg_fm"] = np.ascontiguousarray(np.stack([np.concatenate([kg0, z]), np.concatenate([z, kg0])], 1).astype(np.float32))


def stage4(self):
    ar, S, nc = self.ar, self.S, self.nc
    d = self.din
    S_d = d("nsa_S", [17, 16, 128])
    E_d = d("nsa_E", [32, 2048])
    al_d = d("nsa_allowed", [128, 16, 32])
    ad_d = d("nsa_addc", [128, 16, 32])
    selm_d = d("nsa_selm", [127, 32])
    kcg_d = d("kcg_fm", [128, 2])
    cmp_d = {}
    for kv in ("k", "v"):
        cmp_d[kv] = (d(f"pe_{kv}T", [64, 32]), d(f"w1_{kv}", [2048, 64]), d(f"w2_{kv}", [64, 64]))
    NOH = 3 * 16384 + 17 * 128
    self.obT_d = self.dscr("obT", [1024, SEQ], BF16)
    ident, bones = self.ident, self.bones

    ar.push()
    ks = ar.alloc([4, 2, SEQ], BF16, "ks")
    kw = ar.alloc([4, 2, SEQ], BF16, "kw")
    vs = ar.alloc([16, 4, 65], BF16, "vs")
    vw = ar.alloc([16, 4, 65], BF16, "vw")
    gates = ar.alloc([16, 48], F32, "gates")
    biasT = ar.alloc([3, 16, 128], F32, "biasT")
    Mst = ar.alloc([16, 128], F32, "Mst", parts=17)
    Sc = ar.alloc([16, 128], F32, "Sc", parts=17)
    allowed = ar.alloc([16, 32], F32, "allowed")
    addc = ar.alloc([16, 32], F32, "addc")
    kc = ar.alloc([4, 2, 128], BF16, "kc")
    rhsc = ar.alloc([4, 97], BF16, "rhsc", parts=127)
    for g in range(4):
        for h in range(2):
            S.dma("sp", ks.ap[:, g, h, :], self.ksT_d.ap[g, h], [self.ksT_d], [ks])
            S.dma("sp", kw.ap[:, g, h, :], self.kwT_d.ap[g, h], [self.kwT_d], [kw])
    for g_ in range(4):
        S.dma("pool", ks.ap[64:96, g_, 0, :], E_d.ap, [E_d, ks], [ks])
        S.dma("pool", ks.ap[0:32, g_, 1, :], E_d.ap, [E_d, ks], [ks])
    for (dst, c0) in ((vs, 0), (vw, 256)):
        S.v("dve", "memset", [], [dst], dst.ap[:, :, :, 64:65], 1.0)
        for j in range(16):
            S.dma("sp", dst.ap[:, j, :, 0:64],
                  self.vv_d.ap[j * 128:(j + 1) * 128, c0:c0 + 256].rearrange("p (g d) -> p g d", g=4), [self.vv_d], [dst])
    S.dma("sp", gates.ap, self.gates_d.ap.rearrange("(j p) c -> p j c", p=128), [self.gates_d], [gates])
    S.dma("sp", Sc.ap, S_d.ap, [S_d], [Sc])
    S.dma("sp", allowed.ap, al_d.ap, [al_d], [allowed])
    S.dma("sp", addc.ap, ad_d.ap, [ad_d], [addc])

    ar.push()
    bias_d = self.bias_d
    for dd in range(3):
        S.dma("sp", biasT.ap[:, dd, :, :],
              bias_d.ap[:, dd * 16384:(dd + 1) * 16384].rearrange("h (p t) -> p h t", p=128), [bias_d], [biasT])
    S.dma("sp", Mst.ap, bias_d.ap[:, 3 * 16384:].rearrange("h (r t) -> r h t", r=17), [bias_d], [Mst])
    if self.dbg:
        dbb = self.dscr("dbg_biasT", [128, 3 * 16 * 128])
        S.dma("sp", dbb.ap, biasT.ap.rearrange("p a b c -> p (a b c)"), [biasT], [dbb])
    ar.pop()

    ar.push()
    kvc = ar.alloc([4, SEQ], BF16, "kvc")
    S.dma("sp", kvc.ap, self.kvcT_d.ap.rearrange("(a p) t -> p a t", p=128), [self.kvcT_d], [kvc])
    kcg = ar.alloc([2], F32, "kcg")
    S.dma("sp", kcg.ap, kcg_d.ap, [kcg_d], [kcg])
    S.v("dve", "tensor_scalar", [kcg], [kcg], kcg.ap, kcg.ap, 8.0, None, ALU.mult)
    S.v("dve", "memset", [], [rhsc], rhsc.ap[:, :, 64:65], 1.0)
    for g in range(4):
        S.dma("pool", rhsc.ap[:, g, 65:97], selm_d.ap, [selm_d], [rhsc])
    for kvi, kv in enumerate(("k", "v")):
        pe_d, w1_d, w2_d = cmp_d[kv]
        w1p = ar.alloc([2, 32, 64], BF16, f"w1p{kv}")
        S.v("dve", "memset", [], [w1p], w1p.ap, 0.0)
        w1v = w1_d.ap.rearrange("(i d) e -> d i e", d=64)
        S.dma("pool", w1p.ap[0:64, 0, :, :], w1v, [w1_d, w1p], [w1p])
        S.dma("pool", w1p.ap[64:128, 1, :, :], w1v, [w1_d, w1p], [w1p])
        peT = ar.alloc([32], BF16, f"peT{kv}", parts=64)
        S.dma("pool", peT.ap, pe_d.ap, [pe_d], [peT])
        w2 = ar.alloc([128], BF16, f"w2{kv}", parts=64)
        S.dma("pool", w2.ap[:, 0:64], w2_d.ap, [w2_d], [w2])
        S.dma("pool", w2.ap[:, 64:128], w2_d.ap, [w2_d, w2], [w2])
        Pb = self.bank()
        for i in range(32):
            S.mm(Pb.ap[0:64, 0:1], w1p.ap[0:64, 0, i, :], peT.ap[:, i:i + 1], i == 0, i == 31, [w1p, peT], [Pb])
        cb = ar.alloc([1], F32, f"cb{kv}", parts=64)
        S.act(cb.ap, Pb.ap[0:64, 0:1], AF.Copy, [Pb], [cb])
        for g in range(4):
            tile_, half = kvi * 2 + g // 2, g % 2
            Ph = self.bank()
            for i in range(32):
                S.mm(Ph.ap[0:64, 0:127], w1p.ap[:, half, i, :], kvc.ap[:, tile_, i:i + 16 * 126 + 1:16], i == 0, i == 31,
                     [w1p, kvc], [Ph])
            u = ar.alloc([127], F32, "cu", parts=64)
            t1 = ar.alloc([127], F32, "ct1", parts=64)
            sg = ar.alloc([127], F32, "csg", parts=64)
            hid = ar.alloc([127], BF16, "chid", parts=64)
            S.act(u.ap, Ph.ap[0:64, 0:127], AF.Identity, [Ph, cb], [u], bias=cb.ap[:, 0:1], scale=1.0)
            S.v("dve", "tensor_tensor", [u], [t1], t1.ap, u.ap, u.ap, ALU.mult)
            S.v("dve", "tensor_scalar", [t1], [t1], t1.ap, t1.ap, 0.044715, 1.0, ALU.mult, ALU.add)
            S.v("dve", "tensor_tensor", [t1, u], [t1], t1.ap, t1.ap, u.ap, ALU.mult)
            S.act(sg.ap, t1.ap, AF.Sigmoid, [t1], [sg], scale=1.5957691216057308)
            S.v("dve", "tensor_tensor", [u, sg], [hid], hid.ap, u.ap, sg.ap, ALU.mult)
            if kv == "k":
                Pk = self.bank()
                S.mm(Pk.ap[:, 0:127], w2.ap, hid.ap, True, True, [w2, hid], [Pk])
                sq = ar.alloc([127], F32, "csq")
                S.act(sq.ap, Pk.ap[:, 0:127], AF.Square, [Pk], [sq])
                P2 = self.bank()
                S.mm(P2.ap[:, 0:127], bones.ap, sq.ap, True, True, [bones, sq], [P2])
                sr = ar.alloc([127], F32, "csr")
                S.act(sr.ap, P2.ap[:, 0:127], AF.Sqrt, [P2], [sr], bias=64 * EPS, scale=1.0)
                S.v("dve", "reciprocal", [sr], [sr], sr.ap, sr.ap)
                for h in range(2):
                    S.v("dve", "scalar_tensor_tensor", [Pk, kcg, sr], [kc], kc.ap[:, g, h, 0:127], Pk.ap[:, 0:127],
                        kcg.ap[:, h:h + 1], sr.ap, ALU.mult, ALU.mult)
            else:
                Pv = self.bank()
                S.mm(Pv.ap[0:127, 0:64], hid.ap, w2.ap[:, 0:64], True, True, [w2, hid], [Pv])
                S.act(rhsc.ap[:, g, 0:64], Pv.ap[0:127, 0:64], AF.Copy, [Pv], [rhsc])
    if self.dbg:
        dkc = self.dscr("dbg_kc", [128, 4 * 2 * 128], BF16)
        S.dma("sp", dkc.ap, kc.ap.rearrange("p a b c -> p (a b c)"), [kc], [dkc])
        drc = self.dscr("dbg_rhsc", [127, 4 * 97], BF16)
        S.dma("sp", drc.ap, rhsc.ap.rearrange("p a b -> p (a b)"), [rhsc], [drc])
    S.barrier()
    ar.pop()
    if 6 in self.stages:
        self.precast()

    qis = [ar.alloc([4, 2, 2, 128], BF16, f"qa{i}") for i in range(2)]
    for qa_ in qis:
        S.v("dve", "memset", [], [qa_.sub("q")] + [qa_.sub(("m", g_)) for g_ in range(4)], qa_.ap, 0.0)

    pxs = [ar.alloc([512], BF16, f"px{i}") for i in range(4)]
    ssbs = [ar.alloc([512], F32, f"ssb{i}") for i in range(2)]
    oaccs = [ar.alloc([1024], F32, f"oacc{i}") for i in range(2)]
    obst = [ar.alloc([8, 128], BF16, f"obst{i}") for i in range(1)] * 2
    sm = [dict(rl=ar.alloc([12], F32, f"rl{i}"), imp=ar.alloc([32], F32, f"imp{i}"), m8=ar.alloc([8], F32, f"m8{i}"),
               ns=ar.alloc([128], F32, f"ns{i}"), tmp=ar.alloc([256], F32, f"otmp{i}")) for i in range(2)]
    for w_ in sm:
        S.v("dve", "memset", [], [w_["ns"]], w_["ns"].ap, 0.0)
    score_banks = self.ps[0:4]
    NSB = 4
    Poc_b = self.ps[4]
    Pow_b = [self.ps[5], self.ps[5]]
    Pos_b = self.ps[6:8]
    cnt = {"sb": 0, "px": 0, "ssb": 0}
    jobs = []
    qT_v = self.qT_d.ap.rearrange("(kt p) t -> p kt t", p=128)
    obT_v = self.obT_d.ap.rearrange("(kt p) t -> p kt t", p=128)

    def score_job(qi, g, lhs_lo, lhs_hi, M, extra_mm, bias_ap, pv_fn, deps_k, use_mask=False):
        st = {}

        def qk():
            P = score_banks[cnt["sb"] % NSB]
            cnt["sb"] += 1
            pv4 = P.ap[0:M, :].rearrange("p (a b t) -> p a b t", a=2, b=2)
            qdeps = [qi.sub("q")] + ([qi.sub(("m", g))] if use_mask else [])
            S.mm(pv4[:, :, 0, :], lhs_lo, qi.ap[:, g, 0, :, :], True, False, deps_k + qdeps, [P])
            S.mm(pv4[:, :, 1, :], lhs_hi, qi.ap[:, g, 1, :, :], False, extra_mm is None, deps_k + qdeps, [P])
            if extra_mm is not None:
                lt, rt, dps = extra_mm
                S.mm(P.ap[0:M, :], lt, rt, False, True, dps, [P])
            px = pxs[cnt["px"] % 4]
            cnt["px"] += 1
            if bias_ap is not None:
                sb_ = ssbs[cnt["ssb"] % 2]
                cnt["ssb"] += 1
                S.v("dve", "tensor_tensor", [P, biasT], [sb_], sb_.ap[0:M, :], P.ap[0:M, :], bias_ap, ALU.add)
                S.act(px.ap[0:M, :], sb_.ap[0:M, :], AF.Exp, [sb_], [px])
            else:
                S.act(px.ap[0:M, :], P.ap[0:M, :], AF.Exp, [P], [px])
            st["px"] = px

        def pv():
            pv_fn(st["px"])

        return (qk, pv)

    for i in range(NT):
        qi = qis[i % 2]
        oacc = oaccs[i % 2]

        def load_q(i=i):
            if i < NT:
                qa_ = qis[i % 2]
                for (r0, lh) in ((0, 0), (64, 1)):
                    for g_ in range(4):
                        S.dma("sp", qa_.ap[r0:r0 + 64, g_, lh, :, :],
                              qT_v[r0:r0 + 64, 2 * g_:2 * g_ + 2, i * 128:(i + 1) * 128],
                              [self.qT_d], [qa_.sub("q")])
        if i == 0:
            jobs.append((load_q, None))
        load_next = (lambda i=i: load_q(i + 1))
        for g in range(4):
            it = i * 4 + g
            w = sm[it % 2]
            Pow_, Pos_ = Pow_b[it % 2], Pos_b[it % 2]
            gv = gates.ap[:, i, g * 12:(g + 1) * 12].rearrange("p (h c) -> p h c", c=3)
            osl = oacc.ap[:, g * 256:(g + 1) * 256].rearrange("p (h d) -> p h d", h=4)

            def pv_c(px, g=g, i=i, w=w, gv=gv, osl=osl, qi=qi):
                Poc = Poc_b
                for hs in range(4):
                    S.mm(Poc.ap[:, hs * 97:(hs + 1) * 97], px.ap[0:127, hs * 128:(hs + 1) * 128], rhsc.ap[:, g, :],
                         hs == 0, hs == 3, [px, rhsc], [Poc])
                pc3 = Poc.ap[:, 0:388].rearrange("p (h c) -> p h c", h=4)
                rl = w["rl"]
                S.v("dve", "tensor_scalar", [Poc], [rl], rl.ap[:, 0:4], pc3[:, :, 64], 1e-30, None, ALU.max)
                S.v("dve", "reciprocal", [rl], [rl], rl.ap[:, 0:4], rl.ap[:, 0:4])
                imp = w["imp"]
                S.v("dve", "tensor_scalar", [Poc, rl], [imp], imp.ap, pc3[:, 0, 65:97], rl.ap[:, 0:1], None, ALU.mult)
                for hs in range(1, 4):
                    S.v("dve", "scalar_tensor_tensor", [Poc, rl, imp], [imp], imp.ap, pc3[:, hs, 65:97], rl.ap[:, hs:hs + 1],
                        imp.ap, ALU.mult, ALU.add)
                S.v("dve", "tensor_tensor", [imp, allowed], [imp], imp.ap, imp.ap, allowed.ap[:, i, :], ALU.mult)
                S.v("dve", "tensor_tensor", [imp, addc], [imp], imp.ap, imp.ap, addc.ap[:, i, :], ALU.add)
                m8 = w["m8"]
                S.v("dve", "max", [imp], [m8], out=m8.ap, in_=imp.ap)
                ns = w["ns"]
                S.v("dve", "tensor_scalar", [imp, m8], [ns], ns.ap[:, 0:32], imp.ap, m8.ap[:, 7:8], None, ALU.is_ge)
                S.v("dve", "tensor_scalar", [ns], [ns], ns.ap[:, 64:96], ns.ap[:, 0:32], -1.0, -NEG, ALU.add, ALU.mult)
                S.v("dve", "tensor_scalar", [ns], [ns], ns.ap[:, 0:32], ns.ap[:, 0:32], -1.0, -NEG, ALU.add, ALU.mult)
                Pt = score_banks[cnt["sb"] % NSB]
                cnt["sb"] += 1
                S.tr(Pt.ap[:, 0:128], ns.ap, ident.ap, [ns, ident], [Pt])
                S.act(qi.ap[64:96, g, 0, :, :], Pt.ap[64:96, 0:128].unsqueeze(1).to_broadcast([32, 2, 128]), AF.Copy, [Pt], [qi.sub(("m", g))])
                S.act(qi.ap[0:32, g, 1, :, :], Pt.ap[0:32, 0:128].unsqueeze(1).to_broadcast([32, 2, 128]), AF.Copy, [Pt], [qi.sub(("m", g))])
                S.v("dve", "tensor_tensor", [rl, gates], [rl], rl.ap[:, 0:4], rl.ap[:, 0:4], gv[:, :, 0], ALU.mult)
                S.v("dve", "tensor_tensor", [Poc, rl], [oacc.sub(g)], osl, pc3[:, :, 0:64],
                    rl.ap[:, 0:4].unsqueeze(2).to_broadcast([128, 4, 64]), ALU.mult)
            extra = (Sc.ap[:, i, 0:127], Mst.ap[:, 4 * g:4 * g + 4, :], [Sc, Mst])
            jobs.append(score_job(qi, g, kc.ap[:, g, 0, 0:127], kc.ap[:, g, 1, 0:127], 127, extra, None, pv_c, [kc]))
            if g == 1:
                jobs.append((load_next, None))

            def mk_pv(Pacc, vv, j, first, last, br, g=g, w=w, gv=gv, osl=osl, oacc=oacc):
                def pv(px):
                    for hs in range(4):
                        S.mm(Pacc.ap[:, hs * 65:(hs + 1) * 65], px.ap[:, hs * 128:(hs + 1) * 128], vv.ap[:, j, g, :],
                             first and hs == 0, last and hs == 3, [px, vv], [Pacc])
                    if last:
                        p3 = Pacc.ap[:, 0:260].rearrange("p (h c) -> p h c", h=4)
                        rl = w["rl"]
                        o = 4 * br
                        S.v("dve", "tensor_scalar", [Pacc], [rl], rl.ap[:, o:o + 4], p3[:, :, 64], 1e-30, None, ALU.max)
                        S.v("dve", "reciprocal", [rl], [rl], rl.ap[:, o:o + 4], rl.ap[:, o:o + 4])
                        S.v("dve", "tensor_tensor", [rl, gates], [rl], rl.ap[:, o:o + 4], rl.ap[:, o:o + 4], gv[:, :, br], ALU.mult)
                        tmp = w["tmp"]
                        t3 = tmp.ap.rearrange("p (h d) -> p h d", h=4)
                        S.v("dve", "tensor_tensor", [Pacc, rl], [tmp], t3, p3[:, :, 0:64],
                            rl.ap[:, o:o + 4].unsqueeze(2).to_broadcast([128, 4, 64]), ALU.mult)
                        S.v("dve", "tensor_tensor", [tmp, oacc.sub(g)], [oacc.sub(g)], osl, osl, t3, ALU.add)
                return pv

            js = list(range(max(0, i - 4), i + 1))
            for j in js:
                dd = {0: 0, 1: 1, 4: 2}.get(i - j)
                bias_ap = None if dd is None else biasT.ap[:, dd, 4 * g:4 * g + 4, :].rearrange("p h t -> p (h t)")
                jobs.append(score_job(qi, g, kw.ap[:, g, 0, j * 128:(j + 1) * 128], kw.ap[:, g, 1, j * 128:(j + 1) * 128],
                                      128, None, bias_ap, mk_pv(Pow_, vw, j, j == js[0], j == js[-1], 2), [kw]))
            for j in range(i + 1):
                dd = {0: 0, 1: 1}.get(i - j)
                bias_ap = None if dd is None else biasT.ap[:, dd, 4 * g:4 * g + 4, :].rearrange("p h t -> p (h t)")
                jobs.append(score_job(qi, g, ks.ap[:, g, 0, j * 128:(j + 1) * 128], ks.ap[:, g, 1, j * 128:(j + 1) * 128],
                                      128, None, bias_ap, mk_pv(Pos_, vs, j, j == 0, j == i, 1), [ks], use_mask=True))

        def finish(i=i, oacc=oacc):
            ob = obst[i % 2]
            for half in range(2):
                P = score_banks[cnt["sb"] % NSB]
                cnt["sb"] += 1
                for q in range(4):
                    kt = half * 4 + q
                    S.tr(P.ap[:, q * 128:(q + 1) * 128], oacc.ap[:, kt * 128:(kt + 1) * 128], ident.ap,
                         [oacc.sub(kt // 2), ident], [P])
                S.act(ob.ap[:, half * 4:(half + 1) * 4, :], P.ap.rearrange("p (q t) -> p q t", q=4), AF.Copy, [P], [ob])
            S.dma("sp", obT_v[:, :, i * 128:(i + 1) * 128], ob.ap, [ob], [self.obT_d])
        jobs.append((None, finish))

    pend = []
    for (qk, pv) in jobs:
        if len(pend) >= 2:
            f = pend.pop(0)
            if f is not None:
                f()
        if qk is not None:
            qk()
        pend.append(pv)
    for f in pend:
        if f is not None:
            f()
    S.barrier()
    ar.pop()


KB.stage4 = stage4


def nsa_bias_build(self):
    ar, S = self.ar, self.S
    d = self.din
    NOH = 3 * 16384 + 17 * 128
    oh_d = d("nsa_oh", [33, NOH])
    relb_d = d("rel_bias", [32, 16])
    self.bias_d = bias_d = self.dscr("bias_scr", [16, NOH])
    trel = ar.alloc([16], F32, "trel", parts=33)
    tbl = ar.alloc([16], F32, "tbl", parts=32)
    t31 = ar.alloc([16], F32, "t31", parts=32)
    S.dma("sp", tbl.ap, relb_d.ap, [relb_d], [tbl])
    S.dma("sp", t31.ap, relb_d.ap[31:32, :].partition_broadcast(32), [relb_d], [t31])
    S.v("dve", "memset", [], [trel], trel.ap, NEG)
    S.v("dve", "tensor_tensor", [tbl, t31, trel], [trel], trel.ap[0:32, :], tbl.ap, t31.ap, ALU.subtract)
    ohb = [ar.alloc([2048], F32, f"ohb{i}", parts=33) for i in range(2)]
    bsb = [ar.alloc([2048], F32, f"bsb{i}", parts=16) for i in range(2)]
    nblk = (NOH + 2047) // 2048

    def mk(bi):
        def step():
            c0 = bi * 2048
            n = min(2048, NOH - c0)
            ob, bs = ohb[bi % 2], bsb[bi % 2]
            S.dma("sp", ob.ap[:, 0:n], oh_d.ap[:, c0:c0 + n], [oh_d], [ob])
            for q in range((n + 511) // 512):
                w = min(512, n - q * 512)
                P = self.bank()
                S.mm(P.ap[0:16, 0:w], trel.ap, ob.ap[:, q * 512:q * 512 + w], True, True, [trel, ob], [P])
                S.act(bs.ap[:, q * 512:q * 512 + w], P.ap[0:16, 0:w], AF.Copy, [P], [bs])
            S.dma("sp", bias_d.ap[:, c0:c0 + n], bs.ap[:, 0:n], [bs], [bias_d])
        return step
    return [mk(bi) for bi in range(nblk)]


KB.nsa_bias_build = nsa_bias_build


LAM = 0.6065306597126334


def rwkv_consts():
    c = {}
    p = np.arange(128)
    ut_strict = (p[:, None] < p[None, :]).astype(np.float32)
    ut_incl = (p[:, None] <= p[None, :]).astype(np.float32)
    c["rw_mAB"] = np.ascontiguousarray(np.concatenate([ut_strict, ut_incl], 1))
    c["rw_mLT"] = (p[:, None] > p[None, :]).astype(np.float32)
    rs = np.ones((128, 8, 128), np.float32)
    rs[:, :, 0] = 0.0
    c["rw_reset"] = rs.reshape(128, 1024)
    hm = np.zeros((128, 2), np.float32)
    hm[:64, 0] = 1.0
    hm[64:, 1] = 1.0
    c["rw_hsel"] = hm
    return c


def rwkv_prep(inp, m):
    g = lambda k: inp[k][0]
    m["rw_w0"] = _fm(g("rwkv_w0"))
    m["rw_a0"] = _fm(g("rwkv_a0"))
    m["rw_kk"] = _fm(g("rwkv_k_k"))
    m["rw_ka"] = _fm(g("rwkv_k_a"))
    m["rw_rk"] = _fm(g("rwkv_r_k").reshape(-1))
    m["rw_lnw"] = np.ascontiguousarray(np.broadcast_to(g("rwkv_ln_w")[None, :], (128, 1024)).astype(np.float32))
    m["rw_lnb"] = np.ascontiguousarray(np.broadcast_to(g("rwkv_ln_b")[None, :], (128, 1024)).astype(np.float32))
    z = np.zeros((64, 1024), np.float32)
    m["rw_w2pad"] = np.ascontiguousarray(np.concatenate([g("rwkv_w2"), z], 0))
    m["rw_a2pad"] = np.ascontiguousarray(np.concatenate([z, g("rwkv_a2")], 0))
    m["rw_g2"] = g("rwkv_g2")


def stage3(self):
    ar, S = self.ar, self.S
    d = self.din
    ident, bones = self.ident, self.bones
    self.oaT_d = self.dscr("oaT", [1024, SEQ], BF16)
    ar.push()

    def cload(name, shape, parts=128, src=None):
        dt_ = d(name, [parts] + list(shape)) if src is None else src
        t = ar.alloc(shape, F32, name, parts=parts)
        S.dma("sp", t.ap, dt_.ap, [dt_], [t])
        return t
    w0 = cload("rw_w0", [8])
    a0 = cload("rw_a0", [8])
    kkf = cload("rw_kk", [8])
    kaf = cload("rw_ka", [8])
    rkf = cload("rw_rk", [8])
    lnw = cload("rw_lnw", [1024])
    lnb = cload("rw_lnb", [1024])
    w2p = cload("rw_w2pad", [1024])
    a2p = cload("rw_a2pad", [1024])
    g2_d = d("rw_g2", [160, 1024])
    g2a = ar.alloc([1024], F32, "g2a")
    g2b = ar.alloc([1024], F32, "g2b", parts=32)
    S.dma("sp", g2a.ap, g2_d.ap[0:128, :], [g2_d], [g2a])
    S.dma("sp", g2b.ap, g2_d.ap[128:160, :], [g2_d], [g2b])
    mAB = cload("rw_mAB", [256])
    mLT = cload("rw_mLT", [128])
    reset = cload("rw_reset", [1024])
    hsel = cload("rw_hsel", [2])
    omka = ar.alloc([8], F32, "omka")
    S.v("dve", "tensor_scalar", [kaf], [omka], omka.ap, kaf.ap, -1.0, 1.0, ALU.mult, ALU.add)
    Hp = ar.alloc([16, 64], F32, "Hp")
    S.v("dve", "memset", [], [Hp], Hp.ap, 0.0)

    A = lambda n, shape=(8, 128): ar.alloc(list(shape), F32, n)
    raw = A("raw", (27, 128))
    sgw, cs, E, Eex = A("sgw"), A("cs"), A("E"), A("Eex")
    aT, kkn, kp, bb, btl = A("aT"), A("kkn"), A("kp"), A("bb"), A("btl")
    AR = A("AR", (8, 2, 128))
    blo, bhi, klo, khi, alo, ahi = A("blo"), A("bhi"), A("klo"), A("khi"), A("alo"), A("ahi")
    tmpA, tmpB = A("tmpA"), A("tmpB")
    bh, kh = sgw, cs
    v_tok, bh_tok, kh_tok, g_tok = A("v_tok", (1024,)), A("bh_tok", (1024,)), A("kh_tok", (1024,)), A("g_tok", (1024,))
    th = A("th", (128,))
    sx = A("sx", (128,))
    sx2 = ar.alloc([128], F32, "sx2", parts=32)
    PLs = [A("PL0", (8,)), A("PL1", (8,))]
    nb = A("nb", (8,))
    rk16 = A("rk16", (16,))
    st16 = [A(f"st16_{i}", (16,)) for i in range(4)]
    import os
    SQDT = BF16 if os.environ.get("RW_SQ") == "bf16" else F32
    MASK_POOL = os.environ.get("RW_MASK") == "pool"
    NO_IL = os.environ.get("RW_IL") == "0"
    slots = []
    for s_ in range(2):
        slots.append(dict(
            ABm=A(f"ABm{s_}", (4, 256)), AKm=A(f"AKm{s_}", (4, 256)),
            Yf=[A(f"Yf{s_}{i}", (4, 128)) for i in range(2)] if SQDT != F32 else None,
            Yb=[ar.alloc([4, 128], SQDT, f"Yb{s_}{i}") for i in range(2)],
            XW=[A(f"XW{s_}{i}", (4, 192)) for i in range(2)]))
        if SQDT == F32:
            slots[-1]["Yf"] = slots[-1]["Yb"]
    if SQDT == F32:
        ar.off -= 0
    oast = [ar.alloc([8, 128], BF16, f"oast{i}") for i in range(1)] * 2
    f2 = lambda t: t.ap.rearrange("p a b -> p (a b)")
    rw_v = self.rwT_d.ap.rearrange("(kt p) t -> p kt t", p=128)
    oaT_v = self.oaT_d.ap.rearrange("(kt p) t -> p kt t", p=128)
    dv = lambda meth, reads, writes, *a, **k: S.v("dve", meth, reads, writes, *a, **k)
    pl = lambda meth, reads, writes, *a, **k: S.v("pool", meth, reads, writes, *a, **k)
    bc8 = lambda t: t.ap.unsqueeze(2).to_broadcast([128, 8, 128])

    def P1(c):
            S.dma("sp", raw.ap, rw_v[:, :, c * 128:(c + 1) * 128], [self.rwT_d], [raw])
            yield
            rT, kT, vT = raw.ap[:, 0:8, :], raw.ap[:, 8:16, :], raw.ap[:, 16:24, :]
            yield
            t24 = raw.ap[:, 24, :]
            yield
            S.act(th.ap, t24, AF.Tanh, [raw], [th])
            yield
            Pz = [self.bank(), self.bank()]
            yield
            for kt in range(8):
                P = Pz[kt // 4]
                S.mm(P.ap[:, (kt % 4) * 128:(kt % 4 + 1) * 128], w2p.ap[:, kt * 128:(kt + 1) * 128], th.ap, True, True, [w2p, th], [P])
            yield
            for kt in range(8):
                S.act(sgw.ap[:, kt, :], Pz[kt // 4].ap[:, (kt % 4) * 128:(kt % 4 + 1) * 128], AF.Sigmoid, [Pz[kt // 4], w0], [sgw],
                      bias=w0.ap[:, kt:kt + 1], scale=1.0)
            yield
            Pa = [self.bank(), self.bank()]
            yield
            for kt in range(8):
                P = Pa[kt // 4]
                S.mm(P.ap[:, (kt % 4) * 128:(kt % 4 + 1) * 128], a2p.ap[:, kt * 128:(kt + 1) * 128], t24, True, True, [a2p, raw], [P])
            yield
            for kt in range(8):
                S.act(aT.ap[:, kt, :], Pa[kt // 4].ap[:, (kt % 4) * 128:(kt % 4 + 1) * 128], AF.Sigmoid, [Pa[kt // 4], a0], [aT],
                      bias=a0.ap[:, kt:kt + 1], scale=1.0)
            yield
            dv("tensor_tensor_scan", [reset, sgw], [cs], f2(cs), reset.ap, f2(sgw), 0.0, ALU.mult, ALU.add)
            yield
            S.act(f2(E), f2(cs), AF.Exp, [cs], [E], scale=-LAM)
            yield
            dv("tensor_tensor", [cs, sgw], [Eex], f2(Eex), f2(cs), f2(sgw), ALU.subtract)
            yield
            S.act(f2(Eex), f2(Eex), AF.Exp, [Eex], [Eex], scale=-LAM)
            yield
            dv("tensor_scalar", [cs], [nb], nb.ap, cs.ap[:, :, 127], -LAM, None, ALU.mult)
            yield
            S.act(PLs[c % 2].ap, nb.ap, AF.Exp, [nb], [PLs[c % 2]])
            yield
            dv("tensor_tensor", [raw, kkf], [kkn], kkn.ap, kT, bc8(kkf), ALU.mult)
            yield
            dv("tensor_tensor", [kkn], [tmpB], tmpB.ap, kkn.ap, kkn.ap, ALU.mult)
            yield
            Pn = [self.bank(), self.bank()]
            yield
            for hh in range(2):
                S.mm(Pn[hh].ap, bones.ap, tmpB.ap[:, hh * 4:(hh + 1) * 4, :], True, True, [bones, tmpB], [Pn[hh]])
            yield
            for hh in range(2):
                S.act(tmpB.ap[:, hh * 4:(hh + 1) * 4, :], Pn[hh].ap.rearrange("p (a b) -> p a b", a=4), AF.Sqrt, [Pn[hh]], [tmpB])
            yield
            dv("tensor_scalar", [tmpB], [tmpB], f2(tmpB), f2(tmpB), 1e-12, None, ALU.max)
            yield
            dv("reciprocal", [tmpB], [tmpB], f2(tmpB), f2(tmpB))
            yield
            dv("tensor_tensor", [kkn, tmpB], [kkn], f2(kkn), f2(kkn), f2(tmpB), ALU.mult)
            yield
            dv("tensor_tensor", [aT, kaf], [kp], kp.ap, aT.ap, bc8(kaf), ALU.mult)
            yield
            dv("tensor_tensor", [kp, omka], [kp], kp.ap, kp.ap, bc8(omka), ALU.add)
            yield
            dv("tensor_tensor", [kp, raw], [kp], kp.ap, kp.ap, kT, ALU.mult)
            yield
            dv("tensor_tensor", [kkn, aT], [bb], f2(bb), f2(kkn), f2(aT), ALU.mult)
            yield


    def P2(c):
            rT, kT, vT = raw.ap[:, 0:8, :], raw.ap[:, 8:16, :], raw.ap[:, 16:24, :]
            dv("tensor_tensor", [raw, E], [AR], AR.ap[:, :, 1, :], rT, E.ap, ALU.mult)
            dv("scalar_tensor_tensor", [kkn, Eex], [AR], AR.ap[:, :, 0, :], kkn.ap, -1.0, Eex.ap, ALU.mult, ALU.mult)
            Einv, Elast = E, Eex
            S.act(f2(Einv), f2(cs), AF.Exp, [cs], [Einv], scale=LAM)
            for kt in range(8):
                S.act(Elast.ap[:, kt, :], cs.ap[:, kt, :], AF.Exp, [cs, nb], [Elast], bias=nb.ap[:, kt:kt + 1], scale=LAM)
            dv("tensor_tensor", [bb, Einv], [btl], f2(btl), f2(bb), f2(Einv), ALU.mult)
            dv("tensor_tensor", [kp, Einv], [tmpA], f2(tmpA), f2(kp), f2(Einv), ALU.mult)
            dv("tensor_tensor", [bb, Elast], [bh], f2(bh), f2(bb), f2(Elast), ALU.mult)
            dv("tensor_tensor", [kp, Elast], [kh], f2(kh), f2(kp), f2(Elast), ALU.mult)
            for (dst, src, col) in ((blo, btl, 0), (bhi, btl, 1), (klo, tmpA, 0), (khi, tmpA, 1)):
                if MASK_POOL:
                    pl("tensor_scalar", [src, hsel], [dst], f2(dst), f2(src), hsel.ap[:, col:col + 1], None, ALU.mult)
                    continue
                S.act(f2(dst), f2(src), AF.Identity, [src, hsel], [dst], scale=hsel.ap[:, col:col + 1], bias=0.0)
            for (dst, col) in ((alo, 0), (ahi, 1)):
                if MASK_POOL:
                    pl("tensor_scalar", [AR, hsel], [dst], dst.ap, AR.ap[:, :, 0, :], hsel.ap[:, col:col + 1], None, ALU.mult)
                    continue
                S.act(dst.ap, AR.ap[:, :, 0, :], AF.Identity, [AR, hsel], [dst], scale=hsel.ap[:, col:col + 1], bias=0.0)
            dv("tensor_tensor", [raw, kp], [tmpB], tmpB.ap, rT, kp.ap, ALU.mult)
            dv("tensor_tensor", [tmpB, rkf], [tmpB], tmpB.ap, tmpB.ap, bc8(rkf), ALU.mult)
            Pr = self.bank()
            for kt in range(8):
                S.mm(Pr.ap[:, 2 * kt:2 * kt + 2], tmpB.ap[:, kt, :], hsel.ap, kt == 0, kt == 7, [tmpB, hsel], [Pr])
            S.act(rk16.ap, Pr.ap[:, 0:16], AF.Copy, [Pr], [rk16])
            S.act(sx.ap, raw.ap[:, 25, :], AF.Sigmoid, [raw], [sx])
            S.act(sx2.ap, raw.ap[0:32, 26, :], AF.Sigmoid, [raw], [sx2])
            for hh in range(2):
                P = self.bank()
                S.mm(P.ap, sx.ap, g2a.ap[:, hh * 512:(hh + 1) * 512], True, False, [sx, g2a], [P])
                S.mm(P.ap, sx2.ap, g2b.ap[:, hh * 512:(hh + 1) * 512], False, True, [sx2, g2b], [P])
                S.act(g_tok.ap[:, hh * 512:(hh + 1) * 512], P.ap, AF.Copy, [P], [g_tok])
            for (src_ap, src_t, dst) in ((vT, raw, v_tok), (bh.ap, bh, bh_tok), (kh.ap, kh, kh_tok)):
                for hh in range(2):
                    P = self.bank()
                    for q in range(4):
                        kt = hh * 4 + q
                        S.tr(P.ap[:, q * 128:(q + 1) * 128], src_ap[:, kt, :], ident.ap, [src_t, ident], [P])
                    S.act(dst.ap[:, hh * 512:(hh + 1) * 512], P.ap, AF.Copy, [P], [dst])


    def heads(c, step):
            y_tok = tmpA
            def phaseA(hg, sl):
                heads = [4 * hg + x for x in range(4)]
                ABm, AKm = sl["ABm"], sl["AKm"]
                PA = [self.bank(), self.bank()]
                PB = [self.bank(), self.bank()]
                PX = self.bank()
                for hl, h in enumerate(heads):
                    kt, half = h // 2, h % 2
                    bsel = (blo, bhi)[half]
                    ksel = (klo, khi)[half]
                    asel = (alo, ahi)[half]
                    ar_rhs = AR.ap[:, kt, :, :]
                    oa = PA[hl // 2].ap[:, (hl % 2) * 256:(hl % 2 + 1) * 256]
                    ob_ = PB[hl // 2].ap[:, (hl % 2) * 256:(hl % 2 + 1) * 256]
                    S.mm(oa, bsel.ap[:, kt, :], ar_rhs, hl % 2 == 0, hl % 2 == 1, [bsel, AR], [PA[hl // 2]])
                    S.mm(ob_, ksel.ap[:, kt, :], ar_rhs, hl % 2 == 0, hl % 2 == 1, [ksel, AR], [PB[hl // 2]])
                    S.mm(PX.ap[:, hl * 128:(hl + 1) * 128], asel.ap[:, kt, :], btl.ap[:, kt, :], hl == 0, hl == 3, [asel, btl], [PX])
                mAB2 = mAB.ap.unsqueeze(1).to_broadcast([128, 2, 256])
                for q in range(2):
                    dv("tensor_tensor", [PA[q], mAB], [ABm], ABm.ap[:, 2 * q:2 * q + 2, :],
                       PA[q].ap.rearrange("p (a b) -> p a b", a=2), mAB2, ALU.mult)
                    dv("tensor_tensor", [PB[q], mAB], [AKm], AKm.ap[:, 2 * q:2 * q + 2, :],
                       PB[q].ap.rearrange("p (a b) -> p a b", a=2), mAB2, ALU.mult)
                dv("tensor_tensor", [PX, mLT], [sl["XW"][0]], sl["XW"][0].ap[:, :, 0:128], PX.ap.rearrange("p (a b) -> p a b", a=4),
                   mLT.ap.unsqueeze(1).to_broadcast([128, 4, 128]), ALU.mult)
                S.act(sl["Yb"][0].ap, ABm.ap[:, :, 0:128], AF.Copy, [ABm], [sl["Yb"][0]])
                PW = self.bank()
                for hl, h in enumerate(heads):
                    kt = h // 2
                    o = PW.ap[:, hl * 64:(hl + 1) * 64]
                    S.mm(o, AR.ap[:, kt, 0, :], Hp.ap[:, h, :], hl == 0, False, [AR, Hp.sub(h)], [PW])
                    S.mm(o, AKm.ap[:, hl, 0:128], v_tok.ap[:, h * 64:(h + 1) * 64], False, hl == 3, [AKm, v_tok], [PW])
                S.act(sl["XW"][0].ap[:, :, 128:192], PW.ap[:, 0:256].rearrange("p (h d) -> p h d", h=4), AF.Copy, [PW], [sl["XW"][0]])

            def level(sl, lv):
                Y, XW = sl["Yb"][lv % 2], sl["XW"][lv % 2]
                Yn, XWn = sl["Yb"][(lv + 1) % 2], sl["XW"][(lv + 1) % 2]
                if lv < 6:
                    PUX = [self.bank(), self.bank()]
                    for hl in range(4):
                        o = PUX[hl // 2].ap[:, (hl % 2) * 192:(hl % 2 + 1) * 192]
                        S.mm(o, Y.ap[:, hl, :], XW.ap[:, hl, :], hl % 2 == 0, hl % 2 == 1, [Y, XW], [PUX[hl // 2]])
                    PY2 = self.bank()
                    for hl in range(4):
                        S.mm(PY2.ap[:, hl * 128:(hl + 1) * 128], XW.ap[:, hl, 0:128], Y.ap[:, hl, :], hl == 0, hl == 3, [XW, Y], [PY2])
                    for q in range(2):
                        view = PUX[q].ap[:, 0:384].rearrange("p (a c) -> p a c", a=2)
                        dv("tensor_tensor", [PUX[q], XW], [XWn], XWn.ap[:, 2 * q:2 * q + 2, 128:192], view[:, :, 128:192],
                           XW.ap[:, 2 * q:2 * q + 2, 128:192], ALU.add)
                        S.act(XWn.ap[:, 2 * q:2 * q + 2, 0:128], view[:, :, 0:128], AF.Copy, [PUX[q]], [XWn])
                    dv("tensor_copy", [PY2], [Yn], f2(Yn), PY2.ap)
                else:
                    PU = self.bank()
                    for hl in range(4):
                        S.mm(PU.ap[:, hl * 64:(hl + 1) * 64], Y.ap[:, hl, :], XW.ap[:, hl, 128:192], hl == 0, hl == 3, [Y, XW], [PU])
                    dv("tensor_tensor", [PU, XW], [XWn], XWn.ap[:, :, 128:192], PU.ap[:, 0:256].rearrange("p (h d) -> p h d", h=4),
                       XW.ap[:, :, 128:192], ALU.add)

            def phaseY(hg, sl):
                heads = [4 * hg + x for x in range(4)]
                ABm, AKm = sl["ABm"], sl["AKm"]
                U = sl["XW"][1]
                PY = self.bank()
                for hl, h in enumerate(heads):
                    kt = h // 2
                    o = PY.ap[:, hl * 64:(hl + 1) * 64]
                    S.mm(o, AR.ap[:, kt, 1, :], Hp.ap[:, h, :], hl == 0, False, [AR, Hp.sub(h)], [PY])
                    S.mm(o, ABm.ap[:, hl, 128:256], U.ap[:, hl, 128:192], False, False, [ABm, U], [PY])
                    S.mm(o, AKm.ap[:, hl, 128:256], v_tok.ap[:, h * 64:(h + 1) * 64], False, hl == 3, [AKm, v_tok], [PY])
                PH = self.bank()
                for hl, h in enumerate(heads):
                    kt = h // 2
                    o = PH.ap[:, hl * 64:(hl + 1) * 64]
                    S.mm(o, bh_tok.ap[:, kt * 128:(kt + 1) * 128], U.ap[:, hl, 128:192], hl == 0, False, [bh_tok, U], [PH])
                    S.mm(o, kh_tok.ap[:, kt * 128:(kt + 1) * 128], v_tok.ap[:, h * 64:(h + 1) * 64], False, hl == 3, [kh_tok, v_tok], [PH])
                S.act(y_tok.ap.rearrange("p a b -> p (a b)")[:, hg * 256:(hg + 1) * 256], PY.ap[:, 0:256], AF.Copy, [PY], [y_tok])
                for hl, h in enumerate(heads):
                    kt, half = h // 2, h % 2
                    r0 = 64 * half
                    dv("scalar_tensor_tensor", [Hp.sub(h), PLs[c % 2], PH], [Hp.sub(h)], Hp.ap[r0:r0 + 64, h, :], Hp.ap[r0:r0 + 64, h, :],
                       PLs[c % 2].ap[r0:r0 + 64, kt:kt + 1], PH.ap[r0:r0 + 64, hl * 64:(hl + 1) * 64], ALU.mult, ALU.add)

            for pair in range(2):
                gA, gB = 2 * pair, 2 * pair + 1
                if NO_IL:
                    for (g_, sl_) in ((gA, slots[0]), (gB, slots[1])):
                        phaseA(g_, sl_)
                        for lv in range(7):
                            level(sl_, lv)
                        phaseY(g_, sl_)
                    continue
                phaseA(gA, slots[0])
                phaseA(gB, slots[1])
                for lv in range(7):
                    level(slots[0], lv)
                    step(); step()
                    level(slots[1], lv)
                    step(); step()
                phaseY(gA, slots[0])
                phaseY(gB, slots[1])


    def post(c):
            y_tok = tmpA
            yf = y_tok.ap.rearrange("p a b -> p (a b)")
            y3 = yf.rearrange("p (h d) -> p h d", h=16)
            sum_, sq_, mean, rstd = st16
            t1, t2 = y_tok, btl
            t1f, t2f = f2(t1), f2(t2)
            dv("tensor_reduce", [y_tok], [sum_], sum_.ap, y3, AX.X, ALU.add)
            dv("tensor_tensor", [y_tok], [t2], t2f, yf, yf, ALU.mult)
            dv("tensor_reduce", [t2], [sq_], sq_.ap, t2f.rearrange("p (h d) -> p h d", h=16), AX.X, ALU.add)
            dv("tensor_scalar", [sum_], [mean], mean.ap, sum_.ap, 1.0 / 64, None, ALU.mult)
            dv("tensor_tensor", [mean], [rstd], rstd.ap, mean.ap, mean.ap, ALU.mult)
            dv("scalar_tensor_tensor", [sq_, rstd], [rstd], rstd.ap, sq_.ap, 1.0 / 64, rstd.ap, ALU.mult, ALU.subtract)
            S.act(rstd.ap, rstd.ap, AF.Sqrt, [rstd], [rstd], bias=GN_EPS, scale=1.0)
            dv("reciprocal", [rstd], [rstd], rstd.ap, rstd.ap)
            b16 = lambda t: t.ap.unsqueeze(2).to_broadcast([128, 16, 64])
            t13 = t1f.rearrange("p (h d) -> p h d", h=16)
            t23 = t2f.rearrange("p (h d) -> p h d", h=16)
            dv("tensor_tensor", [y_tok, mean], [t1], t13, y3, b16(mean), ALU.subtract)
            dv("tensor_tensor", [t1, rstd], [t1], t13, t13, b16(rstd), ALU.mult)
            dv("tensor_tensor", [t1, lnw], [t1], t1f, t1f, lnw.ap, ALU.mult)
            dv("tensor_tensor", [t1, lnb], [t1], t1f, t1f, lnb.ap, ALU.add)
            dv("tensor_tensor", [v_tok, rk16], [t2], t23, v_tok.ap.rearrange("p (h d) -> p h d", h=16), b16(rk16), ALU.mult)
            dv("tensor_tensor", [t1, t2], [t1], t1f, t1f, t2f, ALU.add)
            dv("tensor_tensor", [t1, g_tok], [t1], t1f, t1f, g_tok.ap, ALU.mult)
            if self.dbg and c == 0:
                dd = self.dscr("dbg_oa0", [128, 1024])
                S.dma("sp", dd.ap, t1f, [t1], [dd])
                dd2 = self.dscr("dbg_y0", [128, 1024])
                S.dma("sp", dd2.ap, yf, [y_tok], [dd2])
            ob = oast[c % 2]
            for hh in range(2):
                P = self.bank()
                for q in range(4):
                    kt = hh * 4 + q
                    S.tr(P.ap[:, q * 128:(q + 1) * 128], t1f[:, kt * 128:(kt + 1) * 128], ident.ap, [t1, ident], [P])
                S.act(ob.ap[:, hh * 4:(hh + 1) * 4, :], P.ap.rearrange("p (q t) -> p q t", q=4), AF.Copy, [P], [ob])
            S.dma("sp", oaT_v[:, :, c * 128:(c + 1) * 128], ob.ap, [ob], [self.oaT_d])


    gen = P1(0)
    for _ in gen:
        pass
    for c in range(NT):
        P2(c)
        gen = P1(c + 1) if c + 1 < NT else iter(())

        def step(gen=gen):
            next(gen, None)
        heads(c, step)
        for _ in gen:
            pass
        post(c)
    S.barrier()
    ar.pop()


KB.stage3 = stage3


def stage5(self):
    ar, S = self.ar, self.S
    d = self.din
    wor_d = d("w_o_rwkv", [1024, D])
    won_d = d("w_o_nsa", [1024, D])
    wout_d = d("w_out", [D, D])
    ar.push()
    mixT = ar.alloc([16, SEQ], BF16, "mixT")
    ar.push()
    oaT = ar.alloc([8, SEQ], BF16, "oaT")
    obT = ar.alloc([8, SEQ], BF16, "obT")
    S.dma("sp", oaT.ap, self.oaT_d.ap.rearrange("(k p) t -> p k t", p=128), [self.oaT_d], [oaT])
    S.dma("sp", obT.ap, self.obT_d.ap.rearrange("(k p) t -> p k t", p=128), [self.obT_d], [obT])
    woa = [ar.alloc([8, 128], BF16, f"woa{i}") for i in range(2)]
    wob = [ar.alloc([8, 128], BF16, f"wob{i}") for i in range(2)]
    sga = [ar.alloc([SEQ], BF16, f"sga{i}") for i in range(2)]
    sgb = [ar.alloc([SEQ], BF16, f"sgb{i}") for i in range(2)]
    t1s = [ar.alloc([512], F32, f"t1_{i}") for i in range(2)]
    t2s = [ar.alloc([512], F32, f"t2_{i}") for i in range(2)]
    wor_v = wor_d.ap.rearrange("(k p) c -> p k c", p=128)
    won_v = won_d.ap.rearrange("(k p) c -> p k c", p=128)
    cnt = 0
    mod_steps = []
    for jt in range(16):
        wa, wb, sa, sb = woa[jt % 2], wob[jt % 2], sga[jt % 2], sgb[jt % 2]
        S.dma("pool", wa.ap, wor_v[:, :, jt * 128:(jt + 1) * 128], [wor_d], [wa])
        S.dma("pool", wb.ap, won_v[:, :, jt * 128:(jt + 1) * 128], [won_d], [wb])
        S.dma("sp", sa.ap, self.mgT_d.ap[jt * 128:(jt + 1) * 128, :], [self.mgT_d], [sa])
        S.dma("sp", sb.ap, self.mgT_d.ap[2048 + jt * 128:2048 + (jt + 1) * 128, :], [self.mgT_d], [sb])
        if jt >= 1:
            for _ in range(2):
                if mod_steps:
                    mod_steps.pop(0)()
        for n in range(4):
            Pa = self.bank()
            for k in range(8):
                S.mm(Pa.ap, wa.ap[:, k, :], oaT.ap[:, k, n * 512:(n + 1) * 512], k == 0, k == 7, [wa, oaT], [Pa])
            Pb = self.bank()
            for k in range(8):
                S.mm(Pb.ap, wb.ap[:, k, :], obT.ap[:, k, n * 512:(n + 1) * 512], k == 0, k == 7, [wb, obT], [Pb])
            t1, t2 = t1s[cnt % 2], t2s[cnt % 2]
            cnt += 1
            S.v("dve", "tensor_tensor", [Pa, sa], [t1], t1.ap, Pa.ap, sa.ap[:, n * 512:(n + 1) * 512], ALU.mult)
            S.v("dve", "tensor_tensor", [Pb, sb], [t2], t2.ap, Pb.ap, sb.ap[:, n * 512:(n + 1) * 512], ALU.mult)
            S.v("dve", "tensor_tensor", [t1, t2], [mixT.sub(n)], mixT.ap[:, jt, n * 512:(n + 1) * 512], t1.ap, t2.ap, ALU.add)
    while mod_steps:
        mod_steps.pop(0)()
    S.barrier()
    ar.pop()
    wout = ar.alloc([16, D], BF16, "wout")
    wout_v = wout_d.ap.rearrange("(k p) c -> p k c", p=128)
    for nn in range(4):
        S.dma("pool", wout.ap[:, :, nn * 512:(nn + 1) * 512], wout_v[:, :, nn * 512:(nn + 1) * 512], [wout_d], [wout.sub(nn)])
    xbs = [ar.alloc([D], F32, f"x5_{i}") for i in range(2)]
    obs = [ar.alloc([D], F32, f"o5_{i}") for i in range(2)]
    tms = [ar.alloc([512], F32, f"tm5_{i}") for i in range(2)]
    cnt = 0
    for tt in range(NT):
        xb, ob = xbs[tt % 2], obs[tt % 2]
        S.dma("sp", xb.ap, self.x_d.ap[tt * 128:(tt + 1) * 128, :], [self.x_d], [xb])
        for nn in range(4):
            P = self.bank()
            for k in range(16):
                S.mm(P.ap, mixT.ap[:, k, tt * 128:(tt + 1) * 128], wout.ap[:, k, nn * 512:(nn + 1) * 512], k == 0, k == 15,
                     [mixT.sub(tt // 4), wout.sub(nn)], [P])
            tm = tms[cnt % 2]
            cnt += 1
            S.v("dve", "tensor_tensor", [P, self.gt1], [tm], tm.ap, P.ap, self.gt1.ap[:, nn * 512:(nn + 1) * 512], ALU.mult)
            S.v("dve", "tensor_tensor", [tm, xb], [ob], ob.ap[:, nn * 512:(nn + 1) * 512], tm.ap, xb.ap[:, nn * 512:(nn + 1) * 512], ALU.add)
        S.dma("pool", self.x1_d.ap[tt * 128:(tt + 1) * 128, :], ob.ap, [ob], [self.x1_d])
    S.barrier()
    ar.pop()


def stage6(self):
    ar, S = self.ar, self.S
    d = self.din
    hT = self.hT
    ar.push()
    uT = ar.alloc([64, 512], BF16, "uT")
    wups = [ar.alloc([16, 256], BF16, f"wup{i}") for i in range(2)]
    wdns = [ar.alloc([8, 512], BF16, f"wdn{i}") for i in range(3)]
    rts = [ar.alloc([512], F32, f"rt{i}") for i in range(2)]
    xps = [ar.alloc([512], F32, f"xp{i}") for i in range(2)]
    ops_ = [ar.alloc([512], F32, f"op{i}") for i in range(2)]
    tms = [ar.alloc([512], F32, f"tm6_{i}") for i in range(2)]
    c_up = c_dn = c_e = 0
    for c in range(4):
        for fb in range(32):
            wu = wups[c_up % 2]
            c_up += 1
            S.dma("sp", wu.ap, self.wupb.ap[fb].rearrange("p (k c) -> p k c", k=16), [self.wupb], [wu])
            for ft in range(2):
                f = fb * 2 + ft
                P = self.ps[(c_e) % 4]
                rt = rts[c_e % 2]
                c_e += 1
                for k in range(16):
                    S.mm(P.ap, wu.ap[:, k, ft * 128:(ft + 1) * 128], hT.ap[:, k, c * 512:(c + 1) * 512], k == 0, k == 15,
                         [wu, hT.sub(c)], [P])
                S.act(rt.ap, P.ap, AF.Relu, [P], [rt])
                S.v("dve", "tensor_tensor", [rt], [uT.sub(f)], uT.ap[:, f, :], rt.ap, rt.ap, ALU.mult)
        for nn in range(4):
            accs = self.ps[4:8] if (nn % 2 == 0) else self.ps[0:4]
            for f8 in range(8):
                wd = wdns[c_dn % 3]
                c_dn += 1
                S.dma("sp", wd.ap, self.wdnb.ap[nn, f8].rearrange("p (f c) -> p f c", f=8), [self.wdnb], [wd])
                for fi in range(8):
                    f = f8 * 8 + fi
                    for tt in range(4):
                        S.mm(accs[tt].ap, uT.ap[:, f, tt * 128:(tt + 1) * 128], wd.ap[:, fi, :], f == 0, f == 63,
                             [uT.sub(f), wd], [accs[tt]])
            for tt in range(4):
                row = (c * 4 + tt) * 128
                xp, op, tm = xps[c_e % 2], ops_[c_e % 2], tms[c_e % 2]
                c_e += 1
                S.dma("pool", xp.ap, self.x1_d.ap[row:row + 128, nn * 512:(nn + 1) * 512], [self.x1_d], [xp])
                S.v("dve", "tensor_tensor", [accs[tt], self.gt2], [tm], tm.ap, accs[tt].ap, self.gt2.ap[:, nn * 512:(nn + 1) * 512], ALU.mult)
                S.v("dve", "tensor_tensor", [tm, xp], [op], op.ap, tm.ap, xp.ap, ALU.add)
                S.dma("pool", self.out_d.ap[row:row + 128, nn * 512:(nn + 1) * 512], op.ap, [op], [self.out_d])
    S.barrier()
    ar.pop()


KB.stage5 = stage5
KB.stage6 = stage6


def precast(self):
    S = self.S
    self.wup_in = self.din("w_up", [D, DFF])
    self.wdn_in = self.din("w_down", [DFF, D])
    self.wupb = T(self.nc.dram_tensor("wupb_i", [32, 128, 16 * 256], BF16).ap(), "wupb")
    self.wdnb = T(self.nc.dram_tensor("wdnb_i", [4, 8, 128, 8 * 512], BF16).ap(), "wdnb")
    wup_v = self.wup_in.ap.rearrange("(k p) c -> p k c", p=128)
    wdn_v = self.wdn_in.ap.rearrange("(f p) c -> p f c", p=128)
    for fb in range(32):
        S.dma("pool", self.wupb.ap[fb].rearrange("p (k c) -> p k c", k=16), wup_v[:, :, fb * 256:(fb + 1) * 256],
              [self.wup_in], [self.wupb])
    for nn in range(4):
        for f8 in range(8):
            S.dma("pool", self.wdnb.ap[nn, f8].rearrange("p (f c) -> p f c", f=8),
                  wdn_v[:, f8 * 8:(f8 + 1) * 8, nn * 512:(nn + 1) * 512], [self.wdn_in], [self.wdnb])


KB.precast = precast


_CACHE = {}


def _all_consts():
    c = host_consts()
    c.update(nsa_consts())
    c.update(rwkv_consts())
    return c


def prep_all(inp, b, consts):
    m = prep_core(inp, b, consts)
    nsa_prep(inp, m)
    rwkv_prep(inp, m)
    m["w_o_rwkv"] = inp["w_o_rwkv"][0]
    m["w_o_nsa"] = inp["w_o_nsa"][0]
    m["w_out"] = inp["w_out"][0]
    m["w_up"] = inp["w_up"][0]
    m["w_down"] = inp["w_down"][0]
    return m


def kernel(**inputs):
    inp = {k: np.asarray(v) for k, v in inputs.items()}
    if "nc" not in _CACHE:
        _CACHE["nc"] = KB(dbg=False).build()
        _CACHE["consts"] = _all_consts()
    nc = _CACHE["nc"]
    consts = _CACHE["consts"]
    in_maps = [prep_all(inp, b, consts) for b in range(8)]
    res = run_bass_kernel_spmd(nc, in_maps, core_ids=list(range(8)))
    out = np.stack([np.asarray(r["out"]) for r in res.results], axis=0)
    return out.astype(np.float32)
```

```python
import numpy as np
import concourse.bass as bass
import concourse.mybir as mybir

F32 = mybir.dt.float32
BF16 = mybir.dt.bfloat16
AF = mybir.ActivationFunctionType
ALU = mybir.AluOpType
AX = mybir.AxisListType

ENGS = ("pe", "act", "dve", "pool", "sp")
NSLOT = {"sp": 40, "pool": 24}


class Buf:
    __slots__ = ("w", "r_eng", "r_dma", "name")

    def __init__(self, name=""):
        self.w = None
        self.r_eng = {}
        self.r_dma = []
        self.name = name


class T(Buf):
    __slots__ = ("ap", "subs")

    def __init__(self, ap, name=""):
        Buf.__init__(self, name)
        self.ap = ap
        self.subs = {}

    def __getitem__(self, idx):
        return self.ap[idx]

    def sub(self, key):
        b = self.subs.get(key)
        if b is None:
            b = self.subs[key] = Buf(f"{self.name}.{key}")
        return b


class Op:
    __slots__ = ("eng", "fn", "deps", "is_dma", "slot", "sig", "val", "dsem", "dval", "idx")

    def __init__(self, eng, fn, is_dma):
        self.eng = eng
        self.fn = fn
        self.is_dma = is_dma
        self.deps = []
        self.sig = False
        self.val = 0
        self.slot = -1
        self.dsem = None
        self.dval = 0


class Sched:
    def __init__(self, nc):
        self.nc = nc
        self.ops = {e: [] for e in ENGS}
        self.bar = {e: [] for e in ENGS}
        self.dma_since_bar = []
        self.slot_last = {q: [None] * n for q, n in NSLOT.items()}
        self.slot_n = {q: 0 for q in NSLOT}

    def rec(self, eng, fn, reads=(), writes=(), is_dma=False):
        op = Op(eng, fn, is_dma)
        deps = []
        for b in reads:
            if b.w is not None:
                deps.append(b.w)
        for b in writes:
            if b.w is not None:
                deps.append(b.w)
            deps.extend(b.r_eng.values())
            deps.extend(b.r_dma)
        if self.bar[eng]:
            deps.extend(self.bar[eng])
            self.bar[eng] = []
        if is_dma:
            n = self.slot_n[eng]
            self.slot_n[eng] = n + 1
            s = n % NSLOT[eng]
            op.slot = s
            prev = self.slot_last[eng][s]
            if prev is not None:
                deps.append(prev)
            self.slot_last[eng][s] = op
            self.dma_since_bar.append(op)
        seen = set()
        for d in deps:
            if d is op or id(d) in seen:
                continue
            seen.add(id(d))
            op.deps.append(d)
        for b in reads:
            if is_dma:
                b.r_dma.append(op)
            else:
                b.r_eng[eng] = op
        for b in writes:
            b.w = op
            b.r_eng = {}
            b.r_dma = []
        self.ops[eng].append(op)
        return op

    def barrier(self):
        deps = [self.ops[e][-1] for e in ENGS if self.ops[e]] + self.dma_since_bar
        self.dma_since_bar = []
        for e in ENGS:
            self.bar[e] = list(deps)

    def mm(self, out, lhsT, rhs, start, stop, reads, writes, **kw):
        return self.rec("pe", lambda e: e.matmul(out, lhsT, rhs, start=start, stop=stop, **kw), reads, writes)

    def tr(self, out, in_, ident, reads, writes):
        return self.rec("pe", lambda e: e.transpose(out, in_, ident), reads, writes)

    def act(self, out, in_, func, reads, writes, bias=None, scale=None, accum_out=None):
        kw = {}
        if bias is not None:
            kw["bias"] = bias
        if scale is not None:
            kw["scale"] = scale
        if accum_out is not None:
            kw["accum_out"] = accum_out
        return self.rec("act", lambda e: e.activation(out=out, in_=in_, func=func, **kw), reads, writes)

    def v(self, eng, meth, reads, writes, *a, **kw):
        return self.rec(eng, lambda e: getattr(e, meth)(*a, **kw), reads, writes)

    def dma(self, q, out, in_, reads, writes, **kw):
        return self.rec(q, lambda e: e.dma_start(out=out, in_=in_, **kw), reads, writes, is_dma=True)

    def finalize(self):
        for e in ENGS:
            for op in self.ops[e]:
                for d in op.deps:
                    if d.is_dma:
                        continue
                    if d.eng == "pe" and op.eng == "pe" and not op.is_dma:
                        continue
                    d.sig = True
        for e in ENGS:
            n = 0
            for op in self.ops[e]:
                if op.sig and not op.is_dma:
                    n += 1
                    op.val = n

    def emit_all(self, stack):
        nc = self.nc
        self.finalize()
        self.esem = {e: stack.enter_context(nc.semaphore("es_" + e)) for e in ENGS}
        self.dsem = {q: [stack.enter_context(nc.semaphore(f"ds_{q}{i}")) for i in range(n)] for q, n in NSLOT.items()}
        uses = {q: [0] * n for q, n in NSLOT.items()}
        for q in NSLOT:
            for op in self.ops[q]:
                if op.is_dma:
                    uses[q][op.slot] += 1
                    op.dsem = self.dsem[q][op.slot]
                    op.dval = 16 * uses[q][op.slot]
        block = stack.enter_context(nc.Block())
        sched = self

        def emit(name, eng):
            known = {}
            for op in sched.ops[name]:
                for d in op.deps:
                    if d.is_dma:
                        sem, val = d.dsem, d.dval
                    else:
                        if d.eng == "pe" and name == "pe" and not op.is_dma:
                            continue
                        sem, val = sched.esem[d.eng], d.val
                    k = id(sem)
                    if known.get(k, 0) >= val:
                        continue
                    eng.wait_ge(sem, val)
                    known[k] = val
                ins = op.fn(eng)
                if op.is_dma:
                    ins.then_inc(op.dsem, 16)
                elif op.sig:
                    ins.then_inc(sched.esem[name], 1)
            if name == "sp":
                for q in NSLOT:
                    for i, u in enumerate(uses[q]):
                        if u:
                            eng.wait_ge(sched.dsem[q][i], 16 * u)

        @block.tensor
        def _(e):
            emit("pe", e)

        @block.scalar
        def _(e):
            emit("act", e)

        @block.vector
        def _(e):
            emit("dve", e)

        @block.gpsimd
        def _(e):
            emit("pool", e)

        @block.sync
        def _(e):
            emit("sp", e)


class Arena:
    def __init__(self, ap, nwords):
        self.ap = ap
        self.n = nwords
        self.off = 0
        self.marks = []

    def push(self):
        self.marks.append(self.off)

    def pop(self):
        self.off = self.marks.pop()

    def alloc(self, shape, dtype=F32, name="", parts=128):
        n = int(np.prod(shape))
        words = n if dtype == F32 else (n + 1) // 2
        words = (words + 7) // 8 * 8
        assert self.off + words <= self.n, f"arena overflow {name} {self.off}+{words}>{self.n}"
        ap = self.ap[0:parts, self.off:self.off + words]
        self.off += words
        if dtype != F32:
            ap = ap.bitcast(dtype)
        ap = ap[:, 0:n]
        if len(shape) == 2:
            ap = ap.rearrange("p (a b) -> p a b", a=shape[0])
        elif len(shape) == 3:
            ap = ap.rearrange("p (a b c) -> p a b c", a=shape[0], b=shape[1])
        elif len(shape) == 4:
            ap = ap.rearrange("p (a b c d) -> p a b c d", a=shape[0], b=shape[1], c=shape[2])
        return T(ap, name)

from contextlib import ExitStack
from concourse.bass_utils import run_bass_kernel_spmd

D = 2048
SEQ = 2048
NT = 16
RWC = 3360
NB = 3360
MB = 5968
INC = 10064
DFF = 8192
EPS = 1e-6
GN_EPS = 64e-5
NEG = -30000.0
ARENA_WORDS = 52000


class KB:
    def __init__(self, dbg=False, stages=(0, 1, 2, 3, 4, 5, 6)):
        self.nc = bass.Bass("TRN2", target_bir_lowering=False)
        self.S = Sched(self.nc)
        self.dbg = dbg
        self.stages = stages
        self.bank_i = 0

    def din(self, name, shape, dt=F32):
        return T(self.nc.dram_tensor(name, list(shape), dt, kind="ExternalInput").ap(), name)

    def dscr(self, name, shape, dt=F32, out=False):
        kind = "ExternalOutput" if (self.dbg or out) else "Internal"
        return T(self.nc.dram_tensor(name, list(shape), dt, kind=kind).ap(), name)

    def bank(self):
        b = self.ps[self.bank_i % 8]
        self.bank_i += 1
        return b

    def load(self, dst, src, q="sp"):
        self.S.dma(q, dst.ap, src[1], [src[0]], [dst])

    def build(self):
        nc, S = self.nc, self.S
        with ExitStack() as st:
            arena_t = st.enter_context(nc.sbuf_tensor("arena", [128, ARENA_WORDS], F32))
            self.ar = ar = Arena(arena_t, ARENA_WORDS)
            self.ps = [T(st.enter_context(nc.psum_tensor(f"ps{i}", [128, 512], F32))[:], f"ps{i}") for i in range(8)]
            self.declare()
            self.persistent()
            if 0 in self.stages:
                self.stage0()
            if 1 in self.stages:
                ar.push()
                self.hT = ar.alloc([16, SEQ], BF16, "hT")
                self.stage1(self.x_d, self.coef1, self.sh1, self.hT)
                if 2 in self.stages:
                    self.stage2()
                ar.pop()
            if 4 in self.stages:
                self.stage4()
            if 3 in self.stages:
                self.stage3()
            if 5 in self.stages:
                self.stage5()
            if 6 in self.stages:
                ar.push()
                self.hT = ar.alloc([16, SEQ], BF16, "h2T")
                self.stage1(self.x1_d, self.coef2, self.sh2, self.hT)
                self.stage6()
                ar.pop()
            S.emit_all(st)
        return nc

    def declare(self):
        d = self.din
        self.x_d = d("x", [SEQ, D])
        self.c_fm = d("c_fm", [128, 16])
        self.w_ada = d("w_ada", [D, 6 * D])
        self.b_ada = d("b_ada", [1, 6 * D])
        self.n1g = d("n1g_fm", [128, 16])
        self.n2g = d("n2g_fm", [128, 16])
        self.w_in = d("w_in", [D, INC])
        self.ident_d = d("ident", [128, 128])
        self.bones_d = d("bones", [128, 128])
        self.mu_d = d("mu_fm", [128, 27])
        self.qkg_d = d("qkg_fm", [128, 5])
        s = self.dscr
        self.rwT_d = s("rwT", [27 * 128, SEQ])
        self.qT_d = s("qT", [1024, SEQ], BF16)
        self.kvcT_d = s("kvcT", [512, SEQ], BF16)
        self.ksT_d = s("ksT", [4, 2, 128, SEQ], BF16)
        self.kwT_d = s("kwT", [4, 2, 128, SEQ], BF16)
        self.vv_d = s("vv", [SEQ, 512], BF16)
        self.gates_d = s("gates", [SEQ, 48])
        self.mgT_d = s("mgT", [4096, SEQ], BF16)
        self.x1_d = s("x1", [SEQ, D])
        self.out_d = self.dscr("out", [SEQ, D], out=True)

    def persistent(self):
        ar, S = self.ar, self.S
        self.ident = ar.alloc([128], F32, "ident")
        self.bones = ar.alloc([128], F32, "bones")
        S.dma("sp", self.ident.ap, self.ident_d.ap, [self.ident_d], [self.ident])
        S.dma("sp", self.bones.ap, self.bones_d.ap, [self.bones_d], [self.bones])
        self.coef1 = ar.alloc([16], F32, "coef1")
        self.sh1 = ar.alloc([16], F32, "sh1")
        self.coef2 = ar.alloc([16], F32, "coef2")
        self.sh2 = ar.alloc([16], F32, "sh2")
        self.gt1 = ar.alloc([D], F32, "gt1")
        self.gt2 = ar.alloc([D], F32, "gt2")

    def silu_rep(self):
        ar, S = self.ar, self.S
        cs = ar.alloc([16], F32, "cs")
        S.dma("sp", cs.ap, self.c_fm.ap, [self.c_fm], [cs])
        csb = ar.alloc([16], F32, "csb")
        S.act(csb.ap, cs.ap, AF.Silu, [cs], [csb])
        crep = ar.alloc([16, 128], BF16, "crep")
        S.v("dve", "tensor_copy", [csb], [crep], crep.ap, csb.ap.unsqueeze(2).to_broadcast([128, 16, 128]))
        return crep

    def stage0(self):
        ar, S = self.ar, self.S
        ar.push()
        bias_steps = self.nsa_bias_build() if 4 in self.stages else []
        mod = ar.alloc([6 * D], F32, "mod")
        crep = self.silu_rep()
        wbs = [ar.alloc([16, 512], BF16, f"wada{i}") for i in range(2)]
        bbs = [ar.alloc([512], F32, f"bada{i}") for i in range(2)]
        stgs = [ar.alloc([16, 256], F32, f"wstg{i}") for i in range(2)]
        wsrc = self.w_ada.ap.rearrange("(k p) c -> p k c", p=128)
        for blk in range(24):
            wb, bb = wbs[blk % 2], bbs[blk % 2]
            c0 = blk * 512
            for half in range(2):
                stg = stgs[half]
                h0 = half * 256
                S.dma("sp", stg.ap, wsrc[:, :, c0 + h0:c0 + h0 + 256], [self.w_ada], [stg])
                if half == 0:
                    S.act(wb.ap[:, :, h0:h0 + 256], stg.ap, AF.Copy, [stg], [wb])
                else:
                    S.v("dve", "tensor_copy", [stg], [wb], wb.ap[:, :, h0:h0 + 256], stg.ap)
            S.dma("sp", bb.ap, self.b_ada.ap[0:1, c0:c0 + 512].partition_broadcast(128), [self.b_ada], [bb])
            P = self.bank()
            for k in range(16):
                S.mm(P.ap, crep.ap[:, k, :], wb.ap[:, k, :], k == 0, k == 15, [crep, wb], [P])
            S.v("dve", "tensor_tensor", [P, bb], [mod], mod.ap[:, c0:c0 + 512], P.ap, bb.ap, ALU.add)
            for _ in range(2):
                if bias_steps:
                    bias_steps.pop(0)()
        while bias_steps:
            bias_steps.pop(0)()
        tmp = ar.alloc([16, 128], F32, "dtmp")
        sc1 = ar.alloc([16], F32, "sc1")
        sc2 = ar.alloc([16], F32, "sc2")
        for dst, idx in ((self.sh1, 0), (sc1, 1), (self.sh2, 3), (sc2, 4)):
            src = mod.ap[:, idx * D:(idx + 1) * D].rearrange("p (k m) -> p k m", k=16)
            S.v("dve", "tensor_tensor", [mod, self.ident], [tmp], tmp.ap, src,
                self.ident.ap.unsqueeze(1).to_broadcast([128, 16, 128]), ALU.mult)
            S.v("dve", "tensor_reduce", [tmp], [dst], dst.ap, tmp.ap, AX.X, ALU.add)
        g = ar.alloc([16], F32, "gload")
        S.dma("sp", g.ap, self.n1g.ap, [self.n1g], [g])
        S.v("dve", "scalar_tensor_tensor", [sc1, g], [self.coef1], self.coef1.ap, sc1.ap, 1.0, g.ap, ALU.add, ALU.mult)
        g2 = ar.alloc([16], F32, "gload2")
        S.dma("sp", g2.ap, self.n2g.ap, [self.n2g], [g2])
        S.v("dve", "scalar_tensor_tensor", [sc2, g2], [self.coef2], self.coef2.ap, sc2.ap, 1.0, g2.ap, ALU.add, ALU.mult)
        S.v("dve", "tensor_copy", [mod], [self.gt1], self.gt1.ap, mod.ap[:, 2 * D:3 * D])
        S.v("dve", "tensor_copy", [mod], [self.gt2], self.gt2.ap, mod.ap[:, 5 * D:6 * D])
        S.barrier()
        ar.pop()

    def stage0b_setup(self):
        ar, S = self.ar, self.S
        crep = self.silu_rep()
        wb2 = [ar.alloc([16, 256], BF16, f"wada_b{i}") for i in range(2)]
        bb2 = [ar.alloc([256], F32, f"bada_b{i}") for i in range(2)]
        tmp = ar.alloc([2, 128], F32, "dtmp_b")
        sc2 = ar.alloc([16], F32, "sc2")
        wsrc = self.w_ada.ap.rearrange("(k p) c -> p k c", p=128)
        steps = []

        def mk(sb):
            def step():
                wb, bb = wb2[sb % 2], bb2[sb % 2]
                c0 = 3 * D + sb * 256
                S.dma("pool", wb.ap, wsrc[:, :, c0:c0 + 256], [self.w_ada], [wb])
                S.dma("sp", bb.ap, self.b_ada.ap[0:1, c0:c0 + 256].partition_broadcast(128), [self.b_ada], [bb])
                P = self.bank()
                for k in range(16):
                    S.mm(P.ap[:, 0:256], crep.ap[:, k, :], wb.ap[:, k, :], k == 0, k == 15, [crep, wb], [P])
                if sb < 16:
                    dst = self.sh2 if sb < 8 else sc2
                    j = sb % 8
                    t2 = tmp.ap.rearrange("p a b -> p (a b)")
                    S.v("dve", "tensor_tensor", [P, bb], [tmp], t2, P.ap[:, 0:256], bb.ap, ALU.add)
                    S.v("dve", "tensor_tensor", [tmp, self.ident], [tmp], tmp.ap, tmp.ap,
                        self.ident.ap.unsqueeze(1).to_broadcast([128, 2, 128]), ALU.mult)
                    S.v("dve", "tensor_reduce", [tmp], [dst], dst.ap[:, 2 * j:2 * j + 2], tmp.ap, AX.X, ALU.add)
                else:
                    o = (sb - 16) * 256
                    S.v("dve", "tensor_tensor", [P, bb], [self.gt2], self.gt2.ap[:, o:o + 256], P.ap[:, 0:256], bb.ap, ALU.add)
                if sb == 23:
                    g = ar.alloc([16], F32, "gload2")
                    S.dma("sp", g.ap, self.n2g.ap, [self.n2g], [g])
                    S.v("dve", "scalar_tensor_tensor", [sc2, g], [self.coef2], self.coef2.ap, sc2.ap, 1.0, g.ap, ALU.add, ALU.mult)
            return step
        return [mk(sb) for sb in range(24)]

    def stage1(self, src_d, coef, sh, hT):
        ar, S = self.ar, self.S
        ar.push()
        xbs = [ar.alloc([D], F32, f"xb{i}") for i in range(2)]
        junk = ar.alloc([D], F32, "junk")
        xs4s = [ar.alloc([4, D], F32, f"xs4_{i}") for i in range(1)]
        ss = ar.alloc([NT], F32, "ss")
        sr = ar.alloc([NT], F32, "sr")
        rstd = ar.alloc([NT], F32, "rstd")
        for grp in range(4):
            xs4 = xs4s[0]
            for tt in range(4):
                ti = grp * 4 + tt
                xb = xbs[ti % 2]
                S.dma("sp", xb.ap, src_d.ap[ti * 128:(ti + 1) * 128, :], [src_d], [xb])
                sst = ss.sub(ti)
                S.act(junk.ap, xb.ap, AF.Square, [xb], [sst], accum_out=ss.ap[:, ti:ti + 1])
                S.act(sr.ap[:, ti:ti + 1], ss.ap[:, ti:ti + 1], AF.Sqrt, [sst], [sr.sub(ti)], bias=EPS, scale=1.0 / D)
                S.v("dve", "reciprocal", [sr.sub(ti)], [rstd.sub(ti)], rstd.ap[:, ti:ti + 1], sr.ap[:, ti:ti + 1])
                S.v("dve", "tensor_scalar", [xb, rstd.sub(ti)], [xs4.sub(tt)], xs4.ap[:, tt, :], xb.ap,
                    rstd.ap[:, ti:ti + 1], None, ALU.mult)
            for k in range(16):
                P = self.bank()
                for tt in range(4):
                    S.tr(P.ap[:, tt * 128:(tt + 1) * 128], xs4.ap[:, tt, k * 128:(k + 1) * 128], self.ident.ap,
                         [xs4.sub(tt), self.ident], [P])
                S.act(hT.ap[:, k, grp * 512:(grp + 1) * 512], P.ap, AF.Identity, [P, coef, sh], [hT.sub(grp)],
                      bias=sh.ap[:, k:k + 1], scale=coef.ap[:, k:k + 1])
        S.barrier()
        ar.pop()

    def stage2(self):
        ar, S = self.ar, self.S
        hT = self.hT
        ar.push()
        wbs = [ar.alloc([16, 512], BF16, f"win{i}") for i in range(2)]
        wtm = ar.alloc([16, 560], BF16, "wtm")
        raws = [ar.alloc([2056], F32, f"raw{i}") for i in range(2)]
        tmps = [ar.alloc([SEQ], F32, f"mixt{i}") for i in range(2)]
        stg = [ar.alloc([SEQ], BF16, f"stg{i}") for i in range(4)]
        sqs = [ar.alloc([512], F32, f"sq{i}") for i in range(2)]
        srs = [ar.alloc([512], F32, f"sr{i}") for i in range(2)]
        ris = [ar.alloc([512], F32, f"ri{i}") for i in range(2)]
        mu = ar.alloc([27], F32, "mu")
        omu = ar.alloc([27], F32, "omu")
        qkg = ar.alloc([5], F32, "qkg")
        S.dma("sp", mu.ap, self.mu_d.ap, [self.mu_d], [mu])
        S.dma("sp", qkg.ap, self.qkg_d.ap, [self.qkg_d], [qkg])
        S.v("dve", "tensor_scalar", [mu], [omu], omu.ap, mu.ap, -1.0, 1.0, ALU.mult, ALU.add)
        S.v("dve", "tensor_scalar", [qkg], [qkg], qkg.ap[:, 1:5], qkg.ap[:, 1:5], 8.0, None, ALU.mult)
        for r in raws:
            S.v("dve", "memset", [], [r], r.ap[:, 0:1], 0.0)
        wsrc = self.w_in.ap.rearrange("(k p) c -> p k c", p=128)
        cnt = {"w": 0, "raw": 0, "stg": 0, "sq": 0}

        def proj_chunk(wb, m0, M, n):
            P = self.bank()
            for k in range(16):
                S.mm(P.ap[0:M, :], wb.ap[:, k, m0:m0 + M], hT.ap[:, k, n * 512:(n + 1) * 512], k == 0, k == 15,
                     [wb, hT.sub(n)], [P])
            return P

        def next_stg():
            t = stg[cnt["stg"] % 4]
            cnt["stg"] += 1
            return t

        def ep_rw(wb, m0, M, ti):
            raw = raws[cnt["raw"] % 2]
            tmp = tmps[cnt["raw"] % 2]
            cnt["raw"] += 1
            for n in range(4):
                P = proj_chunk(wb, m0, M, n)
                S.act(raw.ap[0:M, 1 + n * 512:1 + (n + 1) * 512], P.ap[0:M, :], AF.Copy, [P], [raw])
            S.v("dve", "tensor_scalar", [raw, mu], [tmp], tmp.ap[0:M, :], raw.ap[0:M, 0:SEQ], mu.ap[0:M, ti:ti + 1], None, ALU.mult)
            S.v("dve", "scalar_tensor_tensor", [raw, omu, tmp], [tmp], tmp.ap[0:M, :], raw.ap[0:M, 1:SEQ + 1],
                omu.ap[0:M, ti:ti + 1], tmp.ap[0:M, :], ALU.mult, ALU.add)
            S.dma("sp", self.rwT_d.ap[ti * 128:ti * 128 + M, :], tmp.ap[0:M, :], [tmp], [self.rwT_d])

        def ep_qk(wb, m0, gcols, dsts):
            outs = [next_stg() for _ in gcols]
            for n in range(4):
                P = proj_chunk(wb, m0, 128, n)
                i = cnt["sq"] % 2
                cnt["sq"] += 1
                sq, sr, ri = sqs[i], srs[i], ris[i]
                S.act(sq.ap, P.ap, AF.Square, [P], [sq])
                P2 = self.bank()
                S.mm(P2.ap, self.bones.ap, sq.ap, True, True, [self.bones, sq], [P2])
                S.act(sr.ap, P2.ap, AF.Sqrt, [P2], [sr], bias=64 * EPS, scale=1.0)
                S.v("dve", "reciprocal", [sr], [ri], ri.ap, sr.ap)
                for gc, o in zip(gcols, outs):
                    S.v("dve", "scalar_tensor_tensor", [P, qkg, ri], [o], o.ap[:, n * 512:(n + 1) * 512], P.ap,
                        qkg.ap[:, gc:gc + 1], ri.ap, ALU.mult, ALU.mult)
            for o, (dt_, dap) in zip(outs, dsts):
                S.dma("sp", dap, o.ap, [o], [dt_])

        def ep_act(wb, m0, func, dt_, dap):
            o = next_stg()
            for n in range(4):
                P = proj_chunk(wb, m0, 128, n)
                S.act(o.ap[:, n * 512:(n + 1) * 512], P.ap, func, [P], [o])
            S.dma("sp", dap, o.ap, [o], [dt_])

        def load_block(segs):
            wb = wbs[cnt["w"] % 2]
            cnt["w"] += 1
            for (c0, n, off) in segs:
                S.dma("pool", wb.ap[:, :, off:off + n], wsrc[:, :, c0:c0 + n], [self.w_in], [wb])
            return wb

        for b in range(7):
            if b < 6:
                wb = load_block([(512 * b, 512, 0)])
                for j in range(4):
                    ep_rw(wb, j * 128, 128, 4 * b + j)
            else:
                wb = load_block([(3072, 288, 0)])
                ep_rw(wb, 0, 128, 24)
                ep_rw(wb, 128, 128, 25)
                ep_rw(wb, 256, 32, 26)
        for b in range(2):
            wb = load_block([(NB + 512 * b, 512, 0)])
            for j in range(4):
                ti = 4 * b + j
                ep_qk(wb, j * 128, [0], [(self.qT_d, self.qT_d.ap[ti * 128:(ti + 1) * 128, :])])
        wb = load_block([(NB + 1024, 512, 0)])
        for j in range(4):
            ep_act(wb, j * 128, AF.Copy, self.kvcT_d, self.kvcT_d.ap[j * 128:(j + 1) * 128, :])
        for (c_base, dst, gc) in ((NB + 1024 + 512, self.ksT_d, 1), (NB + 1024 + 1024, self.kwT_d, 3)):
            segs = []
            for g in range(4):
                segs.append((c_base + 64 * g, 64, g * 128))
                segs.append((c_base + 64 * g, 64, g * 128 + 64))
            wb = load_block(segs)
            for g in range(4):
                ep_qk(wb, g * 128, [gc, gc + 1], [(dst, dst.ap[g, 0]), (dst, dst.ap[g, 1])])
        for b in range(8):
            wb = load_block([(MB + 512 * b, 512, 0)])
            for j in range(4):
                ti = 4 * b + j
                ep_act(wb, j * 128, AF.Sigmoid, self.mgT_d, self.mgT_d.ap[ti * 128:(ti + 1) * 128, :])
        for (c0, n, off) in ((NB + 1024 + 768, 256, 0), (NB + 1024 + 1280, 256, 256), (NB + 2560, 48, 512)):
            S.dma("pool", wtm.ap[:, :, off:off + n], wsrc[:, :, c0:c0 + n], [self.w_in], [wtm])
        vst = [ar.alloc([512], BF16, f"vst{i}") for i in range(2)]
        gst = [ar.alloc([48], F32, f"gst{i}") for i in range(2)]
        for tt in range(NT):
            P = self.bank()
            for k in range(16):
                S.mm(P.ap, hT.ap[:, k, tt * 128:(tt + 1) * 128], wtm.ap[:, k, 0:512], k == 0, k == 15,
                     [hT.sub(tt // 4), wtm], [P])
            v = vst[tt % 2]
            S.act(v.ap, P.ap, AF.Copy, [P], [v])
            S.dma("sp", self.vv_d.ap[tt * 128:(tt + 1) * 128, :], v.ap, [v], [self.vv_d])
            P = self.bank()
            for k in range(16):
                S.mm(P.ap[:, 0:48], hT.ap[:, k, tt * 128:(tt + 1) * 128], wtm.ap[:, k, 512:560], k == 0, k == 15,
                     [hT.sub(tt // 4), wtm], [P])
            gt = gst[tt % 2]
            S.act(gt.ap, P.ap[:, 0:48], AF.Sigmoid, [P], [gt])
            S.dma("sp", self.gates_d.ap[tt * 128:(tt + 1) * 128, :], gt.ap, [gt], [self.gates_d])
        S.barrier()
        ar.pop()


def _fm(v, ntile=None):
    v = np.asarray(v, np.float32).reshape(-1)
    n = (len(v) + 127) // 128 if ntile is None else ntile
    buf = np.zeros(n * 128, np.float32)
    buf[:len(v)] = v
    return np.ascontiguousarray(buf.reshape(n, 128).T)


def host_consts():
    c = {}
    c["ident"] = np.eye(128, dtype=np.float32)
    p = np.arange(128)
    c["bones"] = (p[:, None] // 64 == p[None, :] // 64).astype(np.float32)
    return c


def prep_core(inp, b, consts):
    m = dict(consts)
    m["x"] = np.ascontiguousarray(inp["x"][b])
    m["c_fm"] = _fm(inp["c"][b])
    m["w_ada"] = inp["w_ada"][0]
    m["b_ada"] = inp["b_ada"][0].reshape(1, -1)
    m["n1g_fm"] = _fm(inp["norm1_g"][0])
    m["n2g_fm"] = _fm(inp["norm2_g"][0])
    m["w_in"] = inp["w_in"][0]
    m["mu_fm"] = _fm(inp["rwkv_mu"][0], 27)
    qg = np.tile(inp["q_norm_g"][0], 2)
    kg = inp["k_norm_g"][0]
    z = np.zeros(64, np.float32)
    cols = [qg, np.concatenate([kg[1], z]), np.concatenate([z, kg[1]]),
            np.concatenate([kg[2], z]), np.concatenate([z, kg[2]])]
    m["qkg_fm"] = np.ascontiguousarray(np.stack(cols, axis=1).astype(np.float32))
    return m


def _rel_bucket_np(rel):
    n = np.maximum(rel, 0)
    nf = np.maximum(n, 16).astype(np.float32)
    large = 16 + (np.log(nf / np.float32(16)) / np.float32(np.log(8.0)) * np.float32(16)).astype(np.int32)
    large = np.minimum(large, 31)
    return np.where(n < 16, n, large)


def nsa_consts():
    c = {}
    NOH = 3 * 16384 + 17 * 128
    oh = np.zeros((33, NOH), np.float32)
    pos = np.arange(128)[:, None]
    t = np.arange(128)[None, :]
    for d, base in ((0, 0), (1, 128), (2, 512)):
        rel = base + t - pos
        if d == 0:
            mask = rel < 0
        elif d == 1:
            mask = np.zeros_like(rel, bool)
        else:
            mask = rel >= 512
        b = _rel_bucket_np(rel)
        sec = np.zeros((33, 128, 128), np.float32)
        for bb in range(32):
            sec[bb][(b == bb) & ~mask] = 1.0
        sec[32][mask] = 1.0
        oh[:, d * 16384:(d + 1) * 16384] = sec.reshape(33, -1)
    sec = np.zeros((33, 17, 128), np.float32)
    ti = np.arange(128)
    for r in range(16):
        m = r - 9
        rel = ti - 16 * m - 31
        b = _rel_bucket_np(rel)
        for bb in range(32):
            sec[bb, r, (b == bb) & (rel >= 0)] = 1.0
        sec[32, r, rel < 0] = 1.0
    sec[32, 16, :] = 1.0
    oh[:, 3 * 16384:] = sec.reshape(33, -1)
    c["nsa_oh"] = oh
    S = np.zeros((17, 16, 128), np.float32)
    for i in range(16):
        for n in range(127):
            m = n - 8 * i
            if -9 <= m <= 6:
                S[m + 9, i, n] = 1.0
            elif m > 6:
                S[16, i, n] = 1.0
    c["nsa_S"] = S
    E = np.zeros((32, 2048), np.float32)
    for p in range(2048):
        E[p // 64, p] = 1.0
    c["nsa_E"] = E
    allowed = np.zeros((128, 16, 32), np.float32)
    addc = np.zeros((128, 16, 32), np.float32)
    blk = np.arange(32)
    for i in range(16):
        for tt in range(128):
            cur = (i * 128 + tt) // 64
            al = blk <= cur
            forced = (blk == 0) | (blk == cur) | (blk == cur - 1)
            allowed[tt, i] = (al & ~forced).astype(np.float32)
            addc[tt, i] = np.where(forced, 1e4, np.where(al, 0.0, -1.0))
    c["nsa_allowed"] = allowed
    c["nsa_addc"] = addc
    ncmp = 127
    cs = np.arange(ncmp) * 16
    ss = np.arange(32) * 64
    lo = np.maximum(cs[:, None], ss[None, :])
    hi = np.minimum(cs[:, None] + 32, ss[None, :] + 64)
    c["nsa_selm"] = (np.maximum(hi - lo, 0) / 32).astype(np.float32)
    return c


def nsa_prep(inp, m):
    m["rel_bias"] = np.ascontiguousarray(inp["rel_bias"])
    for kv in ("k", "v"):
        m[f"pe_{kv}T"] = np.ascontiguousarray(inp[f"cmp_pe_{kv}"][0].T)
        m[f"w1_{kv}"] = inp[f"cmp_w1_{kv}"][0]
        m[f"w2_{kv}"] = inp[f"cmp_w2_{kv}"][0]
    kg0 = inp["k_norm_g"][0][0]
    z = np.zeros(64, np.float32)
    m["kcg_fm"] = np.ascontiguousarray(np.stack([np.concatenate([kg0, z]), np.concatenate([z, kg0])], 1).astype(np.float32))


def stage4(self):
    ar, S, nc = self.ar, self.S, self.nc
    d = self.din
    S_d = d("nsa_S", [17, 16, 128])
    E_d = d("nsa_E", [32, 2048])
    al_d = d("nsa_allowed", [128, 16, 32])
    ad_d = d("nsa_addc", [128, 16, 32])
    selm_d = d("nsa_selm", [127, 32])
    kcg_d = d("kcg_fm", [128, 2])
    cmp_d = {}
    for kv in ("k", "v"):
        cmp_d[kv] = (d(f"pe_{kv}T", [64, 32]), d(f"w1_{kv}", [2048, 64]), d(f"w2_{kv}", [64, 64]))
    NOH = 3 * 16384 + 17 * 128
    self.obT_d = self.dscr("obT", [1024, SEQ], BF16)
    ident, bones = self.ident, self.bones

    ar.push()
    ks = ar.alloc([4, 2, SEQ], BF16, "ks")
    kw = ar.alloc([4, 2, SEQ], BF16, "kw")
    vs = ar.alloc([16, 4, 65], BF16, "vs")
    vw = ar.alloc([16, 4, 65], BF16, "vw")
    gates = ar.alloc([16, 48], F32, "gates")
    biasT = ar.alloc([3, 16, 128], F32, "biasT")
    Mst = ar.alloc([16, 128], F32, "Mst", parts=17)
    Sc = ar.alloc([16, 128], F32, "Sc", parts=17)
    allowed = ar.alloc([16, 32], F32, "allowed")
    addc = ar.alloc([16, 32], F32, "addc")
    kc = ar.alloc([4, 2, 128], BF16, "kc")
    rhsc = ar.alloc([4, 97], BF16, "rhsc", parts=127)
    for g in range(4):
        for h in range(2):
            S.dma("sp", ks.ap[:, g, h, :], self.ksT_d.ap[g, h], [self.ksT_d], [ks])
            S.dma("sp", kw.ap[:, g, h, :], self.kwT_d.ap[g, h], [self.kwT_d], [kw])
    for g_ in range(4):
        S.dma("pool", ks.ap[64:96, g_, 0, :], E_d.ap, [E_d, ks], [ks])
        S.dma("pool", ks.ap[0:32, g_, 1, :], E_d.ap, [E_d, ks], [ks])
    for (dst, c0) in ((vs, 0), (vw, 256)):
        S.v("dve", "memset", [], [dst], dst.ap[:, :, :, 64:65], 1.0)
        for j in range(16):
            S.dma("sp", dst.ap[:, j, :, 0:64],
                  self.vv_d.ap[j * 128:(j + 1) * 128, c0:c0 + 256].rearrange("p (g d) -> p g d", g=4), [self.vv_d], [dst])
    S.dma("sp", gates.ap, self.gates_d.ap.rearrange("(j p) c -> p j c", p=128), [self.gates_d], [gates])
    S.dma("sp", Sc.ap, S_d.ap, [S_d], [Sc])
    S.dma("sp", allowed.ap, al_d.ap, [al_d], [allowed])
    S.dma("sp", addc.ap, ad_d.ap, [ad_d], [addc])

    ar.push()
    bias_d = self.bias_d
    for dd in range(3):
        S.dma("sp", biasT.ap[:, dd, :, :],
              bias_d.ap[:, dd * 16384:(dd + 1) * 16384].rearrange("h (p t) -> p h t", p=128), [bias_d], [biasT])
    S.dma("sp", Mst.ap, bias_d.ap[:, 3 * 16384:].rearrange("h (r t) -> r h t", r=17), [bias_d], [Mst])
    if self.dbg:
        dbb = self.dscr("dbg_biasT", [128, 3 * 16 * 128])
        S.dma("sp", dbb.ap, biasT.ap.rearrange("p a b c -> p (a b c)"), [biasT], [dbb])
    ar.pop()

    ar.push()
    kvc = ar.alloc([4, SEQ], BF16, "kvc")
    S.dma("sp", kvc.ap, self.kvcT_d.ap.rearrange("(a p) t -> p a t", p=128), [self.kvcT_d], [kvc])
    kcg = ar.alloc([2], F32, "kcg")
    S.dma("sp", kcg.ap, kcg_d.ap, [kcg_d], [kcg])
    S.v("dve", "tensor_scalar", [kcg], [kcg], kcg.ap, kcg.ap, 8.0, None, ALU.mult)
    S.v("dve", "memset", [], [rhsc], rhsc.ap[:, :, 64:65], 1.0)
    for g in range(4):
        S.dma("pool", rhsc.ap[:, g, 65:97], selm_d.ap, [selm_d], [rhsc])
    for kvi, kv in enumerate(("k", "v")):
        pe_d, w1_d, w2_d = cmp_d[kv]
        w1p = ar.alloc([2, 32, 64], BF16, f"w1p{kv}")
        S.v("dve", "memset", [], [w1p], w1p.ap, 0.0)
        w1v = w1_d.ap.rearrange("(i d) e -> d i e", d=64)
        S.dma("pool", w1p.ap[0:64, 0, :, :], w1v, [w1_d, w1p], [w1p])
        S.dma("pool", w1p.ap[64:128, 1, :, :], w1v, [w1_d, w1p], [w1p])
        peT = ar.alloc([32], BF16, f"peT{kv}", parts=64)
        S.dma("pool", peT.ap, pe_d.ap, [pe_d], [peT])
        w2 = ar.alloc([128], BF16, f"w2{kv}", parts=64)
        S.dma("pool", w2.ap[:, 0:64], w2_d.ap, [w2_d], [w2])
        S.dma("pool", w2.ap[:, 64:128], w2_d.ap, [w2_d, w2], [w2])
        Pb = self.bank()
        for i in range(32):
            S.mm(Pb.ap[0:64, 0:1], w1p.ap[0:64, 0, i, :], peT.ap[:, i:i + 1], i == 0, i == 31, [w1p, peT], [Pb])
        cb = ar.alloc([1], F32, f"cb{kv}", parts=64)
        S.act(cb.ap, Pb.ap[0:64, 0:1], AF.Copy, [Pb], [cb])
        for g in range(4):
            tile_, half = kvi * 2 + g // 2, g % 2
            Ph = self.bank()
            for i in range(32):
                S.mm(Ph.ap[0:64, 0:127], w1p.ap[:, half, i, :], kvc.ap[:, tile_, i:i + 16 * 126 + 1:16], i == 0, i == 31,
                     [w1p, kvc], [Ph])
            u = ar.alloc([127], F32, "cu", parts=64)
            t1 = ar.alloc([127], F32, "ct1", parts=64)
            sg = ar.alloc([127], F32, "csg", parts=64)
            hid = ar.alloc([127], BF16, "chid", parts=64)
            S.act(u.ap, Ph.ap[0:64, 0:127], AF.Identity, [Ph, cb], [u], bias=cb.ap[:, 0:1], scale=1.0)
            S.v("dve", "tensor_tensor", [u], [t1], t1.ap, u.ap, u.ap, ALU.mult)
            S.v("dve", "tensor_scalar", [t1], [t1], t1.ap, t1.ap, 0.044715, 1.0, ALU.mult, ALU.add)
            S.v("dve", "tensor_tensor", [t1, u], [t1], t1.ap, t1.ap, u.ap, ALU.mult)
            S.act(sg.ap, t1.ap, AF.Sigmoid, [t1], [sg], scale=1.5957691216057308)
            S.v("dve", "tensor_tensor", [u, sg], [hid], hid.ap, u.ap, sg.ap, ALU.mult)
            if kv == "k":
                Pk = self.bank()
                S.mm(Pk.ap[:, 0:127], w2.ap, hid.ap, True, True, [w2, hid], [Pk])
                sq = ar.alloc([127], F32, "csq")
                S.act(sq.ap, Pk.ap[:, 0:127], AF.Square, [Pk], [sq])
                P2 = self.bank()
                S.mm(P2.ap[:, 0:127], bones.ap, sq.ap, True, True, [bones, sq], [P2])
                sr = ar.alloc([127], F32, "csr")
                S.act(sr.ap, P2.ap[:, 0:127], AF.Sqrt, [P2], [sr], bias=64 * EPS, scale=1.0)
                S.v("dve", "reciprocal", [sr], [sr], sr.ap, sr.ap)
                for h in range(2):
                    S.v("dve", "scalar_tensor_tensor", [Pk, kcg, sr], [kc], kc.ap[:, g, h, 0:127], Pk.ap[:, 0:127],
                        kcg.ap[:, h:h + 1], sr.ap, ALU.mult, ALU.mult)
            else:
                Pv = self.bank()
                S.mm(Pv.ap[0:127, 0:64], hid.ap, w2.ap[:, 0:64], True, True, [w2, hid], [Pv])
                S.act(rhsc.ap[:, g, 0:64], Pv.ap[0:127, 0:64], AF.Copy, [Pv], [rhsc])
    if self.dbg:
        dkc = self.dscr("dbg_kc", [128, 4 * 2 * 128], BF16)
        S.dma("sp", dkc.ap, kc.ap.rearrange("p a b c -> p (a b c)"), [kc], [dkc])
        drc = self.dscr("dbg_rhsc", [127, 4 * 97], BF16)
        S.dma("sp", drc.ap, rhsc.ap.rearrange("p a b -> p (a b)"), [rhsc], [drc])
    S.barrier()
    ar.pop()
    if 6 in self.stages:
        self.precast()

    qis = [ar.alloc([4, 2, 2, 128], BF16, f"qa{i}") for i in range(2)]
    for qa_ in qis:
        S.v("dve", "memset", [], [qa_.sub("q")] + [qa_.sub(("m", g_)) for g_ in range(4)], qa_.ap, 0.0)

    pxs = [ar.alloc([512], BF16, f"px{i}") for i in range(6)]
    ssbs = [ar.alloc([512], F32, f"ssb{i}") for i in range(2)]
    oaccs = [ar.alloc([1024], F32, f"oacc{i}") for i in range(2)]
    obst = [ar.alloc([8, 128], BF16, f"obst{i}") for i in range(1)] * 2
    sm = [dict(rl=ar.alloc([12], F32, f"rl{i}"), imp=ar.alloc([32], F32, f"imp{i}"), m8=ar.alloc([8], F32, f"m8{i}"),
               ns=ar.alloc([128], F32, f"ns{i}"), tmp=ar.alloc([256], F32, f"otmp{i}")) for i in range(2)]
    for w_ in sm:
        S.v("dve", "memset", [], [w_["ns"]], w_["ns"].ap, 0.0)
    score_banks = self.ps[0:5]
    NSB = 5
    Poc_b = self.ps[5]
    Pow_b = [self.ps[6], self.ps[6]]
    Pos_b = [self.ps[7], self.ps[7]]
    cnt = {"sb": 0, "px": 0, "ssb": 0}
    jobs = []
    qT_v = self.qT_d.ap.rearrange("(kt p) t -> p kt t", p=128)
    obT_v = self.obT_d.ap.rearrange("(kt p) t -> p kt t", p=128)

    def score_job(qi, g, lhs_lo, lhs_hi, M, extra_mm, bias_ap, pv_fn, deps_k, use_mask=False):
        st = {}

        def qk():
            P = score_banks[cnt["sb"] % NSB]
            cnt["sb"] += 1
            pv4 = P.ap[0:M, :].rearrange("p (a b t) -> p a b t", a=2, b=2)
            qdeps = [qi.sub("q")] + ([qi.sub(("m", g))] if use_mask else [])
            S.mm(pv4[:, :, 0, :], lhs_lo, qi.ap[:, g, 0, :, :], True, False, deps_k + qdeps, [P])
            S.mm(pv4[:, :, 1, :], lhs_hi, qi.ap[:, g, 1, :, :], False, extra_mm is None, deps_k + qdeps, [P])
            if extra_mm is not None:
                lt, rt, dps = extra_mm
                S.mm(P.ap[0:M, :], lt, rt, False, True, dps, [P])
            px = pxs[cnt["px"] % 6]
            cnt["px"] += 1
            if bias_ap is not None:
                sb_ = ssbs[cnt["ssb"] % 2]
                cnt["ssb"] += 1
                S.v("dve", "tensor_tensor", [P, biasT], [sb_], sb_.ap[0:M, :], P.ap[0:M, :], bias_ap, ALU.add)
                S.act(px.ap[0:M, :], sb_.ap[0:M, :], AF.Exp, [sb_], [px])
            else:
                S.act(px.ap[0:M, :], P.ap[0:M, :], AF.Exp, [P], [px])
            st["px"] = px

        def pv():
            pv_fn(st["px"])

        return (qk, pv)

    for i in range(NT):
        qi = qis[i % 2]
        oacc = oaccs[i % 2]

        def load_q(i=i):
            if i < NT:
                qa_ = qis[i % 2]
                for (r0, lh) in ((0, 0), (64, 1)):
                    for g_ in range(4):
                        S.dma("sp", qa_.ap[r0:r0 + 64, g_, lh, :, :],
                              qT_v[r0:r0 + 64, 2 * g_:2 * g_ + 2, i * 128:(i + 1) * 128],
                              [self.qT_d], [qa_.sub("q")])
        if i == 0:
            jobs.append((load_q, None))
        load_next = (lambda i=i: load_q(i + 1))
        for g in range(4):
            it = i * 4 + g
            w = sm[it % 2]
            Pow_, Pos_ = Pow_b[it % 2], Pos_b[it % 2]
            gv = gates.ap[:, i, g * 12:(g + 1) * 12].rearrange("p (h c) -> p h c", c=3)
            osl = oacc.ap[:, g * 256:(g + 1) * 256].rearrange("p (h d) -> p h d", h=4)

            def pv_c(px, g=g, i=i, w=w, gv=gv, osl=osl, qi=qi):
                Poc = Poc_b
                for hs in range(4):
                    S.mm(Poc.ap[:, hs * 97:(hs + 1) * 97], px.ap[0:127, hs * 128:(hs + 1) * 128], rhsc.ap[:, g, :],
                         hs == 0, hs == 3, [px, rhsc], [Poc])
                pc3 = Poc.ap[:, 0:388].rearrange("p (h c) -> p h c", h=4)
                rl = w["rl"]
                S.v("dve", "tensor_scalar", [Poc], [rl], rl.ap[:, 0:4], pc3[:, :, 64], 1e-30, None, ALU.max)
                S.v("dve", "reciprocal", [rl], [rl], rl.ap[:, 0:4], rl.ap[:, 0:4])
                imp = w["imp"]
                S.v("dve", "tensor_scalar", [Poc, rl], [imp], imp.ap, pc3[:, 0, 65:97], rl.ap[:, 0:1], None, ALU.mult)
                for hs in range(1, 4):
                    S.v("dve", "scalar_tensor_tensor", [Poc, rl, imp], [imp], imp.ap, pc3[:, hs, 65:97], rl.ap[:, hs:hs + 1],
                        imp.ap, ALU.mult, ALU.add)
                S.v("dve", "tensor_tensor", [imp, allowed], [imp], imp.ap, imp.ap, allowed.ap[:, i, :], ALU.mult)
                S.v("dve", "tensor_tensor", [imp, addc], [imp], imp.ap, imp.ap, addc.ap[:, i, :], ALU.add)
                m8 = w["m8"]
                S.v("dve", "max", [imp], [m8], out=m8.ap, in_=imp.ap)
                ns = w["ns"]
                S.v("dve", "tensor_scalar", [imp, m8], [ns], ns.ap[:, 0:32], imp.ap, m8.ap[:, 7:8], None, ALU.is_ge)
                S.v("dve", "tensor_scalar", [ns], [ns], ns.ap[:, 64:96], ns.ap[:, 0:32], -1.0, -NEG, ALU.add, ALU.mult)
                S.v("dve", "tensor_scalar", [ns], [ns], ns.ap[:, 0:32], ns.ap[:, 0:32], -1.0, -NEG, ALU.add, ALU.mult)
                Pt = score_banks[cnt["sb"] % NSB]
                cnt["sb"] += 1
                S.tr(Pt.ap[:, 0:128], ns.ap, ident.ap, [ns, ident], [Pt])
                S.act(qi.ap[64:96, g, 0, :, :], Pt.ap[64:96, 0:128].unsqueeze(1).to_broadcast([32, 2, 128]), AF.Copy, [Pt], [qi.sub(("m", g))])
                S.act(qi.ap[0:32, g, 1, :, :], Pt.ap[0:32, 0:128].unsqueeze(1).to_broadcast([32, 2, 128]), AF.Copy, [Pt], [qi.sub(("m", g))])
                S.v("dve", "tensor_tensor", [rl, gates], [rl], rl.ap[:, 0:4], rl.ap[:, 0:4], gv[:, :, 0], ALU.mult)
                S.v("dve", "tensor_tensor", [Poc, rl], [oacc.sub(g)], osl, pc3[:, :, 0:64],
                    rl.ap[:, 0:4].unsqueeze(2).to_broadcast([128, 4, 64]), ALU.mult)
            extra = (Sc.ap[:, i, 0:127], Mst.ap[:, 4 * g:4 * g + 4, :], [Sc, Mst])
            jobs.append(score_job(qi, g, kc.ap[:, g, 0, 0:127], kc.ap[:, g, 1, 0:127], 127, extra, None, pv_c, [kc]))
            if g == 1:
                jobs.append((load_next, None))

            def mk_pv(Pacc, vv, j, first, last, br, g=g, w=w, gv=gv, osl=osl, oacc=oacc):
                def pv(px):
                    for hs in range(4):
                        S.mm(Pacc.ap[:, hs * 65:(hs + 1) * 65], px.ap[:, hs * 128:(hs + 1) * 128], vv.ap[:, j, g, :],
                             first and hs == 0, last and hs == 3, [px, vv], [Pacc])
                    if last:
                        p3 = Pacc.ap[:, 0:260].rearrange("p (h c) -> p h c", h=4)
                        rl = w["rl"]
                        o = 4 * br
                        S.v("dve", "tensor_scalar", [Pacc], [rl], rl.ap[:, o:o + 4], p3[:, :, 64], 1e-30, None, ALU.max)
                        S.v("dve", "reciprocal", [rl], [rl], rl.ap[:, o:o + 4], rl.ap[:, o:o + 4])
                        S.v("dve", "tensor_tensor", [rl, gates], [rl], rl.ap[:, o:o + 4], rl.ap[:, o:o + 4], gv[:, :, br], ALU.mult)
                        tmp = w["tmp"]
                        t3 = tmp.ap.rearrange("p (h d) -> p h d", h=4)
                        S.v("dve", "tensor_tensor", [Pacc, rl], [tmp], t3, p3[:, :, 0:64],
                            rl.ap[:, o:o + 4].unsqueeze(2).to_broadcast([128, 4, 64]), ALU.mult)
                        S.v("dve", "tensor_tensor", [tmp, oacc.sub(g)], [oacc.sub(g)], osl, osl, t3, ALU.add)
                return pv

            js = list(range(max(0, i - 4), i + 1))
            for j in js:
                dd = {0: 0, 1: 1, 4: 2}.get(i - j)
                bias_ap = None if dd is None else biasT.ap[:, dd, 4 * g:4 * g + 4, :].rearrange("p h t -> p (h t)")
                jobs.append(score_job(qi, g, kw.ap[:, g, 0, j * 128:(j + 1) * 128], kw.ap[:, g, 1, j * 128:(j + 1) * 128],
                                      128, None, bias_ap, mk_pv(Pow_, vw, j, j == js[0], j == js[-1], 2), [kw]))
            for j in range(i + 1):
                dd = {0: 0, 1: 1}.get(i - j)
                bias_ap = None if dd is None else biasT.ap[:, dd, 4 * g:4 * g + 4, :].rearrange("p h t -> p (h t)")
                jobs.append(score_job(qi, g, ks.ap[:, g, 0, j * 128:(j + 1) * 128], ks.ap[:, g, 1, j * 128:(j + 1) * 128],
                                      128, None, bias_ap, mk_pv(Pos_, vs, j, j == 0, j == i, 1), [ks], use_mask=True))

        def finish(i=i, oacc=oacc):
            ob = obst[i % 2]
            for half in range(2):
                P = score_banks[cnt["sb"] % NSB]
                cnt["sb"] += 1
                for q in range(4):
                    kt = half * 4 + q
                    S.tr(P.ap[:, q * 128:(q + 1) * 128], oacc.ap[:, kt * 128:(kt + 1) * 128], ident.ap,
                         [oacc.sub(kt // 2), ident], [P])
                S.act(ob.ap[:, half * 4:(half + 1) * 4, :], P.ap.rearrange("p (q t) -> p q t", q=4), AF.Copy, [P], [ob])
            S.dma("sp", obT_v[:, :, i * 128:(i + 1) * 128], ob.ap, [ob], [self.obT_d])
        jobs.append((None, finish))

    pend = []
    for (qk, pv) in jobs:
        if len(pend) >= 3:
            f = pend.pop(0)
            if f is not None:
                f()
        if qk is not None:
            qk()
        pend.append(pv)
    for f in pend:
        if f is not None:
            f()
    S.barrier()
    ar.pop()


KB.stage4 = stage4


def nsa_bias_build(self):
    ar, S = self.ar, self.S
    d = self.din
    NOH = 3 * 16384 + 17 * 128
    oh_d = d("nsa_oh", [33, NOH])
    relb_d = d("rel_bias", [32, 16])
    self.bias_d = bias_d = self.dscr("bias_scr", [16, NOH])
    trel = ar.alloc([16], F32, "trel", parts=33)
    tbl = ar.alloc([16], F32, "tbl", parts=32)
    t31 = ar.alloc([16], F32, "t31", parts=32)
    S.dma("sp", tbl.ap, relb_d.ap, [relb_d], [tbl])
    S.dma("sp", t31.ap, relb_d.ap[31:32, :].partition_broadcast(32), [relb_d], [t31])
    S.v("dve", "memset", [], [trel], trel.ap, NEG)
    S.v("dve", "tensor_tensor", [tbl, t31, trel], [trel], trel.ap[0:32, :], tbl.ap, t31.ap, ALU.subtract)
    ohb = [ar.alloc([2048], F32, f"ohb{i}", parts=33) for i in range(2)]
    bsb = [ar.alloc([2048], F32, f"bsb{i}", parts=16) for i in range(2)]
    nblk = (NOH + 2047) // 2048

    def mk(bi):
        def step():
            c0 = bi * 2048
            n = min(2048, NOH - c0)
            ob, bs = ohb[bi % 2], bsb[bi % 2]
            S.dma("sp", ob.ap[:, 0:n], oh_d.ap[:, c0:c0 + n], [oh_d], [ob])
            for q in range((n + 511) // 512):
                w = min(512, n - q * 512)
                P = self.bank()
                S.mm(P.ap[0:16, 0:w], trel.ap, ob.ap[:, q * 512:q * 512 + w], True, True, [trel, ob], [P])
                S.act(bs.ap[:, q * 512:q * 512 + w], P.ap[0:16, 0:w], AF.Copy, [P], [bs])
            S.dma("sp", bias_d.ap[:, c0:c0 + n], bs.ap[:, 0:n], [bs], [bias_d])
        return step
    return [mk(bi) for bi in range(nblk)]


KB.nsa_bias_build = nsa_bias_build


LAM = 0.6065306597126334


def rwkv_consts():
    c = {}
    p = np.arange(128)
    ut_strict = (p[:, None] < p[None, :]).astype(np.float32)
    ut_incl = (p[:, None] <= p[None, :]).astype(np.float32)
    c["rw_mAB"] = np.ascontiguousarray(np.concatenate([ut_strict, ut_incl], 1))
    c["rw_mLT"] = (p[:, None] > p[None, :]).astype(np.float32)
    rs = np.ones((128, 8, 128), np.float32)
    rs[:, :, 0] = 0.0
    c["rw_reset"] = rs.reshape(128, 1024)
    hm = np.zeros((128, 2), np.float32)
    hm[:64, 0] = 1.0
    hm[64:, 1] = 1.0
    c["rw_hsel"] = hm
    return c


def rwkv_prep(inp, m):
    g = lambda k: inp[k][0]
    m["rw_w0"] = _fm(g("rwkv_w0"))
    m["rw_a0"] = _fm(g("rwkv_a0"))
    m["rw_kk"] = _fm(g("rwkv_k_k"))
    m["rw_ka"] = _fm(g("rwkv_k_a"))
    m["rw_rk"] = _fm(g("rwkv_r_k").reshape(-1))
    m["rw_lnw"] = np.ascontiguousarray(np.broadcast_to(g("rwkv_ln_w")[None, :], (128, 1024)).astype(np.float32))
    m["rw_lnb"] = np.ascontiguousarray(np.broadcast_to(g("rwkv_ln_b")[None, :], (128, 1024)).astype(np.float32))
    z = np.zeros((64, 1024), np.float32)
    m["rw_w2pad"] = np.ascontiguousarray(np.concatenate([g("rwkv_w2"), z], 0))
    m["rw_a2pad"] = np.ascontiguousarray(np.concatenate([z, g("rwkv_a2")], 0))
    m["rw_g2"] = g("rwkv_g2")


def stage3(self):
    ar, S = self.ar, self.S
    d = self.din
    ident, bones = self.ident, self.bones
    self.oaT_d = self.dscr("oaT", [1024, SEQ], BF16)
    ar.push()

    def cload(name, shape, parts=128, src=None):
        dt_ = d(name, [parts] + list(shape)) if src is None else src
        t = ar.alloc(shape, F32, name, parts=parts)
        S.dma("sp", t.ap, dt_.ap, [dt_], [t])
        return t
    w0 = cload("rw_w0", [8])
    a0 = cload("rw_a0", [8])
    kkf = cload("rw_kk", [8])
    kaf = cload("rw_ka", [8])
    rkf = cload("rw_rk", [8])
    lnw = cload("rw_lnw", [1024])
    lnb = cload("rw_lnb", [1024])
    w2p = cload("rw_w2pad", [1024])
    a2p = cload("rw_a2pad", [1024])
    g2_d = d("rw_g2", [160, 1024])
    g2a = ar.alloc([1024], F32, "g2a")
    g2b = ar.alloc([1024], F32, "g2b", parts=32)
    S.dma("sp", g2a.ap, g2_d.ap[0:128, :], [g2_d], [g2a])
    S.dma("sp", g2b.ap, g2_d.ap[128:160, :], [g2_d], [g2b])
    mAB = cload("rw_mAB", [256])
    mLT = cload("rw_mLT", [128])
    reset = cload("rw_reset", [1024])
    hsel = cload("rw_hsel", [2])
    omka = ar.alloc([8], F32, "omka")
    S.v("dve", "tensor_scalar", [kaf], [omka], omka.ap, kaf.ap, -1.0, 1.0, ALU.mult, ALU.add)
    Hp = ar.alloc([16, 64], F32, "Hp")
    S.v("dve", "memset", [], [Hp], Hp.ap, 0.0)

    A = lambda n, shape=(8, 128): ar.alloc(list(shape), F32, n)
    raw = A("raw", (27, 128))
    sgw, cs, E, Eex = A("sgw"), A("cs"), A("E"), A("Eex")
    aT, kkn, kp, bb, btl = A("aT"), A("kkn"), A("kp"), A("bb"), A("btl")
    AR = A("AR", (8, 2, 128))
    blo, bhi, klo, khi, alo, ahi = A("blo"), A("bhi"), A("klo"), A("khi"), A("alo"), A("ahi")
    tmpA, tmpB = A("tmpA"), A("tmpB")
    bh, kh = sgw, cs
    v_tok, bh_tok, kh_tok, g_tok = A("v_tok", (1024,)), A("bh_tok", (1024,)), A("kh_tok", (1024,)), A("g_tok", (1024,))
    th = A("th", (128,))
    sx = A("sx", (128,))
    sx2 = ar.alloc([128], F32, "sx2", parts=32)
    PLs = [A("PL0", (8,)), A("PL1", (8,))]
    nb = A("nb", (8,))
    rk16 = A("rk16", (16,))
    st16 = [A(f"st16_{i}", (16,)) for i in range(4)]
    import os
    SQDT = BF16 if os.environ.get("RW_SQ") == "bf16" else F32
    MASK_POOL = os.environ.get("RW_MASK") == "pool"
    NO_IL = os.environ.get("RW_IL") == "0"
    slots = []
    for s_ in range(2):
        slots.append(dict(
            ABm=A(f"ABm{s_}", (4, 256)), AKm=A(f"AKm{s_}", (4, 256)),
            Yf=[A(f"Yf{s_}{i}", (4, 128)) for i in range(2)] if SQDT != F32 else None,
            Yb=[ar.alloc([4, 128], SQDT, f"Yb{s_}{i}") for i in range(2)],
            XW=[A(f"XW{s_}{i}", (4, 192)) for i in range(2)]))
        if SQDT == F32:
            slots[-1]["Yf"] = slots[-1]["Yb"]
    if SQDT == F32:
        ar.off -= 0
    oast = [ar.alloc([8, 128], BF16, f"oast{i}") for i in range(1)] * 2
    f2 = lambda t: t.ap.rearrange("p a b -> p (a b)")
    rw_v = self.rwT_d.ap.rearrange("(kt p) t -> p kt t", p=128)
    oaT_v = self.oaT_d.ap.rearrange("(kt p) t -> p kt t", p=128)
    dv = lambda meth, reads, writes, *a, **k: S.v("dve", meth, reads, writes, *a, **k)
    pl = lambda meth, reads, writes, *a, **k: S.v("pool", meth, reads, writes, *a, **k)
    bc8 = lambda t: t.ap.unsqueeze(2).to_broadcast([128, 8, 128])

    def P1(c):
            S.dma("sp", raw.ap, rw_v[:, :, c * 128:(c + 1) * 128], [self.rwT_d], [raw])
            yield
            rT, kT, vT = raw.ap[:, 0:8, :], raw.ap[:, 8:16, :], raw.ap[:, 16:24, :]
            yield
            t24 = raw.ap[:, 24, :]
            yield
            S.act(th.ap, t24, AF.Tanh, [raw], [th])
            yield
            Pz = [self.bank(), self.bank()]
            yield
            for kt in range(8):
                P = Pz[kt // 4]
                S.mm(P.ap[:, (kt % 4) * 128:(kt % 4 + 1) * 128], w2p.ap[:, kt * 128:(kt + 1) * 128], th.ap, True, True, [w2p, th], [P])
            yield
            for kt in range(8):
                S.act(sgw.ap[:, kt, :], Pz[kt // 4].ap[:, (kt % 4) * 128:(kt % 4 + 1) * 128], AF.Sigmoid, [Pz[kt // 4], w0], [sgw],
                      bias=w0.ap[:, kt:kt + 1], scale=1.0)
            yield
            Pa = [self.bank(), self.bank()]
            yield
            for kt in range(8):
                P = Pa[kt // 4]
                S.mm(P.ap[:, (kt % 4) * 128:(kt % 4 + 1) * 128], a2p.ap[:, kt * 128:(kt + 1) * 128], t24, True, True, [a2p, raw], [P])
            yield
            for kt in range(8):
                S.act(aT.ap[:, kt, :], Pa[kt // 4].ap[:, (kt % 4) * 128:(kt % 4 + 1) * 128], AF.Sigmoid, [Pa[kt // 4], a0], [aT],
                      bias=a0.ap[:, kt:kt + 1], scale=1.0)
            yield
            dv("tensor_tensor_scan", [reset, sgw], [cs], f2(cs), reset.ap, f2(sgw), 0.0, ALU.mult, ALU.add)
            yield
            S.act(f2(E), f2(cs), AF.Exp, [cs], [E], scale=-LAM)
            yield
            dv("tensor_tensor", [cs, sgw], [Eex], f2(Eex), f2(cs), f2(sgw), ALU.subtract)
            yield
            S.act(f2(Eex), f2(Eex), AF.Exp, [Eex], [Eex], scale=-LAM)
            yield
            dv("tensor_scalar", [cs], [nb], nb.ap, cs.ap[:, :, 127], -LAM, None, ALU.mult)
            yield
            S.act(PLs[c % 2].ap, nb.ap, AF.Exp, [nb], [PLs[c % 2]])
            yield
            dv("tensor_tensor", [raw, kkf], [kkn], kkn.ap, kT, bc8(kkf), ALU.mult)
            yield
            dv("tensor_tensor", [kkn], [tmpB], tmpB.ap, kkn.ap, kkn.ap, ALU.mult)
            yield
            Pn = [self.bank(), self.bank()]
            yield
            for hh in range(2):
                S.mm(Pn[hh].ap, bones.ap, tmpB.ap[:, hh * 4:(hh + 1) * 4, :], True, True, [bones, tmpB], [Pn[hh]])
            yield
            for hh in range(2):
                S.act(tmpB.ap[:, hh * 4:(hh + 1) * 4, :], Pn[hh].ap.rearrange("p (a b) -> p a b", a=4), AF.Sqrt, [Pn[hh]], [tmpB])
            yield
            dv("tensor_scalar", [tmpB], [tmpB], f2(tmpB), f2(tmpB), 1e-12, None, ALU.max)
            yield
            dv("reciprocal", [tmpB], [tmpB], f2(tmpB), f2(tmpB))
            yield
            dv("tensor_tensor", [kkn, tmpB], [kkn], f2(kkn), f2(kkn), f2(tmpB), ALU.mult)
            yield
            dv("tensor_tensor", [aT, kaf], [kp], kp.ap, aT.ap, bc8(kaf), ALU.mult)
            yield
            dv("tensor_tensor", [kp, omka], [kp], kp.ap, kp.ap, bc8(omka), ALU.add)
            yield
            dv("tensor_tensor", [kp, raw], [kp], kp.ap, kp.ap, kT, ALU.mult)
            yield
            dv("tensor_tensor", [kkn, aT], [bb], f2(bb), f2(kkn), f2(aT), ALU.mult)
            yield


    def P2(c):
            rT, kT, vT = raw.ap[:, 0:8, :], raw.ap[:, 8:16, :], raw.ap[:, 16:24, :]
            dv("tensor_tensor", [raw, E], [AR], AR.ap[:, :, 1, :], rT, E.ap, ALU.mult)
            dv("scalar_tensor_tensor", [kkn, Eex], [AR], AR.ap[:, :, 0, :], kkn.ap, -1.0, Eex.ap, ALU.mult, ALU.mult)
            Einv, Elast = E, Eex
            S.act(f2(Einv), f2(cs), AF.Exp, [cs], [Einv], scale=LAM)
            for kt in range(8):
                S.act(Elast.ap[:, kt, :], cs.ap[:, kt, :], AF.Exp, [cs, nb], [Elast], bias=nb.ap[:, kt:kt + 1], scale=LAM)
            dv("tensor_tensor", [bb, Einv], [btl], f2(btl), f2(bb), f2(Einv), ALU.mult)
            dv("tensor_tensor", [kp, Einv], [tmpA], f2(tmpA), f2(kp), f2(Einv), ALU.mult)
            dv("tensor_tensor", [bb, Elast], [bh], f2(bh), f2(bb), f2(Elast), ALU.mult)
            dv("tensor_tensor", [kp, Elast], [kh], f2(kh), f2(kp), f2(Elast), ALU.mult)
            for (dst, src, col) in ((blo, btl, 0), (bhi, btl, 1), (klo, tmpA, 0), (khi, tmpA, 1)):
                if MASK_POOL:
                    pl("tensor_scalar", [src, hsel], [dst], f2(dst), f2(src), hsel.ap[:, col:col + 1], None, ALU.mult)
                    continue
                S.act(f2(dst), f2(src), AF.Identity, [src, hsel], [dst], scale=hsel.ap[:, col:col + 1], bias=0.0)
            for (dst, col) in ((alo, 0), (ahi, 1)):
                if MASK_POOL:
                    pl("tensor_scalar", [AR, hsel], [dst], dst.ap, AR.ap[:, :, 0, :], hsel.ap[:, col:col + 1], None, ALU.mult)
                    continue
                S.act(dst.ap, AR.ap[:, :, 0, :], AF.Identity, [AR, hsel], [dst], scale=hsel.ap[:, col:col + 1], bias=0.0)
            dv("tensor_tensor", [raw, kp], [tmpB], tmpB.ap, rT, kp.ap, ALU.mult)
            dv("tensor_tensor", [tmpB, rkf], [tmpB], tmpB.ap, tmpB.ap, bc8(rkf), ALU.mult)
            Pr = self.bank()
            for kt in range(8):
                S.mm(Pr.ap[:, 2 * kt:2 * kt + 2], tmpB.ap[:, kt, :], hsel.ap, kt == 0, kt == 7, [tmpB, hsel], [Pr])
            S.act(rk16.ap, Pr.ap[:, 0:16], AF.Copy, [Pr], [rk16])
            S.act(sx.ap, raw.ap[:, 25, :], AF.Sigmoid, [raw], [sx])
            S.act(sx2.ap, raw.ap[0:32, 26, :], AF.Sigmoid, [raw], [sx2])
            for hh in range(2):
                P = self.bank()
                S.mm(P.ap, sx.ap, g2a.ap[:, hh * 512:(hh + 1) * 512], True, False, [sx, g2a], [P])
                S.mm(P.ap, sx2.ap, g2b.ap[:, hh * 512:(hh + 1) * 512], False, True, [sx2, g2b], [P])
                S.act(g_tok.ap[:, hh * 512:(hh + 1) * 512], P.ap, AF.Copy, [P], [g_tok])
            for (src_ap, src_t, dst) in ((vT, raw, v_tok), (bh.ap, bh, bh_tok), (kh.ap, kh, kh_tok)):
                for hh in range(2):
                    P = self.bank()
                    for q in range(4):
                        kt = hh * 4 + q
                        S.tr(P.ap[:, q * 128:(q + 1) * 128], src_ap[:, kt, :], ident.ap, [src_t, ident], [P])
                    S.act(dst.ap[:, hh * 512:(hh + 1) * 512], P.ap, AF.Copy, [P], [dst])


    def heads(c, step):
            y_tok = tmpA
            def phaseA(hg, sl):
                heads = [4 * hg + x for x in range(4)]
                ABm, AKm = sl["ABm"], sl["AKm"]
                PA = [self.bank(), self.bank()]
                PB = [self.bank(), self.bank()]
                PX = self.bank()
                for hl, h in enumerate(heads):
                    kt, half = h // 2, h % 2
                    bsel = (blo, bhi)[half]
                    ksel = (klo, khi)[half]
                    asel = (alo, ahi)[half]
                    ar_rhs = AR.ap[:, kt, :, :]
                    oa = PA[hl // 2].ap[:, (hl % 2) * 256:(hl % 2 + 1) * 256]
                    ob_ = PB[hl // 2].ap[:, (hl % 2) * 256:(hl % 2 + 1) * 256]
                    S.mm(oa, bsel.ap[:, kt, :], ar_rhs, hl % 2 == 0, hl % 2 == 1, [bsel, AR], [PA[hl // 2]])
                    S.mm(ob_, ksel.ap[:, kt, :], ar_rhs, hl % 2 == 0, hl % 2 == 1, [ksel, AR], [PB[hl // 2]])
                    S.mm(PX.ap[:, hl * 128:(hl + 1) * 128], asel.ap[:, kt, :], btl.ap[:, kt, :], hl == 0, hl == 3, [asel, btl], [PX])
                mAB2 = mAB.ap.unsqueeze(1).to_broadcast([128, 2, 256])
                for q in range(2):
                    dv("tensor_tensor", [PA[q], mAB], [ABm], ABm.ap[:, 2 * q:2 * q + 2, :],
                       PA[q].ap.rearrange("p (a b) -> p a b", a=2), mAB2, ALU.mult)
                    dv("tensor_tensor", [PB[q], mAB], [AKm], AKm.ap[:, 2 * q:2 * q + 2, :],
                       PB[q].ap.rearrange("p (a b) -> p a b", a=2), mAB2, ALU.mult)
                dv("tensor_tensor", [PX, mLT], [sl["XW"][0]], sl["XW"][0].ap[:, :, 0:128], PX.ap.rearrange("p (a b) -> p a b", a=4),
                   mLT.ap.unsqueeze(1).to_broadcast([128, 4, 128]), ALU.mult)
                S.act(sl["Yb"][0].ap, ABm.ap[:, :, 0:128], AF.Copy, [ABm], [sl["Yb"][0]])
                PW = self.bank()
                for hl, h in enumerate(heads):
                    kt = h // 2
                    o = PW.ap[:, hl * 64:(hl + 1) * 64]
                    S.mm(o, AR.ap[:, kt, 0, :], Hp.ap[:, h, :], hl == 0, False, [AR, Hp.sub(h)], [PW])
                    S.mm(o, AKm.ap[:, hl, 0:128], v_tok.ap[:, h * 64:(h + 1) * 64], False, hl == 3, [AKm, v_tok], [PW])
                S.act(sl["XW"][0].ap[:, :, 128:192], PW.ap[:, 0:256].rearrange("p (h d) -> p h d", h=4), AF.Copy, [PW], [sl["XW"][0]])

            def level(sl, lv):
                Y, XW = sl["Yb"][lv % 2], sl["XW"][lv % 2]
                Yn, XWn = sl["Yb"][(lv + 1) % 2], sl["XW"][(lv + 1) % 2]
                if lv < 6:
                    PUX = [self.bank(), self.bank()]
                    for hl in range(4):
                        o = PUX[hl // 2].ap[:, (hl % 2) * 192:(hl % 2 + 1) * 192]
                        S.mm(o, Y.ap[:, hl, :], XW.ap[:, hl, :], hl % 2 == 0, hl % 2 == 1, [Y, XW], [PUX[hl // 2]])
                    PY2 = self.bank()
                    for hl in range(4):
                        S.mm(PY2.ap[:, hl * 128:(hl + 1) * 128], XW.ap[:, hl, 0:128], Y.ap[:, hl, :], hl == 0, hl == 3, [XW, Y], [PY2])
                    for q in range(2):
                        view = PUX[q].ap[:, 0:384].rearrange("p (a c) -> p a c", a=2)
                        dv("tensor_tensor", [PUX[q], XW], [XWn], XWn.ap[:, 2 * q:2 * q + 2, 128:192], view[:, :, 128:192],
                           XW.ap[:, 2 * q:2 * q + 2, 128:192], ALU.add)
                        S.act(XWn.ap[:, 2 * q:2 * q + 2, 0:128], view[:, :, 0:128], AF.Copy, [PUX[q]], [XWn])
                    dv("tensor_copy", [PY2], [Yn], f2(Yn), PY2.ap)
                else:
                    PU = self.bank()
                    for hl in range(4):
                        S.mm(PU.ap[:, hl * 64:(hl + 1) * 64], Y.ap[:, hl, :], XW.ap[:, hl, 128:192], hl == 0, hl == 3, [Y, XW], [PU])
                    dv("tensor_tensor", [PU, XW], [XWn], XWn.ap[:, :, 128:192], PU.ap[:, 0:256].rearrange("p (h d) -> p h d", h=4),
                       XW.ap[:, :, 128:192], ALU.add)

            def phaseY(hg, sl):
                heads = [4 * hg + x for x in range(4)]
                ABm, AKm = sl["ABm"], sl["AKm"]
                U = sl["XW"][1]
                PY = self.bank()
                for hl, h in enumerate(heads):
                    kt = h // 2
                    o = PY.ap[:, hl * 64:(hl + 1) * 64]
                    S.mm(o, AR.ap[:, kt, 1, :], Hp.ap[:, h, :], hl == 0, False, [AR, Hp.sub(h)], [PY])
                    S.mm(o, ABm.ap[:, hl, 128:256], U.ap[:, hl, 128:192], False, False, [ABm, U], [PY])
                    S.mm(o, AKm.ap[:, hl, 128:256], v_tok.ap[:, h * 64:(h + 1) * 64], False, hl == 3, [AKm, v_tok], [PY])
                PH = self.bank()
                for hl, h in enumerate(heads):
                    kt = h // 2
                    o = PH.ap[:, hl * 64:(hl + 1) * 64]
                    S.mm(o, bh_tok.ap[:, kt * 128:(kt + 1) * 128], U.ap[:, hl, 128:192], hl == 0, False, [bh_tok, U], [PH])
                    S.mm(o, kh_tok.ap[:, kt * 128:(kt + 1) * 128], v_tok.ap[:, h * 64:(h + 1) * 64], False, hl == 3, [kh_tok, v_tok], [PH])
                S.act(y_tok.ap.rearrange("p a b -> p (a b)")[:, hg * 256:(hg + 1) * 256], PY.ap[:, 0:256], AF.Copy, [PY], [y_tok])
                for hl, h in enumerate(heads):
                    kt, half = h // 2, h % 2
                    r0 = 64 * half
                    dv("scalar_tensor_tensor", [Hp.sub(h), PLs[c % 2], PH], [Hp.sub(h)], Hp.ap[r0:r0 + 64, h, :], Hp.ap[r0:r0 + 64, h, :],
                       PLs[c % 2].ap[r0:r0 + 64, kt:kt + 1], PH.ap[r0:r0 + 64, hl * 64:(hl + 1) * 64], ALU.mult, ALU.add)

            for pair in range(2):
                gA, gB = 2 * pair, 2 * pair + 1
                if NO_IL:
                    for (g_, sl_) in ((gA, slots[0]), (gB, slots[1])):
                        phaseA(g_, sl_)
                        for lv in range(7):
                            level(sl_, lv)
                        phaseY(g_, sl_)
                    continue
                phaseA(gA, slots[0])
                phaseA(gB, slots[1])
                for lv in range(7):
                    level(slots[0], lv)
                    step(); step()
                    level(slots[1], lv)
                    step(); step()
                phaseY(gA, slots[0])
                phaseY(gB, slots[1])


    def post(c):
            y_tok = tmpA
            yf = y_tok.ap.rearrange("p a b -> p (a b)")
            y3 = yf.rearrange("p (h d) -> p h d", h=16)
            sum_, sq_, mean, rstd = st16
            t1, t2 = y_tok, btl
            t1f, t2f = f2(t1), f2(t2)
            dv("tensor_reduce", [y_tok], [sum_], sum_.ap, y3, AX.X, ALU.add)
            dv("tensor_tensor", [y_tok], [t2], t2f, yf, yf, ALU.mult)
            dv("tensor_reduce", [t2], [sq_], sq_.ap, t2f.rearrange("p (h d) -> p h d", h=16), AX.X, ALU.add)
            dv("tensor_scalar", [sum_], [mean], mean.ap, sum_.ap, 1.0 / 64, None, ALU.mult)
            dv("tensor_tensor", [mean], [rstd], rstd.ap, mean.ap, mean.ap, ALU.mult)
            dv("scalar_tensor_tensor", [sq_, rstd], [rstd], rstd.ap, sq_.ap, 1.0 / 64, rstd.ap, ALU.mult, ALU.subtract)
            S.act(rstd.ap, rstd.ap, AF.Sqrt, [rstd], [rstd], bias=GN_EPS, scale=1.0)
            dv("reciprocal", [rstd], [rstd], rstd.ap, rstd.ap)
            b16 = lambda t: t.ap.unsqueeze(2).to_broadcast([128, 16, 64])
            t13 = t1f.rearrange("p (h d) -> p h d", h=16)
            t23 = t2f.rearrange("p (h d) -> p h d", h=16)
            dv("tensor_tensor", [y_tok, mean], [t1], t13, y3, b16(mean), ALU.subtract)
            dv("tensor_tensor", [t1, rstd], [t1], t13, t13, b16(rstd), ALU.mult)
            dv("tensor_tensor", [t1, lnw], [t1], t1f, t1f, lnw.ap, ALU.mult)
            dv("tensor_tensor", [t1, lnb], [t1], t1f, t1f, lnb.ap, ALU.add)
            dv("tensor_tensor", [v_tok, rk16], [t2], t23, v_tok.ap.rearrange("p (h d) -> p h d", h=16), b16(rk16), ALU.mult)
            dv("tensor_tensor", [t1, t2], [t1], t1f, t1f, t2f, ALU.add)
            dv("tensor_tensor", [t1, g_tok], [t1], t1f, t1f, g_tok.ap, ALU.mult)
            if self.dbg and c == 0:
                dd = self.dscr("dbg_oa0", [128, 1024])
                S.dma("sp", dd.ap, t1f, [t1], [dd])
                dd2 = self.dscr("dbg_y0", [128, 1024])
                S.dma("sp", dd2.ap, yf, [y_tok], [dd2])
            ob = oast[c % 2]
            for hh in range(2):
                P = self.bank()
                for q in range(4):
                    kt = hh * 4 + q
                    S.tr(P.ap[:, q * 128:(q + 1) * 128], t1f[:, kt * 128:(kt + 1) * 128], ident.ap, [t1, ident], [P])
                S.act(ob.ap[:, hh * 4:(hh + 1) * 4, :], P.ap.rearrange("p (q t) -> p q t", q=4), AF.Copy, [P], [ob])
            S.dma("sp", oaT_v[:, :, c * 128:(c + 1) * 128], ob.ap, [ob], [self.oaT_d])


    gen = P1(0)
    for _ in gen:
        pass
    for c in range(NT):
        P2(c)
        gen = P1(c + 1) if c + 1 < NT else iter(())

        def step(gen=gen):
            next(gen, None)
        heads(c, step)
        for _ in gen:
            pass
        post(c)
    S.barrier()
    ar.pop()


KB.stage3 = stage3


def stage5(self):
    ar, S = self.ar, self.S
    d = self.din
    wor_d = d("w_o_rwkv", [1024, D])
    won_d = d("w_o_nsa", [1024, D])
    wout_d = d("w_out", [D, D])
    ar.push()
    mixT = ar.alloc([16, SEQ], BF16, "mixT")
    ar.push()
    oaT = ar.alloc([8, SEQ], BF16, "oaT")
    obT = ar.alloc([8, SEQ], BF16, "obT")
    S.dma("sp", oaT.ap, self.oaT_d.ap.rearrange("(k p) t -> p k t", p=128), [self.oaT_d], [oaT])
    S.dma("sp", obT.ap, self.obT_d.ap.rearrange("(k p) t -> p k t", p=128), [self.obT_d], [obT])
    woa = [ar.alloc([8, 128], BF16, f"woa{i}") for i in range(2)]
    wob = [ar.alloc([8, 128], BF16, f"wob{i}") for i in range(2)]
    sga = [ar.alloc([SEQ], BF16, f"sga{i}") for i in range(2)]
    sgb = [ar.alloc([SEQ], BF16, f"sgb{i}") for i in range(2)]
    t1s = [ar.alloc([512], F32, f"t1_{i}") for i in range(2)]
    t2s = [ar.alloc([512], F32, f"t2_{i}") for i in range(2)]
    wor_v = wor_d.ap.rearrange("(k p) c -> p k c", p=128)
    won_v = won_d.ap.rearrange("(k p) c -> p k c", p=128)
    cnt = 0
    mod_steps = []
    for jt in range(16):
        wa, wb, sa, sb = woa[jt % 2], wob[jt % 2], sga[jt % 2], sgb[jt % 2]
        S.dma("pool", wa.ap, wor_v[:, :, jt * 128:(jt + 1) * 128], [wor_d], [wa])
        S.dma("pool", wb.ap, won_v[:, :, jt * 128:(jt + 1) * 128], [won_d], [wb])
        S.dma("sp", sa.ap, self.mgT_d.ap[jt * 128:(jt + 1) * 128, :], [self.mgT_d], [sa])
        S.dma("sp", sb.ap, self.mgT_d.ap[2048 + jt * 128:2048 + (jt + 1) * 128, :], [self.mgT_d], [sb])
        if jt >= 1:
            for _ in range(2):
                if mod_steps:
                    mod_steps.pop(0)()
        for n in range(4):
            Pa = self.bank()
            for k in range(8):
                S.mm(Pa.ap, wa.ap[:, k, :], oaT.ap[:, k, n * 512:(n + 1) * 512], k == 0, k == 7, [wa, oaT], [Pa])
            Pb = self.bank()
            for k in range(8):
                S.mm(Pb.ap, wb.ap[:, k, :], obT.ap[:, k, n * 512:(n + 1) * 512], k == 0, k == 7, [wb, obT], [Pb])
            t1, t2 = t1s[cnt % 2], t2s[cnt % 2]
            cnt += 1
            S.v("dve", "tensor_tensor", [Pa, sa], [t1], t1.ap, Pa.ap, sa.ap[:, n * 512:(n + 1) * 512], ALU.mult)
            S.v("dve", "tensor_tensor", [Pb, sb], [t2], t2.ap, Pb.ap, sb.ap[:, n * 512:(n + 1) * 512], ALU.mult)
            S.v("dve", "tensor_tensor", [t1, t2], [mixT.sub(n)], mixT.ap[:, jt, n * 512:(n + 1) * 512], t1.ap, t2.ap, ALU.add)
    while mod_steps:
        mod_steps.pop(0)()
    S.barrier()
    ar.pop()
    wout = ar.alloc([16, D], BF16, "wout")
    wout_v = wout_d.ap.rearrange("(k p) c -> p k c", p=128)
    for nn in range(4):
        S.dma("pool", wout.ap[:, :, nn * 512:(nn + 1) * 512], wout_v[:, :, nn * 512:(nn + 1) * 512], [wout_d], [wout.sub(nn)])
    xbs = [ar.alloc([D], F32, f"x5_{i}") for i in range(2)]
    obs = [ar.alloc([D], F32, f"o5_{i}") for i in range(2)]
    tms = [ar.alloc([512], F32, f"tm5_{i}") for i in range(2)]
    cnt = 0
    for tt in range(NT):
        xb, ob = xbs[tt % 2], obs[tt % 2]
        S.dma("sp", xb.ap, self.x_d.ap[tt * 128:(tt + 1) * 128, :], [self.x_d], [xb])
        for nn in range(4):
            P = self.bank()
            for k in range(16):
                S.mm(P.ap, mixT.ap[:, k, tt * 128:(tt + 1) * 128], wout.ap[:, k, nn * 512:(nn + 1) * 512], k == 0, k == 15,
                     [mixT.sub(tt // 4), wout.sub(nn)], [P])
            tm = tms[cnt % 2]
            cnt += 1
            S.v("dve", "tensor_tensor", [P, self.gt1], [tm], tm.ap, P.ap, self.gt1.ap[:, nn * 512:(nn + 1) * 512], ALU.mult)
            S.v("dve", "tensor_tensor", [tm, xb], [ob], ob.ap[:, nn * 512:(nn + 1) * 512], tm.ap, xb.ap[:, nn * 512:(nn + 1) * 512], ALU.add)
        S.dma("pool", self.x1_d.ap[tt * 128:(tt + 1) * 128, :], ob.ap, [ob], [self.x1_d])
    S.barrier()
    ar.pop()


def stage6(self):
    ar, S = self.ar, self.S
    d = self.din
    hT = self.hT
    ar.push()
    uT = ar.alloc([64, 512], BF16, "uT")
    wups = [ar.alloc([16, 256], BF16, f"wup{i}") for i in range(2)]
    wdns = [ar.alloc([8, 512], BF16, f"wdn{i}") for i in range(3)]
    rts = [ar.alloc([512], F32, f"rt{i}") for i in range(2)]
    xps = [ar.alloc([512], F32, f"xp{i}") for i in range(2)]
    ops_ = [ar.alloc([512], F32, f"op{i}") for i in range(2)]
    tms = [ar.alloc([512], F32, f"tm6_{i}") for i in range(2)]
    c_up = c_dn = c_e = 0
    for c in range(4):
        for fb in range(32):
            wu = wups[c_up % 2]
            c_up += 1
            S.dma("sp", wu.ap, self.wupb.ap[fb].rearrange("p (k c) -> p k c", k=16), [self.wupb], [wu])
            for ft in range(2):
                f = fb * 2 + ft
                P = self.ps[(c_e) % 4]
                rt = rts[c_e % 2]
                c_e += 1
                for k in range(16):
                    S.mm(P.ap, wu.ap[:, k, ft * 128:(ft + 1) * 128], hT.ap[:, k, c * 512:(c + 1) * 512], k == 0, k == 15,
                         [wu, hT.sub(c)], [P])
                S.act(rt.ap, P.ap, AF.Relu, [P], [rt])
                S.v("dve", "tensor_tensor", [rt], [uT.sub(f)], uT.ap[:, f, :], rt.ap, rt.ap, ALU.mult)
        for nn in range(4):
            accs = self.ps[4:8] if (nn % 2 == 0) else self.ps[0:4]
            for f8 in range(8):
                wd = wdns[c_dn % 3]
                c_dn += 1
                S.dma("sp", wd.ap, self.wdnb.ap[nn, f8].rearrange("p (f c) -> p f c", f=8), [self.wdnb], [wd])
                for fi in range(8):
                    f = f8 * 8 + fi
                    for tt in range(4):
                        S.mm(accs[tt].ap, uT.ap[:, f, tt * 128:(tt + 1) * 128], wd.ap[:, fi, :], f == 0, f == 63,
                             [uT.sub(f), wd], [accs[tt]])
            for tt in range(4):
                row = (c * 4 + tt) * 128
                xp, op, tm = xps[c_e % 2], ops_[c_e % 2], tms[c_e % 2]
                c_e += 1
                S.dma("pool", xp.ap, self.x1_d.ap[row:row + 128, nn * 512:(nn + 1) * 512], [self.x1_d], [xp])
                S.v("dve", "tensor_tensor", [accs[tt], self.gt2], [tm], tm.ap, accs[tt].ap, self.gt2.ap[:, nn * 512:(nn + 1) * 512], ALU.mult)
                S.v("dve", "tensor_tensor", [tm, xp], [op], op.ap, tm.ap, xp.ap, ALU.add)
                S.dma("pool", self.out_d.ap[row:row + 128, nn * 512:(nn + 1) * 512], op.ap, [op], [self.out_d])
    S.barrier()
    ar.pop()


KB.stage5 = stage5
KB.stage6 = stage6


def precast(self):
    S = self.S
    self.wup_in = self.din("w_up", [D, DFF])
    self.wdn_in = self.din("w_down", [DFF, D])
    self.wupb = T(self.nc.dram_tensor("wupb_i", [32, 128, 16 * 256], BF16).ap(), "wupb")
    self.wdnb = T(self.nc.dram_tensor("wdnb_i", [4, 8, 128, 8 * 512], BF16).ap(), "wdnb")
    wup_v = self.wup_in.ap.rearrange("(k p) c -> p k c", p=128)
    wdn_v = self.wdn_in.ap.rearrange("(f p) c -> p f c", p=128)
    for fb in range(32):
        S.dma("pool", self.wupb.ap[fb].rearrange("p (k c) -> p k c", k=16), wup_v[:, :, fb * 256:(fb + 1) * 256],
              [self.wup_in], [self.wupb])
    for nn in range(4):
        for f8 in range(8):
            S.dma("pool", self.wdnb.ap[nn, f8].rearrange("p (f c) -> p f c", f=8),
                  wdn_v[:, f8 * 8:(f8 + 1) * 8, nn * 512:(nn + 1) * 512], [self.wdn_in], [self.wdnb])


KB.precast = precast


_CACHE = {}


def _all_consts():
    c = host_consts()
    c.update(nsa_consts())
    c.update(rwkv_consts())
    return c


def prep_all(inp, b, consts):
    m = prep_core(inp, b, consts)
    nsa_prep(inp, m)
    rwkv_prep(inp, m)
    m["w_o_rwkv"] = inp["w_o_rwkv"][0]
    m["w_o_nsa"] = inp["w_o_nsa"][0]
    m["w_out"] = inp["w_out"][0]
    m["w_up"] = inp["w_up"][0]
    m["w_down"] = inp["w_down"][0]
    return m


def kernel(**inputs):
    inp = {k: np.asarray(v) for k, v in inputs.items()}
    if "nc" not in _CACHE:
        _CACHE["nc"] = KB(dbg=False).build()
        _CACHE["consts"] = _all_consts()
    nc = _CACHE["nc"]
    consts = _CACHE["consts"]
    in_maps = [prep_all(inp, b, consts) for b in range(8)]
    res = run_bass_kernel_spmd(nc, in_maps, core_ids=list(range(8)))
    out = np.stack([np.asarray(r["out"]) for r in res.results], axis=0)
    return out.astype(np.float32)
```

```python
import numpy as np
import concourse.bass as bass
import concourse.mybir as mybir

F32 = mybir.dt.float32
BF16 = mybir.dt.bfloat16
AF = mybir.ActivationFunctionType
ALU = mybir.AluOpType
AX = mybir.AxisListType

ENGS = ("pe", "act", "dve", "pool", "sp")
NSLOT = {"sp": 40, "pool": 24}


class Buf:
    __slots__ = ("w", "r_eng", "r_dma", "name")

    def __init__(self, name=""):
        self.w = None
        self.r_eng = {}
        self.r_dma = []
        self.name = name


class T(Buf):
    __slots__ = ("ap", "subs")

    def __init__(self, ap, name=""):
        Buf.__init__(self, name)
        self.ap = ap
        self.subs = {}

    def __getitem__(self, idx):
        return self.ap[idx]

    def sub(self, key):
        b = self.subs.get(key)
        if b is None:
            b = self.subs[key] = Buf(f"{self.name}.{key}")
        return b


class Op:
    __slots__ = ("eng", "fn", "deps", "is_dma", "slot", "sig", "val", "dsem", "dval", "idx")

    def __init__(self, eng, fn, is_dma):
        self.eng = eng
        self.fn = fn
        self.is_dma = is_dma
        self.deps = []
        self.sig = False
        self.val = 0
        self.slot = -1
        self.dsem = None
        self.dval = 0


class Sched:
    def __init__(self, nc):
        self.nc = nc
        self.ops = {e: [] for e in ENGS}
        self.bar = {e: [] for e in ENGS}
        self.dma_since_bar = []
        self.slot_last = {q: [None] * n for q, n in NSLOT.items()}
        self.slot_n = {q: 0 for q in NSLOT}

    def rec(self, eng, fn, reads=(), writes=(), is_dma=False):
        op = Op(eng, fn, is_dma)
        deps = []
        for b in reads:
            if b.w is not None:
                deps.append(b.w)
        for b in writes:
            if b.w is not None:
                deps.append(b.w)
            deps.extend(b.r_eng.values())
            deps.extend(b.r_dma)
        if self.bar[eng]:
            deps.extend(self.bar[eng])
            self.bar[eng] = []
        if is_dma:
            n = self.slot_n[eng]
            self.slot_n[eng] = n + 1
            s = n % NSLOT[eng]
            op.slot = s
            prev = self.slot_last[eng][s]
            if prev is not None:
                deps.append(prev)
            self.slot_last[eng][s] = op
            self.dma_since_bar.append(op)
        seen = set()
        for d in deps:
            if d is op or id(d) in seen:
                continue
            seen.add(id(d))
            op.deps.append(d)
        for b in reads:
            if is_dma:
                b.r_dma.append(op)
            else:
                b.r_eng[eng] = op
        for b in writes:
            b.w = op
            b.r_eng = {}
            b.r_dma = []
        self.ops[eng].append(op)
        return op

    def barrier(self):
        deps = [self.ops[e][-1] for e in ENGS if self.ops[e]] + self.dma_since_bar
        self.dma_since_bar = []
        for e in ENGS:
            self.bar[e] = list(deps)

    def mm(self, out, lhsT, rhs, start, stop, reads, writes, **kw):
        return self.rec("pe", lambda e: e.matmul(out, lhsT, rhs, start=start, stop=stop, **kw), reads, writes)

    def tr(self, out, in_, ident, reads, writes):
        return self.rec("pe", lambda e: e.transpose(out, in_, ident), reads, writes)

    def act(self, out, in_, func, reads, writes, bias=None, scale=None, accum_out=None):
        kw = {}
        if bias is not None:
            kw["bias"] = bias
        if scale is not None:
            kw["scale"] = scale
        if accum_out is not None:
            kw["accum_out"] = accum_out
        return self.rec("act", lambda e: e.activation(out=out, in_=in_, func=func, **kw), reads, writes)

    def v(self, eng, meth, reads, writes, *a, **kw):
        return self.rec(eng, lambda e: getattr(e, meth)(*a, **kw), reads, writes)

    def dma(self, q, out, in_, reads, writes, **kw):
        return self.rec(q, lambda e: e.dma_start(out=out, in_=in_, **kw), reads, writes, is_dma=True)

    def finalize(self):
        for e in ENGS:
            for op in self.ops[e]:
                for d in op.deps:
                    if d.is_dma:
                        continue
                    if d.eng == "pe" and op.eng == "pe" and not op.is_dma:
                        continue
                    d.sig = True
        for e in ENGS:
            n = 0
            for op in self.ops[e]:
                if op.sig and not op.is_dma:
                    n += 1
                    op.val = n

    def emit_all(self, stack):
        nc = self.nc
        self.finalize()
        self.esem = {e: stack.enter_context(nc.semaphore("es_" + e)) for e in ENGS}
        self.dsem = {q: [stack.enter_context(nc.semaphore(f"ds_{q}{i}")) for i in range(n)] for q, n in NSLOT.items()}
        uses = {q: [0] * n for q, n in NSLOT.items()}
        for q in NSLOT:
            for op in self.ops[q]:
                if op.is_dma:
                    uses[q][op.slot] += 1
                    op.dsem = self.dsem[q][op.slot]
                    op.dval = 16 * uses[q][op.slot]
        block = stack.enter_context(nc.Block())
        sched = self

        def emit(name, eng):
            known = {}
            for op in sched.ops[name]:
                for d in op.deps:
                    if d.is_dma:
                        sem, val = d.dsem, d.dval
                    else:
                        if d.eng == "pe" and name == "pe" and not op.is_dma:
                            continue
                        sem, val = sched.esem[d.eng], d.val
                    k = id(sem)
                    if known.get(k, 0) >= val:
                        continue
                    eng.wait_ge(sem, val)
                    known[k] = val
                ins = op.fn(eng)
                if op.is_dma:
                    ins.then_inc(op.dsem, 16)
                elif op.sig:
                    ins.then_inc(sched.esem[name], 1)
            if name == "sp":
                for q in NSLOT:
                    for i, u in enumerate(uses[q]):
                        if u:
                            eng.wait_ge(sched.dsem[q][i], 16 * u)

        @block.tensor
        def _(e):
            emit("pe", e)

        @block.scalar
        def _(e):
            emit("act", e)

        @block.vector
        def _(e):
            emit("dve", e)

        @block.gpsimd
        def _(e):
            emit("pool", e)

        @block.sync
        def _(e):
            emit("sp", e)


class Arena:
    def __init__(self, ap, nwords):
        self.ap = ap
        self.n = nwords
        self.off = 0
        self.marks = []

    def push(self):
        self.marks.append(self.off)

    def pop(self):
        self.off = self.marks.pop()

    def alloc(self, shape, dtype=F32, name="", parts=128):
        n = int(np.prod(shape))
        words = n if dtype == F32 else (n + 1) // 2
        words = (words + 7) // 8 * 8
        assert self.off + words <= self.n, f"arena overflow {name} {self.off}+{words}>{self.n}"
        ap = self.ap[0:parts, self.off:self.off + words]
        self.off += words
        if dtype != F32:
            ap = ap.bitcast(dtype)
        ap = ap[:, 0:n]
        if len(shape) == 2:
            ap = ap.rearrange("p (a b) -> p a b", a=shape[0])
        elif len(shape) == 3:
            ap = ap.rearrange("p (a b c) -> p a b c", a=shape[0], b=shape[1])
        elif len(shape) == 4:
            ap = ap.rearrange("p (a b c d) -> p a b c d", a=shape[0], b=shape[1], c=shape[2])
        return T(ap, name)

from contextlib import ExitStack
from concourse.bass_utils import run_bass_kernel_spmd

D = 2048
SEQ = 2048
NT = 16
RWC = 3360
NB = 3360
MB = 5968
INC = 10064
DFF = 8192
EPS = 1e-6
GN_EPS = 64e-5
NEG = -30000.0
ARENA_WORDS = 52000


class KB:
    def __init__(self, dbg=False, stages=(0, 1, 2, 3, 4, 5, 6)):
        self.nc = bass.Bass("TRN2", target_bir_lowering=False)
        self.S = Sched(self.nc)
        self.dbg = dbg
        self.stages = stages
        self.bank_i = 0

    def din(self, name, shape, dt=F32):
        return T(self.nc.dram_tensor(name, list(shape), dt, kind="ExternalInput").ap(), name)

    def dscr(self, name, shape, dt=F32, out=False):
        kind = "ExternalOutput" if (self.dbg or out) else "Internal"
        return T(self.nc.dram_tensor(name, list(shape), dt, kind=kind).ap(), name)

    def bank(self):
        b = self.ps[self.bank_i % 8]
        self.bank_i += 1
        return b

    def load(self, dst, src, q="sp"):
        self.S.dma(q, dst.ap, src[1], [src[0]], [dst])

    def build(self):
        nc, S = self.nc, self.S
        with ExitStack() as st:
            arena_t = st.enter_context(nc.sbuf_tensor("arena", [128, ARENA_WORDS], F32))
            self.ar = ar = Arena(arena_t, ARENA_WORDS)
            self.ps = [T(st.enter_context(nc.psum_tensor(f"ps{i}", [128, 512], F32))[:], f"ps{i}") for i in range(8)]
            self.declare()
            self.persistent()
            if 0 in self.stages:
                self.stage0()
            if 1 in self.stages:
                ar.push()
                self.hT = ar.alloc([16, SEQ], BF16, "hT")
                self.stage1(self.x_d, self.coef1, self.sh1, self.hT)
                if 2 in self.stages:
                    self.stage2()
                ar.pop()
            if 4 in self.stages:
                self.stage4()
            if 3 in self.stages:
                self.stage3()
            if 5 in self.stages:
                self.stage5()
            if 6 in self.stages:
                ar.push()
                self.hT = ar.alloc([16, SEQ], BF16, "h2T")
                self.stage1(self.x1_d, self.coef2, self.sh2, self.hT)
                self.stage6()
                ar.pop()
            S.emit_all(st)
        return nc

    def declare(self):
        d = self.din
        self.x_d = d("x", [SEQ, D])
        self.c_fm = d("c_fm", [128, 16])
        self.w_ada = d("w_ada", [D, 6 * D])
        self.b_ada = d("b_ada", [1, 6 * D])
        self.n1g = d("n1g_fm", [128, 16])
        self.n2g = d("n2g_fm", [128, 16])
        self.w_in = d("w_in", [D, INC])
        self.ident_d = d("ident", [128, 128])
        self.bones_d = d("bones", [128, 128])
        self.mu_d = d("mu_fm", [128, 27])
        self.qkg_d = d("qkg_fm", [128, 5])
        s = self.dscr
        self.rwT_d = s("rwT", [27 * 128, SEQ])
        self.qT_d = s("qT", [1024, SEQ], BF16)
        self.kvcT_d = s("kvcT", [512, SEQ], BF16)
        self.ksT_d = s("ksT", [4, 2, 128, SEQ], BF16)
        self.kwT_d = s("kwT", [4, 2, 128, SEQ], BF16)
        self.vv_d = s("vv", [SEQ, 512], BF16)
        self.gates_d = s("gates", [SEQ, 48])
        self.mgT_d = s("mgT", [4096, SEQ], BF16)
        self.x1_d = s("x1", [SEQ, D])
        self.out_d = self.dscr("out", [SEQ, D], out=True)

    def persistent(self):
        ar, S = self.ar, self.S
        self.ident = ar.alloc([128], F32, "ident")
        self.bones = ar.alloc([128], F32, "bones")
        S.dma("sp", self.ident.ap, self.ident_d.ap, [self.ident_d], [self.ident])
        S.dma("sp", self.bones.ap, self.bones_d.ap, [self.bones_d], [self.bones])
        self.coef1 = ar.alloc([16], F32, "coef1")
        self.sh1 = ar.alloc([16], F32, "sh1")
        self.coef2 = ar.alloc([16], F32, "coef2")
        self.sh2 = ar.alloc([16], F32, "sh2")
        self.gt1 = ar.alloc([D], F32, "gt1")
        self.gt2 = ar.alloc([D], F32, "gt2")

    def silu_rep(self):
        ar, S = self.ar, self.S
        cs = ar.alloc([16], F32, "cs")
        S.dma("sp", cs.ap, self.c_fm.ap, [self.c_fm], [cs])
        csb = ar.alloc([16], F32, "csb")
        S.act(csb.ap, cs.ap, AF.Silu, [cs], [csb])
        crep = ar.alloc([16, 128], BF16, "crep")
        S.v("dve", "tensor_copy", [csb], [crep], crep.ap, csb.ap.unsqueeze(2).to_broadcast([128, 16, 128]))
        return crep

    def stage0(self):
        ar, S = self.ar, self.S
        ar.push()
        bias_steps = self.nsa_bias_build() if 4 in self.stages else []
        mod = ar.alloc([6 * D], F32, "mod")
        crep = self.silu_rep()
        wbs = [ar.alloc([16, 512], BF16, f"wada{i}") for i in range(2)]
        bbs = [ar.alloc([512], F32, f"bada{i}") for i in range(2)]
        wsrc = self.w_ada.ap.rearrange("(k p) c -> p k c", p=128)
        for blk in range(24):
            wb, bb = wbs[blk % 2], bbs[blk % 2]
            c0 = blk * 512
            S.dma("pool", wb.ap, wsrc[:, :, c0:c0 + 512], [self.w_ada], [wb])
            S.dma("sp", bb.ap, self.b_ada.ap[0:1, c0:c0 + 512].partition_broadcast(128), [self.b_ada], [bb])
            P = self.bank()
            for k in range(16):
                S.mm(P.ap, crep.ap[:, k, :], wb.ap[:, k, :], k == 0, k == 15, [crep, wb], [P])
            S.v("dve", "tensor_tensor", [P, bb], [mod], mod.ap[:, c0:c0 + 512], P.ap, bb.ap, ALU.add)
            for _ in range(2):
                if bias_steps:
                    bias_steps.pop(0)()
        while bias_steps:
            bias_steps.pop(0)()
        tmp = ar.alloc([16, 128], F32, "dtmp")
        sc1 = ar.alloc([16], F32, "sc1")
        sc2 = ar.alloc([16], F32, "sc2")
        for dst, idx in ((self.sh1, 0), (sc1, 1), (self.sh2, 3), (sc2, 4)):
            src = mod.ap[:, idx * D:(idx + 1) * D].rearrange("p (k m) -> p k m", k=16)
            S.v("dve", "tensor_tensor", [mod, self.ident], [tmp], tmp.ap, src,
                self.ident.ap.unsqueeze(1).to_broadcast([128, 16, 128]), ALU.mult)
            S.v("dve", "tensor_reduce", [tmp], [dst], dst.ap, tmp.ap, AX.X, ALU.add)
        g = ar.alloc([16], F32, "gload")
        S.dma("sp", g.ap, self.n1g.ap, [self.n1g], [g])
        S.v("dve", "scalar_tensor_tensor", [sc1, g], [self.coef1], self.coef1.ap, sc1.ap, 1.0, g.ap, ALU.add, ALU.mult)
        g2 = ar.alloc([16], F32, "gload2")
        S.dma("sp", g2.ap, self.n2g.ap, [self.n2g], [g2])
        S.v("dve", "scalar_tensor_tensor", [sc2, g2], [self.coef2], self.coef2.ap, sc2.ap, 1.0, g2.ap, ALU.add, ALU.mult)
        S.v("dve", "tensor_copy", [mod], [self.gt1], self.gt1.ap, mod.ap[:, 2 * D:3 * D])
        S.v("dve", "tensor_copy", [mod], [self.gt2], self.gt2.ap, mod.ap[:, 5 * D:6 * D])
        S.barrier()
        ar.pop()

    def stage0b_setup(self):
        ar, S = self.ar, self.S
        crep = self.silu_rep()
        wb2 = [ar.alloc([16, 256], BF16, f"wada_b{i}") for i in range(2)]
        bb2 = [ar.alloc([256], F32, f"bada_b{i}") for i in range(2)]
        tmp = ar.alloc([2, 128], F32, "dtmp_b")
        sc2 = ar.alloc([16], F32, "sc2")
        wsrc = self.w_ada.ap.rearrange("(k p) c -> p k c", p=128)
        steps = []

        def mk(sb):
            def step():
                wb, bb = wb2[sb % 2], bb2[sb % 2]
                c0 = 3 * D + sb * 256
                S.dma("pool", wb.ap, wsrc[:, :, c0:c0 + 256], [self.w_ada], [wb])
                S.dma("sp", bb.ap, self.b_ada.ap[0:1, c0:c0 + 256].partition_broadcast(128), [self.b_ada], [bb])
                P = self.bank()
                for k in range(16):
                    S.mm(P.ap[:, 0:256], crep.ap[:, k, :], wb.ap[:, k, :], k == 0, k == 15, [crep, wb], [P])
                if sb < 16:
                    dst = self.sh2 if sb < 8 else sc2
                    j = sb % 8
                    t2 = tmp.ap.rearrange("p a b -> p (a b)")
                    S.v("dve", "tensor_tensor", [P, bb], [tmp], t2, P.ap[:, 0:256], bb.ap, ALU.add)
                    S.v("dve", "tensor_tensor", [tmp, self.ident], [tmp], tmp.ap, tmp.ap,
                        self.ident.ap.unsqueeze(1).to_broadcast([128, 2, 128]), ALU.mult)
                    S.v("dve", "tensor_reduce", [tmp], [dst], dst.ap[:, 2 * j:2 * j + 2], tmp.ap, AX.X, ALU.add)
                else:
                    o = (sb - 16) * 256
                    S.v("dve", "tensor_tensor", [P, bb], [self.gt2], self.gt2.ap[:, o:o + 256], P.ap[:, 0:256], bb.ap, ALU.add)
                if sb == 23:
                    g = ar.alloc([16], F32, "gload2")
                    S.dma("sp", g.ap, self.n2g.ap, [self.n2g], [g])
                    S.v("dve", "scalar_tensor_tensor", [sc2, g], [self.coef2], self.coef2.ap, sc2.ap, 1.0, g.ap, ALU.add, ALU.mult)
            return step
        return [mk(sb) for sb in range(24)]

    def stage1(self, src_d, coef, sh, hT):
        ar, S = self.ar, self.S
        ar.push()
        xbs = [ar.alloc([D], F32, f"xb{i}") for i in range(2)]
        junk = ar.alloc([D], F32, "junk")
        xs4s = [ar.alloc([4, D], F32, f"xs4_{i}") for i in range(1)]
        ss = ar.alloc([NT], F32, "ss")
        sr = ar.alloc([NT], F32, "sr")
        rstd = ar.alloc([NT], F32, "rstd")
        for grp in range(4):
            xs4 = xs4s[0]
            for tt in range(4):
                ti = grp * 4 + tt
                xb = xbs[ti % 2]
                S.dma("sp", xb.ap, src_d.ap[ti * 128:(ti + 1) * 128, :], [src_d], [xb])
                sst = ss.sub(ti)
                S.act(junk.ap, xb.ap, AF.Square, [xb], [sst], accum_out=ss.ap[:, ti:ti + 1])
                S.act(sr.ap[:, ti:ti + 1], ss.ap[:, ti:ti + 1], AF.Sqrt, [sst], [sr.sub(ti)], bias=EPS, scale=1.0 / D)
                S.v("dve", "reciprocal", [sr.sub(ti)], [rstd.sub(ti)], rstd.ap[:, ti:ti + 1], sr.ap[:, ti:ti + 1])
                S.v("dve", "tensor_scalar", [xb, rstd.sub(ti)], [xs4.sub(tt)], xs4.ap[:, tt, :], xb.ap,
                    rstd.ap[:, ti:ti + 1], None, ALU.mult)
            for k in range(16):
                P = self.bank()
                for tt in range(4):
                    S.tr(P.ap[:, tt * 128:(tt + 1) * 128], xs4.ap[:, tt, k * 128:(k + 1) * 128], self.ident.ap,
                         [xs4.sub(tt), self.ident], [P])
                S.act(hT.ap[:, k, grp * 512:(grp + 1) * 512], P.ap, AF.Identity, [P, coef, sh], [hT.sub(grp)],
                      bias=sh.ap[:, k:k + 1], scale=coef.ap[:, k:k + 1])
        S.barrier()
        ar.pop()

    def stage2(self):
        ar, S = self.ar, self.S
        hT = self.hT
        ar.push()
        wbs = [ar.alloc([16, 512], BF16, f"win{i}") for i in range(2)]
        wtm = ar.alloc([16, 560], BF16, "wtm")
        raws = [ar.alloc([2056], F32, f"raw{i}") for i in range(2)]
        tmps = [ar.alloc([SEQ], F32, f"mixt{i}") for i in range(2)]
        stg = [ar.alloc([SEQ], BF16, f"stg{i}") for i in range(4)]
        sqs = [ar.alloc([512], F32, f"sq{i}") for i in range(2)]
        srs = [ar.alloc([512], F32, f"sr{i}") for i in range(2)]
        ris = [ar.alloc([512], F32, f"ri{i}") for i in range(2)]
        mu = ar.alloc([27], F32, "mu")
        omu = ar.alloc([27], F32, "omu")
        qkg = ar.alloc([5], F32, "qkg")
        S.dma("sp", mu.ap, self.mu_d.ap, [self.mu_d], [mu])
        S.dma("sp", qkg.ap, self.qkg_d.ap, [self.qkg_d], [qkg])
        S.v("dve", "tensor_scalar", [mu], [omu], omu.ap, mu.ap, -1.0, 1.0, ALU.mult, ALU.add)
        S.v("dve", "tensor_scalar", [qkg], [qkg], qkg.ap[:, 1:5], qkg.ap[:, 1:5], 8.0, None, ALU.mult)
        for r in raws:
            S.v("dve", "memset", [], [r], r.ap[:, 0:1], 0.0)
        wsrc = self.w_in.ap.rearrange("(k p) c -> p k c", p=128)
        cnt = {"w": 0, "raw": 0, "stg": 0, "sq": 0}

        def proj_chunk(wb, m0, M, n):
            P = self.bank()
            for k in range(16):
                S.mm(P.ap[0:M, :], wb.ap[:, k, m0:m0 + M], hT.ap[:, k, n * 512:(n + 1) * 512], k == 0, k == 15,
                     [wb, hT.sub(n)], [P])
            return P

        def next_stg():
            t = stg[cnt["stg"] % 4]
            cnt["stg"] += 1
            return t

        def ep_rw(wb, m0, M, ti):
            raw = raws[cnt["raw"] % 2]
            tmp = tmps[cnt["raw"] % 2]
            cnt["raw"] += 1
            for n in range(4):
                P = proj_chunk(wb, m0, M, n)
                S.act(raw.ap[0:M, 1 + n * 512:1 + (n + 1) * 512], P.ap[0:M, :], AF.Copy, [P], [raw])
            S.v("dve", "tensor_scalar", [raw, mu], [tmp], tmp.ap[0:M, :], raw.ap[0:M, 0:SEQ], mu.ap[0:M, ti:ti + 1], None, ALU.mult)
            S.v("dve", "scalar_tensor_tensor", [raw, omu, tmp], [tmp], tmp.ap[0:M, :], raw.ap[0:M, 1:SEQ + 1],
                omu.ap[0:M, ti:ti + 1], tmp.ap[0:M, :], ALU.mult, ALU.add)
            S.dma("sp", self.rwT_d.ap[ti * 128:ti * 128 + M, :], tmp.ap[0:M, :], [tmp], [self.rwT_d])

        def ep_qk(wb, m0, gcols, dsts):
            outs = [next_stg() for _ in gcols]
            for n in range(4):
                P = proj_chunk(wb, m0, 128, n)
                i = cnt["sq"] % 2
                cnt["sq"] += 1
                sq, sr, ri = sqs[i], srs[i], ris[i]
                S.act(sq.ap, P.ap, AF.Square, [P], [sq])
                P2 = self.bank()
                S.mm(P2.ap, self.bones.ap, sq.ap, True, True, [self.bones, sq], [P2])
                S.act(sr.ap, P2.ap, AF.Sqrt, [P2], [sr], bias=64 * EPS, scale=1.0)
                S.v("dve", "reciprocal", [sr], [ri], ri.ap, sr.ap)
                for gc, o in zip(gcols, outs):
                    S.v("dve", "scalar_tensor_tensor", [P, qkg, ri], [o], o.ap[:, n * 512:(n + 1) * 512], P.ap,
                        qkg.ap[:, gc:gc + 1], ri.ap, ALU.mult, ALU.mult)
            for o, (dt_, dap) in zip(outs, dsts):
                S.dma("sp", dap, o.ap, [o], [dt_])

        def ep_act(wb, m0, func, dt_, dap):
            o = next_stg()
            for n in range(4):
                P = proj_chunk(wb, m0, 128, n)
                S.act(o.ap[:, n * 512:(n + 1) * 512], P.ap, func, [P], [o])
            S.dma("sp", dap, o.ap, [o], [dt_])

        def load_block(segs):
            wb = wbs[cnt["w"] % 2]
            cnt["w"] += 1
            for (c0, n, off) in segs:
                S.dma("pool", wb.ap[:, :, off:off + n], wsrc[:, :, c0:c0 + n], [self.w_in], [wb])
            return wb

        for b in range(7):
            if b < 6:
                wb = load_block([(512 * b, 512, 0)])
                for j in range(4):
                    ep_rw(wb, j * 128, 128, 4 * b + j)
            else:
                wb = load_block([(3072, 288, 0)])
                ep_rw(wb, 0, 128, 24)
                ep_rw(wb, 128, 128, 25)
                ep_rw(wb, 256, 32, 26)
        for b in range(2):
            wb = load_block([(NB + 512 * b, 512, 0)])
            for j in range(4):
                ti = 4 * b + j
                ep_qk(wb, j * 128, [0], [(self.qT_d, self.qT_d.ap[ti * 128:(ti + 1) * 128, :])])
        wb = load_block([(NB + 1024, 512, 0)])
        for j in range(4):
            ep_act(wb, j * 128, AF.Copy, self.kvcT_d, self.kvcT_d.ap[j * 128:(j + 1) * 128, :])
        for (c_base, dst, gc) in ((NB + 1024 + 512, self.ksT_d, 1), (NB + 1024 + 1024, self.kwT_d, 3)):
            segs = []
            for g in range(4):
                segs.append((c_base + 64 * g, 64, g * 128))
                segs.append((c_base + 64 * g, 64, g * 128 + 64))
            wb = load_block(segs)
            for g in range(4):
                ep_qk(wb, g * 128, [gc, gc + 1], [(dst, dst.ap[g, 0]), (dst, dst.ap[g, 1])])
        for b in range(8):
            wb = load_block([(MB + 512 * b, 512, 0)])
            for j in range(4):
                ti = 4 * b + j
                ep_act(wb, j * 128, AF.Sigmoid, self.mgT_d, self.mgT_d.ap[ti * 128:(ti + 1) * 128, :])
        for (c0, n, off) in ((NB + 1024 + 768, 256, 0), (NB + 1024 + 1280, 256, 256), (NB + 2560, 48, 512)):
            S.dma("pool", wtm.ap[:, :, off:off + n], wsrc[:, :, c0:c0 + n], [self.w_in], [wtm])
        vst = [ar.alloc([512], BF16, f"vst{i}") for i in range(2)]
        gst = [ar.alloc([48], F32, f"gst{i}") for i in range(2)]
        for tt in range(NT):
            P = self.bank()
            for k in range(16):
                S.mm(P.ap, hT.ap[:, k, tt * 128:(tt + 1) * 128], wtm.ap[:, k, 0:512], k == 0, k == 15,
                     [hT.sub(tt // 4), wtm], [P])
            v = vst[tt % 2]
            S.act(v.ap, P.ap, AF.Copy, [P], [v])
            S.dma("sp", self.vv_d.ap[tt * 128:(tt + 1) * 128, :], v.ap, [v], [self.vv_d])
            P = self.bank()
            for k in range(16):
                S.mm(P.ap[:, 0:48], hT.ap[:, k, tt * 128:(tt + 1) * 128], wtm.ap[:, k, 512:560], k == 0, k == 15,
                     [hT.sub(tt // 4), wtm], [P])
            gt = gst[tt % 2]
            S.act(gt.ap, P.ap[:, 0:48], AF.Sigmoid, [P], [gt])
            S.dma("sp", self.gates_d.ap[tt * 128:(tt + 1) * 128, :], gt.ap, [gt], [self.gates_d])
        S.barrier()
        ar.pop()


def _fm(v, ntile=None):
    v = np.asarray(v, np.float32).reshape(-1)
    n = (len(v) + 127) // 128 if ntile is None else ntile
    buf = np.zeros(n * 128, np.float32)
    buf[:len(v)] = v
    return np.ascontiguousarray(buf.reshape(n, 128).T)


def host_consts():
    c = {}
    c["ident"] = np.eye(128, dtype=np.float32)
    p = np.arange(128)
    c["bones"] = (p[:, None] // 64 == p[None, :] // 64).astype(np.float32)
    return c


def prep_core(inp, b, consts):
    m = dict(consts)
    m["x"] = np.ascontiguousarray(inp["x"][b])
    m["c_fm"] = _fm(inp["c"][b])
    m["w_ada"] = inp["w_ada"][0]
    m["b_ada"] = inp["b_ada"][0].reshape(1, -1)
    m["n1g_fm"] = _fm(inp["norm1_g"][0])
    m["n2g_fm"] = _fm(inp["norm2_g"][0])
    m["w_in"] = inp["w_in"][0]
    m["mu_fm"] = _fm(inp["rwkv_mu"][0], 27)
    qg = np.tile(inp["q_norm_g"][0], 2)
    kg = inp["k_norm_g"][0]
    z = np.zeros(64, np.float32)
    cols = [qg, np.concatenate([kg[1], z]), np.concatenate([z, kg[1]]),
            np.concatenate([kg[2], z]), np.concatenate([z, kg[2]])]
    m["qkg_fm"] = np.ascontiguousarray(np.stack(cols, axis=1).astype(np.float32))
    return m


def _rel_bucket_np(rel):
    n = np.maximum(rel, 0)
    nf = np.maximum(n, 16).astype(np.float32)
    large = 16 + (np.log(nf / np.float32(16)) / np.float32(np.log(8.0)) * np.float32(16)).astype(np.int32)
    large = np.minimum(large, 31)
    return np.where(n < 16, n, large)


def nsa_consts():
    c = {}
    NOH = 3 * 16384 + 17 * 128
    oh = np.zeros((33, NOH), np.float32)
    pos = np.arange(128)[:, None]
    t = np.arange(128)[None, :]
    for d, base in ((0, 0), (1, 128), (2, 512)):
        rel = base + t - pos
        if d == 0:
            mask = rel < 0
        elif d == 1:
            mask = np.zeros_like(rel, bool)
        else:
            mask = rel >= 512
        b = _rel_bucket_np(rel)
        sec = np.zeros((33, 128, 128), np.float32)
        for bb in range(32):
            sec[bb][(b == bb) & ~mask] = 1.0
        sec[32][mask] = 1.0
        oh[:, d * 16384:(d + 1) * 16384] = sec.reshape(33, -1)
    sec = np.zeros((33, 17, 128), np.float32)
    ti = np.arange(128)
    for r in range(16):
        m = r - 9
        rel = ti - 16 * m - 31
        b = _rel_bucket_np(rel)
        for bb in range(32):
            sec[bb, r, (b == bb) & (rel >= 0)] = 1.0
        sec[32, r, rel < 0] = 1.0
    sec[32, 16, :] = 1.0
    oh[:, 3 * 16384:] = sec.reshape(33, -1)
    c["nsa_oh"] = oh
    S = np.zeros((17, 16, 128), np.float32)
    for i in range(16):
        for n in range(127):
            m = n - 8 * i
            if -9 <= m <= 6:
                S[m + 9, i, n] = 1.0
            elif m > 6:
                S[16, i, n] = 1.0
    c["nsa_S"] = S
    E = np.zeros((32, 2048), np.float32)
    for p in range(2048):
        E[p // 64, p] = 1.0
    c["nsa_E"] = E
    allowed = np.zeros((128, 16, 32), np.float32)
    addc = np.zeros((128, 16, 32), np.float32)
    blk = np.arange(32)
    for i in range(16):
        for tt in range(128):
            cur = (i * 128 + tt) // 64
            al = blk <= cur
            forced = (blk == 0) | (blk == cur) | (blk == cur - 1)
            allowed[tt, i] = (al & ~forced).astype(np.float32)
            addc[tt, i] = np.where(forced, 1e4, np.where(al, 0.0, -1.0))
    c["nsa_allowed"] = allowed
    c["nsa_addc"] = addc
    ncmp = 127
    cs = np.arange(ncmp) * 16
    ss = np.arange(32) * 64
    lo = np.maximum(cs[:, None], ss[None, :])
    hi = np.minimum(cs[:, None] + 32, ss[None, :] + 64)
    c["nsa_selm"] = (np.maximum(hi - lo, 0) / 32).astype(np.float32)
    return c


def nsa_prep(inp, m):
    m["rel_bias"] = np.ascontiguousarray(inp["rel_bias"])
    for kv in ("k", "v"):
        m[f"pe_{kv}T"] = np.ascontiguousarray(inp[f"cmp_pe_{kv}"][0].T)
        m[f"w1_{kv}"] = inp[f"cmp_w1_{kv}"][0]
        m[f"w2_{kv}"] = inp[f"cmp_w2_{kv}"][0]
    kg0 = inp["k_norm_g"][0][0]
    z = np.zeros(64, np.float32)
    m["kcg_fm"] = np.ascontiguousarray(np.stack([np.concatenate([kg0, z]), np.concatenate([z, kg0])], 1).astype(np.float32))


def stage4(self):
    ar, S, nc = self.ar, self.S, self.nc
    d = self.din
    S_d = d("nsa_S", [17, 16, 128])
    E_d = d("nsa_E", [32, 2048])
    al_d = d("nsa_allowed", [128, 16, 32])
    ad_d = d("nsa_addc", [128, 16, 32])
    selm_d = d("nsa_selm", [127, 32])
    kcg_d = d("kcg_fm", [128, 2])
    cmp_d = {}
    for kv in ("k", "v"):
        cmp_d[kv] = (d(f"pe_{kv}T", [64, 32]), d(f"w1_{kv}", [2048, 64]), d(f"w2_{kv}", [64, 64]))
    NOH = 3 * 16384 + 17 * 128
    self.obT_d = self.dscr("obT", [1024, SEQ], BF16)
    ident, bones = self.ident, self.bones

    ar.push()
    ks = ar.alloc([4, 2, SEQ], BF16, "ks")
    kw = ar.alloc([4, 2, SEQ], BF16, "kw")
    vs = ar.alloc([16, 4, 65], BF16, "vs")
    vw = ar.alloc([16, 4, 65], BF16, "vw")
    gates = ar.alloc([16, 48], F32, "gates")
    biasT = ar.alloc([3, 16, 128], F32, "biasT")
    Mst = ar.alloc([16, 128], F32, "Mst", parts=17)
    Sc = ar.alloc([16, 128], F32, "Sc", parts=17)
    allowed = ar.alloc([16, 32], F32, "allowed")
    addc = ar.alloc([16, 32], F32, "addc")
    kc = ar.alloc([4, 2, 128], BF16, "kc")
    rhsc = ar.alloc([4, 97], BF16, "rhsc", parts=127)
    for g in range(4):
        for h in range(2):
            S.dma("sp", ks.ap[:, g, h, :], self.ksT_d.ap[g, h], [self.ksT_d], [ks])
            S.dma("sp", kw.ap[:, g, h, :], self.kwT_d.ap[g, h], [self.kwT_d], [kw])
    for g_ in range(4):
        S.dma("pool", ks.ap[64:96, g_, 0, :], E_d.ap, [E_d, ks], [ks])
        S.dma("pool", ks.ap[0:32, g_, 1, :], E_d.ap, [E_d, ks], [ks])
    for (dst, c0) in ((vs, 0), (vw, 256)):
        S.v("dve", "memset", [], [dst], dst.ap[:, :, :, 64:65], 1.0)
        for j in range(16):
            S.dma("sp", dst.ap[:, j, :, 0:64],
                  self.vv_d.ap[j * 128:(j + 1) * 128, c0:c0 + 256].rearrange("p (g d) -> p g d", g=4), [self.vv_d], [dst])
    S.dma("sp", gates.ap, self.gates_d.ap.rearrange("(j p) c -> p j c", p=128), [self.gates_d], [gates])
    S.dma("sp", Sc.ap, S_d.ap, [S_d], [Sc])
    S.dma("sp", allowed.ap, al_d.ap, [al_d], [allowed])
    S.dma("sp", addc.ap, ad_d.ap, [ad_d], [addc])

    ar.push()
    bias_d = self.bias_d
    for dd in range(3):
        S.dma("sp", biasT.ap[:, dd, :, :],
              bias_d.ap[:, dd * 16384:(dd + 1) * 16384].rearrange("h (p t) -> p h t", p=128), [bias_d], [biasT])
    S.dma("sp", Mst.ap, bias_d.ap[:, 3 * 16384:].rearrange("h (r t) -> r h t", r=17), [bias_d], [Mst])
    if self.dbg:
        dbb = self.dscr("dbg_biasT", [128, 3 * 16 * 128])
        S.dma("sp", dbb.ap, biasT.ap.rearrange("p a b c -> p (a b c)"), [biasT], [dbb])
    ar.pop()

    ar.push()
    kvc = ar.alloc([4, SEQ], BF16, "kvc")
    S.dma("sp", kvc.ap, self.kvcT_d.ap.rearrange("(a p) t -> p a t", p=128), [self.kvcT_d], [kvc])
    kcg = ar.alloc([2], F32, "kcg")
    S.dma("sp", kcg.ap, kcg_d.ap, [kcg_d], [kcg])
    S.v("dve", "tensor_scalar", [kcg], [kcg], kcg.ap, kcg.ap, 8.0, None, ALU.mult)
    S.v("dve", "memset", [], [rhsc], rhsc.ap[:, :, 64:65], 1.0)
    for g in range(4):
        S.dma("pool", rhsc.ap[:, g, 65:97], selm_d.ap, [selm_d], [rhsc])
    for kvi, kv in enumerate(("k", "v")):
        pe_d, w1_d, w2_d = cmp_d[kv]
        w1p = ar.alloc([2, 32, 64], BF16, f"w1p{kv}")
        S.v("dve", "memset", [], [w1p], w1p.ap, 0.0)
        w1v = w1_d.ap.rearrange("(i d) e -> d i e", d=64)
        S.dma("pool", w1p.ap[0:64, 0, :, :], w1v, [w1_d, w1p], [w1p])
        S.dma("pool", w1p.ap[64:128, 1, :, :], w1v, [w1_d, w1p], [w1p])
        peT = ar.alloc([32], BF16, f"peT{kv}", parts=64)
        S.dma("pool", peT.ap, pe_d.ap, [pe_d], [peT])
        w2 = ar.alloc([128], BF16, f"w2{kv}", parts=64)
        S.dma("pool", w2.ap[:, 0:64], w2_d.ap, [w2_d], [w2])
        S.dma("pool", w2.ap[:, 64:128], w2_d.ap, [w2_d, w2], [w2])
        Pb = self.bank()
        for i in range(32):
            S.mm(Pb.ap[0:64, 0:1], w1p.ap[0:64, 0, i, :], peT.ap[:, i:i + 1], i == 0, i == 31, [w1p, peT], [Pb])
        cb = ar.alloc([1], F32, f"cb{kv}", parts=64)
        S.act(cb.ap, Pb.ap[0:64, 0:1], AF.Copy, [Pb], [cb])
        for g in range(4):
            tile_, half = kvi * 2 + g // 2, g % 2
            Ph = self.bank()
            for i in range(32):
                S.mm(Ph.ap[0:64, 0:127], w1p.ap[:, half, i, :], kvc.ap[:, tile_, i:i + 16 * 126 + 1:16], i == 0, i == 31,
                     [w1p, kvc], [Ph])
            u = ar.alloc([127], F32, "cu", parts=64)
            t1 = ar.alloc([127], F32, "ct1", parts=64)
            sg = ar.alloc([127], F32, "csg", parts=64)
            hid = ar.alloc([127], BF16, "chid", parts=64)
            S.act(u.ap, Ph.ap[0:64, 0:127], AF.Identity, [Ph, cb], [u], bias=cb.ap[:, 0:1], scale=1.0)
            S.v("dve", "tensor_tensor", [u], [t1], t1.ap, u.ap, u.ap, ALU.mult)
            S.v("dve", "tensor_scalar", [t1], [t1], t1.ap, t1.ap, 0.044715, 1.0, ALU.mult, ALU.add)
            S.v("dve", "tensor_tensor", [t1, u], [t1], t1.ap, t1.ap, u.ap, ALU.mult)
            S.act(sg.ap, t1.ap, AF.Sigmoid, [t1], [sg], scale=1.5957691216057308)
            S.v("dve", "tensor_tensor", [u, sg], [hid], hid.ap, u.ap, sg.ap, ALU.mult)
            if kv == "k":
                Pk = self.bank()
                S.mm(Pk.ap[:, 0:127], w2.ap, hid.ap, True, True, [w2, hid], [Pk])
                sq = ar.alloc([127], F32, "csq")
                S.act(sq.ap, Pk.ap[:, 0:127], AF.Square, [Pk], [sq])
                P2 = self.bank()
                S.mm(P2.ap[:, 0:127], bones.ap, sq.ap, True, True, [bones, sq], [P2])
                sr = ar.alloc([127], F32, "csr")
                S.act(sr.ap, P2.ap[:, 0:127], AF.Sqrt, [P2], [sr], bias=64 * EPS, scale=1.0)
                S.v("dve", "reciprocal", [sr], [sr], sr.ap, sr.ap)
                for h in range(2):
                    S.v("dve", "scalar_tensor_tensor", [Pk, kcg, sr], [kc], kc.ap[:, g, h, 0:127], Pk.ap[:, 0:127],
                        kcg.ap[:, h:h + 1], sr.ap, ALU.mult, ALU.mult)
            else:
                Pv = self.bank()
                S.mm(Pv.ap[0:127, 0:64], hid.ap, w2.ap[:, 0:64], True, True, [w2, hid], [Pv])
                S.act(rhsc.ap[:, g, 0:64], Pv.ap[0:127, 0:64], AF.Copy, [Pv], [rhsc])
    if self.dbg:
        dkc = self.dscr("dbg_kc", [128, 4 * 2 * 128], BF16)
        S.dma("sp", dkc.ap, kc.ap.rearrange("p a b c -> p (a b c)"), [kc], [dkc])
        drc = self.dscr("dbg_rhsc", [127, 4 * 97], BF16)
        S.dma("sp", drc.ap, rhsc.ap.rearrange("p a b -> p (a b)"), [rhsc], [drc])
    S.barrier()
    ar.pop()
    if 6 in self.stages:
        self.precast()

    qis = [ar.alloc([4, 2, 2, 128], BF16, f"qa{i}") for i in range(2)]
    for qa_ in qis:
        S.v("dve", "memset", [], [qa_.sub("q")] + [qa_.sub(("m", g_)) for g_ in range(4)], qa_.ap, 0.0)

    pxs = [ar.alloc([512], BF16, f"px{i}") for i in range(6)]
    ssbs = [ar.alloc([512], F32, f"ssb{i}") for i in range(2)]
    oaccs = [ar.alloc([1024], F32, f"oacc{i}") for i in range(2)]
    obst = [ar.alloc([8, 128], BF16, f"obst{i}") for i in range(1)] * 2
    sm = [dict(rl=ar.alloc([12], F32, f"rl{i}"), imp=ar.alloc([32], F32, f"imp{i}"), m8=ar.alloc([8], F32, f"m8{i}"),
               ns=ar.alloc([128], F32, f"ns{i}"), tmp=ar.alloc([256], F32, f"otmp{i}")) for i in range(2)]
    for w_ in sm:
        S.v("dve", "memset", [], [w_["ns"]], w_["ns"].ap, 0.0)
    score_banks = self.ps[0:5]
    NSB = 5
    Poc_b = self.ps[5]
    Pow_b = [self.ps[6], self.ps[6]]
    Pos_b = [self.ps[7], self.ps[7]]
    cnt = {"sb": 0, "px": 0, "ssb": 0}
    jobs = []
    qT_v = self.qT_d.ap.rearrange("(kt p) t -> p kt t", p=128)
    obT_v = self.obT_d.ap.rearrange("(kt p) t -> p kt t", p=128)

    def score_job(qi, g, lhs_lo, lhs_hi, M, extra_mm, bias_ap, pv_fn, deps_k, use_mask=False):
        st = {}

        def qk():
            P = score_banks[cnt["sb"] % NSB]
            cnt["sb"] += 1
            pv4 = P.ap[0:M, :].rearrange("p (a b t) -> p a b t", a=2, b=2)
            qdeps = [qi.sub("q")] + ([qi.sub(("m", g))] if use_mask else [])
            S.mm(pv4[:, :, 0, :], lhs_lo, qi.ap[:, g, 0, :, :], True, False, deps_k + qdeps, [P])
            S.mm(pv4[:, :, 1, :], lhs_hi, qi.ap[:, g, 1, :, :], False, extra_mm is None, deps_k + qdeps, [P])
            if extra_mm is not None:
                lt, rt, dps = extra_mm
                S.mm(P.ap[0:M, :], lt, rt, False, True, dps, [P])
            px = pxs[cnt["px"] % 6]
            cnt["px"] += 1
            if bias_ap is not None:
                sb_ = ssbs[cnt["ssb"] % 2]
                cnt["ssb"] += 1
                S.v("dve", "tensor_tensor", [P, biasT], [sb_], sb_.ap[0:M, :], P.ap[0:M, :], bias_ap, ALU.add)
                S.act(px.ap[0:M, :], sb_.ap[0:M, :], AF.Exp, [sb_], [px])
            else:
                S.act(px.ap[0:M, :], P.ap[0:M, :], AF.Exp, [P], [px])
            st["px"] = px

        def pv():
            pv_fn(st["px"])

        return (qk, pv)

    for i in range(NT):
        qi = qis[i % 2]
        oacc = oaccs[i % 2]

        def load_q(i=i):
            if i < NT:
                qa_ = qis[i % 2]
                for (r0, lh) in ((0, 0), (64, 1)):
                    for g_ in range(4):
                        S.dma("sp", qa_.ap[r0:r0 + 64, g_, lh, :, :],
                              qT_v[r0:r0 + 64, 2 * g_:2 * g_ + 2, i * 128:(i + 1) * 128],
                              [self.qT_d], [qa_.sub("q")])
        if i == 0:
            jobs.append((load_q, None))
        load_next = (lambda i=i: load_q(i + 1))
        for g in range(4):
            it = i * 4 + g
            w = sm[it % 2]
            Pow_, Pos_ = Pow_b[it % 2], Pos_b[it % 2]
            gv = gates.ap[:, i, g * 12:(g + 1) * 12].rearrange("p (h c) -> p h c", c=3)
            osl = oacc.ap[:, g * 256:(g + 1) * 256].rearrange("p (h d) -> p h d", h=4)

            def pv_c(px, g=g, i=i, w=w, gv=gv, osl=osl, qi=qi):
                Poc = Poc_b
                for hs in range(4):
                    S.mm(Poc.ap[:, hs * 97:(hs + 1) * 97], px.ap[0:127, hs * 128:(hs + 1) * 128], rhsc.ap[:, g, :],
                         hs == 0, hs == 3, [px, rhsc], [Poc])
                pc3 = Poc.ap[:, 0:388].rearrange("p (h c) -> p h c", h=4)
                rl = w["rl"]
                S.v("dve", "tensor_scalar", [Poc], [rl], rl.ap[:, 0:4], pc3[:, :, 64], 1e-30, None, ALU.max)
                S.v("dve", "reciprocal", [rl], [rl], rl.ap[:, 0:4], rl.ap[:, 0:4])
                imp = w["imp"]
                S.v("dve", "tensor_scalar", [Poc, rl], [imp], imp.ap, pc3[:, 0, 65:97], rl.ap[:, 0:1], None, ALU.mult)
                for hs in range(1, 4):
                    S.v("dve", "scalar_tensor_tensor", [Poc, rl, imp], [imp], imp.ap, pc3[:, hs, 65:97], rl.ap[:, hs:hs + 1],
                        imp.ap, ALU.mult, ALU.add)
                S.v("dve", "tensor_tensor", [imp, allowed], [imp], imp.ap, imp.ap, allowed.ap[:, i, :], ALU.mult)
                S.v("dve", "tensor_tensor", [imp, addc], [imp], imp.ap, imp.ap, addc.ap[:, i, :], ALU.add)
                m8 = w["m8"]
                S.v("dve", "max", [imp], [m8], out=m8.ap, in_=imp.ap)
                ns = w["ns"]
                S.v("dve", "tensor_scalar", [imp, m8], [ns], ns.ap[:, 0:32], imp.ap, m8.ap[:, 7:8], None, ALU.is_ge)
                S.v("dve", "tensor_scalar", [ns], [ns], ns.ap[:, 64:96], ns.ap[:, 0:32], -1.0, -NEG, ALU.add, ALU.mult)
                S.v("dve", "tensor_scalar", [ns], [ns], ns.ap[:, 0:32], ns.ap[:, 0:32], -1.0, -NEG, ALU.add, ALU.mult)
                Pt = score_banks[cnt["sb"] % NSB]
                cnt["sb"] += 1
                S.tr(Pt.ap[:, 0:128], ns.ap, ident.ap, [ns, ident], [Pt])
                S.act(qi.ap[64:96, g, 0, :, :], Pt.ap[64:96, 0:128].unsqueeze(1).to_broadcast([32, 2, 128]), AF.Copy, [Pt], [qi.sub(("m", g))])
                S.act(qi.ap[0:32, g, 1, :, :], Pt.ap[0:32, 0:128].unsqueeze(1).to_broadcast([32, 2, 128]), AF.Copy, [Pt], [qi.sub(("m", g))])
                S.v("dve", "tensor_tensor", [rl, gates], [rl], rl.ap[:, 0:4], rl.ap[:, 0:4], gv[:, :, 0], ALU.mult)
                S.v("dve", "tensor_tensor", [Poc, rl], [oacc.sub(g)], osl, pc3[:, :, 0:64],
                    rl.ap[:, 0:4].unsqueeze(2).to_broadcast([128, 4, 64]), ALU.mult)
            extra = (Sc.ap[:, i, 0:127], Mst.ap[:, 4 * g:4 * g + 4, :], [Sc, Mst])
            jobs.append(score_job(qi, g, kc.ap[:, g, 0, 0:127], kc.ap[:, g, 1, 0:127], 127, extra, None, pv_c, [kc]))
            if g == 1:
                jobs.append((load_next, None))

            def mk_pv(Pacc, vv, j, first, last, br, g=g, w=w, gv=gv, osl=osl, oacc=oacc):
                def pv(px):
                    for hs in range(4):
                        S.mm(Pacc.ap[:, hs * 65:(hs + 1) * 65], px.ap[:, hs * 128:(hs + 1) * 128], vv.ap[:, j, g, :],
                             first and hs == 0, last and hs == 3, [px, vv], [Pacc])
                    if last:
                        p3 = Pacc.ap[:, 0:260].rearrange("p (h c) -> p h c", h=4)
                        rl = w["rl"]
                        o = 4 * br
                        S.v("dve", "tensor_scalar", [Pacc], [rl], rl.ap[:, o:o + 4], p3[:, :, 64], 1e-30, None, ALU.max)
                        S.v("dve", "reciprocal", [rl], [rl], rl.ap[:, o:o + 4], rl.ap[:, o:o + 4])
                        S.v("dve", "tensor_tensor", [rl, gates], [rl], rl.ap[:, o:o + 4], rl.ap[:, o:o + 4], gv[:, :, br], ALU.mult)
                        tmp = w["tmp"]
                        t3 = tmp.ap.rearrange("p (h d) -> p h d", h=4)
                        S.v("dve", "tensor_tensor", [Pacc, rl], [tmp], t3, p3[:, :, 0:64],
                            rl.ap[:, o:o + 4].unsqueeze(2).to_broadcast([128, 4, 64]), ALU.mult)
                        S.v("dve", "tensor_tensor", [tmp, oacc.sub(g)], [oacc.sub(g)], osl, osl, t3, ALU.add)
                return pv

            js = list(range(max(0, i - 4), i + 1))
            for j in js:
                dd = {0: 0, 1: 1, 4: 2}.get(i - j)
                bias_ap = None if dd is None else biasT.ap[:, dd, 4 * g:4 * g + 4, :].rearrange("p h t -> p (h t)")
                jobs.append(score_job(qi, g, kw.ap[:, g, 0, j * 128:(j + 1) * 128], kw.ap[:, g, 1, j * 128:(j + 1) * 128],
                                      128, None, bias_ap, mk_pv(Pow_, vw, j, j == js[0], j == js[-1], 2), [kw]))
            for j in range(i + 1):
                dd = {0: 0, 1: 1}.get(i - j)
                bias_ap = None if dd is None else biasT.ap[:, dd, 4 * g:4 * g + 4, :].rearrange("p h t -> p (h t)")
                jobs.append(score_job(qi, g, ks.ap[:, g, 0, j * 128:(j + 1) * 128], ks.ap[:, g, 1, j * 128:(j + 1) * 128],
                                      128, None, bias_ap, mk_pv(Pos_, vs, j, j == 0, j == i, 1), [ks], use_mask=True))

        def finish(i=i, oacc=oacc):
            ob = obst[i % 2]
            for half in range(2):
                P = score_banks[cnt["sb"] % NSB]
                cnt["sb"] += 1
                for q in range(4):
                    kt = half * 4 + q
                    S.tr(P.ap[:, q * 128:(q + 1) * 128], oacc.ap[:, kt * 128:(kt + 1) * 128], ident.ap,
                         [oacc.sub(kt // 2), ident], [P])
                S.act(ob.ap[:, half * 4:(half + 1) * 4, :], P.ap.rearrange("p (q t) -> p q t", q=4), AF.Copy, [P], [ob])
            S.dma("sp", obT_v[:, :, i * 128:(i + 1) * 128], ob.ap, [ob], [self.obT_d])
        jobs.append((None, finish))

    pend = []
    for (qk, pv) in jobs:
        if len(pend) >= 3:
            f = pend.pop(0)
            if f is not None:
                f()
        if qk is not None:
            qk()
        pend.append(pv)
    for f in pend:
        if f is not None:
            f()
    S.barrier()
    ar.pop()


KB.stage4 = stage4


def nsa_bias_build(self):
    ar, S = self.ar, self.S
    d = self.din
    NOH = 3 * 16384 + 17 * 128
    oh_d = d("nsa_oh", [33, NOH])
    relb_d = d("rel_bias", [32, 16])
    self.bias_d = bias_d = self.dscr("bias_scr", [16, NOH])
    trel = ar.alloc([16], F32, "trel", parts=33)
    tbl = ar.alloc([16], F32, "tbl", parts=32)
    t31 = ar.alloc([16], F32, "t31", parts=32)
    S.dma("sp", tbl.ap, relb_d.ap, [relb_d], [tbl])
    S.dma("sp", t31.ap, relb_d.ap[31:32, :].partition_broadcast(32), [relb_d], [t31])
    S.v("dve", "memset", [], [trel], trel.ap, NEG)
    S.v("dve", "tensor_tensor", [tbl, t31, trel], [trel], trel.ap[0:32, :], tbl.ap, t31.ap, ALU.subtract)
    ohb = [ar.alloc([2048], F32, f"ohb{i}", parts=33) for i in range(2)]
    bsb = [ar.alloc([2048], F32, f"bsb{i}", parts=16) for i in range(2)]
    nblk = (NOH + 2047) // 2048

    def mk(bi):
        def step():
            c0 = bi * 2048
            n = min(2048, NOH - c0)
            ob, bs = ohb[bi % 2], bsb[bi % 2]
            S.dma("sp", ob.ap[:, 0:n], oh_d.ap[:, c0:c0 + n], [oh_d], [ob])
            for q in range((n + 511) // 512):
                w = min(512, n - q * 512)
                P = self.bank()
                S.mm(P.ap[0:16, 0:w], trel.ap, ob.ap[:, q * 512:q * 512 + w], True, True, [trel, ob], [P])
                S.act(bs.ap[:, q * 512:q * 512 + w], P.ap[0:16, 0:w], AF.Copy, [P], [bs])
            S.dma("sp", bias_d.ap[:, c0:c0 + n], bs.ap[:, 0:n], [bs], [bias_d])
        return step
    return [mk(bi) for bi in range(nblk)]


KB.nsa_bias_build = nsa_bias_build


LAM = 0.6065306597126334


def rwkv_consts():
    c = {}
    p = np.arange(128)
    ut_strict = (p[:, None] < p[None, :]).astype(np.float32)
    ut_incl = (p[:, None] <= p[None, :]).astype(np.float32)
    c["rw_mAB"] = np.ascontiguousarray(np.concatenate([ut_strict, ut_incl], 1))
    c["rw_mLT"] = (p[:, None] > p[None, :]).astype(np.float32)
    rs = np.ones((128, 8, 128), np.float32)
    rs[:, :, 0] = 0.0
    c["rw_reset"] = rs.reshape(128, 1024)
    hm = np.zeros((128, 2), np.float32)
    hm[:64, 0] = 1.0
    hm[64:, 1] = 1.0
    c["rw_hsel"] = hm
    return c


def rwkv_prep(inp, m):
    g = lambda k: inp[k][0]
    m["rw_w0"] = _fm(g("rwkv_w0"))
    m["rw_a0"] = _fm(g("rwkv_a0"))
    m["rw_kk"] = _fm(g("rwkv_k_k"))
    m["rw_ka"] = _fm(g("rwkv_k_a"))
    m["rw_rk"] = _fm(g("rwkv_r_k").reshape(-1))
    m["rw_lnw"] = np.ascontiguousarray(np.broadcast_to(g("rwkv_ln_w")[None, :], (128, 1024)).astype(np.float32))
    m["rw_lnb"] = np.ascontiguousarray(np.broadcast_to(g("rwkv_ln_b")[None, :], (128, 1024)).astype(np.float32))
    z = np.zeros((64, 1024), np.float32)
    m["rw_w2pad"] = np.ascontiguousarray(np.concatenate([g("rwkv_w2"), z], 0))
    m["rw_a2pad"] = np.ascontiguousarray(np.concatenate([z, g("rwkv_a2")], 0))
    m["rw_g2"] = g("rwkv_g2")


def stage3(self):
    ar, S = self.ar, self.S
    d = self.din
    ident, bones = self.ident, self.bones
    self.oaT_d = self.dscr("oaT", [1024, SEQ], BF16)
    ar.push()

    def cload(name, shape, parts=128, src=None):
        dt_ = d(name, [parts] + list(shape)) if src is None else src
        t = ar.alloc(shape, F32, name, parts=parts)
        S.dma("sp", t.ap, dt_.ap, [dt_], [t])
        return t
    w0 = cload("rw_w0", [8])
    a0 = cload("rw_a0", [8])
    kkf = cload("rw_kk", [8])
    kaf = cload("rw_ka", [8])
    rkf = cload("rw_rk", [8])
    lnw = cload("rw_lnw", [1024])
    lnb = cload("rw_lnb", [1024])
    w2p = cload("rw_w2pad", [1024])
    a2p = cload("rw_a2pad", [1024])
    g2_d = d("rw_g2", [160, 1024])
    g2a = ar.alloc([1024], F32, "g2a")
    g2b = ar.alloc([1024], F32, "g2b", parts=32)
    S.dma("sp", g2a.ap, g2_d.ap[0:128, :], [g2_d], [g2a])
    S.dma("sp", g2b.ap, g2_d.ap[128:160, :], [g2_d], [g2b])
    mAB = cload("rw_mAB", [256])
    mLT = cload("rw_mLT", [128])
    reset = cload("rw_reset", [1024])
    hsel = cload("rw_hsel", [2])
    omka = ar.alloc([8], F32, "omka")
    S.v("dve", "tensor_scalar", [kaf], [omka], omka.ap, kaf.ap, -1.0, 1.0, ALU.mult, ALU.add)
    Hp = ar.alloc([16, 64], F32, "Hp")
    S.v("dve", "memset", [], [Hp], Hp.ap, 0.0)

    A = lambda n, shape=(8, 128): ar.alloc(list(shape), F32, n)
    raw = A("raw", (27, 128))
    sgw, cs, E, Eex = A("sgw"), A("cs"), A("E"), A("Eex")
    aT, kkn, kp, bb, btl = A("aT"), A("kkn"), A("kp"), A("bb"), A("btl")
    AR = A("AR", (8, 2, 128))
    blo, bhi, klo, khi, alo, ahi = A("blo"), A("bhi"), A("klo"), A("khi"), A("alo"), A("ahi")
    tmpA, tmpB = A("tmpA"), A("tmpB")
    bh, kh = sgw, cs
    v_tok, bh_tok, kh_tok, g_tok = A("v_tok", (1024,)), A("bh_tok", (1024,)), A("kh_tok", (1024,)), A("g_tok", (1024,))
    th = A("th", (128,))
    sx = A("sx", (128,))
    sx2 = ar.alloc([128], F32, "sx2", parts=32)
    PLs = [A("PL0", (8,)), A("PL1", (8,))]
    nb = A("nb", (8,))
    rk16 = A("rk16", (16,))
    st16 = [A(f"st16_{i}", (16,)) for i in range(4)]
    import os
    SQDT = BF16 if os.environ.get("RW_SQ") == "bf16" else F32
    MASK_POOL = os.environ.get("RW_MASK") == "pool"
    NO_IL = os.environ.get("RW_IL") == "0"
    slots = []
    for s_ in range(2):
        slots.append(dict(
            ABm=A(f"ABm{s_}", (4, 256)), AKm=A(f"AKm{s_}", (4, 256)),
            Yf=[A(f"Yf{s_}{i}", (4, 128)) for i in range(2)] if SQDT != F32 else None,
            Yb=[ar.alloc([4, 128], SQDT, f"Yb{s_}{i}") for i in range(2)],
            XW=[A(f"XW{s_}{i}", (4, 192)) for i in range(2)]))
        if SQDT == F32:
            slots[-1]["Yf"] = slots[-1]["Yb"]
    if SQDT == F32:
        ar.off -= 0
    oast = [ar.alloc([8, 128], BF16, f"oast{i}") for i in range(1)] * 2
    f2 = lambda t: t.ap.rearrange("p a b -> p (a b)")
    rw_v = self.rwT_d.ap.rearrange("(kt p) t -> p kt t", p=128)
    oaT_v = self.oaT_d.ap.rearrange("(kt p) t -> p kt t", p=128)
    dv = lambda meth, reads, writes, *a, **k: S.v("dve", meth, reads, writes, *a, **k)
    pl = lambda meth, reads, writes, *a, **k: S.v("pool", meth, reads, writes, *a, **k)
    bc8 = lambda t: t.ap.unsqueeze(2).to_broadcast([128, 8, 128])

    def P1(c):
            S.dma("sp", raw.ap, rw_v[:, :, c * 128:(c + 1) * 128], [self.rwT_d], [raw])
            yield
            rT, kT, vT = raw.ap[:, 0:8, :], raw.ap[:, 8:16, :], raw.ap[:, 16:24, :]
            yield
            t24 = raw.ap[:, 24, :]
            yield
            S.act(th.ap, t24, AF.Tanh, [raw], [th])
            yield
            Pz = [self.bank(), self.bank()]
            yield
            for kt in range(8):
                P = Pz[kt // 4]
                S.mm(P.ap[:, (kt % 4) * 128:(kt % 4 + 1) * 128], w2p.ap[:, kt * 128:(kt + 1) * 128], th.ap, True, True, [w2p, th], [P])
            yield
            for kt in range(8):
                S.act(sgw.ap[:, kt, :], Pz[kt // 4].ap[:, (kt % 4) * 128:(kt % 4 + 1) * 128], AF.Sigmoid, [Pz[kt // 4], w0], [sgw],
                      bias=w0.ap[:, kt:kt + 1], scale=1.0)
            yield
            Pa = [self.bank(), self.bank()]
            yield
            for kt in range(8):
                P = Pa[kt // 4]
                S.mm(P.ap[:, (kt % 4) * 128:(kt % 4 + 1) * 128], a2p.ap[:, kt * 128:(kt + 1) * 128], t24, True, True, [a2p, raw], [P])
            yield
            for kt in range(8):
                S.act(aT.ap[:, kt, :], Pa[kt // 4].ap[:, (kt % 4) * 128:(kt % 4 + 1) * 128], AF.Sigmoid, [Pa[kt // 4], a0], [aT],
                      bias=a0.ap[:, kt:kt + 1], scale=1.0)
            yield
            dv("tensor_tensor_scan", [reset, sgw], [cs], f2(cs), reset.ap, f2(sgw), 0.0, ALU.mult, ALU.add)
            yield
            S.act(f2(E), f2(cs), AF.Exp, [cs], [E], scale=-LAM)
            yield
            dv("tensor_tensor", [cs, sgw], [Eex], f2(Eex), f2(cs), f2(sgw), ALU.subtract)
            yield
            S.act(f2(Eex), f2(Eex), AF.Exp, [Eex], [Eex], scale=-LAM)
            yield
            dv("tensor_scalar", [cs], [nb], nb.ap, cs.ap[:, :, 127], -LAM, None, ALU.mult)
            yield
            S.act(PLs[c % 2].ap, nb.ap, AF.Exp, [nb], [PLs[c % 2]])
            yield
            dv("tensor_tensor", [raw, kkf], [kkn], kkn.ap, kT, bc8(kkf), ALU.mult)
            yield
            dv("tensor_tensor", [kkn], [tmpB], tmpB.ap, kkn.ap, kkn.ap, ALU.mult)
            yield
            Pn = [self.bank(), self.bank()]
            yield
            for hh in range(2):
                S.mm(Pn[hh].ap, bones.ap, tmpB.ap[:, hh * 4:(hh + 1) * 4, :], True, True, [bones, tmpB], [Pn[hh]])
            yield
            for hh in range(2):
                S.act(tmpB.ap[:, hh * 4:(hh + 1) * 4, :], Pn[hh].ap.rearrange("p (a b) -> p a b", a=4), AF.Sqrt, [Pn[hh]], [tmpB])
            yield
            dv("tensor_scalar", [tmpB], [tmpB], f2(tmpB), f2(tmpB), 1e-12, None, ALU.max)
            yield
            dv("reciprocal", [tmpB], [tmpB], f2(tmpB), f2(tmpB))
            yield
            dv("tensor_tensor", [kkn, tmpB], [kkn], f2(kkn), f2(kkn), f2(tmpB), ALU.mult)
            yield
            dv("tensor_tensor", [aT, kaf], [kp], kp.ap, aT.ap, bc8(kaf), ALU.mult)
            yield
            dv("tensor_tensor", [kp, omka], [kp], kp.ap, kp.ap, bc8(omka), ALU.add)
            yield
            dv("tensor_tensor", [kp, raw], [kp], kp.ap, kp.ap, kT, ALU.mult)
            yield
            dv("tensor_tensor", [kkn, aT], [bb], f2(bb), f2(kkn), f2(aT), ALU.mult)
            yield


    def P2(c):
            rT, kT, vT = raw.ap[:, 0:8, :], raw.ap[:, 8:16, :], raw.ap[:, 16:24, :]
            dv("tensor_tensor", [raw, E], [AR], AR.ap[:, :, 1, :], rT, E.ap, ALU.mult)
            dv("scalar_tensor_tensor", [kkn, Eex], [AR], AR.ap[:, :, 0, :], kkn.ap, -1.0, Eex.ap, ALU.mult, ALU.mult)
            Einv, Elast = E, Eex
            S.act(f2(Einv), f2(cs), AF.Exp, [cs], [Einv], scale=LAM)
            for kt in range(8):
                S.act(Elast.ap[:, kt, :], cs.ap[:, kt, :], AF.Exp, [cs, nb], [Elast], bias=nb.ap[:, kt:kt + 1], scale=LAM)
            dv("tensor_tensor", [bb, Einv], [btl], f2(btl), f2(bb), f2(Einv), ALU.mult)
            dv("tensor_tensor", [kp, Einv], [tmpA], f2(tmpA), f2(kp), f2(Einv), ALU.mult)
            dv("tensor_tensor", [bb, Elast], [bh], f2(bh), f2(bb), f2(Elast), ALU.mult)
            dv("tensor_tensor", [kp, Elast], [kh], f2(kh), f2(kp), f2(Elast), ALU.mult)
            for (dst, src, col) in ((blo, btl, 0), (bhi, btl, 1), (klo, tmpA, 0), (khi, tmpA, 1)):
                if MASK_POOL:
                    pl("tensor_scalar", [src, hsel], [dst], f2(dst), f2(src), hsel.ap[:, col:col + 1], None, ALU.mult)
                    continue
                S.act(f2(dst), f2(src), AF.Identity, [src, hsel], [dst], scale=hsel.ap[:, col:col + 1], bias=0.0)
            for (dst, col) in ((alo, 0), (ahi, 1)):
                if MASK_POOL:
                    pl("tensor_scalar", [AR, hsel], [dst], dst.ap, AR.ap[:, :, 0, :], hsel.ap[:, col:col + 1], None, ALU.mult)
                    continue
                S.act(dst.ap, AR.ap[:, :, 0, :], AF.Identity, [AR, hsel], [dst], scale=hsel.ap[:, col:col + 1], bias=0.0)
            dv("tensor_tensor", [raw, kp], [tmpB], tmpB.ap, rT, kp.ap, ALU.mult)
            dv("tensor_tensor", [tmpB, rkf], [tmpB], tmpB.ap, tmpB.ap, bc8(rkf), ALU.mult)
            Pr = self.bank()
            for kt in range(8):
                S.mm(Pr.ap[:, 2 * kt:2 * kt + 2], tmpB.ap[:, kt, :], hsel.ap, kt == 0, kt == 7, [tmpB, hsel], [Pr])
            S.act(rk16.ap, Pr.ap[:, 0:16], AF.Copy, [Pr], [rk16])
            S.act(sx.ap, raw.ap[:, 25, :], AF.Sigmoid, [raw], [sx])
            S.act(sx2.ap, raw.ap[0:32, 26, :], AF.Sigmoid, [raw], [sx2])
            for hh in range(2):
                P = self.bank()
                S.mm(P.ap, sx.ap, g2a.ap[:, hh * 512:(hh + 1) * 512], True, False, [sx, g2a], [P])
                S.mm(P.ap, sx2.ap, g2b.ap[:, hh * 512:(hh + 1) * 512], False, True, [sx2, g2b], [P])
                S.act(g_tok.ap[:, hh * 512:(hh + 1) * 512], P.ap, AF.Copy, [P], [g_tok])
            for (src_ap, src_t, dst) in ((vT, raw, v_tok), (bh.ap, bh, bh_tok), (kh.ap, kh, kh_tok)):
                for hh in range(2):
                    P = self.bank()
                    for q in range(4):
                        kt = hh * 4 + q
                        S.tr(P.ap[:, q * 128:(q + 1) * 128], src_ap[:, kt, :], ident.ap, [src_t, ident], [P])
                    S.act(dst.ap[:, hh * 512:(hh + 1) * 512], P.ap, AF.Copy, [P], [dst])


    def heads(c, step):
            y_tok = tmpA
            def phaseA(hg, sl):
                heads = [4 * hg + x for x in range(4)]
                ABm, AKm = sl["ABm"], sl["AKm"]
                PA = [self.bank(), self.bank()]
                PB = [self.bank(), self.bank()]
                PX = self.bank()
                for hl, h in enumerate(heads):
                    kt, half = h // 2, h % 2
                    bsel = (blo, bhi)[half]
                    ksel = (klo, khi)[half]
                    asel = (alo, ahi)[half]
                    ar_rhs = AR.ap[:, kt, :, :]
                    oa = PA[hl // 2].ap[:, (hl % 2) * 256:(hl % 2 + 1) * 256]
                    ob_ = PB[hl // 2].ap[:, (hl % 2) * 256:(hl % 2 + 1) * 256]
                    S.mm(oa, bsel.ap[:, kt, :], ar_rhs, hl % 2 == 0, hl % 2 == 1, [bsel, AR], [PA[hl // 2]])
                    S.mm(ob_, ksel.ap[:, kt, :], ar_rhs, hl % 2 == 0, hl % 2 == 1, [ksel, AR], [PB[hl // 2]])
                    S.mm(PX.ap[:, hl * 128:(hl + 1) * 128], asel.ap[:, kt, :], btl.ap[:, kt, :], hl == 0, hl == 3, [asel, btl], [PX])
                mAB2 = mAB.ap.unsqueeze(1).to_broadcast([128, 2, 256])
                for q in range(2):
                    dv("tensor_tensor", [PA[q], mAB], [ABm], ABm.ap[:, 2 * q:2 * q + 2, :],
                       PA[q].ap.rearrange("p (a b) -> p a b", a=2), mAB2, ALU.mult)
                    dv("tensor_tensor", [PB[q], mAB], [AKm], AKm.ap[:, 2 * q:2 * q + 2, :],
                       PB[q].ap.rearrange("p (a b) -> p a b", a=2), mAB2, ALU.mult)
                dv("tensor_tensor", [PX, mLT], [sl["XW"][0]], sl["XW"][0].ap[:, :, 0:128], PX.ap.rearrange("p (a b) -> p a b", a=4),
                   mLT.ap.unsqueeze(1).to_broadcast([128, 4, 128]), ALU.mult)
                S.act(sl["Yb"][0].ap, ABm.ap[:, :, 0:128], AF.Copy, [ABm], [sl["Yb"][0]])
                PW = self.bank()
                for hl, h in enumerate(heads):
                    kt = h // 2
                    o = PW.ap[:, hl * 64:(hl + 1) * 64]
                    S.mm(o, AR.ap[:, kt, 0, :], Hp.ap[:, h, :], hl == 0, False, [AR, Hp.sub(h)], [PW])
                    S.mm(o, AKm.ap[:, hl, 0:128], v_tok.ap[:, h * 64:(h + 1) * 64], False, hl == 3, [AKm, v_tok], [PW])
                S.act(sl["XW"][0].ap[:, :, 128:192], PW.ap[:, 0:256].rearrange("p (h d) -> p h d", h=4), AF.Copy, [PW], [sl["XW"][0]])

            def level(sl, lv):
                Y, XW = sl["Yb"][lv % 2], sl["XW"][lv % 2]
                Yn, XWn = sl["Yb"][(lv + 1) % 2], sl["XW"][(lv + 1) % 2]
                if lv < 6:
                    PUX = [self.bank(), self.bank()]
                    for hl in range(4):
                        o = PUX[hl // 2].ap[:, (hl % 2) * 192:(hl % 2 + 1) * 192]
                        S.mm(o, Y.ap[:, hl, :], XW.ap[:, hl, :], hl % 2 == 0, hl % 2 == 1, [Y, XW], [PUX[hl // 2]])
                    PY2 = self.bank()
                    for hl in range(4):
                        S.mm(PY2.ap[:, hl * 128:(hl + 1) * 128], XW.ap[:, hl, 0:128], Y.ap[:, hl, :], hl == 0, hl == 3, [XW, Y], [PY2])
                    for q in range(2):
                        view = PUX[q].ap[:, 0:384].rearrange("p (a c) -> p a c", a=2)
                        dv("tensor_tensor", [PUX[q], XW], [XWn], XWn.ap[:, 2 * q:2 * q + 2, 128:192], view[:, :, 128:192],
                           XW.ap[:, 2 * q:2 * q + 2, 128:192], ALU.add)
                        S.act(XWn.ap[:, 2 * q:2 * q + 2, 0:128], view[:, :, 0:128], AF.Copy, [PUX[q]], [XWn])
                    dv("tensor_copy", [PY2], [Yn], f2(Yn), PY2.ap)
                else:
                    PU = self.bank()
                    for hl in range(4):
                        S.mm(PU.ap[:, hl * 64:(hl + 1) * 64], Y.ap[:, hl, :], XW.ap[:, hl, 128:192], hl == 0, hl == 3, [Y, XW], [PU])
                    dv("tensor_tensor", [PU, XW], [XWn], XWn.ap[:, :, 128:192], PU.ap[:, 0:256].rearrange("p (h d) -> p h d", h=4),
                       XW.ap[:, :, 128:192], ALU.add)

            def phaseY(hg, sl):
                heads = [4 * hg + x for x in range(4)]
                ABm, AKm = sl["ABm"], sl["AKm"]
                U = sl["XW"][1]
                PY = self.bank()
                for hl, h in enumerate(heads):
                    kt = h // 2
                    o = PY.ap[:, hl * 64:(hl + 1) * 64]
                    S.mm(o, AR.ap[:, kt, 1, :], Hp.ap[:, h, :], hl == 0, False, [AR, Hp.sub(h)], [PY])
                    S.mm(o, ABm.ap[:, hl, 128:256], U.ap[:, hl, 128:192], False, False, [ABm, U], [PY])
                    S.mm(o, AKm.ap[:, hl, 128:256], v_tok.ap[:, h * 64:(h + 1) * 64], False, hl == 3, [AKm, v_tok], [PY])
                PH = self.bank()
                for hl, h in enumerate(heads):
                    kt = h // 2
                    o = PH.ap[:, hl * 64:(hl + 1) * 64]
                    S.mm(o, bh_tok.ap[:, kt * 128:(kt + 1) * 128], U.ap[:, hl, 128:192], hl == 0, False, [bh_tok, U], [PH])
                    S.mm(o, kh_tok.ap[:, kt * 128:(kt + 1) * 128], v_tok.ap[:, h * 64:(h + 1) * 64], False, hl == 3, [kh_tok, v_tok], [PH])
                S.act(y_tok.ap.rearrange("p a b -> p (a b)")[:, hg * 256:(hg + 1) * 256], PY.ap[:, 0:256], AF.Copy, [PY], [y_tok])
                for hl, h in enumerate(heads):
                    kt, half = h // 2, h % 2
                    r0 = 64 * half
                    dv("scalar_tensor_tensor", [Hp.sub(h), PLs[c % 2], PH], [Hp.sub(h)], Hp.ap[r0:r0 + 64, h, :], Hp.ap[r0:r0 + 64, h, :],
                       PLs[c % 2].ap[r0:r0 + 64, kt:kt + 1], PH.ap[r0:r0 + 64, hl * 64:(hl + 1) * 64], ALU.mult, ALU.add)

            for pair in range(2):
                gA, gB = 2 * pair, 2 * pair + 1
                if NO_IL:
                    for (g_, sl_) in ((gA, slots[0]), (gB, slots[1])):
                        phaseA(g_, sl_)
                        for lv in range(7):
                            level(sl_, lv)
                        phaseY(g_, sl_)
                    continue
                phaseA(gA, slots[0])
                phaseA(gB, slots[1])
                for lv in range(7):
                    level(slots[0], lv)
                    step(); step()
                    level(slots[1], lv)
                    step(); step()
                phaseY(gA, slots[0])
                phaseY(gB, slots[1])


    def post(c):
            y_tok = tmpA
            yf = y_tok.ap.rearrange("p a b -> p (a b)")
            y3 = yf.rearrange("p (h d) -> p h d", h=16)
            sum_, sq_, mean, rstd = st16
            t1, t2 = y_tok, btl
            t1f, t2f = f2(t1), f2(t2)
            dv("tensor_reduce", [y_tok], [sum_], sum_.ap, y3, AX.X, ALU.add)
            dv("tensor_tensor", [y_tok], [t2], t2f, yf, yf, ALU.mult)
            dv("tensor_reduce", [t2], [sq_], sq_.ap, t2f.rearrange("p (h d) -> p h d", h=16), AX.X, ALU.add)
            dv("tensor_scalar", [sum_], [mean], mean.ap, sum_.ap, 1.0 / 64, None, ALU.mult)
            dv("tensor_tensor", [mean], [rstd], rstd.ap, mean.ap, mean.ap, ALU.mult)
            dv("scalar_tensor_tensor", [sq_, rstd], [rstd], rstd.ap, sq_.ap, 1.0 / 64, rstd.ap, ALU.mult, ALU.subtract)
            S.act(rstd.ap, rstd.ap, AF.Sqrt, [rstd], [rstd], bias=GN_EPS, scale=1.0)
            dv("reciprocal", [rstd], [rstd], rstd.ap, rstd.ap)
            b16 = lambda t: t.ap.unsqueeze(2).to_broadcast([128, 16, 64])
            t13 = t1f.rearrange("p (h d) -> p h d", h=16)
            t23 = t2f.rearrange("p (h d) -> p h d", h=16)
            dv("tensor_tensor", [y_tok, mean], [t1], t13, y3, b16(mean), ALU.subtract)
            dv("tensor_tensor", [t1, rstd], [t1], t13, t13, b16(rstd), ALU.mult)
            dv("tensor_tensor", [t1, lnw], [t1], t1f, t1f, lnw.ap, ALU.mult)
            dv("tensor_tensor", [t1, lnb], [t1], t1f, t1f, lnb.ap, ALU.add)
            dv("tensor_tensor", [v_tok, rk16], [t2], t23, v_tok.ap.rearrange("p (h d) -> p h d", h=16), b16(rk16), ALU.mult)
            dv("tensor_tensor", [t1, t2], [t1], t1f, t1f, t2f, ALU.add)
            dv("tensor_tensor", [t1, g_tok], [t1], t1f, t1f, g_tok.ap, ALU.mult)
            if self.dbg and c == 0:
                dd = self.dscr("dbg_oa0", [128, 1024])
                S.dma("sp", dd.ap, t1f, [t1], [dd])
                dd2 = self.dscr("dbg_y0", [128, 1024])
                S.dma("sp", dd2.ap, yf, [y_tok], [dd2])
            ob = oast[c % 2]
            for hh in range(2):
                P = self.bank()
                for q in range(4):
                    kt = hh * 4 + q
                    S.tr(P.ap[:, q * 128:(q + 1) * 128], t1f[:, kt * 128:(kt + 1) * 128], ident.ap, [t1, ident], [P])
                S.act(ob.ap[:, hh * 4:(hh + 1) * 4, :], P.ap.rearrange("p (q t) -> p q t", q=4), AF.Copy, [P], [ob])
            S.dma("sp", oaT_v[:, :, c * 128:(c + 1) * 128], ob.ap, [ob], [self.oaT_d])


    gen = P1(0)
    for _ in gen:
        pass
    for c in range(NT):
        P2(c)
        gen = P1(c + 1) if c + 1 < NT else iter(())

        def step(gen=gen):
            next(gen, None)
        heads(c, step)
        for _ in gen:
            pass
        post(c)
    S.barrier()
    ar.pop()


KB.stage3 = stage3


def stage5(self):
    ar, S = self.ar, self.S
    d = self.din
    wor_d = d("w_o_rwkv", [1024, D])
    won_d = d("w_o_nsa", [1024, D])
    wout_d = d("w_out", [D, D])
    ar.push()
    mixT = ar.alloc([16, SEQ], BF16, "mixT")
    ar.push()
    oaT = ar.alloc([8, SEQ], BF16, "oaT")
    obT = ar.alloc([8, SEQ], BF16, "obT")
    S.dma("sp", oaT.ap, self.oaT_d.ap.rearrange("(k p) t -> p k t", p=128), [self.oaT_d], [oaT])
    S.dma("sp", obT.ap, self.obT_d.ap.rearrange("(k p) t -> p k t", p=128), [self.obT_d], [obT])
    woa = [ar.alloc([8, 128], BF16, f"woa{i}") for i in range(2)]
    wob = [ar.alloc([8, 128], BF16, f"wob{i}") for i in range(2)]
    sga = [ar.alloc([SEQ], BF16, f"sga{i}") for i in range(2)]
    sgb = [ar.alloc([SEQ], BF16, f"sgb{i}") for i in range(2)]
    t1s = [ar.alloc([512], F32, f"t1_{i}") for i in range(2)]
    t2s = [ar.alloc([512], F32, f"t2_{i}") for i in range(2)]
    wor_v = wor_d.ap.rearrange("(k p) c -> p k c", p=128)
    won_v = won_d.ap.rearrange("(k p) c -> p k c", p=128)
    cnt = 0
    mod_steps = []
    for jt in range(16):
        wa, wb, sa, sb = woa[jt % 2], wob[jt % 2], sga[jt % 2], sgb[jt % 2]
        S.dma("pool", wa.ap, wor_v[:, :, jt * 128:(jt + 1) * 128], [wor_d], [wa])
        S.dma("pool", wb.ap, won_v[:, :, jt * 128:(jt + 1) * 128], [won_d], [wb])
        S.dma("sp", sa.ap, self.mgT_d.ap[jt * 128:(jt + 1) * 128, :], [self.mgT_d], [sa])
        S.dma("sp", sb.ap, self.mgT_d.ap[2048 + jt * 128:2048 + (jt + 1) * 128, :], [self.mgT_d], [sb])
        if jt >= 1:
            for _ in range(2):
                if mod_steps:
                    mod_steps.pop(0)()
        for n in range(4):
            Pa = self.bank()
            for k in range(8):
                S.mm(Pa.ap, wa.ap[:, k, :], oaT.ap[:, k, n * 512:(n + 1) * 512], k == 0, k == 7, [wa, oaT], [Pa])
            Pb = self.bank()
            for k in range(8):
                S.mm(Pb.ap, wb.ap[:, k, :], obT.ap[:, k, n * 512:(n + 1) * 512], k == 0, k == 7, [wb, obT], [Pb])
            t1, t2 = t1s[cnt % 2], t2s[cnt % 2]
            cnt += 1
            S.v("dve", "tensor_tensor", [Pa, sa], [t1], t1.ap, Pa.ap, sa.ap[:, n * 512:(n + 1) * 512], ALU.mult)
            S.v("dve", "tensor_tensor", [Pb, sb], [t2], t2.ap, Pb.ap, sb.ap[:, n * 512:(n + 1) * 512], ALU.mult)
            S.v("dve", "tensor_tensor", [t1, t2], [mixT.sub(n)], mixT.ap[:, jt, n * 512:(n + 1) * 512], t1.ap, t2.ap, ALU.add)
    while mod_steps:
        mod_steps.pop(0)()
    S.barrier()
    ar.pop()
    wout = ar.alloc([16, D], BF16, "wout")
    wout_v = wout_d.ap.rearrange("(k p) c -> p k c", p=128)
    for nn in range(4):
        S.dma("pool", wout.ap[:, :, nn * 512:(nn + 1) * 512], wout_v[:, :, nn * 512:(nn + 1) * 512], [wout_d], [wout.sub(nn)])
    xbs = [ar.alloc([D], F32, f"x5_{i}") for i in range(2)]
    obs = [ar.alloc([D], F32, f"o5_{i}") for i in range(2)]
    tms = [ar.alloc([512], F32, f"tm5_{i}") for i in range(2)]
    cnt = 0
    for tt in range(NT):
        xb, ob = xbs[tt % 2], obs[tt % 2]
        S.dma("sp", xb.ap, self.x_d.ap[tt * 128:(tt + 1) * 128, :], [self.x_d], [xb])
        for nn in range(4):
            P = self.bank()
            for k in range(16):
                S.mm(P.ap, mixT.ap[:, k, tt * 128:(tt + 1) * 128], wout.ap[:, k, nn * 512:(nn + 1) * 512], k == 0, k == 15,
                     [mixT.sub(tt // 4), wout.sub(nn)], [P])
            tm = tms[cnt % 2]
            cnt += 1
            S.v("dve", "tensor_tensor", [P, self.gt1], [tm], tm.ap, P.ap, self.gt1.ap[:, nn * 512:(nn + 1) * 512], ALU.mult)
            S.v("dve", "tensor_tensor", [tm, xb], [ob], ob.ap[:, nn * 512:(nn + 1) * 512], tm.ap, xb.ap[:, nn * 512:(nn + 1) * 512], ALU.add)
        S.dma("pool", self.x1_d.ap[tt * 128:(tt + 1) * 128, :], ob.ap, [ob], [self.x1_d])
    S.barrier()
    ar.pop()


def stage6(self):
    ar, S = self.ar, self.S
    d = self.din
    hT = self.hT
    ar.push()
    uT = ar.alloc([64, 512], BF16, "uT")
    wups = [ar.alloc([16, 256], BF16, f"wup{i}") for i in range(2)]
    wdns = [ar.alloc([8, 512], BF16, f"wdn{i}") for i in range(3)]
    rts = [ar.alloc([512], F32, f"rt{i}") for i in range(2)]
    xps = [ar.alloc([512], F32, f"xp{i}") for i in range(2)]
    ops_ = [ar.alloc([512], F32, f"op{i}") for i in range(2)]
    tms = [ar.alloc([512], F32, f"tm6_{i}") for i in range(2)]
    c_up = c_dn = c_e = 0
    for c in range(4):
        for fb in range(32):
            wu = wups[c_up % 2]
            c_up += 1
            S.dma("sp", wu.ap, self.wupb.ap[fb].rearrange("p (k c) -> p k c", k=16), [self.wupb], [wu])
            for ft in range(2):
                f = fb * 2 + ft
                P = self.ps[(c_e) % 4]
                rt = rts[c_e % 2]
                c_e += 1
                for k in range(16):
                    S.mm(P.ap, wu.ap[:, k, ft * 128:(ft + 1) * 128], hT.ap[:, k, c * 512:(c + 1) * 512], k == 0, k == 15,
                         [wu, hT.sub(c)], [P])
                S.act(rt.ap, P.ap, AF.Relu, [P], [rt])
                S.v("dve", "tensor_tensor", [rt], [uT.sub(f)], uT.ap[:, f, :], rt.ap, rt.ap, ALU.mult)
        for nn in range(4):
            accs = self.ps[4:8] if (nn % 2 == 0) else self.ps[0:4]
            for f8 in range(8):
                wd = wdns[c_dn % 3]
                c_dn += 1
                S.dma("sp", wd.ap, self.wdnb.ap[nn, f8].rearrange("p (f c) -> p f c", f=8), [self.wdnb], [wd])
                for fi in range(8):
                    f = f8 * 8 + fi
                    for tt in range(4):
                        S.mm(accs[tt].ap, uT.ap[:, f, tt * 128:(tt + 1) * 128], wd.ap[:, fi, :], f == 0, f == 63,
                             [uT.sub(f), wd], [accs[tt]])
            for tt in range(4):
                row = (c * 4 + tt) * 128
                xp, op, tm = xps[c_e % 2], ops_[c_e % 2], tms[c_e % 2]
                c_e += 1
                S.dma("pool", xp.ap, self.x1_d.ap[row:row + 128, nn * 512:(nn + 1) * 512], [self.x1_d], [xp])
                S.v("dve", "tensor_tensor", [accs[tt], self.gt2], [tm], tm.ap, accs[tt].ap, self.gt2.ap[:, nn * 512:(nn + 1) * 512], ALU.mult)
                S.v("dve", "tensor_tensor", [tm, xp], [op], op.ap, tm.ap, xp.ap, ALU.add)
                S.dma("pool", self.out_d.ap[row:row + 128, nn * 512:(nn + 1) * 512], op.ap, [op], [self.out_d])
    S.barrier()
    ar.pop()


KB.stage5 = stage5
KB.stage6 = stage6


def precast(self):
    S = self.S
    self.wup_in = self.din("w_up", [D, DFF])
    self.wdn_in = self.din("w_down", [DFF, D])
    self.wupb = T(self.nc.dram_tensor("wupb_i", [32, 128, 16 * 256], BF16).ap(), "wupb")
    self.wdnb = T(self.nc.dram_tensor("wdnb_i", [4, 8, 128, 8 * 512], BF16).ap(), "wdnb")
    wup_v = self.wup_in.ap.rearrange("(k p) c -> p k c", p=128)
    wdn_v = self.wdn_in.ap.rearrange("(f p) c -> p f c", p=128)
    for fb in range(32):
        S.dma("pool", self.wupb.ap[fb].rearrange("p (k c) -> p k c", k=16), wup_v[:, :, fb * 256:(fb + 1) * 256],
              [self.wup_in], [self.wupb])
    for nn in range(4):
        for f8 in range(8):
            S.dma("pool", self.wdnb.ap[nn, f8].rearrange("p (f c) -> p f c", f=8),
                  wdn_v[:, f8 * 8:(f8 + 1) * 8, nn * 512:(nn + 1) * 512], [self.wdn_in], [self.wdnb])


KB.precast = precast


_CACHE = {}


def _all_consts():
    c = host_consts()
    c.update(nsa_consts())
    c.update(rwkv_consts())
    return c


def prep_all(inp, b, consts):
    m = prep_core(inp, b, consts)
    nsa_prep(inp, m)
    rwkv_prep(inp, m)
    m["w_o_rwkv"] = inp["w_o_rwkv"][0]
    m["w_o_nsa"] = inp["w_o_nsa"][0]
    m["w_out"] = inp["w_out"][0]
    m["w_up"] = inp["w_up"][0]
    m["w_down"] = inp["w_down"][0]
    return m


def kernel(**inputs):
    inp = {k: np.asarray(v) for k, v in inputs.items()}
    if "nc" not in _CACHE:
        _CACHE["nc"] = KB(dbg=False).build()
        _CACHE["consts"] = _all_consts()
    nc = _CACHE["nc"]
    consts = _CACHE["consts"]
    in_maps = [prep_all(inp, b, consts) for b in range(8)]
    res = run_bass_kernel_spmd(nc, in_maps, core_ids=list(range(8)))
    out = np.stack([np.asarray(r["out"]) for r in res.results], axis=0)
    return out.astype(np.float32)
```

```python
import numpy as np
import concourse.bass as bass
import concourse.mybir as mybir

F32 = mybir.dt.float32
BF16 = mybir.dt.bfloat16
AF = mybir.ActivationFunctionType
ALU = mybir.AluOpType
AX = mybir.AxisListType

ENGS = ("pe", "act", "dve", "pool", "sp")
NSLOT = {"sp": 40, "pool": 24}


class Buf:
    __slots__ = ("w", "r_eng", "r_dma", "name")

    def __init__(self, name=""):
        self.w = None
        self.r_eng = {}
        self.r_dma = []
        self.name = name


class T(Buf):
    __slots__ = ("ap", "subs")

    def __init__(self, ap, name=""):
        Buf.__init__(self, name)
        self.ap = ap
        self.subs = {}

    def __getitem__(self, idx):
        return self.ap[idx]

    def sub(self, key):
        b = self.subs.get(key)
        if b is None:
            b = self.subs[key] = Buf(f"{self.name}.{key}")
        return b


class Op:
    __slots__ = ("eng", "fn", "deps", "is_dma", "slot", "sig", "val", "dsem", "dval", "idx")

    def __init__(self, eng, fn, is_dma):
        self.eng = eng
        self.fn = fn
        self.is_dma = is_dma
        self.deps = []
        self.sig = False
        self.val = 0
        self.slot = -1
        self.dsem = None
        self.dval = 0


class Sched:
    def __init__(self, nc):
        self.nc = nc
        self.ops = {e: [] for e in ENGS}
        self.bar = {e: [] for e in ENGS}
        self.dma_since_bar = []
        self.slot_last = {q: [None] * n for q, n in NSLOT.items()}
        self.slot_n = {q: 0 for q in NSLOT}

    def rec(self, eng, fn, reads=(), writes=(), is_dma=False):
        op = Op(eng, fn, is_dma)
        deps = []
        for b in reads:
            if b.w is not None:
                deps.append(b.w)
        for b in writes:
            if b.w is not None:
                deps.append(b.w)
            deps.extend(b.r_eng.values())
            deps.extend(b.r_dma)
        if self.bar[eng]:
            deps.extend(self.bar[eng])
            self.bar[eng] = []
        if is_dma:
            n = self.slot_n[eng]
            self.slot_n[eng] = n + 1
            s = n % NSLOT[eng]
            op.slot = s
            prev = self.slot_last[eng][s]
            if prev is not None:
                deps.append(prev)
            self.slot_last[eng][s] = op
            self.dma_since_bar.append(op)
        seen = set()
        for d in deps:
            if d is op or id(d) in seen:
                continue
            seen.add(id(d))
            op.deps.append(d)
        for b in reads:
            if is_dma:
                b.r_dma.append(op)
            else:
                b.r_eng[eng] = op
        for b in writes:
            b.w = op
            b.r_eng = {}
            b.r_dma = []
        self.ops[eng].append(op)
        return op

    def barrier(self):
        deps = [self.ops[e][-1] for e in ENGS if self.ops[e]] + self.dma_since_bar
        self.dma_since_bar = []
        for e in ENGS:
            self.bar[e] = list(deps)

    def mm(self, out, lhsT, rhs, start, stop, reads, writes, **kw):
        return self.rec("pe", lambda e: e.matmul(out, lhsT, rhs, start=start, stop=stop, **kw), reads, writes)

    def tr(self, out, in_, ident, reads, writes):
        return self.rec("pe", lambda e: e.transpose(out, in_, ident), reads, writes)

    def act(self, out, in_, func, reads, writes, bias=None, scale=None, accum_out=None):
        kw = {}
        if bias is not None:
            kw["bias"] = bias
        if scale is not None:
            kw["scale"] = scale
        if accum_out is not None:
            kw["accum_out"] = accum_out
        return self.rec("act", lambda e: e.activation(out=out, in_=in_, func=func, **kw), reads, writes)

    def v(self, eng, meth, reads, writes, *a, **kw):
        return self.rec(eng, lambda e: getattr(e, meth)(*a, **kw), reads, writes)

    def dma(self, q, out, in_, reads, writes, **kw):
        return self.rec(q, lambda e: e.dma_start(out=out, in_=in_, **kw), reads, writes, is_dma=True)

    def finalize(self):
        for e in ENGS:
            for op in self.ops[e]:
                for d in op.deps:
                    if d.is_dma:
                        continue
                    if d.eng == "pe" and op.eng == "pe" and not op.is_dma:
                        continue
                    d.sig = True
        for e in ENGS:
            n = 0
            for op in self.ops[e]:
                if op.sig and not op.is_dma:
                    n += 1
                    op.val = n

    def emit_all(self, stack):
        nc = self.nc
        self.finalize()
        self.esem = {e: stack.enter_context(nc.semaphore("es_" + e)) for e in ENGS}
        self.dsem = {q: [stack.enter_context(nc.semaphore(f"ds_{q}{i}")) for i in range(n)] for q, n in NSLOT.items()}
        uses = {q: [0] * n for q, n in NSLOT.items()}
        for q in NSLOT:
            for op in self.ops[q]:
                if op.is_dma:
                    uses[q][op.slot] += 1
                    op.dsem = self.dsem[q][op.slot]
                    op.dval = 16 * uses[q][op.slot]
        block = stack.enter_context(nc.Block())
        sched = self

        def emit(name, eng):
            known = {}
            for op in sched.ops[name]:
                for d in op.deps:
                    if d.is_dma:
                        sem, val = d.dsem, d.dval
                    else:
                        if d.eng == "pe" and name == "pe" and not op.is_dma:
                            continue
                        sem, val = sched.esem[d.eng], d.val
                    k = id(sem)
                    if known.get(k, 0) >= val:
                        continue
                    eng.wait_ge(sem, val)
                    known[k] = val
                ins = op.fn(eng)
                if op.is_dma:
                    ins.then_inc(op.dsem, 16)
                elif op.sig:
                    ins.then_inc(sched.esem[name], 1)
            if name == "sp":
                for q in NSLOT:
                    for i, u in enumerate(uses[q]):
                        if u:
                            eng.wait_ge(sched.dsem[q][i], 16 * u)

        @block.tensor
        def _(e):
            emit("pe", e)

        @block.scalar
        def _(e):
            emit("act", e)

        @block.vector
        def _(e):
            emit("dve", e)

        @block.gpsimd
        def _(e):
            emit("pool", e)

        @block.sync
        def _(e):
            emit("sp", e)


class Arena:
    def __init__(self, ap, nwords):
        self.ap = ap
        self.n = nwords
        self.off = 0
        self.marks = []

    def push(self):
        self.marks.append(self.off)

    def pop(self):
        self.off = self.marks.pop()

    def alloc(self, shape, dtype=F32, name="", parts=128):
        n = int(np.prod(shape))
        words = n if dtype == F32 else (n + 1) // 2
        words = (words + 7) // 8 * 8
        assert self.off + words <= self.n, f"arena overflow {name} {self.off}+{words}>{self.n}"
        ap = self.ap[0:parts, self.off:self.off + words]
        self.off += words
        if dtype != F32:
            ap = ap.bitcast(dtype)
        ap = ap[:, 0:n]
        if len(shape) == 2:
            ap = ap.rearrange("p (a b) -> p a b", a=shape[0])
        elif len(shape) == 3:
            ap = ap.rearrange("p (a b c) -> p a b c", a=shape[0], b=shape[1])
        elif len(shape) == 4:
            ap = ap.rearrange("p (a b c d) -> p a b c d", a=shape[0], b=shape[1], c=shape[2])
        return T(ap, name)

from contextlib import ExitStack
from concourse.bass_utils import run_bass_kernel_spmd

D = 2048
SEQ = 2048
NT = 16
RWC = 3360
NB = 3360
MB = 5968
INC = 10064
DFF = 8192
EPS = 1e-6
GN_EPS = 64e-5
NEG = -30000.0
ARENA_WORDS = 52000


class KB:
    def __init__(self, dbg=False, stages=(0, 1, 2, 3, 4, 5, 6)):
        self.nc = bass.Bass("TRN2", target_bir_lowering=False)
        self.S = Sched(self.nc)
        self.dbg = dbg
        self.stages = stages
        self.bank_i = 0

    def din(self, name, shape, dt=F32):
        return T(self.nc.dram_tensor(name, list(shape), dt, kind="ExternalInput").ap(), name)

    def dscr(self, name, shape, dt=F32, out=False):
        kind = "ExternalOutput" if (self.dbg or out) else "Internal"
        return T(self.nc.dram_tensor(name, list(shape), dt, kind=kind).ap(), name)

    def bank(self):
        b = self.ps[self.bank_i % 8]
        self.bank_i += 1
        return b

    def load(self, dst, src, q="sp"):
        self.S.dma(q, dst.ap, src[1], [src[0]], [dst])

    def build(self):
        nc, S = self.nc, self.S
        with ExitStack() as st:
            arena_t = st.enter_context(nc.sbuf_tensor("arena", [128, ARENA_WORDS], F32))
            self.ar = ar = Arena(arena_t, ARENA_WORDS)
            self.ps = [T(st.enter_context(nc.psum_tensor(f"ps{i}", [128, 512], F32))[:], f"ps{i}") for i in range(8)]
            self.declare()
            self.persistent()
            if 0 in self.stages:
                self.stage0()
            if 1 in self.stages:
                ar.push()
                self.hT = ar.alloc([16, SEQ], BF16, "hT")
                self.stage1(self.x_d, self.coef1, self.sh1, self.hT)
                if 2 in self.stages:
                    self.stage2()
                ar.pop()
            if 4 in self.stages:
                self.stage4()
            if 3 in self.stages:
                self.stage3()
            if 5 in self.stages:
                self.stage5()
            if 6 in self.stages:
                ar.push()
                self.hT = ar.alloc([16, SEQ], BF16, "h2T")
                self.stage1(self.x1_d, self.coef2, self.sh2, self.hT)
                self.stage6()
                ar.pop()
            S.emit_all(st)
        return nc

    def declare(self):
        d = self.din
        self.x_d = d("x", [SEQ, D])
        self.c_fm = d("c_fm", [128, 16])
        self.w_ada = d("w_ada", [D, 6 * D])
        self.b_ada = d("b_ada", [1, 6 * D])
        self.n1g = d("n1g_fm", [128, 16])
        self.n2g = d("n2g_fm", [128, 16])
        self.w_in = d("w_in", [D, INC])
        self.ident_d = d("ident", [128, 128])
        self.bones_d = d("bones", [128, 128])
        self.mu_d = d("mu_fm", [128, 27])
        self.qkg_d = d("qkg_fm", [128, 5])
        s = self.dscr
        self.rwT_d = s("rwT", [27 * 128, SEQ])
        self.qT_d = s("qT", [1024, SEQ], BF16)
        self.kvcT_d = s("kvcT", [512, SEQ], BF16)
        self.ksT_d = s("ksT", [4, 2, 128, SEQ], BF16)
        self.kwT_d = s("kwT", [4, 2, 128, SEQ], BF16)
        self.vv_d = s("vv", [SEQ, 512], BF16)
        self.gates_d = s("gates", [SEQ, 48])
        self.mgT_d = s("mgT", [4096, SEQ], BF16)
        self.x1_d = s("x1", [SEQ, D])
        self.out_d = self.dscr("out", [SEQ, D], out=True)

    def persistent(self):
        ar, S = self.ar, self.S
        self.ident = ar.alloc([128], F32, "ident")
        self.bones = ar.alloc([128], F32, "bones")
        S.dma("sp", self.ident.ap, self.ident_d.ap, [self.ident_d], [self.ident])
        S.dma("sp", self.bones.ap, self.bones_d.ap, [self.bones_d], [self.bones])
        self.coef1 = ar.alloc([16], F32, "coef1")
        self.sh1 = ar.alloc([16], F32, "sh1")
        self.coef2 = ar.alloc([16], F32, "coef2")
        self.sh2 = ar.alloc([16], F32, "sh2")
        self.gt1 = ar.alloc([D], F32, "gt1")
        self.gt2 = ar.alloc([D], F32, "gt2")

    def silu_rep(self):
        ar, S = self.ar, self.S
        cs = ar.alloc([16], F32, "cs")
        S.dma("sp", cs.ap, self.c_fm.ap, [self.c_fm], [cs])
        csb = ar.alloc([16], F32, "csb")
        S.act(csb.ap, cs.ap, AF.Silu, [cs], [csb])
        crep = ar.alloc([16, 128], BF16, "crep")
        S.v("dve", "tensor_copy", [csb], [crep], crep.ap, csb.ap.unsqueeze(2).to_broadcast([128, 16, 128]))
        return crep

    def stage0(self):
        ar, S = self.ar, self.S
        ar.push()
        bias_steps = self.nsa_bias_build() if 4 in self.stages else []
        mod = ar.alloc([6 * D], F32, "mod")
        crep = self.silu_rep()
        wbs = [ar.alloc([16, 512], BF16, f"wada{i}") for i in range(2)]
        bbs = [ar.alloc([512], F32, f"bada{i}") for i in range(2)]
        wsrc = self.w_ada.ap.rearrange("(k p) c -> p k c", p=128)
        for blk in range(24):
            wb, bb = wbs[blk % 2], bbs[blk % 2]
            c0 = blk * 512
            S.dma("pool", wb.ap, wsrc[:, :, c0:c0 + 512], [self.w_ada], [wb])
            S.dma("sp", bb.ap, self.b_ada.ap[0:1, c0:c0 + 512].partition_broadcast(128), [self.b_ada], [bb])
            P = self.bank()
            for k in range(16):
                S.mm(P.ap, crep.ap[:, k, :], wb.ap[:, k, :], k == 0, k == 15, [crep, wb], [P])
            S.v("dve", "tensor_tensor", [P, bb], [mod], mod.ap[:, c0:c0 + 512], P.ap, bb.ap, ALU.add)
            for _ in range(2):
                if bias_steps:
                    bias_steps.pop(0)()
        while bias_steps:
            bias_steps.pop(0)()
        tmp = ar.alloc([16, 128], F32, "dtmp")
        sc1 = ar.alloc([16], F32, "sc1")
        sc2 = ar.alloc([16], F32, "sc2")
        for dst, idx in ((self.sh1, 0), (sc1, 1), (self.sh2, 3), (sc2, 4)):
            src = mod.ap[:, idx * D:(idx + 1) * D].rearrange("p (k m) -> p k m", k=16)
            S.v("dve", "tensor_tensor", [mod, self.ident], [tmp], tmp.ap, src,
                self.ident.ap.unsqueeze(1).to_broadcast([128, 16, 128]), ALU.mult)
            S.v("dve", "tensor_reduce", [tmp], [dst], dst.ap, tmp.ap, AX.X, ALU.add)
        g = ar.alloc([16], F32, "gload")
        S.dma("sp", g.ap, self.n1g.ap, [self.n1g], [g])
        S.v("dve", "scalar_tensor_tensor", [sc1, g], [self.coef1], self.coef1.ap, sc1.ap, 1.0, g.ap, ALU.add, ALU.mult)
        g2 = ar.alloc([16], F32, "gload2")
        S.dma("sp", g2.ap, self.n2g.ap, [self.n2g], [g2])
        S.v("dve", "scalar_tensor_tensor", [sc2, g2], [self.coef2], self.coef2.ap, sc2.ap, 1.0, g2.ap, ALU.add, ALU.mult)
        S.v("dve", "tensor_copy", [mod], [self.gt1], self.gt1.ap, mod.ap[:, 2 * D:3 * D])
        S.v("dve", "tensor_copy", [mod], [self.gt2], self.gt2.ap, mod.ap[:, 5 * D:6 * D])
        S.barrier()
        ar.pop()

    def stage0b_setup(self):
        ar, S = self.ar, self.S
        crep = self.silu_rep()
        wb2 = [ar.alloc([16, 256], BF16, f"wada_b{i}") for i in range(2)]
        bb2 = [ar.alloc([256], F32, f"bada_b{i}") for i in range(2)]
        tmp = ar.alloc([2, 128], F32, "dtmp_b")
        sc2 = ar.alloc([16], F32, "sc2")
        wsrc = self.w_ada.ap.rearrange("(k p) c -> p k c", p=128)
        steps = []

        def mk(sb):
            def step():
                wb, bb = wb2[sb % 2], bb2[sb % 2]
                c0 = 3 * D + sb * 256
                S.dma("pool", wb.ap, wsrc[:, :, c0:c0 + 256], [self.w_ada], [wb])
                S.dma("sp", bb.ap, self.b_ada.ap[0:1, c0:c0 + 256].partition_broadcast(128), [self.b_ada], [bb])
                P = self.bank()
                for k in range(16):
                    S.mm(P.ap[:, 0:256], crep.ap[:, k, :], wb.ap[:, k, :], k == 0, k == 15, [crep, wb], [P])
                if sb < 16:
                    dst = self.sh2 if sb < 8 else sc2
                    j = sb % 8
                    t2 = tmp.ap.rearrange("p a b -> p (a b)")
                    S.v("dve", "tensor_tensor", [P, bb], [tmp], t2, P.ap[:, 0:256], bb.ap, ALU.add)
                    S.v("dve", "tensor_tensor", [tmp, self.ident], [tmp], tmp.ap, tmp.ap,
                        self.ident.ap.unsqueeze(1).to_broadcast([128, 2, 128]), ALU.mult)
                    S.v("dve", "tensor_reduce", [tmp], [dst], dst.ap[:, 2 * j:2 * j + 2], tmp.ap, AX.X, ALU.add)
                else:
                    o = (sb - 16) * 256
                    S.v("dve", "tensor_tensor", [P, bb], [self.gt2], self.gt2.ap[:, o:o + 256], P.ap[:, 0:256], bb.ap, ALU.add)
                if sb == 23:
                    g = ar.alloc([16], F32, "gload2")
                    S.dma("sp", g.ap, self.n2g.ap, [self.n2g], [g])
                    S.v("dve", "scalar_tensor_tensor", [sc2, g], [self.coef2], self.coef2.ap, sc2.ap, 1.0, g.ap, ALU.add, ALU.mult)
            return step
        return [mk(sb) for sb in range(24)]

    def stage1(self, src_d, coef, sh, hT):
        ar, S = self.ar, self.S
        ar.push()
        xbs = [ar.alloc([D], F32, f"xb{i}") for i in range(2)]
        junk = ar.alloc([D], F32, "junk")
        xs4s = [ar.alloc([4, D], F32, f"xs4_{i}") for i in range(1)]
        ss = ar.alloc([NT], F32, "ss")
        sr = ar.alloc([NT], F32, "sr")
        rstd = ar.alloc([NT], F32, "rstd")
        for grp in range(4):
            xs4 = xs4s[0]
            for tt in range(4):
                ti = grp * 4 + tt
                xb = xbs[ti % 2]
                S.dma("sp", xb.ap, src_d.ap[ti * 128:(ti + 1) * 128, :], [src_d], [xb])
                sst = ss.sub(ti)
                S.act(junk.ap, xb.ap, AF.Square, [xb], [sst], accum_out=ss.ap[:, ti:ti + 1])
                S.act(sr.ap[:, ti:ti + 1], ss.ap[:, ti:ti + 1], AF.Sqrt, [sst], [sr.sub(ti)], bias=EPS, scale=1.0 / D)
                S.v("dve", "reciprocal", [sr.sub(ti)], [rstd.sub(ti)], rstd.ap[:, ti:ti + 1], sr.ap[:, ti:ti + 1])
                S.v("dve", "tensor_scalar", [xb, rstd.sub(ti)], [xs4.sub(tt)], xs4.ap[:, tt, :], xb.ap,
                    rstd.ap[:, ti:ti + 1], None, ALU.mult)
            for k in range(16):
                P = self.bank()
                for tt in range(4):
                    S.tr(P.ap[:, tt * 128:(tt + 1) * 128], xs4.ap[:, tt, k * 128:(k + 1) * 128], self.ident.ap,
                         [xs4.sub(tt), self.ident], [P])
                S.act(hT.ap[:, k, grp * 512:(grp + 1) * 512], P.ap, AF.Identity, [P, coef, sh], [hT.sub(grp)],
                      bias=sh.ap[:, k:k + 1], scale=coef.ap[:, k:k + 1])
        S.barrier()
        ar.pop()

    def stage2(self):
        ar, S = self.ar, self.S
        hT = self.hT
        ar.push()
        wbs = [ar.alloc([16, 512], BF16, f"win{i}") for i in range(2)]
        wtm = ar.alloc([16, 560], BF16, "wtm")
        raws = [ar.alloc([2056], F32, f"raw{i}") for i in range(2)]
        tmps = [ar.alloc([SEQ], F32, f"mixt{i}") for i in range(2)]
        stg = [ar.alloc([SEQ], BF16, f"stg{i}") for i in range(4)]
        sqs = [ar.alloc([512], F32, f"sq{i}") for i in range(2)]
        srs = [ar.alloc([512], F32, f"sr{i}") for i in range(2)]
        ris = [ar.alloc([512], F32, f"ri{i}") for i in range(2)]
        mu = ar.alloc([27], F32, "mu")
        omu = ar.alloc([27], F32, "omu")
        qkg = ar.alloc([5], F32, "qkg")
        S.dma("sp", mu.ap, self.mu_d.ap, [self.mu_d], [mu])
        S.dma("sp", qkg.ap, self.qkg_d.ap, [self.qkg_d], [qkg])
        S.v("dve", "tensor_scalar", [mu], [omu], omu.ap, mu.ap, -1.0, 1.0, ALU.mult, ALU.add)
        S.v("dve", "tensor_scalar", [qkg], [qkg], qkg.ap[:, 1:5], qkg.ap[:, 1:5], 8.0, None, ALU.mult)
        for r in raws:
            S.v("dve", "memset", [], [r], r.ap[:, 0:1], 0.0)
        wsrc = self.w_in.ap.rearrange("(k p) c -> p k c", p=128)
        cnt = {"w": 0, "raw": 0, "stg": 0, "sq": 0}

        def proj_chunk(wb, m0, M, n):
            P = self.bank()
            for k in range(16):
                S.mm(P.ap[0:M, :], wb.ap[:, k, m0:m0 + M], hT.ap[:, k, n * 512:(n + 1) * 512], k == 0, k == 15,
                     [wb, hT.sub(n)], [P])
            return P

        def next_stg():
            t = stg[cnt["stg"] % 4]
            cnt["stg"] += 1
            return t

        def ep_rw(wb, m0, M, ti):
            raw = raws[cnt["raw"] % 2]
            tmp = tmps[cnt["raw"] % 2]
            cnt["raw"] += 1
            for n in range(4):
                P = proj_chunk(wb, m0, M, n)
                S.act(raw.ap[0:M, 1 + n * 512:1 + (n + 1) * 512], P.ap[0:M, :], AF.Copy, [P], [raw])
            S.v("dve", "tensor_scalar", [raw, mu], [tmp], tmp.ap[0:M, :], raw.ap[0:M, 0:SEQ], mu.ap[0:M, ti:ti + 1], None, ALU.mult)
            S.v("dve", "scalar_tensor_tensor", [raw, omu, tmp], [tmp], tmp.ap[0:M, :], raw.ap[0:M, 1:SEQ + 1],
                omu.ap[0:M, ti:ti + 1], tmp.ap[0:M, :], ALU.mult, ALU.add)
            S.dma("sp", self.rwT_d.ap[ti * 128:ti * 128 + M, :], tmp.ap[0:M, :], [tmp], [self.rwT_d])

        def ep_qk(wb, m0, gcols, dsts):
            outs = [next_stg() for _ in gcols]
            for n in range(4):
                P = proj_chunk(wb, m0, 128, n)
                i = cnt["sq"] % 2
                cnt["sq"] += 1
                sq, sr, ri = sqs[i], srs[i], ris[i]
                S.act(sq.ap, P.ap, AF.Square, [P], [sq])
                P2 = self.bank()
                S.mm(P2.ap, self.bones.ap, sq.ap, True, True, [self.bones, sq], [P2])
                S.act(sr.ap, P2.ap, AF.Sqrt, [P2], [sr], bias=64 * EPS, scale=1.0)
                S.v("dve", "reciprocal", [sr], [ri], ri.ap, sr.ap)
                for gc, o in zip(gcols, outs):
                    S.v("dve", "scalar_tensor_tensor", [P, qkg, ri], [o], o.ap[:, n * 512:(n + 1) * 512], P.ap,
                        qkg.ap[:, gc:gc + 1], ri.ap, ALU.mult, ALU.mult)
            for o, (dt_, dap) in zip(outs, dsts):
                S.dma("sp", dap, o.ap, [o], [dt_])

        def ep_act(wb, m0, func, dt_, dap):
            o = next_stg()
            for n in range(4):
                P = proj_chunk(wb, m0, 128, n)
                S.act(o.ap[:, n * 512:(n + 1) * 512], P.ap, func, [P], [o])
            S.dma("sp", dap, o.ap, [o], [dt_])

        def load_block(segs):
            wb = wbs[cnt["w"] % 2]
            cnt["w"] += 1
            for (c0, n, off) in segs:
                S.dma("pool", wb.ap[:, :, off:off + n], wsrc[:, :, c0:c0 + n], [self.w_in], [wb])
            return wb

        for b in range(7):
            if b < 6:
                wb = load_block([(512 * b, 512, 0)])
                for j in range(4):
                    ep_rw(wb, j * 128, 128, 4 * b + j)
            else:
                wb = load_block([(3072, 288, 0)])
                ep_rw(wb, 0, 128, 24)
                ep_rw(wb, 128, 128, 25)
                ep_rw(wb, 256, 32, 26)
        for b in range(2):
            wb = load_block([(NB + 512 * b, 512, 0)])
            for j in range(4):
                ti = 4 * b + j
                ep_qk(wb, j * 128, [0], [(self.qT_d, self.qT_d.ap[ti * 128:(ti + 1) * 128, :])])
        wb = load_block([(NB + 1024, 512, 0)])
        for j in range(4):
            ep_act(wb, j * 128, AF.Copy, self.kvcT_d, self.kvcT_d.ap[j * 128:(j + 1) * 128, :])
        for (c_base, dst, gc) in ((NB + 1024 + 512, self.ksT_d, 1), (NB + 1024 + 1024, self.kwT_d, 3)):
            segs = []
            for g in range(4):
                segs.append((c_base + 64 * g, 64, g * 128))
                segs.append((c_base + 64 * g, 64, g * 128 + 64))
            wb = load_block(segs)
            for g in range(4):
                ep_qk(wb, g * 128, [gc, gc + 1], [(dst, dst.ap[g, 0]), (dst, dst.ap[g, 1])])
        for b in range(8):
            wb = load_block([(MB + 512 * b, 512, 0)])
            for j in range(4):
                ti = 4 * b + j
                ep_act(wb, j * 128, AF.Sigmoid, self.mgT_d, self.mgT_d.ap[ti * 128:(ti + 1) * 128, :])
        for (c0, n, off) in ((NB + 1024 + 768, 256, 0), (NB + 1024 + 1280, 256, 256), (NB + 2560, 48, 512)):
            S.dma("pool", wtm.ap[:, :, off:off + n], wsrc[:, :, c0:c0 + n], [self.w_in], [wtm])
        vst = [ar.alloc([512], BF16, f"vst{i}") for i in range(2)]
        gst = [ar.alloc([48], F32, f"gst{i}") for i in range(2)]
        for tt in range(NT):
            P = self.bank()
            for k in range(16):
                S.mm(P.ap, hT.ap[:, k, tt * 128:(tt + 1) * 128], wtm.ap[:, k, 0:512], k == 0, k == 15,
                     [hT.sub(tt // 4), wtm], [P])
            v = vst[tt % 2]
            S.act(v.ap, P.ap, AF.Copy, [P], [v])
            S.dma("sp", self.vv_d.ap[tt * 128:(tt + 1) * 128, :], v.ap, [v], [self.vv_d])
            P = self.bank()
            for k in range(16):
                S.mm(P.ap[:, 0:48], hT.ap[:, k, tt * 128:(tt + 1) * 128], wtm.ap[:, k, 512:560], k == 0, k == 15,
                     [hT.sub(tt // 4), wtm], [P])
            gt = gst[tt % 2]
            S.act(gt.ap, P.ap[:, 0:48], AF.Sigmoid, [P], [gt])
            S.dma("sp", self.gates_d.ap[tt * 128:(tt + 1) * 128, :], gt.ap, [gt], [self.gates_d])
        S.barrier()
        ar.pop()


def _fm(v, ntile=None):
    v = np.asarray(v, np.float32).reshape(-1)
    n = (len(v) + 127) // 128 if ntile is None else ntile
    buf = np.zeros(n * 128, np.float32)
    buf[:len(v)] = v
    return np.ascontiguousarray(buf.reshape(n, 128).T)


def host_consts():
    c = {}
    c["ident"] = np.eye(128, dtype=np.float32)
    p = np.arange(128)
    c["bones"] = (p[:, None] // 64 == p[None, :] // 64).astype(np.float32)
    return c


def prep_core(inp, b, consts):
    m = dict(consts)
    m["x"] = np.ascontiguousarray(inp["x"][b])
    m["c_fm"] = _fm(inp["c"][b])
    m["w_ada"] = inp["w_ada"][0]
    m["b_ada"] = inp["b_ada"][0].reshape(1, -1)
    m["n1g_fm"] = _fm(inp["norm1_g"][0])
    m["n2g_fm"] = _fm(inp["norm2_g"][0])
    m["w_in"] = inp["w_in"][0]
    m["mu_fm"] = _fm(inp["rwkv_mu"][0], 27)
    qg = np.tile(inp["q_norm_g"][0], 2)
    kg = inp["k_norm_g"][0]
    z = np.zeros(64, np.float32)
    cols = [qg, np.concatenate([kg[1], z]), np.concatenate([z, kg[1]]),
            np.concatenate([kg[2], z]), np.concatenate([z, kg[2]])]
    m["qkg_fm"] = np.ascontiguousarray(np.stack(cols, axis=1).astype(np.float32))
    return m


def _rel_bucket_np(rel):
    n = np.maximum(rel, 0)
    nf = np.maximum(n, 16).astype(np.float32)
    large = 16 + (np.log(nf / np.float32(16)) / np.float32(np.log(8.0)) * np.float32(16)).astype(np.int32)
    large = np.minimum(large, 31)
    return np.where(n < 16, n, large)


def nsa_consts():
    c = {}
    NOH = 3 * 16384 + 17 * 128
    oh = np.zeros((33, NOH), np.float32)
    pos = np.arange(128)[:, None]
    t = np.arange(128)[None, :]
    for d, base in ((0, 0), (1, 128), (2, 512)):
        rel = base + t - pos
        if d == 0:
            mask = rel < 0
        elif d == 1:
            mask = np.zeros_like(rel, bool)
        else:
            mask = rel >= 512
        b = _rel_bucket_np(rel)
        sec = np.zeros((33, 128, 128), np.float32)
        for bb in range(32):
            sec[bb][(b == bb) & ~mask] = 1.0
        sec[32][mask] = 1.0
        oh[:, d * 16384:(d + 1) * 16384] = sec.reshape(33, -1)
    sec = np.zeros((33, 17, 128), np.float32)
    ti = np.arange(128)
    for r in range(16):
        m = r - 9
        rel = ti - 16 * m - 31
        b = _rel_bucket_np(rel)
        for bb in range(32):
            sec[bb, r, (b == bb) & (rel >= 0)] = 1.0
        sec[32, r, rel < 0] = 1.0
    sec[32, 16, :] = 1.0
    oh[:, 3 * 16384:] = sec.reshape(33, -1)
    c["nsa_oh"] = oh
    S = np.zeros((17, 16, 128), np.float32)
    for i in range(16):
        for n in range(127):
            m = n - 8 * i
            if -9 <= m <= 6:
                S[m + 9, i, n] = 1.0
            elif m > 6:
                S[16, i, n] = 1.0
    c["nsa_S"] = S
    E = np.zeros((32, 2048), np.float32)
    for p in range(2048):
        E[p // 64, p] = 1.0
    c["nsa_E"] = E
    allowed = np.zeros((128, 16, 32), np.float32)
    addc = np.zeros((128, 16, 32), np.float32)
    blk = np.arange(32)
    for i in range(16):
        for tt in range(128):
            cur = (i * 128 + tt) // 64
            al = blk <= cur
            forced = (blk == 0) | (blk == cur) | (blk == cur - 1)
            allowed[tt, i] = (al & ~forced).astype(np.float32)
            addc[tt, i] = np.where(forced, 1e4, np.where(al, 0.0, -1.0))
    c["nsa_allowed"] = allowed
    c["nsa_addc"] = addc
    ncmp = 127
    cs = np.arange(ncmp) * 16
    ss = np.arange(32) * 64
    lo = np.maximum(cs[:, None], ss[None, :])
    hi = np.minimum(cs[:, None] + 32, ss[None, :] + 64)
    c["nsa_selm"] = (np.maximum(hi - lo, 0) / 32).astype(np.float32)
    return c


def nsa_prep(inp, m):
    m["rel_bias"] = np.ascontiguousarray(inp["rel_bias"])
    for kv in ("k", "v"):
        m[f"pe_{kv}T"] = np.ascontiguousarray(inp[f"cmp_pe_{kv}"][0].T)
        m[f"w1_{kv}"] = inp[f"cmp_w1_{kv}"][0]
        m[f"w2_{kv}"] = inp[f"cmp_w2_{kv}"][0]
    kg0 = inp["k_norm_g"][0][0]
    z = np.zeros(64, np.float32)
    m["kcg_fm"] = np.ascontiguousarray(np.stack([np.concatenate([kg0, z]), np.concatenate([z, kg0])], 1).astype(np.float32))


def stage4(self):
    ar, S, nc = self.ar, self.S, self.nc
    d = self.din
    S_d = d("nsa_S", [17, 16, 128])
    E_d = d("nsa_E", [32, 2048])
    al_d = d("nsa_allowed", [128, 16, 32])
    ad_d = d("nsa_addc", [128, 16, 32])
    selm_d = d("nsa_selm", [127, 32])
    kcg_d = d("kcg_fm", [128, 2])
    cmp_d = {}
    for kv in ("k", "v"):
        cmp_d[kv] = (d(f"pe_{kv}T", [64, 32]), d(f"w1_{kv}", [2048, 64]), d(f"w2_{kv}", [64, 64]))
    NOH = 3 * 16384 + 17 * 128
    self.obT_d = self.dscr("obT", [1024, SEQ], BF16)
    ident, bones = self.ident, self.bones

    ar.push()
    ks = ar.alloc([4, 2, SEQ], BF16, "ks")
    kw = ar.alloc([4, 2, SEQ], BF16, "kw")
    vs = ar.alloc([16, 4, 65], BF16, "vs")
    vw = ar.alloc([16, 4, 65], BF16, "vw")
    gates = ar.alloc([16, 48], F32, "gates")
    biasT = ar.alloc([3, 16, 128], F32, "biasT")
    Mst = ar.alloc([16, 128], F32, "Mst", parts=17)
    Sc = ar.alloc([16, 128], F32, "Sc", parts=17)
    allowed = ar.alloc([16, 32], F32, "allowed")
    addc = ar.alloc([16, 32], F32, "addc")
    kc = ar.alloc([4, 2, 128], BF16, "kc")
    rhsc = ar.alloc([4, 97], BF16, "rhsc", parts=127)
    for g in range(4):
        for h in range(2):
            S.dma("sp", ks.ap[:, g, h, :], self.ksT_d.ap[g, h], [self.ksT_d], [ks])
            S.dma("sp", kw.ap[:, g, h, :], self.kwT_d.ap[g, h], [self.kwT_d], [kw])
    for g_ in range(4):
        S.dma("pool", ks.ap[64:96, g_, 0, :], E_d.ap, [E_d, ks], [ks])
        S.dma("pool", ks.ap[0:32, g_, 1, :], E_d.ap, [E_d, ks], [ks])
    for (dst, c0) in ((vs, 0), (vw, 256)):
        S.v("dve", "memset", [], [dst], dst.ap[:, :, :, 64:65], 1.0)
        for j in range(16):
            S.dma("sp", dst.ap[:, j, :, 0:64],
                  self.vv_d.ap[j * 128:(j + 1) * 128, c0:c0 + 256].rearrange("p (g d) -> p g d", g=4), [self.vv_d], [dst])
    S.dma("sp", gates.ap, self.gates_d.ap.rearrange("(j p) c -> p j c", p=128), [self.gates_d], [gates])
    S.dma("sp", Sc.ap, S_d.ap, [S_d], [Sc])
    S.dma("sp", allowed.ap, al_d.ap, [al_d], [allowed])
    S.dma("sp", addc.ap, ad_d.ap, [ad_d], [addc])

    ar.push()
    bias_d = self.bias_d
    for dd in range(3):
        S.dma("sp", biasT.ap[:, dd, :, :],
              bias_d.ap[:, dd * 16384:(dd + 1) * 16384].rearrange("h (p t) -> p h t", p=128), [bias_d], [biasT])
    S.dma("sp", Mst.ap, bias_d.ap[:, 3 * 16384:].rearrange("h (r t) -> r h t", r=17), [bias_d], [Mst])
    if self.dbg:
        dbb = self.dscr("dbg_biasT", [128, 3 * 16 * 128])
        S.dma("sp", dbb.ap, biasT.ap.rearrange("p a b c -> p (a b c)"), [biasT], [dbb])
    ar.pop()

    ar.push()
    kvc = ar.alloc([4, SEQ], BF16, "kvc")
    S.dma("sp", kvc.ap, self.kvcT_d.ap.rearrange("(a p) t -> p a t", p=128), [self.kvcT_d], [kvc])
    kcg = ar.alloc([2], F32, "kcg")
    S.dma("sp", kcg.ap, kcg_d.ap, [kcg_d], [kcg])
    S.v("dve", "tensor_scalar", [kcg], [kcg], kcg.ap, kcg.ap, 8.0, None, ALU.mult)
    S.v("dve", "memset", [], [rhsc], rhsc.ap[:, :, 64:65], 1.0)
    for g in range(4):
        S.dma("pool", rhsc.ap[:, g, 65:97], selm_d.ap, [selm_d], [rhsc])
    for kvi, kv in enumerate(("k", "v")):
        pe_d, w1_d, w2_d = cmp_d[kv]
        w1p = ar.alloc([2, 32, 64], BF16, f"w1p{kv}")
        S.v("dve", "memset", [], [w1p], w1p.ap, 0.0)
        w1v = w1_d.ap.rearrange("(i d) e -> d i e", d=64)
        S.dma("pool", w1p.ap[0:64, 0, :, :], w1v, [w1_d, w1p], [w1p])
        S.dma("pool", w1p.ap[64:128, 1, :, :], w1v, [w1_d, w1p], [w1p])
        peT = ar.alloc([32], BF16, f"peT{kv}", parts=64)
        S.dma("pool", peT.ap, pe_d.ap, [pe_d], [peT])
        w2 = ar.alloc([128], BF16, f"w2{kv}", parts=64)
        S.dma("pool", w2.ap[:, 0:64], w2_d.ap, [w2_d], [w2])
        S.dma("pool", w2.ap[:, 64:128], w2_d.ap, [w2_d, w2], [w2])
        Pb = self.bank()
        for i in range(32):
            S.mm(Pb.ap[0:64, 0:1], w1p.ap[0:64, 0, i, :], peT.ap[:, i:i + 1], i == 0, i == 31, [w1p, peT], [Pb])
        cb = ar.alloc([1], F32, f"cb{kv}", parts=64)
        S.act(cb.ap, Pb.ap[0:64, 0:1], AF.Copy, [Pb], [cb])
        for g in range(4):
            tile_, half = kvi * 2 + g // 2, g % 2
            Ph = self.bank()
            for i in range(32):
                S.mm(Ph.ap[0:64, 0:127], w1p.ap[:, half, i, :], kvc.ap[:, tile_, i:i + 16 * 126 + 1:16], i == 0, i == 31,
                     [w1p, kvc], [Ph])
            u = ar.alloc([127], F32, "cu", parts=64)
            t1 = ar.alloc([127], F32, "ct1", parts=64)
            sg = ar.alloc([127], F32, "csg", parts=64)
            hid = ar.alloc([127], BF16, "chid", parts=64)
            S.act(u.ap, Ph.ap[0:64, 0:127], AF.Identity, [Ph, cb], [u], bias=cb.ap[:, 0:1], scale=1.0)
            S.v("dve", "tensor_tensor", [u], [t1], t1.ap, u.ap, u.ap, ALU.mult)
            S.v("dve", "tensor_scalar", [t1], [t1], t1.ap, t1.ap, 0.044715, 1.0, ALU.mult, ALU.add)
            S.v("dve", "tensor_tensor", [t1, u], [t1], t1.ap, t1.ap, u.ap, ALU.mult)
            S.act(sg.ap, t1.ap, AF.Sigmoid, [t1], [sg], scale=1.5957691216057308)
            S.v("dve", "tensor_tensor", [u, sg], [hid], hid.ap, u.ap, sg.ap, ALU.mult)
            if kv == "k":
                Pk = self.bank()
                S.mm(Pk.ap[:, 0:127], w2.ap, hid.ap, True, True, [w2, hid], [Pk])
                sq = ar.alloc([127], F32, "csq")
                S.act(sq.ap, Pk.ap[:, 0:127], AF.Square, [Pk], [sq])
                P2 = self.bank()
                S.mm(P2.ap[:, 0:127], bones.ap, sq.ap, True, True, [bones, sq], [P2])
                sr = ar.alloc([127], F32, "csr")
                S.act(sr.ap, P2.ap[:, 0:127], AF.Sqrt, [P2], [sr], bias=64 * EPS, scale=1.0)
                S.v("dve", "reciprocal", [sr], [sr], sr.ap, sr.ap)
                for h in range(2):
                    S.v("dve", "scalar_tensor_tensor", [Pk, kcg, sr], [kc], kc.ap[:, g, h, 0:127], Pk.ap[:, 0:127],
                        kcg.ap[:, h:h + 1], sr.ap, ALU.mult, ALU.mult)
            else:
                Pv = self.bank()
                S.mm(Pv.ap[0:127, 0:64], hid.ap, w2.ap[:, 0:64], True, True, [w2, hid], [Pv])
                S.act(rhsc.ap[:, g, 0:64], Pv.ap[0:127, 0:64], AF.Copy, [Pv], [rhsc])
    if self.dbg:
        dkc = self.dscr("dbg_kc", [128, 4 * 2 * 128], BF16)
        S.dma("sp", dkc.ap, kc.ap.rearrange("p a b c -> p (a b c)"), [kc], [dkc])
        drc = self.dscr("dbg_rhsc", [127, 4 * 97], BF16)
        S.dma("sp", drc.ap, rhsc.ap.rearrange("p a b -> p (a b)"), [rhsc], [drc])
    S.barrier()
    ar.pop()
    if 6 in self.stages:
        self.precast()

    qis = [ar.alloc([4, 2, 2, 128], BF16, f"qa{i}") for i in range(2)]
    for qa_ in qis:
        S.v("dve", "memset", [], [qa_.sub("q")] + [qa_.sub(("m", g_)) for g_ in range(4)], qa_.ap, 0.0)

    pxs = [ar.alloc([512], BF16, f"px{i}") for i in range(8)]
    ssbs = [ar.alloc([512], F32, f"ssb{i}") for i in range(3)]
    oaccs = [ar.alloc([1024], F32, f"oacc{i}") for i in range(2)]
    obst = [ar.alloc([8, 128], BF16, f"obst{i}") for i in range(1)] * 2
    sm = [dict(rl=ar.alloc([12], F32, f"rl{i}"), imp=ar.alloc([32], F32, f"imp{i}"), m8=ar.alloc([8], F32, f"m8{i}"),
               ns=ar.alloc([128], F32, f"ns{i}"), tmp=ar.alloc([256], F32, f"otmp{i}")) for i in range(2)]
    for w_ in sm:
        S.v("dve", "memset", [], [w_["ns"]], w_["ns"].ap, 0.0)
    score_banks = self.ps[0:5]
    NSB = 5
    Poc_b = self.ps[5]
    Pow_b = [self.ps[6], self.ps[6]]
    Pos_b = [self.ps[7], self.ps[7]]
    cnt = {"sb": 0, "px": 0, "ssb": 0}
    jobs = []
    qT_v = self.qT_d.ap.rearrange("(kt p) t -> p kt t", p=128)
    obT_v = self.obT_d.ap.rearrange("(kt p) t -> p kt t", p=128)

    def score_job(qi, g, lhs_lo, lhs_hi, M, extra_mm, bias_ap, pv_fn, deps_k, use_mask=False):
        st = {}

        def qk():
            P = score_banks[cnt["sb"] % NSB]
            cnt["sb"] += 1
            pv4 = P.ap[0:M, :].rearrange("p (a b t) -> p a b t", a=2, b=2)
            qdeps = [qi.sub("q")] + ([qi.sub(("m", g))] if use_mask else [])
            S.mm(pv4[:, :, 0, :], lhs_lo, qi.ap[:, g, 0, :, :], True, False, deps_k + qdeps, [P])
            S.mm(pv4[:, :, 1, :], lhs_hi, qi.ap[:, g, 1, :, :], False, extra_mm is None, deps_k + qdeps, [P])
            if extra_mm is not None:
                lt, rt, dps = extra_mm
                S.mm(P.ap[0:M, :], lt, rt, False, True, dps, [P])
            px = pxs[cnt["px"] % 8]
            cnt["px"] += 1
            if bias_ap is not None:
                sb_ = ssbs[cnt["ssb"] % 3]
                cnt["ssb"] += 1
                S.v("dve", "tensor_tensor", [P, biasT], [sb_], sb_.ap[0:M, :], P.ap[0:M, :], bias_ap, ALU.add)
                S.act(px.ap[0:M, :], sb_.ap[0:M, :], AF.Exp, [sb_], [px])
            else:
                S.act(px.ap[0:M, :], P.ap[0:M, :], AF.Exp, [P], [px])
            st["px"] = px

        def pv():
            pv_fn(st["px"])

        return (qk, pv)

    for i in range(NT):
        qi = qis[i % 2]
        oacc = oaccs[i % 2]

        def load_q(i=i):
            if i < NT:
                qa_ = qis[i % 2]
                for (r0, lh) in ((0, 0), (64, 1)):
                    for g_ in range(4):
                        S.dma("sp", qa_.ap[r0:r0 + 64, g_, lh, :, :],
                              qT_v[r0:r0 + 64, 2 * g_:2 * g_ + 2, i * 128:(i + 1) * 128],
                              [self.qT_d], [qa_.sub("q")])
        if i == 0:
            jobs.append((load_q, None))
        load_next = (lambda i=i: load_q(i + 1))
        for g in range(4):
            it = i * 4 + g
            w = sm[it % 2]
            Pow_, Pos_ = Pow_b[it % 2], Pos_b[it % 2]
            gv = gates.ap[:, i, g * 12:(g + 1) * 12].rearrange("p (h c) -> p h c", c=3)
            osl = oacc.ap[:, g * 256:(g + 1) * 256].rearrange("p (h d) -> p h d", h=4)

            def pv_c(px, g=g, i=i, w=w, gv=gv, osl=osl, qi=qi):
                Poc = Poc_b
                for hs in range(4):
                    S.mm(Poc.ap[:, hs * 97:(hs + 1) * 97], px.ap[0:127, hs * 128:(hs + 1) * 128], rhsc.ap[:, g, :],
                         hs == 0, hs == 3, [px, rhsc], [Poc])
                pc3 = Poc.ap[:, 0:388].rearrange("p (h c) -> p h c", h=4)
                rl = w["rl"]
                S.v("dve", "tensor_scalar", [Poc], [rl], rl.ap[:, 0:4], pc3[:, :, 64], 1e-30, None, ALU.max)
                S.v("dve", "reciprocal", [rl], [rl], rl.ap[:, 0:4], rl.ap[:, 0:4])
                imp = w["imp"]
                S.v("dve", "tensor_scalar", [Poc, rl], [imp], imp.ap, pc3[:, 0, 65:97], rl.ap[:, 0:1], None, ALU.mult)
                for hs in range(1, 4):
                    S.v("dve", "scalar_tensor_tensor", [Poc, rl, imp], [imp], imp.ap, pc3[:, hs, 65:97], rl.ap[:, hs:hs + 1],
                        imp.ap, ALU.mult, ALU.add)
                S.v("dve", "tensor_tensor", [imp, allowed], [imp], imp.ap, imp.ap, allowed.ap[:, i, :], ALU.mult)
                S.v("dve", "tensor_tensor", [imp, addc], [imp], imp.ap, imp.ap, addc.ap[:, i, :], ALU.add)
                m8 = w["m8"]
                S.v("dve", "max", [imp], [m8], out=m8.ap, in_=imp.ap)
                ns = w["ns"]
                S.v("dve", "tensor_scalar", [imp, m8], [ns], ns.ap[:, 0:32], imp.ap, m8.ap[:, 7:8], None, ALU.is_ge)
                S.v("dve", "tensor_scalar", [ns], [ns], ns.ap[:, 64:96], ns.ap[:, 0:32], -1.0, -NEG, ALU.add, ALU.mult)
                S.v("dve", "tensor_scalar", [ns], [ns], ns.ap[:, 0:32], ns.ap[:, 0:32], -1.0, -NEG, ALU.add, ALU.mult)
                Pt = score_banks[cnt["sb"] % NSB]
                cnt["sb"] += 1
                S.tr(Pt.ap[:, 0:128], ns.ap, ident.ap, [ns, ident], [Pt])
                S.act(qi.ap[64:96, g, 0, :, :], Pt.ap[64:96, 0:128].unsqueeze(1).to_broadcast([32, 2, 128]), AF.Copy, [Pt], [qi.sub(("m", g))])
                S.act(qi.ap[0:32, g, 1, :, :], Pt.ap[0:32, 0:128].unsqueeze(1).to_broadcast([32, 2, 128]), AF.Copy, [Pt], [qi.sub(("m", g))])
                S.v("dve", "tensor_tensor", [rl, gates], [rl], rl.ap[:, 0:4], rl.ap[:, 0:4], gv[:, :, 0], ALU.mult)
                S.v("dve", "tensor_tensor", [Poc, rl], [oacc.sub(g)], osl, pc3[:, :, 0:64],
                    rl.ap[:, 0:4].unsqueeze(2).to_broadcast([128, 4, 64]), ALU.mult)
            extra = (Sc.ap[:, i, 0:127], Mst.ap[:, 4 * g:4 * g + 4, :], [Sc, Mst])
            jobs.append(score_job(qi, g, kc.ap[:, g, 0, 0:127], kc.ap[:, g, 1, 0:127], 127, extra, None, pv_c, [kc]))
            if g == 1:
                jobs.append((load_next, None))

            def mk_pv(Pacc, vv, j, first, last, br, g=g, w=w, gv=gv, osl=osl, oacc=oacc):
                def pv(px):
                    for hs in range(4):
                        S.mm(Pacc.ap[:, hs * 65:(hs + 1) * 65], px.ap[:, hs * 128:(hs + 1) * 128], vv.ap[:, j, g, :],
                             first and hs == 0, last and hs == 3, [px, vv], [Pacc])
                    if last:
                        p3 = Pacc.ap[:, 0:260].rearrange("p (h c) -> p h c", h=4)
                        rl = w["rl"]
                        o = 4 * br
                        S.v("dve", "tensor_scalar", [Pacc], [rl], rl.ap[:, o:o + 4], p3[:, :, 64], 1e-30, None, ALU.max)
                        S.v("dve", "reciprocal", [rl], [rl], rl.ap[:, o:o + 4], rl.ap[:, o:o + 4])
                        S.v("dve", "tensor_tensor", [rl, gates], [rl], rl.ap[:, o:o + 4], rl.ap[:, o:o + 4], gv[:, :, br], ALU.mult)
                        tmp = w["tmp"]
                        t3 = tmp.ap.rearrange("p (h d) -> p h d", h=4)
                        S.v("dve", "tensor_tensor", [Pacc, rl], [tmp], t3, p3[:, :, 0:64],
                            rl.ap[:, o:o + 4].unsqueeze(2).to_broadcast([128, 4, 64]), ALU.mult)
                        S.v("dve", "tensor_tensor", [tmp, oacc.sub(g)], [oacc.sub(g)], osl, osl, t3, ALU.add)
                return pv

            js = list(range(max(0, i - 4), i + 1))
            for j in js:
                dd = {0: 0, 1: 1, 4: 2}.get(i - j)
                bias_ap = None if dd is None else biasT.ap[:, dd, 4 * g:4 * g + 4, :].rearrange("p h t -> p (h t)")
                jobs.append(score_job(qi, g, kw.ap[:, g, 0, j * 128:(j + 1) * 128], kw.ap[:, g, 1, j * 128:(j + 1) * 128],
                                      128, None, bias_ap, mk_pv(Pow_, vw, j, j == js[0], j == js[-1], 2), [kw]))
            for j in range(i + 1):
                dd = {0: 0, 1: 1}.get(i - j)
                bias_ap = None if dd is None else biasT.ap[:, dd, 4 * g:4 * g + 4, :].rearrange("p h t -> p (h t)")
                jobs.append(score_job(qi, g, ks.ap[:, g, 0, j * 128:(j + 1) * 128], ks.ap[:, g, 1, j * 128:(j + 1) * 128],
                                      128, None, bias_ap, mk_pv(Pos_, vs, j, j == 0, j == i, 1), [ks], use_mask=True))

        def finish(i=i, oacc=oacc):
            ob = obst[i % 2]
            for half in range(2):
                P = score_banks[cnt["sb"] % NSB]
                cnt["sb"] += 1
                for q in range(4):
                    kt = half * 4 + q
                    S.tr(P.ap[:, q * 128:(q + 1) * 128], oacc.ap[:, kt * 128:(kt + 1) * 128], ident.ap,
                         [oacc.sub(kt // 2), ident], [P])
                S.act(ob.ap[:, half * 4:(half + 1) * 4, :], P.ap.rearrange("p (q t) -> p q t", q=4), AF.Copy, [P], [ob])
            S.dma("sp", obT_v[:, :, i * 128:(i + 1) * 128], ob.ap, [ob], [self.obT_d])
        jobs.append((None, finish))

    pend = []
    for (qk, pv) in jobs:
        if len(pend) >= 4:
            f = pend.pop(0)
            if f is not None:
                f()
        if qk is not None:
            qk()
        pend.append(pv)
    for f in pend:
        if f is not None:
            f()
    S.barrier()
    ar.pop()


KB.stage4 = stage4


def nsa_bias_build(self):
    ar, S = self.ar, self.S
    d = self.din
    NOH = 3 * 16384 + 17 * 128
    oh_d = d("nsa_oh", [33, NOH])
    relb_d = d("rel_bias", [32, 16])
    self.bias_d = bias_d = self.dscr("bias_scr", [16, NOH])
    trel = ar.alloc([16], F32, "trel", parts=33)
    tbl = ar.alloc([16], F32, "tbl", parts=32)
    t31 = ar.alloc([16], F32, "t31", parts=32)
    S.dma("sp", tbl.ap, relb_d.ap, [relb_d], [tbl])
    S.dma("sp", t31.ap, relb_d.ap[31:32, :].partition_broadcast(32), [relb_d], [t31])
    S.v("dve", "memset", [], [trel], trel.ap, NEG)
    S.v("dve", "tensor_tensor", [tbl, t31, trel], [trel], trel.ap[0:32, :], tbl.ap, t31.ap, ALU.subtract)
    ohb = [ar.alloc([2048], F32, f"ohb{i}", parts=33) for i in range(2)]
    bsb = [ar.alloc([2048], F32, f"bsb{i}", parts=16) for i in range(2)]
    nblk = (NOH + 2047) // 2048

    def mk(bi):
        def step():
            c0 = bi * 2048
            n = min(2048, NOH - c0)
            ob, bs = ohb[bi % 2], bsb[bi % 2]
            S.dma("sp", ob.ap[:, 0:n], oh_d.ap[:, c0:c0 + n], [oh_d], [ob])
            for q in range((n + 511) // 512):
                w = min(512, n - q * 512)
                P = self.bank()
                S.mm(P.ap[0:16, 0:w], trel.ap, ob.ap[:, q * 512:q * 512 + w], True, True, [trel, ob], [P])
                S.act(bs.ap[:, q * 512:q * 512 + w], P.ap[0:16, 0:w], AF.Copy, [P], [bs])
            S.dma("sp", bias_d.ap[:, c0:c0 + n], bs.ap[:, 0:n], [bs], [bias_d])
        return step
    return [mk(bi) for bi in range(nblk)]


KB.nsa_bias_build = nsa_bias_build


LAM = 0.6065306597126334


def rwkv_consts():
    c = {}
    p = np.arange(128)
    ut_strict = (p[:, None] < p[None, :]).astype(np.float32)
    ut_incl = (p[:, None] <= p[None, :]).astype(np.float32)
    c["rw_mAB"] = np.ascontiguousarray(np.concatenate([ut_strict, ut_incl], 1))
    c["rw_mLT"] = (p[:, None] > p[None, :]).astype(np.float32)
    rs = np.ones((128, 8, 128), np.float32)
    rs[:, :, 0] = 0.0
    c["rw_reset"] = rs.reshape(128, 1024)
    hm = np.zeros((128, 2), np.float32)
    hm[:64, 0] = 1.0
    hm[64:, 1] = 1.0
    c["rw_hsel"] = hm
    return c


def rwkv_prep(inp, m):
    g = lambda k: inp[k][0]
    m["rw_w0"] = _fm(g("rwkv_w0"))
    m["rw_a0"] = _fm(g("rwkv_a0"))
    m["rw_kk"] = _fm(g("rwkv_k_k"))
    m["rw_ka"] = _fm(g("rwkv_k_a"))
    m["rw_rk"] = _fm(g("rwkv_r_k").reshape(-1))
    m["rw_lnw"] = np.ascontiguousarray(np.broadcast_to(g("rwkv_ln_w")[None, :], (128, 1024)).astype(np.float32))
    m["rw_lnb"] = np.ascontiguousarray(np.broadcast_to(g("rwkv_ln_b")[None, :], (128, 1024)).astype(np.float32))
    z = np.zeros((64, 1024), np.float32)
    m["rw_w2pad"] = np.ascontiguousarray(np.concatenate([g("rwkv_w2"), z], 0))
    m["rw_a2pad"] = np.ascontiguousarray(np.concatenate([z, g("rwkv_a2")], 0))
    m["rw_g2"] = g("rwkv_g2")


def stage3(self):
    ar, S = self.ar, self.S
    d = self.din
    ident, bones = self.ident, self.bones
    self.oaT_d = self.dscr("oaT", [1024, SEQ], BF16)
    ar.push()

    def cload(name, shape, parts=128, src=None):
        dt_ = d(name, [parts] + list(shape)) if src is None else src
        t = ar.alloc(shape, F32, name, parts=parts)
        S.dma("sp", t.ap, dt_.ap, [dt_], [t])
        return t
    w0 = cload("rw_w0", [8])
    a0 = cload("rw_a0", [8])
    kkf = cload("rw_kk", [8])
    kaf = cload("rw_ka", [8])
    rkf = cload("rw_rk", [8])
    lnw = cload("rw_lnw", [1024])
    lnb = cload("rw_lnb", [1024])
    w2p = cload("rw_w2pad", [1024])
    a2p = cload("rw_a2pad", [1024])
    g2_d = d("rw_g2", [160, 1024])
    g2a = ar.alloc([1024], F32, "g2a")
    g2b = ar.alloc([1024], F32, "g2b", parts=32)
    S.dma("sp", g2a.ap, g2_d.ap[0:128, :], [g2_d], [g2a])
    S.dma("sp", g2b.ap, g2_d.ap[128:160, :], [g2_d], [g2b])
    mAB = cload("rw_mAB", [256])
    mLT = cload("rw_mLT", [128])
    reset = cload("rw_reset", [1024])
    hsel = cload("rw_hsel", [2])
    omka = ar.alloc([8], F32, "omka")
    S.v("dve", "tensor_scalar", [kaf], [omka], omka.ap, kaf.ap, -1.0, 1.0, ALU.mult, ALU.add)
    Hp = ar.alloc([16, 64], F32, "Hp")
    S.v("dve", "memset", [], [Hp], Hp.ap, 0.0)

    A = lambda n, shape=(8, 128): ar.alloc(list(shape), F32, n)
    raw = A("raw", (27, 128))
    sgw, cs, E, Eex = A("sgw"), A("cs"), A("E"), A("Eex")
    aT, kkn, kp, bb, btl = A("aT"), A("kkn"), A("kp"), A("bb"), A("btl")
    AR = A("AR", (8, 2, 128))
    blo, bhi, klo, khi, alo, ahi = A("blo"), A("bhi"), A("klo"), A("khi"), A("alo"), A("ahi")
    tmpA, tmpB = A("tmpA"), A("tmpB")
    bh, kh = sgw, cs
    v_tok, bh_tok, kh_tok, g_tok = A("v_tok", (1024,)), A("bh_tok", (1024,)), A("kh_tok", (1024,)), A("g_tok", (1024,))
    th = A("th", (128,))
    sx = A("sx", (128,))
    sx2 = ar.alloc([128], F32, "sx2", parts=32)
    PLs = [A("PL0", (8,)), A("PL1", (8,))]
    nb = A("nb", (8,))
    rk16 = A("rk16", (16,))
    st16 = [A(f"st16_{i}", (16,)) for i in range(4)]
    import os
    SQDT = BF16 if os.environ.get("RW_SQ") == "bf16" else F32
    MASK_POOL = os.environ.get("RW_MASK") == "pool"
    NO_IL = os.environ.get("RW_IL") == "0"
    slots = []
    for s_ in range(2):
        slots.append(dict(
            ABm=A(f"ABm{s_}", (4, 256)), AKm=A(f"AKm{s_}", (4, 256)),
            Yf=[A(f"Yf{s_}{i}", (4, 128)) for i in range(2)] if SQDT != F32 else None,
            Yb=[ar.alloc([4, 128], SQDT, f"Yb{s_}{i}") for i in range(2)],
            XW=[A(f"XW{s_}{i}", (4, 192)) for i in range(2)]))
        if SQDT == F32:
            slots[-1]["Yf"] = slots[-1]["Yb"]
    if SQDT == F32:
        ar.off -= 0
    oast = [ar.alloc([8, 128], BF16, f"oast{i}") for i in range(1)] * 2
    f2 = lambda t: t.ap.rearrange("p a b -> p (a b)")
    rw_v = self.rwT_d.ap.rearrange("(kt p) t -> p kt t", p=128)
    oaT_v = self.oaT_d.ap.rearrange("(kt p) t -> p kt t", p=128)
    dv = lambda meth, reads, writes, *a, **k: S.v("dve", meth, reads, writes, *a, **k)
    pl = lambda meth, reads, writes, *a, **k: S.v("pool", meth, reads, writes, *a, **k)
    bc8 = lambda t: t.ap.unsqueeze(2).to_broadcast([128, 8, 128])

    def P1(c):
            S.dma("sp", raw.ap, rw_v[:, :, c * 128:(c + 1) * 128], [self.rwT_d], [raw])
            yield
            rT, kT, vT = raw.ap[:, 0:8, :], raw.ap[:, 8:16, :], raw.ap[:, 16:24, :]
            yield
            t24 = raw.ap[:, 24, :]
            yield
            S.act(th.ap, t24, AF.Tanh, [raw], [th])
            yield
            Pz = [self.bank(), self.bank()]
            yield
            for kt in range(8):
                P = Pz[kt // 4]
                S.mm(P.ap[:, (kt % 4) * 128:(kt % 4 + 1) * 128], w2p.ap[:, kt * 128:(kt + 1) * 128], th.ap, True, True, [w2p, th], [P])
            yield
            for kt in range(8):
                S.act(sgw.ap[:, kt, :], Pz[kt // 4].ap[:, (kt % 4) * 128:(kt % 4 + 1) * 128], AF.Sigmoid, [Pz[kt // 4], w0], [sgw],
                      bias=w0.ap[:, kt:kt + 1], scale=1.0)
            yield
            Pa = [self.bank(), self.bank()]
            yield
            for kt in range(8):
                P = Pa[kt // 4]
                S.mm(P.ap[:, (kt % 4) * 128:(kt % 4 + 1) * 128], a2p.ap[:, kt * 128:(kt + 1) * 128], t24, True, True, [a2p, raw], [P])
            yield
            for kt in range(8):
                S.act(aT.ap[:, kt, :], Pa[kt // 4].ap[:, (kt % 4) * 128:(kt % 4 + 1) * 128], AF.Sigmoid, [Pa[kt // 4], a0], [aT],
                      bias=a0.ap[:, kt:kt + 1], scale=1.0)
            yield
            dv("tensor_tensor_scan", [reset, sgw], [cs], f2(cs), reset.ap, f2(sgw), 0.0, ALU.mult, ALU.add)
            yield
            S.act(f2(E), f2(cs), AF.Exp, [cs], [E], scale=-LAM)
            yield
            dv("tensor_tensor", [cs, sgw], [Eex], f2(Eex), f2(cs), f2(sgw), ALU.subtract)
            yield
            S.act(f2(Eex), f2(Eex), AF.Exp, [Eex], [Eex], scale=-LAM)
            yield
            dv("tensor_scalar", [cs], [nb], nb.ap, cs.ap[:, :, 127], -LAM, None, ALU.mult)
            yield
            S.act(PLs[c % 2].ap, nb.ap, AF.Exp, [nb], [PLs[c % 2]])
            yield
            dv("tensor_tensor", [raw, kkf], [kkn], kkn.ap, kT, bc8(kkf), ALU.mult)
            yield
            dv("tensor_tensor", [kkn], [tmpB], tmpB.ap, kkn.ap, kkn.ap, ALU.mult)
            yield
            Pn = [self.bank(), self.bank()]
            yield
            for hh in range(2):
                S.mm(Pn[hh].ap, bones.ap, tmpB.ap[:, hh * 4:(hh + 1) * 4, :], True, True, [bones, tmpB], [Pn[hh]])
            yield
            for hh in range(2):
                S.act(tmpB.ap[:, hh * 4:(hh + 1) * 4, :], Pn[hh].ap.rearrange("p (a b) -> p a b", a=4), AF.Sqrt, [Pn[hh]], [tmpB])
            yield
            dv("tensor_scalar", [tmpB], [tmpB], f2(tmpB), f2(tmpB), 1e-12, None, ALU.max)
            yield
            dv("reciprocal", [tmpB], [tmpB], f2(tmpB), f2(tmpB))
            yield
            dv("tensor_tensor", [kkn, tmpB], [kkn], f2(kkn), f2(kkn), f2(tmpB), ALU.mult)
            yield
            dv("tensor_tensor", [aT, kaf], [kp], kp.ap, aT.ap, bc8(kaf), ALU.mult)
            yield
            dv("tensor_tensor", [kp, omka], [kp], kp.ap, kp.ap, bc8(omka), ALU.add)
            yield
            dv("tensor_tensor", [kp, raw], [kp], kp.ap, kp.ap, kT, ALU.mult)
            yield
            dv("tensor_tensor", [kkn, aT], [bb], f2(bb), f2(kkn), f2(aT), ALU.mult)
            yield


    def P2(c):
            rT, kT, vT = raw.ap[:, 0:8, :], raw.ap[:, 8:16, :], raw.ap[:, 16:24, :]
            dv("tensor_tensor", [raw, E], [AR], AR.ap[:, :, 1, :], rT, E.ap, ALU.mult)
            dv("scalar_tensor_tensor", [kkn, Eex], [AR], AR.ap[:, :, 0, :], kkn.ap, -1.0, Eex.ap, ALU.mult, ALU.mult)
            Einv, Elast = E, Eex
            S.act(f2(Einv), f2(cs), AF.Exp, [cs], [Einv], scale=LAM)
            for kt in range(8):
                S.act(Elast.ap[:, kt, :], cs.ap[:, kt, :], AF.Exp, [cs, nb], [Elast], bias=nb.ap[:, kt:kt + 1], scale=LAM)
            dv("tensor_tensor", [bb, Einv], [btl], f2(btl), f2(bb), f2(Einv), ALU.mult)
            dv("tensor_tensor", [kp, Einv], [tmpA], f2(tmpA), f2(kp), f2(Einv), ALU.mult)
            dv("tensor_tensor", [bb, Elast], [bh], f2(bh), f2(bb), f2(Elast), ALU.mult)
            dv("tensor_tensor", [kp, Elast], [kh], f2(kh), f2(kp), f2(Elast), ALU.mult)
            for (dst, src, col) in ((blo, btl, 0), (bhi, btl, 1), (klo, tmpA, 0), (khi, tmpA, 1)):
                if MASK_POOL:
                    pl("tensor_scalar", [src, hsel], [dst], f2(dst), f2(src), hsel.ap[:, col:col + 1], None, ALU.mult)
                    continue
                S.act(f2(dst), f2(src), AF.Identity, [src, hsel], [dst], scale=hsel.ap[:, col:col + 1], bias=0.0)
            for (dst, col) in ((alo, 0), (ahi, 1)):
                if MASK_POOL:
                    pl("tensor_scalar", [AR, hsel], [dst], dst.ap, AR.ap[:, :, 0, :], hsel.ap[:, col:col + 1], None, ALU.mult)
                    continue
                S.act(dst.ap, AR.ap[:, :, 0, :], AF.Identity, [AR, hsel], [dst], scale=hsel.ap[:, col:col + 1], bias=0.0)
            dv("tensor_tensor", [raw, kp], [tmpB], tmpB.ap, rT, kp.ap, ALU.mult)
            dv("tensor_tensor", [tmpB, rkf], [tmpB], tmpB.ap, tmpB.ap, bc8(rkf), ALU.mult)
            Pr = self.bank()
            for kt in range(8):
                S.mm(Pr.ap[:, 2 * kt:2 * kt + 2], tmpB.ap[:, kt, :], hsel.ap, kt == 0, kt == 7, [tmpB, hsel], [Pr])
            S.act(rk16.ap, Pr.ap[:, 0:16], AF.Copy, [Pr], [rk16])
            S.act(sx.ap, raw.ap[:, 25, :], AF.Sigmoid, [raw], [sx])
            S.act(sx2.ap, raw.ap[0:32, 26, :], AF.Sigmoid, [raw], [sx2])
            for hh in range(2):
                P = self.bank()
                S.mm(P.ap, sx.ap, g2a.ap[:, hh * 512:(hh + 1) * 512], True, False, [sx, g2a], [P])
                S.mm(P.ap, sx2.ap, g2b.ap[:, hh * 512:(hh + 1) * 512], False, True, [sx2, g2b], [P])
                S.act(g_tok.ap[:, hh * 512:(hh + 1) * 512], P.ap, AF.Copy, [P], [g_tok])
            for (src_ap, src_t, dst) in ((vT, raw, v_tok), (bh.ap, bh, bh_tok), (kh.ap, kh, kh_tok)):
                for hh in range(2):
                    P = self.bank()
                    for q in range(4):
                        kt = hh * 4 + q
                        S.tr(P.ap[:, q * 128:(q + 1) * 128], src_ap[:, kt, :], ident.ap, [src_t, ident], [P])
                    S.act(dst.ap[:, hh * 512:(hh + 1) * 512], P.ap, AF.Copy, [P], [dst])


    def heads(c, step):
            y_tok = tmpA
            def phaseA(hg, sl):
                heads = [4 * hg + x for x in range(4)]
                ABm, AKm = sl["ABm"], sl["AKm"]
                PA = [self.bank(), self.bank()]
                PB = [self.bank(), self.bank()]
                PX = self.bank()
                for hl, h in enumerate(heads):
                    kt, half = h // 2, h % 2
                    bsel = (blo, bhi)[half]
                    ksel = (klo, khi)[half]
                    asel = (alo, ahi)[half]
                    ar_rhs = AR.ap[:, kt, :, :]
                    oa = PA[hl // 2].ap[:, (hl % 2) * 256:(hl % 2 + 1) * 256]
                    ob_ = PB[hl // 2].ap[:, (hl % 2) * 256:(hl % 2 + 1) * 256]
                    S.mm(oa, bsel.ap[:, kt, :], ar_rhs, hl % 2 == 0, hl % 2 == 1, [bsel, AR], [PA[hl // 2]])
                    S.mm(ob_, ksel.ap[:, kt, :], ar_rhs, hl % 2 == 0, hl % 2 == 1, [ksel, AR], [PB[hl // 2]])
                    S.mm(PX.ap[:, hl * 128:(hl + 1) * 128], asel.ap[:, kt, :], btl.ap[:, kt, :], hl == 0, hl == 3, [asel, btl], [PX])
                mAB2 = mAB.ap.unsqueeze(1).to_broadcast([128, 2, 256])
                for q in range(2):
                    dv("tensor_tensor", [PA[q], mAB], [ABm], ABm.ap[:, 2 * q:2 * q + 2, :],
                       PA[q].ap.rearrange("p (a b) -> p a b", a=2), mAB2, ALU.mult)
                    dv("tensor_tensor", [PB[q], mAB], [AKm], AKm.ap[:, 2 * q:2 * q + 2, :],
                       PB[q].ap.rearrange("p (a b) -> p a b", a=2), mAB2, ALU.mult)
                dv("tensor_tensor", [PX, mLT], [sl["XW"][0]], sl["XW"][0].ap[:, :, 0:128], PX.ap.rearrange("p (a b) -> p a b", a=4),
                   mLT.ap.unsqueeze(1).to_broadcast([128, 4, 128]), ALU.mult)
                S.act(sl["Yb"][0].ap, ABm.ap[:, :, 0:128], AF.Copy, [ABm], [sl["Yb"][0]])
                PW = self.bank()
                for hl, h in enumerate(heads):
                    kt = h // 2
                    o = PW.ap[:, hl * 64:(hl + 1) * 64]
                    S.mm(o, AR.ap[:, kt, 0, :], Hp.ap[:, h, :], hl == 0, False, [AR, Hp.sub(h)], [PW])
                    S.mm(o, AKm.ap[:, hl, 0:128], v_tok.ap[:, h * 64:(h + 1) * 64], False, hl == 3, [AKm, v_tok], [PW])
                S.act(sl["XW"][0].ap[:, :, 128:192], PW.ap[:, 0:256].rearrange("p (h d) -> p h d", h=4), AF.Copy, [PW], [sl["XW"][0]])

            def level(sl, lv):
                Y, XW = sl["Yb"][lv % 2], sl["XW"][lv % 2]
                Yn, XWn = sl["Yb"][(lv + 1) % 2], sl["XW"][(lv + 1) % 2]
                if lv < 6:
                    PUX = [self.bank(), self.bank()]
                    for hl in range(4):
                        o = PUX[hl // 2].ap[:, (hl % 2) * 192:(hl % 2 + 1) * 192]
                        S.mm(o, Y.ap[:, hl, :], XW.ap[:, hl, :], hl % 2 == 0, hl % 2 == 1, [Y, XW], [PUX[hl // 2]])
                    PY2 = self.bank()
                    for hl in range(4):
                        S.mm(PY2.ap[:, hl * 128:(hl + 1) * 128], XW.ap[:, hl, 0:128], Y.ap[:, hl, :], hl == 0, hl == 3, [XW, Y], [PY2])
                    for q in range(2):
                        view = PUX[q].ap[:, 0:384].rearrange("p (a c) -> p a c", a=2)
                        dv("tensor_tensor", [PUX[q], XW], [XWn], XWn.ap[:, 2 * q:2 * q + 2, 128:192], view[:, :, 128:192],
                           XW.ap[:, 2 * q:2 * q + 2, 128:192], ALU.add)
                        S.act(XWn.ap[:, 2 * q:2 * q + 2, 0:128], view[:, :, 0:128], AF.Copy, [PUX[q]], [XWn])
                    dv("tensor_copy", [PY2], [Yn], f2(Yn), PY2.ap)
                else:
                    PU = self.bank()
                    for hl in range(4):
                        S.mm(PU.ap[:, hl * 64:(hl + 1) * 64], Y.ap[:, hl, :], XW.ap[:, hl, 128:192], hl == 0, hl == 3, [Y, XW], [PU])
                    dv("tensor_tensor", [PU, XW], [XWn], XWn.ap[:, :, 128:192], PU.ap[:, 0:256].rearrange("p (h d) -> p h d", h=4),
                       XW.ap[:, :, 128:192], ALU.add)

            def phaseY(hg, sl):
                heads = [4 * hg + x for x in range(4)]
                ABm, AKm = sl["ABm"], sl["AKm"]
                U = sl["XW"][1]
                PY = self.bank()
                for hl, h in enumerate(heads):
                    kt = h // 2
                    o = PY.ap[:, hl * 64:(hl + 1) * 64]
                    S.mm(o, AR.ap[:, kt, 1, :], Hp.ap[:, h, :], hl == 0, False, [AR, Hp.sub(h)], [PY])
                    S.mm(o, ABm.ap[:, hl, 128:256], U.ap[:, hl, 128:192], False, False, [ABm, U], [PY])
                    S.mm(o, AKm.ap[:, hl, 128:256], v_tok.ap[:, h * 64:(h + 1) * 64], False, hl == 3, [AKm, v_tok], [PY])
                PH = self.bank()
                for hl, h in enumerate(heads):
                    kt = h // 2
                    o = PH.ap[:, hl * 64:(hl + 1) * 64]
                    S.mm(o, bh_tok.ap[:, kt * 128:(kt + 1) * 128], U.ap[:, hl, 128:192], hl == 0, False, [bh_tok, U], [PH])
                    S.mm(o, kh_tok.ap[:, kt * 128:(kt + 1) * 128], v_tok.ap[:, h * 64:(h + 1) * 64], False, hl == 3, [kh_tok, v_tok], [PH])
                S.act(y_tok.ap.rearrange("p a b -> p (a b)")[:, hg * 256:(hg + 1) * 256], PY.ap[:, 0:256], AF.Copy, [PY], [y_tok])
                for hl, h in enumerate(heads):
                    kt, half = h // 2, h % 2
                    r0 = 64 * half
                    dv("scalar_tensor_tensor", [Hp.sub(h), PLs[c % 2], PH], [Hp.sub(h)], Hp.ap[r0:r0 + 64, h, :], Hp.ap[r0:r0 + 64, h, :],
                       PLs[c % 2].ap[r0:r0 + 64, kt:kt + 1], PH.ap[r0:r0 + 64, hl * 64:(hl + 1) * 64], ALU.mult, ALU.add)

            for pair in range(2):
                gA, gB = 2 * pair, 2 * pair + 1
                if NO_IL:
                    for (g_, sl_) in ((gA, slots[0]), (gB, slots[1])):
                        phaseA(g_, sl_)
                        for lv in range(7):
                            level(sl_, lv)
                        phaseY(g_, sl_)
                    continue
                phaseA(gA, slots[0])
                phaseA(gB, slots[1])
                for lv in range(7):
                    level(slots[0], lv)
                    step(); step()
                    level(slots[1], lv)
                    step(); step()
                phaseY(gA, slots[0])
                phaseY(gB, slots[1])


    def post(c):
            y_tok = tmpA
            yf = y_tok.ap.rearrange("p a b -> p (a b)")
            y3 = yf.rearrange("p (h d) -> p h d", h=16)
            sum_, sq_, mean, rstd = st16
            t1, t2 = y_tok, btl
            t1f, t2f = f2(t1), f2(t2)
            dv("tensor_reduce", [y_tok], [sum_], sum_.ap, y3, AX.X, ALU.add)
            dv("tensor_tensor", [y_tok], [t2], t2f, yf, yf, ALU.mult)
            dv("tensor_reduce", [t2], [sq_], sq_.ap, t2f.rearrange("p (h d) -> p h d", h=16), AX.X, ALU.add)
            dv("tensor_scalar", [sum_], [mean], mean.ap, sum_.ap, 1.0 / 64, None, ALU.mult)
            dv("tensor_tensor", [mean], [rstd], rstd.ap, mean.ap, mean.ap, ALU.mult)
            dv("scalar_tensor_tensor", [sq_, rstd], [rstd], rstd.ap, sq_.ap, 1.0 / 64, rstd.ap, ALU.mult, ALU.subtract)
            S.act(rstd.ap, rstd.ap, AF.Sqrt, [rstd], [rstd], bias=GN_EPS, scale=1.0)
            dv("reciprocal", [rstd], [rstd], rstd.ap, rstd.ap)
            b16 = lambda t: t.ap.unsqueeze(2).to_broadcast([128, 16, 64])
            t13 = t1f.rearrange("p (h d) -> p h d", h=16)
            t23 = t2f.rearrange("p (h d) -> p h d", h=16)
            dv("tensor_tensor", [y_tok, mean], [t1], t13, y3, b16(mean), ALU.subtract)
            dv("tensor_tensor", [t1, rstd], [t1], t13, t13, b16(rstd), ALU.mult)
            dv("tensor_tensor", [t1, lnw], [t1], t1f, t1f, lnw.ap, ALU.mult)
            dv("tensor_tensor", [t1, lnb], [t1], t1f, t1f, lnb.ap, ALU.add)
            dv("tensor_tensor", [v_tok, rk16], [t2], t23, v_tok.ap.rearrange("p (h d) -> p h d", h=16), b16(rk16), ALU.mult)
            dv("tensor_tensor", [t1, t2], [t1], t1f, t1f, t2f, ALU.add)
            dv("tensor_tensor", [t1, g_tok], [t1], t1f, t1f, g_tok.ap, ALU.mult)
            if self.dbg and c == 0:
                dd = self.dscr("dbg_oa0", [128, 1024])
                S.dma("sp", dd.ap, t1f, [t1], [dd])
                dd2 = self.dscr("dbg_y0", [128, 1024])
                S.dma("sp", dd2.ap, yf, [y_tok], [dd2])
            ob = oast[c % 2]
            for hh in range(2):
                P = self.bank()
                for q in range(4):
                    kt = hh * 4 + q
                    S.tr(P.ap[:, q * 128:(q + 1) * 128], t1f[:, kt * 128:(kt + 1) * 128], ident.ap, [t1, ident], [P])
                S.act(ob.ap[:, hh * 4:(hh + 1) * 4, :], P.ap.rearrange("p (q t) -> p q t", q=4), AF.Copy, [P], [ob])
            S.dma("sp", oaT_v[:, :, c * 128:(c + 1) * 128], ob.ap, [ob], [self.oaT_d])


    gen = P1(0)
    for _ in gen:
        pass
    for c in range(NT):
        P2(c)
        gen = P1(c + 1) if c + 1 < NT else iter(())

        def step(gen=gen):
            next(gen, None)
        heads(c, step)
        for _ in gen:
            pass
        post(c)
    S.barrier()
    ar.pop()


KB.stage3 = stage3


def stage5(self):
    ar, S = self.ar, self.S
    d = self.din
    wor_d = d("w_o_rwkv", [1024, D])
    won_d = d("w_o_nsa", [1024, D])
    wout_d = d("w_out", [D, D])
    ar.push()
    mixT = ar.alloc([16, SEQ], BF16, "mixT")
    ar.push()
    oaT = ar.alloc([8, SEQ], BF16, "oaT")
    obT = ar.alloc([8, SEQ], BF16, "obT")
    S.dma("sp", oaT.ap, self.oaT_d.ap.rearrange("(k p) t -> p k t", p=128), [self.oaT_d], [oaT])
    S.dma("sp", obT.ap, self.obT_d.ap.rearrange("(k p) t -> p k t", p=128), [self.obT_d], [obT])
    woa = [ar.alloc([8, 128], BF16, f"woa{i}") for i in range(2)]
    wob = [ar.alloc([8, 128], BF16, f"wob{i}") for i in range(2)]
    sga = [ar.alloc([SEQ], BF16, f"sga{i}") for i in range(2)]
    sgb = [ar.alloc([SEQ], BF16, f"sgb{i}") for i in range(2)]
    t1s = [ar.alloc([512], F32, f"t1_{i}") for i in range(2)]
    t2s = [ar.alloc([512], F32, f"t2_{i}") for i in range(2)]
    wor_v = wor_d.ap.rearrange("(k p) c -> p k c", p=128)
    won_v = won_d.ap.rearrange("(k p) c -> p k c", p=128)
    cnt = 0
    mod_steps = []
    for jt in range(16):
        wa, wb, sa, sb = woa[jt % 2], wob[jt % 2], sga[jt % 2], sgb[jt % 2]
        S.dma("pool", wa.ap, wor_v[:, :, jt * 128:(jt + 1) * 128], [wor_d], [wa])
        S.dma("pool", wb.ap, won_v[:, :, jt * 128:(jt + 1) * 128], [won_d], [wb])
        S.dma("sp", sa.ap, self.mgT_d.ap[jt * 128:(jt + 1) * 128, :], [self.mgT_d], [sa])
        S.dma("sp", sb.ap, self.mgT_d.ap[2048 + jt * 128:2048 + (jt + 1) * 128, :], [self.mgT_d], [sb])
        if jt >= 1:
            for _ in range(2):
                if mod_steps:
                    mod_steps.pop(0)()
        for n in range(4):
            Pa = self.bank()
            for k in range(8):
                S.mm(Pa.ap, wa.ap[:, k, :], oaT.ap[:, k, n * 512:(n + 1) * 512], k == 0, k == 7, [wa, oaT], [Pa])
            Pb = self.bank()
            for k in range(8):
                S.mm(Pb.ap, wb.ap[:, k, :], obT.ap[:, k, n * 512:(n + 1) * 512], k == 0, k == 7, [wb, obT], [Pb])
            t1, t2 = t1s[cnt % 2], t2s[cnt % 2]
            cnt += 1
            S.v("dve", "tensor_tensor", [Pa, sa], [t1], t1.ap, Pa.ap, sa.ap[:, n * 512:(n + 1) * 512], ALU.mult)
            S.v("dve", "tensor_tensor", [Pb, sb], [t2], t2.ap, Pb.ap, sb.ap[:, n * 512:(n + 1) * 512], ALU.mult)
            S.v("dve", "tensor_tensor", [t1, t2], [mixT.sub(n)], mixT.ap[:, jt, n * 512:(n + 1) * 512], t1.ap, t2.ap, ALU.add)
    while mod_steps:
        mod_steps.pop(0)()
    S.barrier()
    ar.pop()
    wout = ar.alloc([16, D], BF16, "wout")
    wout_v = wout_d.ap.rearrange("(k p) c -> p k c", p=128)
    for nn in range(4):
        S.dma("pool", wout.ap[:, :, nn * 512:(nn + 1) * 512], wout_v[:, :, nn * 512:(nn + 1) * 512], [wout_d], [wout.sub(nn)])
    xbs = [ar.alloc([D], F32, f"x5_{i}") for i in range(2)]
    obs = [ar.alloc([D], F32, f"o5_{i}") for i in range(2)]
    tms = [ar.alloc([512], F32, f"tm5_{i}") for i in range(2)]
    cnt = 0
    for tt in range(NT):
        xb, ob = xbs[tt % 2], obs[tt % 2]
        S.dma("sp", xb.ap, self.x_d.ap[tt * 128:(tt + 1) * 128, :], [self.x_d], [xb])
        for nn in range(4):
            P = self.bank()
            for k in range(16):
                S.mm(P.ap, mixT.ap[:, k, tt * 128:(tt + 1) * 128], wout.ap[:, k, nn * 512:(nn + 1) * 512], k == 0, k == 15,
                     [mixT.sub(tt // 4), wout.sub(nn)], [P])
            tm = tms[cnt % 2]
            cnt += 1
            S.v("dve", "tensor_tensor", [P, self.gt1], [tm], tm.ap, P.ap, self.gt1.ap[:, nn * 512:(nn + 1) * 512], ALU.mult)
            S.v("dve", "tensor_tensor", [tm, xb], [ob], ob.ap[:, nn * 512:(nn + 1) * 512], tm.ap, xb.ap[:, nn * 512:(nn + 1) * 512], ALU.add)
        S.dma("pool", self.x1_d.ap[tt * 128:(tt + 1) * 128, :], ob.ap, [ob], [self.x1_d])
    S.barrier()
    ar.pop()


def stage6(self):
    ar, S = self.ar, self.S
    d = self.din
    hT = self.hT
    ar.push()
    uT = ar.alloc([64, 512], BF16, "uT")
    wups = [ar.alloc([16, 256], BF16, f"wup{i}") for i in range(2)]
    wdns = [ar.alloc([8, 512], BF16, f"wdn{i}") for i in range(3)]
    rts = [ar.alloc([512], F32, f"rt{i}") for i in range(2)]
    xps = [ar.alloc([512], F32, f"xp{i}") for i in range(2)]
    ops_ = [ar.alloc([512], F32, f"op{i}") for i in range(2)]
    tms = [ar.alloc([512], F32, f"tm6_{i}") for i in range(2)]
    c_up = c_dn = c_e = 0
    for c in range(4):
        for fb in range(32):
            wu = wups[c_up % 2]
            c_up += 1
            S.dma("sp", wu.ap, self.wupb.ap[fb].rearrange("p (k c) -> p k c", k=16), [self.wupb], [wu])
            for ft in range(2):
                f = fb * 2 + ft
                P = self.ps[(c_e) % 4]
                rt = rts[c_e % 2]
                c_e += 1
                for k in range(16):
                    S.mm(P.ap, wu.ap[:, k, ft * 128:(ft + 1) * 128], hT.ap[:, k, c * 512:(c + 1) * 512], k == 0, k == 15,
                         [wu, hT.sub(c)], [P])
                S.act(rt.ap, P.ap, AF.Relu, [P], [rt])
                S.v("dve", "tensor_tensor", [rt], [uT.sub(f)], uT.ap[:, f, :], rt.ap, rt.ap, ALU.mult)
        for nn in range(4):
            accs = self.ps[4:8] if (nn % 2 == 0) else self.ps[0:4]
            for f8 in range(8):
                wd = wdns[c_dn % 3]
                c_dn += 1
                S.dma("sp", wd.ap, self.wdnb.ap[nn, f8].rearrange("p (f c) -> p f c", f=8), [self.wdnb], [wd])
                for fi in range(8):
                    f = f8 * 8 + fi
                    for tt in range(4):
                        S.mm(accs[tt].ap, uT.ap[:, f, tt * 128:(tt + 1) * 128], wd.ap[:, fi, :], f == 0, f == 63,
                             [uT.sub(f), wd], [accs[tt]])
            for tt in range(4):
                row = (c * 4 + tt) * 128
                xp, op, tm = xps[c_e % 2], ops_[c_e % 2], tms[c_e % 2]
                c_e += 1
                S.dma("pool", xp.ap, self.x1_d.ap[row:row + 128, nn * 512:(nn + 1) * 512], [self.x1_d], [xp])
                S.v("dve", "tensor_tensor", [accs[tt], self.gt2], [tm], tm.ap, accs[tt].ap, self.gt2.ap[:, nn * 512:(nn + 1) * 512], ALU.mult)
                S.v("dve", "tensor_tensor", [tm, xp], [op], op.ap, tm.ap, xp.ap, ALU.add)
                S.dma("pool", self.out_d.ap[row:row + 128, nn * 512:(nn + 1) * 512], op.ap, [op], [self.out_d])
    S.barrier()
    ar.pop()


KB.stage5 = stage5
KB.stage6 = stage6


def precast(self):
    S = self.S
    self.wup_in = self.din("w_up", [D, DFF])
    self.wdn_in = self.din("w_down", [DFF, D])
    self.wupb = T(self.nc.dram_tensor("wupb_i", [32, 128, 16 * 256], BF16).ap(), "wupb")
    self.wdnb = T(self.nc.dram_tensor("wdnb_i", [4, 8, 128, 8 * 512], BF16).ap(), "wdnb")
    wup_v = self.wup_in.ap.rearrange("(k p) c -> p k c", p=128)
    wdn_v = self.wdn_in.ap.rearrange("(f p) c -> p f c", p=128)
    for fb in range(32):
        S.dma("pool", self.wupb.ap[fb].rearrange("p (k c) -> p k c", k=16), wup_v[:, :, fb * 256:(fb + 1) * 256],
              [self.wup_in], [self.wupb])
    for nn in range(4):
        for f8 in range(8):
            S.dma("pool", self.wdnb.ap[nn, f8].rearrange("p (f c) -> p f c", f=8),
                  wdn_v[:, f8 * 8:(f8 + 1) * 8, nn * 512:(nn + 1) * 512], [self.wdn_in], [self.wdnb])


KB.precast = precast


_CACHE = {}


def _all_consts():
    c = host_consts()
    c.update(nsa_consts())
    c.update(rwkv_consts())
    return c


def prep_all(inp, b, consts):
    m = prep_core(inp, b, consts)
    nsa_prep(inp, m)
    rwkv_prep(inp, m)
    m["w_o_rwkv"] = inp["w_o_rwkv"][0]
    m["w_o_nsa"] = inp["w_o_nsa"][0]
    m["w_out"] = inp["w_out"][0]
    m["w_up"] = inp["w_up"][0]
    m["w_down"] = inp["w_down"][0]
    return m


def kernel(**inputs):
    inp = {k: np.asarray(v) for k, v in inputs.items()}
    if "nc" not in _CACHE:
        _CACHE["nc"] = KB(dbg=False).build()
        _CACHE["consts"] = _all_consts()
    nc = _CACHE["nc"]
    consts = _CACHE["consts"]
    in_maps = [prep_all(inp, b, consts) for b in range(8)]
    res = run_bass_kernel_spmd(nc, in_maps, core_ids=list(range(8)))
    out = np.stack([np.asarray(r["out"]) for r in res.results], axis=0)
    return out.astype(np.float32)
```

```python
import numpy as np
import concourse.bass as bass
import concourse.mybir as mybir

F32 = mybir.dt.float32
BF16 = mybir.dt.bfloat16
AF = mybir.ActivationFunctionType
ALU = mybir.AluOpType
AX = mybir.AxisListType

ENGS = ("pe", "act", "dve", "pool", "sp")
NSLOT = {"sp": 40, "pool": 24}


class Buf:
    __slots__ = ("w", "r_eng", "r_dma", "name")

    def __init__(self, name=""):
        self.w = None
        self.r_eng = {}
        self.r_dma = []
        self.name = name


class T(Buf):
    __slots__ = ("ap", "subs")

    def __init__(self, ap, name=""):
        Buf.__init__(self, name)
        self.ap = ap
        self.subs = {}

    def __getitem__(self, idx):
        return self.ap[idx]

    def sub(self, key):
        b = self.subs.get(key)
        if b is None:
            b = self.subs[key] = Buf(f"{self.name}.{key}")
        return b


class Op:
    __slots__ = ("eng", "fn", "deps", "is_dma", "slot", "sig", "val", "dsem", "dval", "idx")

    def __init__(self, eng, fn, is_dma):
        self.eng = eng
        self.fn = fn
        self.is_dma = is_dma
        self.deps = []
        self.sig = False
        self.val = 0
        self.slot = -1
        self.dsem = None
        self.dval = 0


class Sched:
    def __init__(self, nc):
        self.nc = nc
        self.ops = {e: [] for e in ENGS}
        self.bar = {e: [] for e in ENGS}
        self.dma_since_bar = []
        self.slot_last = {q: [None] * n for q, n in NSLOT.items()}
        self.slot_n = {q: 0 for q in NSLOT}

    def rec(self, eng, fn, reads=(), writes=(), is_dma=False):
        op = Op(eng, fn, is_dma)
        deps = []
        for b in reads:
            if b.w is not None:
                deps.append(b.w)
        for b in writes:
            if b.w is not None:
                deps.append(b.w)
            deps.extend(b.r_eng.values())
            deps.extend(b.r_dma)
        if self.bar[eng]:
            deps.extend(self.bar[eng])
            self.bar[eng] = []
        if is_dma:
            n = self.slot_n[eng]
            self.slot_n[eng] = n + 1
            s = n % NSLOT[eng]
            op.slot = s
            prev = self.slot_last[eng][s]
            if prev is not None:
                deps.append(prev)
            self.slot_last[eng][s] = op
            self.dma_since_bar.append(op)
        seen = set()
        for d in deps:
            if d is op or id(d) in seen:
                continue
            seen.add(id(d))
            op.deps.append(d)
        for b in reads:
            if is_dma:
                b.r_dma.append(op)
            else:
                b.r_eng[eng] = op
        for b in writes:
            b.w = op
            b.r_eng = {}
            b.r_dma = []
        self.ops[eng].append(op)
        return op

    def barrier(self):
        deps = [self.ops[e][-1] for e in ENGS if self.ops[e]] + self.dma_since_bar
        self.dma_since_bar = []
        for e in ENGS:
            self.bar[e] = list(deps)

    def mm(self, out, lhsT, rhs, start, stop, reads, writes, **kw):
        return self.rec("pe", lambda e: e.matmul(out, lhsT, rhs, start=start, stop=stop, **kw), reads, writes)

    def tr(self, out, in_, ident, reads, writes):
        return self.rec("pe", lambda e: e.transpose(out, in_, ident), reads, writes)

    def act(self, out, in_, func, reads, writes, bias=None, scale=None, accum_out=None):
        kw = {}
        if bias is not None:
            kw["bias"] = bias
        if scale is not None:
            kw["scale"] = scale
        if accum_out is not None:
            kw["accum_out"] = accum_out
        return self.rec("act", lambda e: e.activation(out=out, in_=in_, func=func, **kw), reads, writes)

    def v(self, eng, meth, reads, writes, *a, **kw):
        return self.rec(eng, lambda e: getattr(e, meth)(*a, **kw), reads, writes)

    def dma(self, q, out, in_, reads, writes, **kw):
        return self.rec(q, lambda e: e.dma_start(out=out, in_=in_, **kw), reads, writes, is_dma=True)

    def finalize(self):
        for e in ENGS:
            for op in self.ops[e]:
                for d in op.deps:
                    if d.is_dma:
                        continue
                    if d.eng == "pe" and op.eng == "pe" and not op.is_dma:
                        continue
                    d.sig = True
        for e in ENGS:
            n = 0
            for op in self.ops[e]:
                if op.sig and not op.is_dma:
                    n += 1
                    op.val = n

    def emit_all(self, stack):
        nc = self.nc
        self.finalize()
        self.esem = {e: stack.enter_context(nc.semaphore("es_" + e)) for e in ENGS}
        self.dsem = {q: [stack.enter_context(nc.semaphore(f"ds_{q}{i}")) for i in range(n)] for q, n in NSLOT.items()}
        uses = {q: [0] * n for q, n in NSLOT.items()}
        for q in NSLOT:
            for op in self.ops[q]:
                if op.is_dma:
                    uses[q][op.slot] += 1
                    op.dsem = self.dsem[q][op.slot]
                    op.dval = 16 * uses[q][op.slot]
        block = stack.enter_context(nc.Block())
        sched = self

        def emit(name, eng):
            known = {}
            for op in sched.ops[name]:
                for d in op.deps:
                    if d.is_dma:
                        sem, val = d.dsem, d.dval
                    else:
                        if d.eng == "pe" and name == "pe" and not op.is_dma:
                            continue
                        sem, val = sched.esem[d.eng], d.val
                    k = id(sem)
                    if known.get(k, 0) >= val:
                        continue
                    eng.wait_ge(sem, val)
                    known[k] = val
                ins = op.fn(eng)
                if op.is_dma:
                    ins.then_inc(op.dsem, 16)
                elif op.sig:
                    ins.then_inc(sched.esem[name], 1)
            if name == "sp":
                for q in NSLOT:
                    for i, u in enumerate(uses[q]):
                        if u:
                            eng.wait_ge(sched.dsem[q][i], 16 * u)

        @block.tensor
        def _(e):
            emit("pe", e)

        @block.scalar
        def _(e):
            emit("act", e)

        @block.vector
        def _(e):
            emit("dve", e)

        @block.gpsimd
        def _(e):
            emit("pool", e)

        @block.sync
        def _(e):
            emit("sp", e)


class Arena:
    def __init__(self, ap, nwords):
        self.ap = ap
        self.n = nwords
        self.off = 0
        self.marks = []

    def push(self):
        self.marks.append(self.off)

    def pop(self):
        self.off = self.marks.pop()

    def alloc(self, shape, dtype=F32, name="", parts=128):
        n = int(np.prod(shape))
        words = n if dtype == F32 else (n + 1) // 2
        words = (words + 7) // 8 * 8
        assert self.off + words <= self.n, f"arena overflow {name} {self.off}+{words}>{self.n}"
        ap = self.ap[0:parts, self.off:self.off + words]
        self.off += words
        if dtype != F32:
            ap = ap.bitcast(dtype)
        ap = ap[:, 0:n]
        if len(shape) == 2:
            ap = ap.rearrange("p (a b) -> p a b", a=shape[0])
        elif len(shape) == 3:
            ap = ap.rearrange("p (a b c) -> p a b c", a=shape[0], b=shape[1])
        elif len(shape) == 4:
            ap = ap.rearrange("p (a b c d) -> p a b c d", a=shape[0], b=shape[1], c=shape[2])
        return T(ap, name)

from contextlib import ExitStack
from concourse.bass_utils import run_bass_kernel_spmd

D = 2048
SEQ = 2048
NT = 16
RWC = 3360
NB = 3360
MB = 5968
INC = 10064
DFF = 8192
EPS = 1e-6
GN_EPS = 64e-5
NEG = -30000.0
ARENA_WORDS = 52000


class KB:
    def __init__(self, dbg=False, stages=(0, 1, 2, 3, 4, 5, 6)):
        self.nc = bass.Bass("TRN2", target_bir_lowering=False)
        self.S = Sched(self.nc)
        self.dbg = dbg
        self.stages = stages
        self.bank_i = 0

    def din(self, name, shape, dt=F32):
        return T(self.nc.dram_tensor(name, list(shape), dt, kind="ExternalInput").ap(), name)

    def dscr(self, name, shape, dt=F32, out=False):
        kind = "ExternalOutput" if (self.dbg or out) else "Internal"
        return T(self.nc.dram_tensor(name, list(shape), dt, kind=kind).ap(), name)

    def bank(self):
        b = self.ps[self.bank_i % 8]
        self.bank_i += 1
        return b

    def load(self, dst, src, q="sp"):
        self.S.dma(q, dst.ap, src[1], [src[0]], [dst])

    def build(self):
        nc, S = self.nc, self.S
        with ExitStack() as st:
            arena_t = st.enter_context(nc.sbuf_tensor("arena", [128, ARENA_WORDS], F32))
            self.ar = ar = Arena(arena_t, ARENA_WORDS)
            self.ps = [T(st.enter_context(nc.psum_tensor(f"ps{i}", [128, 512], F32))[:], f"ps{i}") for i in range(8)]
            self.declare()
            self.persistent()
            if 0 in self.stages:
                self.stage0()
            if 1 in self.stages:
                ar.push()
                self.hT = ar.alloc([16, SEQ], BF16, "hT")
                self.stage1(self.x_d, self.coef1, self.sh1, self.hT)
                if 2 in self.stages:
                    self.stage2()
                ar.pop()
            if 4 in self.stages:
                self.stage4()
            if 3 in self.stages:
                self.stage3()
            if 5 in self.stages:
                self.stage5()
            if 6 in self.stages:
                ar.push()
                self.hT = ar.alloc([16, SEQ], BF16, "h2T")
                self.stage1(self.x1_d, self.coef2, self.sh2, self.hT)
                self.stage6()
                ar.pop()
            S.emit_all(st)
        return nc

    def declare(self):
        d = self.din
        self.x_d = d("x", [SEQ, D])
        self.c_fm = d("c_fm", [128, 16])
        self.w_ada = d("w_ada", [D, 6 * D])
        self.b_ada = d("b_ada", [1, 6 * D])
        self.n1g = d("n1g_fm", [128, 16])
        self.n2g = d("n2g_fm", [128, 16])
        self.w_in = d("w_in", [D, INC])
        self.ident_d = d("ident", [128, 128])
        self.bones_d = d("bones", [128, 128])
        self.mu_d = d("mu_fm", [128, 27])
        self.qkg_d = d("qkg_fm", [128, 5])
        s = self.dscr
        self.rwT_d = s("rwT", [27 * 128, SEQ])
        self.qT_d = s("qT", [1024, SEQ], BF16)
        self.kvcT_d = s("kvcT", [512, SEQ], BF16)
        self.ksT_d = s("ksT", [4, 2, 128, SEQ], BF16)
        self.kwT_d = s("kwT", [4, 2, 128, SEQ], BF16)
        self.vv_d = s("vv", [SEQ, 512], BF16)
        self.gates_d = s("gates", [SEQ, 48])
        self.mgT_d = s("mgT", [4096, SEQ], BF16)
        self.x1_d = s("x1", [SEQ, D])
        self.out_d = self.dscr("out", [SEQ, D], out=True)

    def persistent(self):
        ar, S = self.ar, self.S
        self.ident = ar.alloc([128], F32, "ident")
        self.bones = ar.alloc([128], F32, "bones")
        S.dma("sp", self.ident.ap, self.ident_d.ap, [self.ident_d], [self.ident])
        S.dma("sp", self.bones.ap, self.bones_d.ap, [self.bones_d], [self.bones])
        self.coef1 = ar.alloc([16], F32, "coef1")
        self.sh1 = ar.alloc([16], F32, "sh1")
        self.coef2 = ar.alloc([16], F32, "coef2")
        self.sh2 = ar.alloc([16], F32, "sh2")
        self.gt1 = ar.alloc([D], F32, "gt1")
        self.gt2 = ar.alloc([D], F32, "gt2")

    def silu_rep(self):
        ar, S = self.ar, self.S
        cs = ar.alloc([16], F32, "cs")
        S.dma("sp", cs.ap, self.c_fm.ap, [self.c_fm], [cs])
        csb = ar.alloc([16], F32, "csb")
        S.act(csb.ap, cs.ap, AF.Silu, [cs], [csb])
        crep = ar.alloc([16, 128], BF16, "crep")
        S.v("dve", "tensor_copy", [csb], [crep], crep.ap, csb.ap.unsqueeze(2).to_broadcast([128, 16, 128]))
        return crep

    def stage0(self):
        ar, S = self.ar, self.S
        ar.push()
        bias_steps = self.nsa_bias_build() if 4 in self.stages else []
        mod = ar.alloc([6 * D], F32, "mod")
        crep = self.silu_rep()
        wbs = [ar.alloc([16, 512], BF16, f"wada{i}") for i in range(2)]
        bbs = [ar.alloc([512], F32, f"bada{i}") for i in range(2)]
        wsrc = self.w_ada.ap.rearrange("(k p) c -> p k c", p=128)
        for blk in range(24):
            wb, bb = wbs[blk % 2], bbs[blk % 2]
            c0 = blk * 512
            S.dma("pool", wb.ap, wsrc[:, :, c0:c0 + 512], [self.w_ada], [wb])
            S.dma("sp", bb.ap, self.b_ada.ap[0:1, c0:c0 + 512].partition_broadcast(128), [self.b_ada], [bb])
            P = self.bank()
            for k in range(16):
                S.mm(P.ap, crep.ap[:, k, :], wb.ap[:, k, :], k == 0, k == 15, [crep, wb], [P])
            S.v("dve", "tensor_tensor", [P, bb], [mod], mod.ap[:, c0:c0 + 512], P.ap, bb.ap, ALU.add)
            for _ in range(2):
                if bias_steps:
                    bias_steps.pop(0)()
        while bias_steps:
            bias_steps.pop(0)()
        tmp = ar.alloc([16, 128], F32, "dtmp")
        sc1 = ar.alloc([16], F32, "sc1")
        sc2 = ar.alloc([16], F32, "sc2")
        for dst, idx in ((self.sh1, 0), (sc1, 1), (self.sh2, 3), (sc2, 4)):
            src = mod.ap[:, idx * D:(idx + 1) * D].rearrange("p (k m) -> p k m", k=16)
            S.v("dve", "tensor_tensor", [mod, self.ident], [tmp], tmp.ap, src,
                self.ident.ap.unsqueeze(1).to_broadcast([128, 16, 128]), ALU.mult)
            S.v("dve", "tensor_reduce", [tmp], [dst], dst.ap, tmp.ap, AX.X, ALU.add)
        g = ar.alloc([16], F32, "gload")
        S.dma("sp", g.ap, self.n1g.ap, [self.n1g], [g])
        S.v("dve", "scalar_tensor_tensor", [sc1, g], [self.coef1], self.coef1.ap, sc1.ap, 1.0, g.ap, ALU.add, ALU.mult)
        g2 = ar.alloc([16], F32, "gload2")
        S.dma("sp", g2.ap, self.n2g.ap, [self.n2g], [g2])
        S.v("dve", "scalar_tensor_tensor", [sc2, g2], [self.coef2], self.coef2.ap, sc2.ap, 1.0, g2.ap, ALU.add, ALU.mult)
        S.v("dve", "tensor_copy", [mod], [self.gt1], self.gt1.ap, mod.ap[:, 2 * D:3 * D])
        S.v("dve", "tensor_copy", [mod], [self.gt2], self.gt2.ap, mod.ap[:, 5 * D:6 * D])
        S.barrier()
        ar.pop()

    def stage0b_setup(self):
        ar, S = self.ar, self.S
        crep = self.silu_rep()
        wb2 = [ar.alloc([16, 256], BF16, f"wada_b{i}") for i in range(2)]
        bb2 = [ar.alloc([256], F32, f"bada_b{i}") for i in range(2)]
        tmp = ar.alloc([2, 128], F32, "dtmp_b")
        sc2 = ar.alloc([16], F32, "sc2")
        wsrc = self.w_ada.ap.rearrange("(k p) c -> p k c", p=128)
        steps = []

        def mk(sb):
            def step():
                wb, bb = wb2[sb % 2], bb2[sb % 2]
                c0 = 3 * D + sb * 256
                S.dma("pool", wb.ap, wsrc[:, :, c0:c0 + 256], [self.w_ada], [wb])
                S.dma("sp", bb.ap, self.b_ada.ap[0:1, c0:c0 + 256].partition_broadcast(128), [self.b_ada], [bb])
                P = self.bank()
                for k in range(16):
                    S.mm(P.ap[:, 0:256], crep.ap[:, k, :], wb.ap[:, k, :], k == 0, k == 15, [crep, wb], [P])
                if sb < 16:
                    dst = self.sh2 if sb < 8 else sc2
                    j = sb % 8
                    t2 = tmp.ap.rearrange("p a b -> p (a b)")
                    S.v("dve", "tensor_tensor", [P, bb], [tmp], t2, P.ap[:, 0:256], bb.ap, ALU.add)
                    S.v("dve", "tensor_tensor", [tmp, self.ident], [tmp], tmp.ap, tmp.ap,
                        self.ident.ap.unsqueeze(1).to_broadcast([128, 2, 128]), ALU.mult)
                    S.v("dve", "tensor_reduce", [tmp], [dst], dst.ap[:, 2 * j:2 * j + 2], tmp.ap, AX.X, ALU.add)
                else:
                    o = (sb - 16) * 256
                    S.v("dve", "tensor_tensor", [P, bb], [self.gt2], self.gt2.ap[:, o:o + 256], P.ap[:, 0:256], bb.ap, ALU.add)
                if sb == 23:
                    g = ar.alloc([16], F32, "gload2")
                    S.dma("sp", g.ap, self.n2g.ap, [self.n2g], [g])
                    S.v("dve", "scalar_tensor_tensor", [sc2, g], [self.coef2], self.coef2.ap, sc2.ap, 1.0, g.ap, ALU.add, ALU.mult)
            return step
        return [mk(sb) for sb in range(24)]

    def stage1(self, src_d, coef, sh, hT):
        ar, S = self.ar, self.S
        ar.push()
        xbs = [ar.alloc([D], F32, f"xb{i}") for i in range(2)]
        junk = ar.alloc([D], F32, "junk")
        xs4s = [ar.alloc([4, D], F32, f"xs4_{i}") for i in range(1)]
        ss = ar.alloc([NT], F32, "ss")
        sr = ar.alloc([NT], F32, "sr")
        rstd = ar.alloc([NT], F32, "rstd")
        for grp in range(4):
            xs4 = xs4s[0]
            for tt in range(4):
                ti = grp * 4 + tt
                xb = xbs[ti % 2]
                S.dma("sp", xb.ap, src_d.ap[ti * 128:(ti + 1) * 128, :], [src_d], [xb])
                sst = ss.sub(ti)
                S.act(junk.ap, xb.ap, AF.Square, [xb], [sst], accum_out=ss.ap[:, ti:ti + 1])
                S.act(sr.ap[:, ti:ti + 1], ss.ap[:, ti:ti + 1], AF.Sqrt, [sst], [sr.sub(ti)], bias=EPS, scale=1.0 / D)
                S.v("dve", "reciprocal", [sr.sub(ti)], [rstd.sub(ti)], rstd.ap[:, ti:ti + 1], sr.ap[:, ti:ti + 1])
                S.v("dve", "tensor_scalar", [xb, rstd.sub(ti)], [xs4.sub(tt)], xs4.ap[:, tt, :], xb.ap,
                    rstd.ap[:, ti:ti + 1], None, ALU.mult)
            for k in range(16):
                P = self.bank()
                for tt in range(4):
                    S.tr(P.ap[:, tt * 128:(tt + 1) * 128], xs4.ap[:, tt, k * 128:(k + 1) * 128], self.ident.ap,
                         [xs4.sub(tt), self.ident], [P])
                S.act(hT.ap[:, k, grp * 512:(grp + 1) * 512], P.ap, AF.Identity, [P, coef, sh], [hT.sub(grp)],
                      bias=sh.ap[:, k:k + 1], scale=coef.ap[:, k:k + 1])
        S.barrier()
        ar.pop()

    def stage2(self):
        ar, S = self.ar, self.S
        hT = self.hT
        ar.push()
        wbs = [ar.alloc([16, 512], BF16, f"win{i}") for i in range(2)]
        wtm = ar.alloc([16, 560], BF16, "wtm")
        raws = [ar.alloc([2056], F32, f"raw{i}") for i in range(2)]
        tmps = [ar.alloc([SEQ], F32, f"mixt{i}") for i in range(2)]
        stg = [ar.alloc([SEQ], BF16, f"stg{i}") for i in range(4)]
        sqs = [ar.alloc([512], F32, f"sq{i}") for i in range(2)]
        srs = [ar.alloc([512], F32, f"sr{i}") for i in range(2)]
        ris = [ar.alloc([512], F32, f"ri{i}") for i in range(2)]
        mu = ar.alloc([27], F32, "mu")
        omu = ar.alloc([27], F32, "omu")
        qkg = ar.alloc([5], F32, "qkg")
        S.dma("sp", mu.ap, self.mu_d.ap, [self.mu_d], [mu])
        S.dma("sp", qkg.ap, self.qkg_d.ap, [self.qkg_d], [qkg])
        S.v("dve", "tensor_scalar", [mu], [omu], omu.ap, mu.ap, -1.0, 1.0, ALU.mult, ALU.add)
        S.v("dve", "tensor_scalar", [qkg], [qkg], qkg.ap[:, 1:5], qkg.ap[:, 1:5], 8.0, None, ALU.mult)
        for r in raws:
            S.v("dve", "memset", [], [r], r.ap[:, 0:1], 0.0)
        wsrc = self.w_in.ap.rearrange("(k p) c -> p k c", p=128)
        cnt = {"w": 0, "raw": 0, "stg": 0, "sq": 0}

        def proj_chunk(wb, m0, M, n):
            P = self.bank()
            for k in range(16):
                S.mm(P.ap[0:M, :], wb.ap[:, k, m0:m0 + M], hT.ap[:, k, n * 512:(n + 1) * 512], k == 0, k == 15,
                     [wb, hT.sub(n)], [P])
            return P

        def next_stg():
            t = stg[cnt["stg"] % 4]
            cnt["stg"] += 1
            return t

        def ep_rw(wb, m0, M, ti):
            raw = raws[cnt["raw"] % 2]
            tmp = tmps[cnt["raw"] % 2]
            cnt["raw"] += 1
            for n in range(4):
                P = proj_chunk(wb, m0, M, n)
                S.act(raw.ap[0:M, 1 + n * 512:1 + (n + 1) * 512], P.ap[0:M, :], AF.Copy, [P], [raw])
            S.v("dve", "tensor_scalar", [raw, mu], [tmp], tmp.ap[0:M, :], raw.ap[0:M, 0:SEQ], mu.ap[0:M, ti:ti + 1], None, ALU.mult)
            S.v("dve", "scalar_tensor_tensor", [raw, omu, tmp], [tmp], tmp.ap[0:M, :], raw.ap[0:M, 1:SEQ + 1],
                omu.ap[0:M, ti:ti + 1], tmp.ap[0:M, :], ALU.mult, ALU.add)
            S.dma("sp", self.rwT_d.ap[ti * 128:ti * 128 + M, :], tmp.ap[0:M, :], [tmp], [self.rwT_d])

        def ep_qk(wb, m0, gcols, dsts):
            outs = [next_stg() for _ in gcols]
            for n in range(4):
                P = proj_chunk(wb, m0, 128, n)
                i = cnt["sq"] % 2
                cnt["sq"] += 1
                sq, sr, ri = sqs[i], srs[i], ris[i]
                S.act(sq.ap, P.ap, AF.Square, [P], [sq])
                P2 = self.bank()
                S.mm(P2.ap, self.bones.ap, sq.ap, True, True, [self.bones, sq], [P2])
                S.act(sr.ap, P2.ap, AF.Sqrt, [P2], [sr], bias=64 * EPS, scale=1.0)
                S.v("dve", "reciprocal", [sr], [ri], ri.ap, sr.ap)
                for gc, o in zip(gcols, outs):
                    S.v("dve", "scalar_tensor_tensor", [P, qkg, ri], [o], o.ap[:, n * 512:(n + 1) * 512], P.ap,
                        qkg.ap[:, gc:gc + 1], ri.ap, ALU.mult, ALU.mult)
            for o, (dt_, dap) in zip(outs, dsts):
                S.dma("sp", dap, o.ap, [o], [dt_])

        def ep_act(wb, m0, func, dt_, dap):
            o = next_stg()
            for n in range(4):
                P = proj_chunk(wb, m0, 128, n)
                S.act(o.ap[:, n * 512:(n + 1) * 512], P.ap, func, [P], [o])
            S.dma("sp", dap, o.ap, [o], [dt_])

        def load_block(segs):
            wb = wbs[cnt["w"] % 2]
            cnt["w"] += 1
            for (c0, n, off) in segs:
                S.dma("pool", wb.ap[:, :, off:off + n], wsrc[:, :, c0:c0 + n], [self.w_in], [wb])
            return wb

        for b in range(7):
            if b < 6:
                wb = load_block([(512 * b, 512, 0)])
                for j in range(4):
                    ep_rw(wb, j * 128, 128, 4 * b + j)
            else:
                wb = load_block([(3072, 288, 0)])
                ep_rw(wb, 0, 128, 24)
                ep_rw(wb, 128, 128, 25)
                ep_rw(wb, 256, 32, 26)
        for b in range(2):
            wb = load_block([(NB + 512 * b, 512, 0)])
            for j in range(4):
                ti = 4 * b + j
                ep_qk(wb, j * 128, [0], [(self.qT_d, self.qT_d.ap[ti * 128:(ti + 1) * 128, :])])
        wb = load_block([(NB + 1024, 512, 0)])
        for j in range(4):
            ep_act(wb, j * 128, AF.Copy, self.kvcT_d, self.kvcT_d.ap[j * 128:(j + 1) * 128, :])
        for (c_base, dst, gc) in ((NB + 1024 + 512, self.ksT_d, 1), (NB + 1024 + 1024, self.kwT_d, 3)):
            segs = []
            for g in range(4):
                segs.append((c_base + 64 * g, 64, g * 128))
                segs.append((c_base + 64 * g, 64, g * 128 + 64))
            wb = load_block(segs)
            for g in range(4):
                ep_qk(wb, g * 128, [gc, gc + 1], [(dst, dst.ap[g, 0]), (dst, dst.ap[g, 1])])
        for b in range(8):
            wb = load_block([(MB + 512 * b, 512, 0)])
            for j in range(4):
                ti = 4 * b + j
                ep_act(wb, j * 128, AF.Sigmoid, self.mgT_d, self.mgT_d.ap[ti * 128:(ti + 1) * 128, :])
        for (c0, n, off) in ((NB + 1024 + 768, 256, 0), (NB + 1024 + 1280, 256, 256), (NB + 2560, 48, 512)):
            S.dma("pool", wtm.ap[:, :, off:off + n], wsrc[:, :, c0:c0 + n], [self.w_in], [wtm])
        vst = [ar.alloc([512], BF16, f"vst{i}") for i in range(2)]
        gst = [ar.alloc([48], F32, f"gst{i}") for i in range(2)]
        for tt in range(NT):
            P = self.bank()
            for k in range(16):
                S.mm(P.ap, hT.ap[:, k, tt * 128:(tt + 1) * 128], wtm.ap[:, k, 0:512], k == 0, k == 15,
                     [hT.sub(tt // 4), wtm], [P])
            v = vst[tt % 2]
            S.act(v.ap, P.ap, AF.Copy, [P], [v])
            S.dma("sp", self.vv_d.ap[tt * 128:(tt + 1) * 128, :], v.ap, [v], [self.vv_d])
            P = self.bank()
            for k in range(16):
                S.mm(P.ap[:, 0:48], hT.ap[:, k, tt * 128:(tt + 1) * 128], wtm.ap[:, k, 512:560], k == 0, k == 15,
                     [hT.sub(tt // 4), wtm], [P])
            gt = gst[tt % 2]
            S.act(gt.ap, P.ap[:, 0:48], AF.Sigmoid, [P], [gt])
            S.dma("sp", self.gates_d.ap[tt * 128:(tt + 1) * 128, :], gt.ap, [gt], [self.gates_d])
        S.barrier()
        ar.pop()


def _fm(v, ntile=None):
    v = np.asarray(v, np.float32).reshape(-1)
    n = (len(v) + 127) // 128 if ntile is None else ntile
    buf = np.zeros(n * 128, np.float32)
    buf[:len(v)] = v
    return np.ascontiguousarray(buf.reshape(n, 128).T)


def host_consts():
    c = {}
    c["ident"] = np.eye(128, dtype=np.float32)
    p = np.arange(128)
    c["bones"] = (p[:, None] // 64 == p[None, :] // 64).astype(np.float32)
    return c


def prep_core(inp, b, consts):
    m = dict(consts)
    m["x"] = np.ascontiguousarray(inp["x"][b])
    m["c_fm"] = _fm(inp["c"][b])
    m["w_ada"] = inp["w_ada"][0]
    m["b_ada"] = inp["b_ada"][0].reshape(1, -1)
    m["n1g_fm"] = _fm(inp["norm1_g"][0])
    m["n2g_fm"] = _fm(inp["norm2_g"][0])
    m["w_in"] = inp["w_in"][0]
    m["mu_fm"] = _fm(inp["rwkv_mu"][0], 27)
    qg = np.tile(inp["q_norm_g"][0], 2)
    kg = inp["k_norm_g"][0]
    z = np.zeros(64, np.float32)
    cols = [qg, np.concatenate([kg[1], z]), np.concatenate([z, kg[1]]),
            np.concatenate([kg[2], z]), np.concatenate([z, kg[2]])]
    m["qkg_fm"] = np.ascontiguousarray(np.stack(cols, axis=1).astype(np.float32))
    return m


def _rel_bucket_np(rel):
    n = np.maximum(rel, 0)
    nf = np.maximum(n, 16).astype(np.float32)
    large = 16 + (np.log(nf / np.float32(16)) / np.float32(np.log(8.0)) * np.float32(16)).astype(np.int32)
    large = np.minimum(large, 31)
    return np.where(n < 16, n, large)


def nsa_consts():
    c = {}
    NOH = 3 * 16384 + 17 * 128
    oh = np.zeros((33, NOH), np.float32)
    pos = np.arange(128)[:, None]
    t = np.arange(128)[None, :]
    for d, base in ((0, 0), (1, 128), (2, 512)):
        rel = base + t - pos
        if d == 0:
            mask = rel < 0
        elif d == 1:
            mask = np.zeros_like(rel, bool)
        else:
            mask = rel >= 512
        b = _rel_bucket_np(rel)
        sec = np.zeros((33, 128, 128), np.float32)
        for bb in range(32):
            sec[bb][(b == bb) & ~mask] = 1.0
        sec[32][mask] = 1.0
        oh[:, d * 16384:(d + 1) * 16384] = sec.reshape(33, -1)
    sec = np.zeros((33, 17, 128), np.float32)
    ti = np.arange(128)
    for r in range(16):
        m = r - 9
        rel = ti - 16 * m - 31
        b = _rel_bucket_np(rel)
        for bb in range(32):
            sec[bb, r, (b == bb) & (rel >= 0)] = 1.0
        sec[32, r, rel < 0] = 1.0
    sec[32, 16, :] = 1.0
    oh[:, 3 * 16384:] = sec.reshape(33, -1)
    c["nsa_oh"] = oh
    S = np.zeros((17, 16, 128), np.float32)
    for i in range(16):
        for n in range(127):
            m = n - 8 * i
            if -9 <= m <= 6:
                S[m + 9, i, n] = 1.0
            elif m > 6:
                S[16, i, n] = 1.0
    c["nsa_S"] = S
    E = np.zeros((32, 2048), np.float32)
    for p in range(2048):
        E[p // 64, p] = 1.0
    c["nsa_E"] = E
    allowed = np.zeros((128, 16, 32), np.float32)
    addc = np.zeros((128, 16, 32), np.float32)
    blk = np.arange(32)
    for i in range(16):
        for tt in range(128):
            cur = (i * 128 + tt) // 64
            al = blk <= cur
            forced = (blk == 0) | (blk == cur) | (blk == cur - 1)
            allowed[tt, i] = (al & ~forced).astype(np.float32)
            addc[tt, i] = np.where(forced, 1e4, np.where(al, 0.0, -1.0))
    c["nsa_allowed"] = allowed
    c["nsa_addc"] = addc
    ncmp = 127
    cs = np.arange(ncmp) * 16
    ss = np.arange(32) * 64
    lo = np.maximum(cs[:, None], ss[None, :])
    hi = np.minimum(cs[:, None] + 32, ss[None, :] + 64)
    c["nsa_selm"] = (np.maximum(hi - lo, 0) / 32).astype(np.float32)
    return c


def nsa_prep(inp, m):
    m["rel_bias"] = np.ascontiguousarray(inp["rel_bias"])
    for kv in ("k", "v"):
        m[f"pe_{kv}T"] = np.ascontiguousarray(inp[f"cmp_pe_{kv}"][0].T)
        m[f"w1_{kv}"] = inp[f"cmp_w1_{kv}"][0]
        m[f"w2_{kv}"] = inp[f"cmp_w2_{kv}"][0]
    kg0 = inp["k_norm_g"][0][0]
    z = np.zeros(64, np.float32)
    m["kcg_fm"] = np.ascontiguousarray(np.stack([np.concatenate([kg0, z]), np.concatenate([z, kg0])], 1).astype(np.float32))


def stage4(self):
    ar, S, nc = self.ar, self.S, self.nc
    d = self.din
    S_d = d("nsa_S", [17, 16, 128])
    E_d = d("nsa_E", [32, 2048])
    al_d = d("nsa_allowed", [128, 16, 32])
    ad_d = d("nsa_addc", [128, 16, 32])
    selm_d = d("nsa_selm", [127, 32])
    kcg_d = d("kcg_fm", [128, 2])
    cmp_d = {}
    for kv in ("k", "v"):
        cmp_d[kv] = (d(f"pe_{kv}T", [64, 32]), d(f"w1_{kv}", [2048, 64]), d(f"w2_{kv}", [64, 64]))
    NOH = 3 * 16384 + 17 * 128
    self.obT_d = self.dscr("obT", [1024, SEQ], BF16)
    ident, bones = self.ident, self.bones

    ar.push()
    ks = ar.alloc([4, 2, SEQ], BF16, "ks")
    kw = ar.alloc([4, 2, SEQ], BF16, "kw")
    vs = ar.alloc([16, 4, 65], BF16, "vs")
    vw = ar.alloc([16, 4, 65], BF16, "vw")
    gates = ar.alloc([16, 48], F32, "gates")
    biasT = ar.alloc([3, 16, 128], F32, "biasT")
    Mst = ar.alloc([16, 128], F32, "Mst", parts=17)
    Sc = ar.alloc([16, 128], F32, "Sc", parts=17)
    allowed = ar.alloc([16, 32], F32, "allowed")
    addc = ar.alloc([16, 32], F32, "addc")
    kc = ar.alloc([4, 2, 128], BF16, "kc")
    rhsc = ar.alloc([4, 97], BF16, "rhsc", parts=127)
    ar.push()
    kvc = ar.alloc([4, SEQ], BF16, "kvc")
    S.dma("sp", kvc.ap, self.kvcT_d.ap.rearrange("(a p) t -> p a t", p=128), [self.kvcT_d], [kvc])
    kcg = ar.alloc([2], F32, "kcg")
    S.dma("sp", kcg.ap, kcg_d.ap, [kcg_d], [kcg])
    S.v("dve", "tensor_scalar", [kcg], [kcg], kcg.ap, kcg.ap, 8.0, None, ALU.mult)
    S.v("dve", "memset", [], [rhsc], rhsc.ap[:, :, 64:65], 1.0)
    for g in range(4):
        S.dma("pool", rhsc.ap[:, g, 65:97], selm_d.ap, [selm_d], [rhsc])
    for kvi, kv in enumerate(("k", "v")):
        pe_d, w1_d, w2_d = cmp_d[kv]
        w1p = ar.alloc([2, 32, 64], BF16, f"w1p{kv}")
        S.v("dve", "memset", [], [w1p], w1p.ap, 0.0)
        w1v = w1_d.ap.rearrange("(i d) e -> d i e", d=64)
        S.dma("pool", w1p.ap[0:64, 0, :, :], w1v, [w1_d, w1p], [w1p])
        S.dma("pool", w1p.ap[64:128, 1, :, :], w1v, [w1_d, w1p], [w1p])
        peT = ar.alloc([32], BF16, f"peT{kv}", parts=64)
        S.dma("pool", peT.ap, pe_d.ap, [pe_d], [peT])
        w2 = ar.alloc([128], BF16, f"w2{kv}", parts=64)
        S.dma("pool", w2.ap[:, 0:64], w2_d.ap, [w2_d], [w2])
        S.dma("pool", w2.ap[:, 64:128], w2_d.ap, [w2_d, w2], [w2])
        Pb = self.bank()
        for i in range(32):
            S.mm(Pb.ap[0:64, 0:1], w1p.ap[0:64, 0, i, :], peT.ap[:, i:i + 1], i == 0, i == 31, [w1p, peT], [Pb])
        cb = ar.alloc([1], F32, f"cb{kv}", parts=64)
        S.act(cb.ap, Pb.ap[0:64, 0:1], AF.Copy, [Pb], [cb])
        for g in range(4):
            tile_, half = kvi * 2 + g // 2, g % 2
            Ph = self.bank()
            for i in range(32):
                S.mm(Ph.ap[0:64, 0:127], w1p.ap[:, half, i, :], kvc.ap[:, tile_, i:i + 16 * 126 + 1:16], i == 0, i == 31,
                     [w1p, kvc], [Ph])
            u = ar.alloc([127], F32, "cu", parts=64)
            t1 = ar.alloc([127], F32, "ct1", parts=64)
            sg = ar.alloc([127], F32, "csg", parts=64)
            hid = ar.alloc([127], BF16, "chid", parts=64)
            S.act(u.ap, Ph.ap[0:64, 0:127], AF.Identity, [Ph, cb], [u], bias=cb.ap[:, 0:1], scale=1.0)
            S.v("dve", "tensor_tensor", [u], [t1], t1.ap, u.ap, u.ap, ALU.mult)
            S.v("dve", "tensor_scalar", [t1], [t1], t1.ap, t1.ap, 0.044715, 1.0, ALU.mult, ALU.add)
            S.v("dve", "tensor_tensor", [t1, u], [t1], t1.ap, t1.ap, u.ap, ALU.mult)
            S.act(sg.ap, t1.ap, AF.Sigmoid, [t1], [sg], scale=1.5957691216057308)
            S.v("dve", "tensor_tensor", [u, sg], [hid], hid.ap, u.ap, sg.ap, ALU.mult)
            if kv == "k":
                Pk = self.bank()
                S.mm(Pk.ap[:, 0:127], w2.ap, hid.ap, True, True, [w2, hid], [Pk])
                sq = ar.alloc([127], F32, "csq")
                S.act(sq.ap, Pk.ap[:, 0:127], AF.Square, [Pk], [sq])
                P2 = self.bank()
                S.mm(P2.ap[:, 0:127], bones.ap, sq.ap, True, True, [bones, sq], [P2])
                sr = ar.alloc([127], F32, "csr")
                S.act(sr.ap, P2.ap[:, 0:127], AF.Sqrt, [P2], [sr], bias=64 * EPS, scale=1.0)
                S.v("dve", "reciprocal", [sr], [sr], sr.ap, sr.ap)
                for h in range(2):
                    S.v("dve", "scalar_tensor_tensor", [Pk, kcg, sr], [kc], kc.ap[:, g, h, 0:127], Pk.ap[:, 0:127],
                        kcg.ap[:, h:h + 1], sr.ap, ALU.mult, ALU.mult)
            else:
                Pv = self.bank()
                S.mm(Pv.ap[0:127, 0:64], hid.ap, w2.ap[:, 0:64], True, True, [w2, hid], [Pv])
                S.act(rhsc.ap[:, g, 0:64], Pv.ap[0:127, 0:64], AF.Copy, [Pv], [rhsc])
    if self.dbg:
        dkc = self.dscr("dbg_kc", [128, 4 * 2 * 128], BF16)
        S.dma("sp", dkc.ap, kc.ap.rearrange("p a b c -> p (a b c)"), [kc], [dkc])
        drc = self.dscr("dbg_rhsc", [127, 4 * 97], BF16)
        S.dma("sp", drc.ap, rhsc.ap.rearrange("p a b -> p (a b)"), [rhsc], [drc])
    for g in range(4):
        for h in range(2):
            S.dma("sp", ks.ap[:, g, h, :], self.ksT_d.ap[g, h], [self.ksT_d], [ks])
            S.dma("sp", kw.ap[:, g, h, :], self.kwT_d.ap[g, h], [self.kwT_d], [kw])
    for g_ in range(4):
        S.dma("pool", ks.ap[64:96, g_, 0, :], E_d.ap, [E_d, ks], [ks])
        S.dma("pool", ks.ap[0:32, g_, 1, :], E_d.ap, [E_d, ks], [ks])
    for (dst, c0) in ((vs, 0), (vw, 256)):
        S.v("dve", "memset", [], [dst], dst.ap[:, :, :, 64:65], 1.0)
        for j in range(16):
            S.dma("sp", dst.ap[:, j, :, 0:64],
                  self.vv_d.ap[j * 128:(j + 1) * 128, c0:c0 + 256].rearrange("p (g d) -> p g d", g=4), [self.vv_d], [dst])
    S.dma("sp", gates.ap, self.gates_d.ap.rearrange("(j p) c -> p j c", p=128), [self.gates_d], [gates])
    S.dma("sp", Sc.ap, S_d.ap, [S_d], [Sc])
    S.dma("sp", allowed.ap, al_d.ap, [al_d], [allowed])
    S.dma("sp", addc.ap, ad_d.ap, [ad_d], [addc])

    ar.push()
    bias_d = self.bias_d
    for dd in range(3):
        S.dma("sp", biasT.ap[:, dd, :, :],
              bias_d.ap[:, dd * 16384:(dd + 1) * 16384].rearrange("h (p t) -> p h t", p=128), [bias_d], [biasT])
    S.dma("sp", Mst.ap, bias_d.ap[:, 3 * 16384:].rearrange("h (r t) -> r h t", r=17), [bias_d], [Mst])
    if self.dbg:
        dbb = self.dscr("dbg_biasT", [128, 3 * 16 * 128])
        S.dma("sp", dbb.ap, biasT.ap.rearrange("p a b c -> p (a b c)"), [biasT], [dbb])
    ar.pop()

    S.barrier()
    ar.pop()
    if 6 in self.stages:
        self.precast()

    qis = [ar.alloc([4, 2, 2, 128], BF16, f"qa{i}") for i in range(2)]
    for qa_ in qis:
        S.v("dve", "memset", [], [qa_.sub("q")] + [qa_.sub(("m", g_)) for g_ in range(4)], qa_.ap, 0.0)

    pxs = [ar.alloc([512], BF16, f"px{i}") for i in range(8)]
    ssbs = [ar.alloc([512], F32, f"ssb{i}") for i in range(3)]
    oaccs = [ar.alloc([1024], F32, f"oacc{i}") for i in range(2)]
    obst = [ar.alloc([8, 128], BF16, f"obst{i}") for i in range(1)] * 2
    sm = [dict(rl=ar.alloc([12], F32, f"rl{i}"), imp=ar.alloc([32], F32, f"imp{i}"), m8=ar.alloc([8], F32, f"m8{i}"),
               ns=ar.alloc([128], F32, f"ns{i}"), tmp=ar.alloc([256], F32, f"otmp{i}")) for i in range(2)]
    for w_ in sm:
        S.v("dve", "memset", [], [w_["ns"]], w_["ns"].ap, 0.0)
    score_banks = self.ps[0:5]
    NSB = 5
    Poc_b = self.ps[5]
    Pow_b = [self.ps[6], self.ps[6]]
    Pos_b = [self.ps[7], self.ps[7]]
    cnt = {"sb": 0, "px": 0, "ssb": 0}
    jobs = []
    qT_v = self.qT_d.ap.rearrange("(kt p) t -> p kt t", p=128)
    obT_v = self.obT_d.ap.rearrange("(kt p) t -> p kt t", p=128)

    def score_job(qi, g, lhs_lo, lhs_hi, M, extra_mm, bias_ap, pv_fn, deps_k, use_mask=False):
        st = {}

        def qk():
            P = score_banks[cnt["sb"] % NSB]
            cnt["sb"] += 1
            pv4 = P.ap[0:M, :].rearrange("p (a b t) -> p a b t", a=2, b=2)
            qdeps = [qi.sub("q")] + ([qi.sub(("m", g))] if use_mask else [])
            S.mm(pv4[:, :, 0, :], lhs_lo, qi.ap[:, g, 0, :, :], True, False, deps_k + qdeps, [P])
            S.mm(pv4[:, :, 1, :], lhs_hi, qi.ap[:, g, 1, :, :], False, extra_mm is None, deps_k + qdeps, [P])
            if extra_mm is not None:
                lt, rt, dps = extra_mm
                S.mm(P.ap[0:M, :], lt, rt, False, True, dps, [P])
            px = pxs[cnt["px"] % 8]
            cnt["px"] += 1
            if bias_ap is not None:
                sb_ = ssbs[cnt["ssb"] % 3]
                cnt["ssb"] += 1
                S.v("dve", "tensor_tensor", [P, biasT], [sb_], sb_.ap[0:M, :], P.ap[0:M, :], bias_ap, ALU.add)
                S.act(px.ap[0:M, :], sb_.ap[0:M, :], AF.Exp, [sb_], [px])
            else:
                S.act(px.ap[0:M, :], P.ap[0:M, :], AF.Exp, [P], [px])
            st["px"] = px

        def pv():
            pv_fn(st["px"])

        return (qk, pv)

    for i in range(NT):
        qi = qis[i % 2]
        oacc = oaccs[i % 2]

        def load_q(i=i):
            if i < NT:
                qa_ = qis[i % 2]
                for (r0, lh) in ((0, 0), (64, 1)):
                    for g_ in range(4):
                        S.dma("sp", qa_.ap[r0:r0 + 64, g_, lh, :, :],
                              qT_v[r0:r0 + 64, 2 * g_:2 * g_ + 2, i * 128:(i + 1) * 128],
                              [self.qT_d], [qa_.sub("q")])
        if i == 0:
            jobs.append((load_q, None))
        load_next = (lambda i=i: load_q(i + 1))
        for g in range(4):
            it = i * 4 + g
            w = sm[it % 2]
            Pow_, Pos_ = Pow_b[it % 2], Pos_b[it % 2]
            gv = gates.ap[:, i, g * 12:(g + 1) * 12].rearrange("p (h c) -> p h c", c=3)
            osl = oacc.ap[:, g * 256:(g + 1) * 256].rearrange("p (h d) -> p h d", h=4)

            def pv_c(px, g=g, i=i, w=w, gv=gv, osl=osl, qi=qi):
                Poc = Poc_b
                for hs in range(4):
                    S.mm(Poc.ap[:, hs * 97:(hs + 1) * 97], px.ap[0:127, hs * 128:(hs + 1) * 128], rhsc.ap[:, g, :],
                         hs == 0, hs == 3, [px, rhsc], [Poc])
                pc3 = Poc.ap[:, 0:388].rearrange("p (h c) -> p h c", h=4)
                rl = w["rl"]
                S.v("dve", "tensor_scalar", [Poc], [rl], rl.ap[:, 0:4], pc3[:, :, 64], 1e-30, None, ALU.max)
                S.v("dve", "reciprocal", [rl], [rl], rl.ap[:, 0:4], rl.ap[:, 0:4])
                imp = w["imp"]
                S.v("dve", "tensor_scalar", [Poc, rl], [imp], imp.ap, pc3[:, 0, 65:97], rl.ap[:, 0:1], None, ALU.mult)
                for hs in range(1, 4):
                    S.v("dve", "scalar_tensor_tensor", [Poc, rl, imp], [imp], imp.ap, pc3[:, hs, 65:97], rl.ap[:, hs:hs + 1],
                        imp.ap, ALU.mult, ALU.add)
                S.v("dve", "tensor_tensor", [imp, allowed], [imp], imp.ap, imp.ap, allowed.ap[:, i, :], ALU.mult)
                S.v("dve", "tensor_tensor", [imp, addc], [imp], imp.ap, imp.ap, addc.ap[:, i, :], ALU.add)
                m8 = w["m8"]
                S.v("dve", "max", [imp], [m8], out=m8.ap, in_=imp.ap)
                ns = w["ns"]
                S.v("dve", "tensor_scalar", [imp, m8], [ns], ns.ap[:, 0:32], imp.ap, m8.ap[:, 7:8], None, ALU.is_ge)
                S.v("dve", "tensor_scalar", [ns], [ns], ns.ap[:, 64:96], ns.ap[:, 0:32], -1.0, -NEG, ALU.add, ALU.mult)
                S.v("dve", "tensor_scalar", [ns], [ns], ns.ap[:, 0:32], ns.ap[:, 0:32], -1.0, -NEG, ALU.add, ALU.mult)
                Pt = score_banks[cnt["sb"] % NSB]
                cnt["sb"] += 1
                S.tr(Pt.ap[:, 0:128], ns.ap, ident.ap, [ns, ident], [Pt])
                S.act(qi.ap[64:96, g, 0, :, :], Pt.ap[64:96, 0:128].unsqueeze(1).to_broadcast([32, 2, 128]), AF.Copy, [Pt], [qi.sub(("m", g))])
                S.act(qi.ap[0:32, g, 1, :, :], Pt.ap[0:32, 0:128].unsqueeze(1).to_broadcast([32, 2, 128]), AF.Copy, [Pt], [qi.sub(("m", g))])
                S.v("dve", "tensor_tensor", [rl, gates], [rl], rl.ap[:, 0:4], rl.ap[:, 0:4], gv[:, :, 0], ALU.mult)
                S.v("dve", "tensor_tensor", [Poc, rl], [oacc.sub(g)], osl, pc3[:, :, 0:64],
                    rl.ap[:, 0:4].unsqueeze(2).to_broadcast([128, 4, 64]), ALU.mult)
            extra = (Sc.ap[:, i, 0:127], Mst.ap[:, 4 * g:4 * g + 4, :], [Sc, Mst])
            jobs.append(score_job(qi, g, kc.ap[:, g, 0, 0:127], kc.ap[:, g, 1, 0:127], 127, extra, None, pv_c, [kc]))
            if g == 1:
                jobs.append((load_next, None))

            def mk_pv(Pacc, vv, j, first, last, br, g=g, w=w, gv=gv, osl=osl, oacc=oacc):
                def pv(px):
                    for hs in range(4):
                        S.mm(Pacc.ap[:, hs * 65:(hs + 1) * 65], px.ap[:, hs * 128:(hs + 1) * 128], vv.ap[:, j, g, :],
                             first and hs == 0, last and hs == 3, [px, vv], [Pacc])
                    if last:
                        p3 = Pacc.ap[:, 0:260].rearrange("p (h c) -> p h c", h=4)
                        rl = w["rl"]
                        o = 4 * br
                        S.v("dve", "tensor_scalar", [Pacc], [rl], rl.ap[:, o:o + 4], p3[:, :, 64], 1e-30, None, ALU.max)
                        S.v("dve", "reciprocal", [rl], [rl], rl.ap[:, o:o + 4], rl.ap[:, o:o + 4])
                        S.v("dve", "tensor_tensor", [rl, gates], [rl], rl.ap[:, o:o + 4], rl.ap[:, o:o + 4], gv[:, :, br], ALU.mult)
                        tmp = w["tmp"]
                        t3 = tmp.ap.rearrange("p (h d) -> p h d", h=4)
                        S.v("dve", "tensor_tensor", [Pacc, rl], [tmp], t3, p3[:, :, 0:64],
                            rl.ap[:, o:o + 4].unsqueeze(2).to_broadcast([128, 4, 64]), ALU.mult)
                        S.v("dve", "tensor_tensor", [tmp, oacc.sub(g)], [oacc.sub(g)], osl, osl, t3, ALU.add)
                return pv

            js = list(range(max(0, i - 4), i + 1))
            for j in js:
                dd = {0: 0, 1: 1, 4: 2}.get(i - j)
                bias_ap = None if dd is None else biasT.ap[:, dd, 4 * g:4 * g + 4, :].rearrange("p h t -> p (h t)")
                jobs.append(score_job(qi, g, kw.ap[:, g, 0, j * 128:(j + 1) * 128], kw.ap[:, g, 1, j * 128:(j + 1) * 128],
                                      128, None, bias_ap, mk_pv(Pow_, vw, j, j == js[0], j == js[-1], 2), [kw]))
            for j in range(i + 1):
                dd = {0: 0, 1: 1}.get(i - j)
                bias_ap = None if dd is None else biasT.ap[:, dd, 4 * g:4 * g + 4, :].rearrange("p h t -> p (h t)")
                jobs.append(score_job(qi, g, ks.ap[:, g, 0, j * 128:(j + 1) * 128], ks.ap[:, g, 1, j * 128:(j + 1) * 128],
                                      128, None, bias_ap, mk_pv(Pos_, vs, j, j == 0, j == i, 1), [ks], use_mask=True))

        def finish(i=i, oacc=oacc):
            ob = obst[i % 2]
            for half in range(2):
                P = score_banks[cnt["sb"] % NSB]
                cnt["sb"] += 1
                for q in range(4):
                    kt = half * 4 + q
                    S.tr(P.ap[:, q * 128:(q + 1) * 128], oacc.ap[:, kt * 128:(kt + 1) * 128], ident.ap,
                         [oacc.sub(kt // 2), ident], [P])
                S.act(ob.ap[:, half * 4:(half + 1) * 4, :], P.ap.rearrange("p (q t) -> p q t", q=4), AF.Copy, [P], [ob])
            S.dma("sp", obT_v[:, :, i * 128:(i + 1) * 128], ob.ap, [ob], [self.obT_d])
        jobs.append((None, finish))

    pend = []
    for (qk, pv) in jobs:
        if len(pend) >= 4:
            f = pend.pop(0)
            if f is not None:
                f()
        if qk is not None:
            qk()
        pend.append(pv)
    for f in pend:
        if f is not None:
            f()
    S.barrier()
    ar.pop()


KB.stage4 = stage4


def nsa_bias_build(self):
    ar, S = self.ar, self.S
    d = self.din
    NOH = 3 * 16384 + 17 * 128
    oh_d = d("nsa_oh", [33, NOH])
    relb_d = d("rel_bias", [32, 16])
    self.bias_d = bias_d = self.dscr("bias_scr", [16, NOH])
    trel = ar.alloc([16], F32, "trel", parts=33)
    tbl = ar.alloc([16], F32, "tbl", parts=32)
    t31 = ar.alloc([16], F32, "t31", parts=32)
    S.dma("sp", tbl.ap, relb_d.ap, [relb_d], [tbl])
    S.dma("sp", t31.ap, relb_d.ap[31:32, :].partition_broadcast(32), [relb_d], [t31])
    S.v("dve", "memset", [], [trel], trel.ap, NEG)
    S.v("dve", "tensor_tensor", [tbl, t31, trel], [trel], trel.ap[0:32, :], tbl.ap, t31.ap, ALU.subtract)
    ohb = [ar.alloc([2048], F32, f"ohb{i}", parts=33) for i in range(2)]
    bsb = [ar.alloc([2048], F32, f"bsb{i}", parts=16) for i in range(2)]
    nblk = (NOH + 2047) // 2048

    def mk(bi):
        def step():
            c0 = bi * 2048
            n = min(2048, NOH - c0)
            ob, bs = ohb[bi % 2], bsb[bi % 2]
            S.dma("sp", ob.ap[:, 0:n], oh_d.ap[:, c0:c0 + n], [oh_d], [ob])
            for q in range((n + 511) // 512):
                w = min(512, n - q * 512)
                P = self.bank()
                S.mm(P.ap[0:16, 0:w], trel.ap, ob.ap[:, q * 512:q * 512 + w], True, True, [trel, ob], [P])
                S.act(bs.ap[:, q * 512:q * 512 + w], P.ap[0:16, 0:w], AF.Copy, [P], [bs])
            S.dma("sp", bias_d.ap[:, c0:c0 + n], bs.ap[:, 0:n], [bs], [bias_d])
        return step
    return [mk(bi) for bi in range(nblk)]


KB.nsa_bias_build = nsa_bias_build


LAM = 0.6065306597126334


def rwkv_consts():
    c = {}
    p = np.arange(128)
    ut_strict = (p[:, None] < p[None, :]).astype(np.float32)
    ut_incl = (p[:, None] <= p[None, :]).astype(np.float32)
    c["rw_mAB"] = np.ascontiguousarray(np.concatenate([ut_strict, ut_incl], 1))
    c["rw_mLT"] = (p[:, None] > p[None, :]).astype(np.float32)
    rs = np.ones((128, 8, 128), np.float32)
    rs[:, :, 0] = 0.0
    c["rw_reset"] = rs.reshape(128, 1024)
    hm = np.zeros((128, 2), np.float32)
    hm[:64, 0] = 1.0
    hm[64:, 1] = 1.0
    c["rw_hsel"] = hm
    return c


def rwkv_prep(inp, m):
    g = lambda k: inp[k][0]
    m["rw_w0"] = _fm(g("rwkv_w0"))
    m["rw_a0"] = _fm(g("rwkv_a0"))
    m["rw_kk"] = _fm(g("rwkv_k_k"))
    m["rw_ka"] = _fm(g("rwkv_k_a"))
    m["rw_rk"] = _fm(g("rwkv_r_k").reshape(-1))
    m["rw_lnw"] = np.ascontiguousarray(np.broadcast_to(g("rwkv_ln_w")[None, :], (128, 1024)).astype(np.float32))
    m["rw_lnb"] = np.ascontiguousarray(np.broadcast_to(g("rwkv_ln_b")[None, :], (128, 1024)).astype(np.float32))
    z = np.zeros((64, 1024), np.float32)
    m["rw_w2pad"] = np.ascontiguousarray(np.concatenate([g("rwkv_w2"), z], 0))
    m["rw_a2pad"] = np.ascontiguousarray(np.concatenate([z, g("rwkv_a2")], 0))
    m["rw_g2"] = g("rwkv_g2")


def stage3(self):
    ar, S = self.ar, self.S
    d = self.din
    ident, bones = self.ident, self.bones
    self.oaT_d = self.dscr("oaT", [1024, SEQ], BF16)
    ar.push()

    def cload(name, shape, parts=128, src=None):
        dt_ = d(name, [parts] + list(shape)) if src is None else src
        t = ar.alloc(shape, F32, name, parts=parts)
        S.dma("sp", t.ap, dt_.ap, [dt_], [t])
        return t
    w0 = cload("rw_w0", [8])
    a0 = cload("rw_a0", [8])
    kkf = cload("rw_kk", [8])
    kaf = cload("rw_ka", [8])
    rkf = cload("rw_rk", [8])
    lnw = cload("rw_lnw", [1024])
    lnb = cload("rw_lnb", [1024])
    w2p = cload("rw_w2pad", [1024])
    a2p = cload("rw_a2pad", [1024])
    g2_d = d("rw_g2", [160, 1024])
    g2a = ar.alloc([1024], F32, "g2a")
    g2b = ar.alloc([1024], F32, "g2b", parts=32)
    S.dma("sp", g2a.ap, g2_d.ap[0:128, :], [g2_d], [g2a])
    S.dma("sp", g2b.ap, g2_d.ap[128:160, :], [g2_d], [g2b])
    mAB = cload("rw_mAB", [256])
    mLT = cload("rw_mLT", [128])
    reset = cload("rw_reset", [1024])
    hsel = cload("rw_hsel", [2])
    omka = ar.alloc([8], F32, "omka")
    S.v("dve", "tensor_scalar", [kaf], [omka], omka.ap, kaf.ap, -1.0, 1.0, ALU.mult, ALU.add)
    Hp = ar.alloc([16, 64], F32, "Hp")
    S.v("dve", "memset", [], [Hp], Hp.ap, 0.0)

    A = lambda n, shape=(8, 128): ar.alloc(list(shape), F32, n)
    raw = A("raw", (27, 128))
    sgw, cs, E, Eex = A("sgw"), A("cs"), A("E"), A("Eex")
    aT, kkn, kp, bb, btl = A("aT"), A("kkn"), A("kp"), A("bb"), A("btl")
    AR = A("AR", (8, 2, 128))
    blo, bhi, klo, khi, alo, ahi = A("blo"), A("bhi"), A("klo"), A("khi"), A("alo"), A("ahi")
    tmpA, tmpB = A("tmpA"), A("tmpB")
    bh, kh = sgw, cs
    v_tok, bh_tok, kh_tok, g_tok = A("v_tok", (1024,)), A("bh_tok", (1024,)), A("kh_tok", (1024,)), A("g_tok", (1024,))
    th = A("th", (128,))
    sx = A("sx", (128,))
    sx2 = ar.alloc([128], F32, "sx2", parts=32)
    PLs = [A("PL0", (8,)), A("PL1", (8,))]
    nb = A("nb", (8,))
    rk16 = A("rk16", (16,))
    st16 = [A(f"st16_{i}", (16,)) for i in range(4)]
    import os
    SQDT = BF16 if os.environ.get("RW_SQ") == "bf16" else F32
    MASK_POOL = os.environ.get("RW_MASK") == "pool"
    NO_IL = os.environ.get("RW_IL") == "0"
    slots = []
    for s_ in range(2):
        slots.append(dict(
            ABm=A(f"ABm{s_}", (4, 256)), AKm=A(f"AKm{s_}", (4, 256)),
            Yf=[A(f"Yf{s_}{i}", (4, 128)) for i in range(2)] if SQDT != F32 else None,
            Yb=[ar.alloc([4, 128], SQDT, f"Yb{s_}{i}") for i in range(2)],
            XW=[A(f"XW{s_}{i}", (4, 192)) for i in range(2)]))
        if SQDT == F32:
            slots[-1]["Yf"] = slots[-1]["Yb"]
    if SQDT == F32:
        ar.off -= 0
    oast = [ar.alloc([8, 128], BF16, f"oast{i}") for i in range(1)] * 2
    f2 = lambda t: t.ap.rearrange("p a b -> p (a b)")
    rw_v = self.rwT_d.ap.rearrange("(kt p) t -> p kt t", p=128)
    oaT_v = self.oaT_d.ap.rearrange("(kt p) t -> p kt t", p=128)
    dv = lambda meth, reads, writes, *a, **k: S.v("dve", meth, reads, writes, *a, **k)
    pl = lambda meth, reads, writes, *a, **k: S.v("pool", meth, reads, writes, *a, **k)
    bc8 = lambda t: t.ap.unsqueeze(2).to_broadcast([128, 8, 128])

    def P1(c):
            S.dma("sp", raw.ap, rw_v[:, :, c * 128:(c + 1) * 128], [self.rwT_d], [raw])
            yield
            rT, kT, vT = raw.ap[:, 0:8, :], raw.ap[:, 8:16, :], raw.ap[:, 16:24, :]
            yield
            t24 = raw.ap[:, 24, :]
            yield
            S.act(th.ap, t24, AF.Tanh, [raw], [th])
            yield
            Pz = [self.bank(), self.bank()]
            yield
            for kt in range(8):
                P = Pz[kt // 4]
                S.mm(P.ap[:, (kt % 4) * 128:(kt % 4 + 1) * 128], w2p.ap[:, kt * 128:(kt + 1) * 128], th.ap, True, True, [w2p, th], [P])
            yield
            for kt in range(8):
                S.act(sgw.ap[:, kt, :], Pz[kt // 4].ap[:, (kt % 4) * 128:(kt % 4 + 1) * 128], AF.Sigmoid, [Pz[kt // 4], w0], [sgw],
                      bias=w0.ap[:, kt:kt + 1], scale=1.0)
            yield
            Pa = [self.bank(), self.bank()]
            yield
            for kt in range(8):
                P = Pa[kt // 4]
                S.mm(P.ap[:, (kt % 4) * 128:(kt % 4 + 1) * 128], a2p.ap[:, kt * 128:(kt + 1) * 128], t24, True, True, [a2p, raw], [P])
            yield
            for kt in range(8):
                S.act(aT.ap[:, kt, :], Pa[kt // 4].ap[:, (kt % 4) * 128:(kt % 4 + 1) * 128], AF.Sigmoid, [Pa[kt // 4], a0], [aT],
                      bias=a0.ap[:, kt:kt + 1], scale=1.0)
            yield
            dv("tensor_tensor_scan", [reset, sgw], [cs], f2(cs), reset.ap, f2(sgw), 0.0, ALU.mult, ALU.add)
            yield
            S.act(f2(E), f2(cs), AF.Exp, [cs], [E], scale=-LAM)
            yield
            dv("tensor_tensor", [cs, sgw], [Eex], f2(Eex), f2(cs), f2(sgw), ALU.subtract)
            yield
            S.act(f2(Eex), f2(Eex), AF.Exp, [Eex], [Eex], scale=-LAM)
            yield
            dv("tensor_scalar", [cs], [nb], nb.ap, cs.ap[:, :, 127], -LAM, None, ALU.mult)
            yield
            S.act(PLs[c % 2].ap, nb.ap, AF.Exp, [nb], [PLs[c % 2]])
            yield
            dv("tensor_tensor", [raw, kkf], [kkn], kkn.ap, kT, bc8(kkf), ALU.mult)
            yield
            dv("tensor_tensor", [kkn], [tmpB], tmpB.ap, kkn.ap, kkn.ap, ALU.mult)
            yield
            Pn = [self.bank(), self.bank()]
            yield
            for hh in range(2):
                S.mm(Pn[hh].ap, bones.ap, tmpB.ap[:, hh * 4:(hh + 1) * 4, :], True, True, [bones, tmpB], [Pn[hh]])
            yield
            for hh in range(2):
                S.act(tmpB.ap[:, hh * 4:(hh + 1) * 4, :], Pn[hh].ap.rearrange("p (a b) -> p a b", a=4), AF.Sqrt, [Pn[hh]], [tmpB])
            yield
            dv("tensor_scalar", [tmpB], [tmpB], f2(tmpB), f2(tmpB), 1e-12, None, ALU.max)
            yield
            dv("reciprocal", [tmpB], [tmpB], f2(tmpB), f2(tmpB))
            yield
            dv("tensor_tensor", [kkn, tmpB], [kkn], f2(kkn), f2(kkn), f2(tmpB), ALU.mult)
            yield
            dv("tensor_tensor", [aT, kaf], [kp], kp.ap, aT.ap, bc8(kaf), ALU.mult)
            yield
            dv("tensor_tensor", [kp, omka], [kp], kp.ap, kp.ap, bc8(omka), ALU.add)
            yield
            dv("tensor_tensor", [kp, raw], [kp], kp.ap, kp.ap, kT, ALU.mult)
            yield
            dv("tensor_tensor", [kkn, aT], [bb], f2(bb), f2(kkn), f2(aT), ALU.mult)
            yield


    def P2(c):
            rT, kT, vT = raw.ap[:, 0:8, :], raw.ap[:, 8:16, :], raw.ap[:, 16:24, :]
            dv("tensor_tensor", [raw, E], [AR], AR.ap[:, :, 1, :], rT, E.ap, ALU.mult)
            dv("scalar_tensor_tensor", [kkn, Eex], [AR], AR.ap[:, :, 0, :], kkn.ap, -1.0, Eex.ap, ALU.mult, ALU.mult)
            Einv, Elast = E, Eex
            S.act(f2(Einv), f2(cs), AF.Exp, [cs], [Einv], scale=LAM)
            for kt in range(8):
                S.act(Elast.ap[:, kt, :], cs.ap[:, kt, :], AF.Exp, [cs, nb], [Elast], bias=nb.ap[:, kt:kt + 1], scale=LAM)
            dv("tensor_tensor", [bb, Einv], [btl], f2(btl), f2(bb), f2(Einv), ALU.mult)
            dv("tensor_tensor", [kp, Einv], [tmpA], f2(tmpA), f2(kp), f2(Einv), ALU.mult)
            dv("tensor_tensor", [bb, Elast], [bh], f2(bh), f2(bb), f2(Elast), ALU.mult)
            dv("tensor_tensor", [kp, Elast], [kh], f2(kh), f2(kp), f2(Elast), ALU.mult)
            for (dst, src, col) in ((blo, btl, 0), (bhi, btl, 1), (klo, tmpA, 0), (khi, tmpA, 1)):
                if MASK_POOL:
                    pl("tensor_scalar", [src, hsel], [dst], f2(dst), f2(src), hsel.ap[:, col:col + 1], None, ALU.mult)
                    continue
                S.act(f2(dst), f2(src), AF.Identity, [src, hsel], [dst], scale=hsel.ap[:, col:col + 1], bias=0.0)
            for (dst, col) in ((alo, 0), (ahi, 1)):
                if MASK_POOL:
                    pl("tensor_scalar", [AR, hsel], [dst], dst.ap, AR.ap[:, :, 0, :], hsel.ap[:, col:col + 1], None, ALU.mult)
                    continue
                S.act(dst.ap, AR.ap[:, :, 0, :], AF.Identity, [AR, hsel], [dst], scale=hsel.ap[:, col:col + 1], bias=0.0)
            dv("tensor_tensor", [raw, kp], [tmpB], tmpB.ap, rT, kp.ap, ALU.mult)
            dv("tensor_tensor", [tmpB, rkf], [tmpB], tmpB.ap, tmpB.ap, bc8(rkf), ALU.mult)
            Pr = self.bank()
            for kt in range(8):
                S.mm(Pr.ap[:, 2 * kt:2 * kt + 2], tmpB.ap[:, kt, :], hsel.ap, kt == 0, kt == 7, [tmpB, hsel], [Pr])
            S.act(rk16.ap, Pr.ap[:, 0:16], AF.Copy, [Pr], [rk16])
            S.act(sx.ap, raw.ap[:, 25, :], AF.Sigmoid, [raw], [sx])
            S.act(sx2.ap, raw.ap[0:32, 26, :], AF.Sigmoid, [raw], [sx2])
            for hh in range(2):
                P = self.bank()
                S.mm(P.ap, sx.ap, g2a.ap[:, hh * 512:(hh + 1) * 512], True, False, [sx, g2a], [P])
                S.mm(P.ap, sx2.ap, g2b.ap[:, hh * 512:(hh + 1) * 512], False, True, [sx2, g2b], [P])
                S.act(g_tok.ap[:, hh * 512:(hh + 1) * 512], P.ap, AF.Copy, [P], [g_tok])
            for (src_ap, src_t, dst) in ((vT, raw, v_tok), (bh.ap, bh, bh_tok), (kh.ap, kh, kh_tok)):
                for hh in range(2):
                    P = self.bank()
                    for q in range(4):
                        kt = hh * 4 + q
                        S.tr(P.ap[:, q * 128:(q + 1) * 128], src_ap[:, kt, :], ident.ap, [src_t, ident], [P])
                    S.act(dst.ap[:, hh * 512:(hh + 1) * 512], P.ap, AF.Copy, [P], [dst])


    def heads(c, step):
            y_tok = tmpA
            def phaseA(hg, sl):
                heads = [4 * hg + x for x in range(4)]
                ABm, AKm = sl["ABm"], sl["AKm"]
                PA = [self.bank(), self.bank()]
                PB = [self.bank(), self.bank()]
                PX = self.bank()
                for hl, h in enumerate(heads):
                    kt, half = h // 2, h % 2
                    bsel = (blo, bhi)[half]
                    ksel = (klo, khi)[half]
                    asel = (alo, ahi)[half]
                    ar_rhs = AR.ap[:, kt, :, :]
                    oa = PA[hl // 2].ap[:, (hl % 2) * 256:(hl % 2 + 1) * 256]
                    ob_ = PB[hl // 2].ap[:, (hl % 2) * 256:(hl % 2 + 1) * 256]
                    S.mm(oa, bsel.ap[:, kt, :], ar_rhs, hl % 2 == 0, hl % 2 == 1, [bsel, AR], [PA[hl // 2]])
                    S.mm(ob_, ksel.ap[:, kt, :], ar_rhs, hl % 2 == 0, hl % 2 == 1, [ksel, AR], [PB[hl // 2]])
                    S.mm(PX.ap[:, hl * 128:(hl + 1) * 128], asel.ap[:, kt, :], btl.ap[:, kt, :], hl == 0, hl == 3, [asel, btl], [PX])
                mAB2 = mAB.ap.unsqueeze(1).to_broadcast([128, 2, 256])
                for q in range(2):
                    dv("tensor_tensor", [PA[q], mAB], [ABm], ABm.ap[:, 2 * q:2 * q + 2, :],
                       PA[q].ap.rearrange("p (a b) -> p a b", a=2), mAB2, ALU.mult)
                    dv("tensor_tensor", [PB[q], mAB], [AKm], AKm.ap[:, 2 * q:2 * q + 2, :],
                       PB[q].ap.rearrange("p (a b) -> p a b", a=2), mAB2, ALU.mult)
                dv("tensor_tensor", [PX, mLT], [sl["XW"][0]], sl["XW"][0].ap[:, :, 0:128], PX.ap.rearrange("p (a b) -> p a b", a=4),
                   mLT.ap.unsqueeze(1).to_broadcast([128, 4, 128]), ALU.mult)
                S.act(sl["Yb"][0].ap, ABm.ap[:, :, 0:128], AF.Copy, [ABm], [sl["Yb"][0]])
                PW = self.bank()
                for hl, h in enumerate(heads):
                    kt = h // 2
                    o = PW.ap[:, hl * 64:(hl + 1) * 64]
                    S.mm(o, AR.ap[:, kt, 0, :], Hp.ap[:, h, :], hl == 0, False, [AR, Hp.sub(h)], [PW])
                    S.mm(o, AKm.ap[:, hl, 0:128], v_tok.ap[:, h * 64:(h + 1) * 64], False, hl == 3, [AKm, v_tok], [PW])
                S.act(sl["XW"][0].ap[:, :, 128:192], PW.ap[:, 0:256].rearrange("p (h d) -> p h d", h=4), AF.Copy, [PW], [sl["XW"][0]])

            def level(sl, lv):
                Y, XW = sl["Yb"][lv % 2], sl["XW"][lv % 2]
                Yn, XWn = sl["Yb"][(lv + 1) % 2], sl["XW"][(lv + 1) % 2]
                if lv < 6:
                    PUX = [self.bank(), self.bank()]
                    for hl in range(4):
                        o = PUX[hl // 2].ap[:, (hl % 2) * 192:(hl % 2 + 1) * 192]
                        S.mm(o, Y.ap[:, hl, :], XW.ap[:, hl, :], hl % 2 == 0, hl % 2 == 1, [Y, XW], [PUX[hl // 2]])
                    PY2 = self.bank()
                    for hl in range(4):
                        S.mm(PY2.ap[:, hl * 128:(hl + 1) * 128], XW.ap[:, hl, 0:128], Y.ap[:, hl, :], hl == 0, hl == 3, [XW, Y], [PY2])
                    for q in range(2):
                        view = PUX[q].ap[:, 0:384].rearrange("p (a c) -> p a c", a=2)
                        dv("tensor_tensor", [PUX[q], XW], [XWn], XWn.ap[:, 2 * q:2 * q + 2, 128:192], view[:, :, 128:192],
                           XW.ap[:, 2 * q:2 * q + 2, 128:192], ALU.add)
                        S.act(XWn.ap[:, 2 * q:2 * q + 2, 0:128], view[:, :, 0:128], AF.Copy, [PUX[q]], [XWn])
                    dv("tensor_copy", [PY2], [Yn], f2(Yn), PY2.ap)
                else:
                    PU = self.bank()
                    for hl in range(4):
                        S.mm(PU.ap[:, hl * 64:(hl + 1) * 64], Y.ap[:, hl, :], XW.ap[:, hl, 128:192], hl == 0, hl == 3, [Y, XW], [PU])
                    dv("tensor_tensor", [PU, XW], [XWn], XWn.ap[:, :, 128:192], PU.ap[:, 0:256].rearrange("p (h d) -> p h d", h=4),
                       XW.ap[:, :, 128:192], ALU.add)

            def phaseY(hg, sl):
                heads = [4 * hg + x for x in range(4)]
                ABm, AKm = sl["ABm"], sl["AKm"]
                U = sl["XW"][1]
                PY = self.bank()
                for hl, h in enumerate(heads):
                    kt = h // 2
                    o = PY.ap[:, hl * 64:(hl + 1) * 64]
                    S.mm(o, AR.ap[:, kt, 1, :], Hp.ap[:, h, :], hl == 0, False, [AR, Hp.sub(h)], [PY])
                    S.mm(o, ABm.ap[:, hl, 128:256], U.ap[:, hl, 128:192], False, False, [ABm, U], [PY])
                    S.mm(o, AKm.ap[:, hl, 128:256], v_tok.ap[:, h * 64:(h + 1) * 64], False, hl == 3, [AKm, v_tok], [PY])
                PH = self.bank()
                for hl, h in enumerate(heads):
                    kt = h // 2
                    o = PH.ap[:, hl * 64:(hl + 1) * 64]
                    S.mm(o, bh_tok.ap[:, kt * 128:(kt + 1) * 128], U.ap[:, hl, 128:192], hl == 0, False, [bh_tok, U], [PH])
                    S.mm(o, kh_tok.ap[:, kt * 128:(kt + 1) * 128], v_tok.ap[:, h * 64:(h + 1) * 64], False, hl == 3, [kh_tok, v_tok], [PH])
                S.act(y_tok.ap.rearrange("p a b -> p (a b)")[:, hg * 256:(hg + 1) * 256], PY.ap[:, 0:256], AF.Copy, [PY], [y_tok])
                for hl, h in enumerate(heads):
                    kt, half = h // 2, h % 2
                    r0 = 64 * half
                    dv("scalar_tensor_tensor", [Hp.sub(h), PLs[c % 2], PH], [Hp.sub(h)], Hp.ap[r0:r0 + 64, h, :], Hp.ap[r0:r0 + 64, h, :],
                       PLs[c % 2].ap[r0:r0 + 64, kt:kt + 1], PH.ap[r0:r0 + 64, hl * 64:(hl + 1) * 64], ALU.mult, ALU.add)

            for pair in range(2):
                gA, gB = 2 * pair, 2 * pair + 1
                if NO_IL:
                    for (g_, sl_) in ((gA, slots[0]), (gB, slots[1])):
                        phaseA(g_, sl_)
                        for lv in range(7):
                            level(sl_, lv)
                        phaseY(g_, sl_)
                    continue
                phaseA(gA, slots[0])
                phaseA(gB, slots[1])
                for lv in range(7):
                    level(slots[0], lv)
                    step(); step()
                    level(slots[1], lv)
                    step(); step()
                phaseY(gA, slots[0])
                phaseY(gB, slots[1])


    def post(c):
            y_tok = tmpA
            yf = y_tok.ap.rearrange("p a b -> p (a b)")
            y3 = yf.rearrange("p (h d) -> p h d", h=16)
            sum_, sq_, mean, rstd = st16
            t1, t2 = y_tok, btl
            t1f, t2f = f2(t1), f2(t2)
            dv("tensor_reduce", [y_tok], [sum_], sum_.ap, y3, AX.X, ALU.add)
            dv("tensor_tensor", [y_tok], [t2], t2f, yf, yf, ALU.mult)
            dv("tensor_reduce", [t2], [sq_], sq_.ap, t2f.rearrange("p (h d) -> p h d", h=16), AX.X, ALU.add)
            dv("tensor_scalar", [sum_], [mean], mean.ap, sum_.ap, 1.0 / 64, None, ALU.mult)
            dv("tensor_tensor", [mean], [rstd], rstd.ap, mean.ap, mean.ap, ALU.mult)
            dv("scalar_tensor_tensor", [sq_, rstd], [rstd], rstd.ap, sq_.ap, 1.0 / 64, rstd.ap, ALU.mult, ALU.subtract)
            S.act(rstd.ap, rstd.ap, AF.Sqrt, [rstd], [rstd], bias=GN_EPS, scale=1.0)
            dv("reciprocal", [rstd], [rstd], rstd.ap, rstd.ap)
            b16 = lambda t: t.ap.unsqueeze(2).to_broadcast([128, 16, 64])
            t13 = t1f.rearrange("p (h d) -> p h d", h=16)
            t23 = t2f.rearrange("p (h d) -> p h d", h=16)
            dv("tensor_tensor", [y_tok, mean], [t1], t13, y3, b16(mean), ALU.subtract)
            dv("tensor_tensor", [t1, rstd], [t1], t13, t13, b16(rstd), ALU.mult)
            dv("tensor_tensor", [t1, lnw], [t1], t1f, t1f, lnw.ap, ALU.mult)
            dv("tensor_tensor", [t1, lnb], [t1], t1f, t1f, lnb.ap, ALU.add)
            dv("tensor_tensor", [v_tok, rk16], [t2], t23, v_tok.ap.rearrange("p (h d) -> p h d", h=16), b16(rk16), ALU.mult)
            dv("tensor_tensor", [t1, t2], [t1], t1f, t1f, t2f, ALU.add)
            dv("tensor_tensor", [t1, g_tok], [t1], t1f, t1f, g_tok.ap, ALU.mult)
            if self.dbg and c == 0:
                dd = self.dscr("dbg_oa0", [128, 1024])
                S.dma("sp", dd.ap, t1f, [t1], [dd])
                dd2 = self.dscr("dbg_y0", [128, 1024])
                S.dma("sp", dd2.ap, yf, [y_tok], [dd2])
            ob = oast[c % 2]
            for hh in range(2):
                P = self.bank()
                for q in range(4):
                    kt = hh * 4 + q
                    S.tr(P.ap[:, q * 128:(q + 1) * 128], t1f[:, kt * 128:(kt + 1) * 128], ident.ap, [t1, ident], [P])
                S.act(ob.ap[:, hh * 4:(hh + 1) * 4, :], P.ap.rearrange("p (q t) -> p q t", q=4), AF.Copy, [P], [ob])
            S.dma("sp", oaT_v[:, :, c * 128:(c + 1) * 128], ob.ap, [ob], [self.oaT_d])


    gen = P1(0)
    for _ in gen:
        pass
    for c in range(NT):
        P2(c)
        gen = P1(c + 1) if c + 1 < NT else iter(())

        def step(gen=gen):
            next(gen, None)
        heads(c, step)
        for _ in gen:
            pass
        post(c)
    S.barrier()
    ar.pop()


KB.stage3 = stage3


def stage5(self):
    ar, S = self.ar, self.S
    d = self.din
    wor_d = d("w_o_rwkv", [1024, D])
    won_d = d("w_o_nsa", [1024, D])
    wout_d = d("w_out", [D, D])
    ar.push()
    mixT = ar.alloc([16, SEQ], BF16, "mixT")
    ar.push()
    oaT = ar.alloc([8, SEQ], BF16, "oaT")
    obT = ar.alloc([8, SEQ], BF16, "obT")
    S.dma("sp", oaT.ap, self.oaT_d.ap.rearrange("(k p) t -> p k t", p=128), [self.oaT_d], [oaT])
    S.dma("sp", obT.ap, self.obT_d.ap.rearrange("(k p) t -> p k t", p=128), [self.obT_d], [obT])
    woa = [ar.alloc([8, 128], BF16, f"woa{i}") for i in range(2)]
    wob = [ar.alloc([8, 128], BF16, f"wob{i}") for i in range(2)]
    sga = [ar.alloc([SEQ], BF16, f"sga{i}") for i in range(2)]
    sgb = [ar.alloc([SEQ], BF16, f"sgb{i}") for i in range(2)]
    t1s = [ar.alloc([512], F32, f"t1_{i}") for i in range(2)]
    t2s = [ar.alloc([512], F32, f"t2_{i}") for i in range(2)]
    wor_v = wor_d.ap.rearrange("(k p) c -> p k c", p=128)
    won_v = won_d.ap.rearrange("(k p) c -> p k c", p=128)
    cnt = 0
    mod_steps = []
    for jt in range(16):
        wa, wb, sa, sb = woa[jt % 2], wob[jt % 2], sga[jt % 2], sgb[jt % 2]
        S.dma("pool", wa.ap, wor_v[:, :, jt * 128:(jt + 1) * 128], [wor_d], [wa])
        S.dma("pool", wb.ap, won_v[:, :, jt * 128:(jt + 1) * 128], [won_d], [wb])
        S.dma("sp", sa.ap, self.mgT_d.ap[jt * 128:(jt + 1) * 128, :], [self.mgT_d], [sa])
        S.dma("sp", sb.ap, self.mgT_d.ap[2048 + jt * 128:2048 + (jt + 1) * 128, :], [self.mgT_d], [sb])
        if jt >= 1:
            for _ in range(2):
                if mod_steps:
                    mod_steps.pop(0)()
        for n in range(4):
            Pa = self.bank()
            for k in range(8):
                S.mm(Pa.ap, wa.ap[:, k, :], oaT.ap[:, k, n * 512:(n + 1) * 512], k == 0, k == 7, [wa, oaT], [Pa])
            Pb = self.bank()
            for k in range(8):
                S.mm(Pb.ap, wb.ap[:, k, :], obT.ap[:, k, n * 512:(n + 1) * 512], k == 0, k == 7, [wb, obT], [Pb])
            t1, t2 = t1s[cnt % 2], t2s[cnt % 2]
            cnt += 1
            S.v("dve", "tensor_tensor", [Pa, sa], [t1], t1.ap, Pa.ap, sa.ap[:, n * 512:(n + 1) * 512], ALU.mult)
            S.v("dve", "tensor_tensor", [Pb, sb], [t2], t2.ap, Pb.ap, sb.ap[:, n * 512:(n + 1) * 512], ALU.mult)
            S.v("dve", "tensor_tensor", [t1, t2], [mixT.sub(n)], mixT.ap[:, jt, n * 512:(n + 1) * 512], t1.ap, t2.ap, ALU.add)
    while mod_steps:
        mod_steps.pop(0)()
    S.barrier()
    ar.pop()
    wout = ar.alloc([16, D], BF16, "wout")
    wout_v = wout_d.ap.rearrange("(k p) c -> p k c", p=128)
    for nn in range(4):
        S.dma("pool", wout.ap[:, :, nn * 512:(nn + 1) * 512], wout_v[:, :, nn * 512:(nn + 1) * 512], [wout_d], [wout.sub(nn)])
    xbs = [ar.alloc([D], F32, f"x5_{i}") for i in range(2)]
    obs = [ar.alloc([D], F32, f"o5_{i}") for i in range(2)]
    tms = [ar.alloc([512], F32, f"tm5_{i}") for i in range(2)]
    cnt = 0
    for tt in range(NT):
        xb, ob = xbs[tt % 2], obs[tt % 2]
        S.dma("sp", xb.ap, self.x_d.ap[tt * 128:(tt + 1) * 128, :], [self.x_d], [xb])
        for nn in range(4):
            P = self.bank()
            for k in range(16):
                S.mm(P.ap, mixT.ap[:, k, tt * 128:(tt + 1) * 128], wout.ap[:, k, nn * 512:(nn + 1) * 512], k == 0, k == 15,
                     [mixT.sub(tt // 4), wout.sub(nn)], [P])
            tm = tms[cnt % 2]
            cnt += 1
            S.v("dve", "tensor_tensor", [P, self.gt1], [tm], tm.ap, P.ap, self.gt1.ap[:, nn * 512:(nn + 1) * 512], ALU.mult)
            S.v("dve", "tensor_tensor", [tm, xb], [ob], ob.ap[:, nn * 512:(nn + 1) * 512], tm.ap, xb.ap[:, nn * 512:(nn + 1) * 512], ALU.add)
        S.dma("pool", self.x1_d.ap[tt * 128:(tt + 1) * 128, :], ob.ap, [ob], [self.x1_d])
    S.barrier()
    ar.pop()


def stage6(self):
    ar, S = self.ar, self.S
    d = self.din
    hT = self.hT
    ar.push()
    uT = ar.alloc([64, 512], BF16, "uT")
    wups = [ar.alloc([16, 256], BF16, f"wup{i}") for i in range(2)]
    wdns = [ar.alloc([8, 512], BF16, f"wdn{i}") for i in range(3)]
    rts = [ar.alloc([512], F32, f"rt{i}") for i in range(2)]
    xps = [ar.alloc([512], F32, f"xp{i}") for i in range(2)]
    ops_ = [ar.alloc([512], F32, f"op{i}") for i in range(2)]
    tms = [ar.alloc([512], F32, f"tm6_{i}") for i in range(2)]
    c_up = c_dn = c_e = 0
    for c in range(4):
        for fb in range(32):
            wu = wups[c_up % 2]
            c_up += 1
            S.dma("sp", wu.ap, self.wupb.ap[fb].rearrange("p (k c) -> p k c", k=16), [self.wupb], [wu])
            for ft in range(2):
                f = fb * 2 + ft
                P = self.ps[(c_e) % 4]
                rt = rts[c_e % 2]
                c_e += 1
                for k in range(16):
                    S.mm(P.ap, wu.ap[:, k, ft * 128:(ft + 1) * 128], hT.ap[:, k, c * 512:(c + 1) * 512], k == 0, k == 15,
                         [wu, hT.sub(c)], [P])
                S.act(rt.ap, P.ap, AF.Relu, [P], [rt])
                S.v("dve", "tensor_tensor", [rt], [uT.sub(f)], uT.ap[:, f, :], rt.ap, rt.ap, ALU.mult)
        for nn in range(4):
            accs = self.ps[4:8] if (nn % 2 == 0) else self.ps[0:4]
            for f8 in range(8):
                wd = wdns[c_dn % 3]
                c_dn += 1
                S.dma("sp", wd.ap, self.wdnb.ap[nn, f8].rearrange("p (f c) -> p f c", f=8), [self.wdnb], [wd])
                for fi in range(8):
                    f = f8 * 8 + fi
                    for tt in range(4):
                        S.mm(accs[tt].ap, uT.ap[:, f, tt * 128:(tt + 1) * 128], wd.ap[:, fi, :], f == 0, f == 63,
                             [uT.sub(f), wd], [accs[tt]])
            for tt in range(4):
                row = (c * 4 + tt) * 128
                xp, op, tm = xps[c_e % 2], ops_[c_e % 2], tms[c_e % 2]
                c_e += 1
                S.dma("pool", xp.ap, self.x1_d.ap[row:row + 128, nn * 512:(nn + 1) * 512], [self.x1_d], [xp])
                S.v("dve", "tensor_tensor", [accs[tt], self.gt2], [tm], tm.ap, accs[tt].ap, self.gt2.ap[:, nn * 512:(nn + 1) * 512], ALU.mult)
                S.v("dve", "tensor_tensor", [tm, xp], [op], op.ap, tm.ap, xp.ap, ALU.add)
                S.dma("pool", self.out_d.ap[row:row + 128, nn * 512:(nn + 1) * 512], op.ap, [op], [self.out_d])
    S.barrier()
    ar.pop()


KB.stage5 = stage5
KB.stage6 = stage6


def precast(self):
    S = self.S
    self.wup_in = self.din("w_up", [D, DFF])
    self.wdn_in = self.din("w_down", [DFF, D])
    self.wupb = T(self.nc.dram_tensor("wupb_i", [32, 128, 16 * 256], BF16).ap(), "wupb")
    self.wdnb = T(self.nc.dram_tensor("wdnb_i", [4, 8, 128, 8 * 512], BF16).ap(), "wdnb")
    wup_v = self.wup_in.ap.rearrange("(k p) c -> p k c", p=128)
    wdn_v = self.wdn_in.ap.rearrange("(f p) c -> p f c", p=128)
    for fb in range(32):
        S.dma("pool", self.wupb.ap[fb].rearrange("p (k c) -> p k c", k=16), wup_v[:, :, fb * 256:(fb + 1) * 256],
              [self.wup_in], [self.wupb])
    for nn in range(4):
        for f8 in range(8):
            S.dma("pool", self.wdnb.ap[nn, f8].rearrange("p (f c) -> p f c", f=8),
                  wdn_v[:, f8 * 8:(f8 + 1) * 8, nn * 512:(nn + 1) * 512], [self.wdn_in], [self.wdnb])


KB.precast = precast


_CACHE = {}


def _all_consts():
    c = host_consts()
    c.update(nsa_consts())
    c.update(rwkv_consts())
    return c


def prep_all(inp, b, consts):
    m = prep_core(inp, b, consts)
    nsa_prep(inp, m)
    rwkv_prep(inp, m)
    m["w_o_rwkv"] = inp["w_o_rwkv"][0]
    m["w_o_nsa"] = inp["w_o_nsa"][0]
    m["w_out"] = inp["w_out"][0]
    m["w_up"] = inp["w_up"][0]
    m["w_down"] = inp["w_down"][0]
    return m


def kernel(**inputs):
    inp = {k: np.asarray(v) for k, v in inputs.items()}
    if "nc" not in _CACHE:
        _CACHE["nc"] = KB(dbg=False).build()
        _CACHE["consts"] = _all_consts()
    nc = _CACHE["nc"]
    consts = _CACHE["consts"]
    in_maps = [prep_all(inp, b, consts) for b in range(8)]
    res = run_bass_kernel_spmd(nc, in_maps, core_ids=list(range(8)))
    out = np.stack([np.asarray(r["out"]) for r in res.results], axis=0)
    return out.astype(np.float32)
```
